# Optimizing a Trainium2 kernel written in Bass

```python
import math
import jax, jax.numpy as jnp
from jax import lax
import numpy as np

D_MODEL = 1024
BATCH = 2
SEQ = 16384
DEPTH = 1
DEC_BATCH = 16
DEC_SEQ = 16
PAST_LEN = 2048

CHUNK = 64
D_SSM = D_MODEL // 2
SSM_GROUP = 16
N_SSM_GROUPS = D_SSM // SSM_GROUP
SSM_STATE = 64
D_CONV = D_MODEL - D_SSM
CONV_WIDTH = 31
N_MEM = 256
MEM_HEADS = 4
MEM_HEAD_DIM = D_MODEL // MEM_HEADS
D_FF = 4 * D_MODEL
D_IN = D_SSM + 2 * D_CONV
DT_MIN = 1e-3
DT_MAX = 1e-1
LN_EPS = 1e-5
ALPHA = (2.0 * DEPTH) ** 0.25
BETA = (8.0 * DEPTH) ** -0.25

kernel_name = "hybrid_s5_conformer_stream_step"


def layer_norm(x, g, b):
    xf = x.astype(jnp.float32)
    mu = jnp.mean(xf, axis=-1, keepdims=True)
    var = jnp.mean(jnp.square(xf - mu), axis=-1, keepdims=True)
    y = (xf - mu) * lax.rsqrt(var + LN_EPS) * g.astype(jnp.float32) + b.astype(jnp.float32)
    return y.astype(x.dtype)


def s5_discretize(a_re, a_im, log_dt, b_re, b_im):
    f32 = jnp.float32
    a_re = a_re.astype(f32)
    a_im = a_im.astype(f32)
    b_re = b_re.astype(f32)
    b_im = b_im.astype(f32)
    dt = jnp.exp(log_dt.astype(f32))[:, None]
    mag = jnp.exp(a_re * dt)
    ang = a_im * dt
    ab_re = mag * jnp.cos(ang)
    ab_im = mag * jnp.sin(ang)
    den = a_re * a_re + a_im * a_im
    p = ab_re - 1.0
    q = ab_im
    c_re = ((p * a_re + q * a_im) / den)[..., None]
    c_im = ((q * a_re - p * a_im) / den)[..., None]
    bb_re = c_re * b_re - c_im * b_im
    bb_im = c_re * b_im + c_im * b_re
    return ab_re, ab_im, bb_re, bb_im


def s5_block(h0_re, h0_im, u, ab_re, ab_im, bb_re, bb_im, c_re, c_im):
    bu_re = jnp.einsum("gnp,btgp->btgn", bb_re, u)
    bu_im = jnp.einsum("gnp,btgp->btgn", bb_im, u)
    a_re = jnp.broadcast_to(ab_re, bu_re.shape)
    a_im = jnp.broadcast_to(ab_im, bu_im.shape)

    def combine(e1, e2):
        a1r, a1i, b1r, b1i = e1
        a2r, a2i, b2r, b2i = e2
        return (a2r * a1r - a2i * a1i,
                a2r * a1i + a2i * a1r,
                a2r * b1r - a2i * b1i + b2r,
                a2r * b1i + a2i * b1r + b2i)

    pw_re, pw_im, hz_re, hz_im = lax.associative_scan(combine, (a_re, a_im, bu_re, bu_im), axis=1)
    h_re = hz_re + pw_re * h0_re[:, None] - pw_im * h0_im[:, None]
    h_im = hz_im + pw_re * h0_im[:, None] + pw_im * h0_re[:, None]
    y = (jnp.einsum("gpn,btgn->btgp", c_re, h_re)
         - jnp.einsum("gpn,btgn->btgp", c_im, h_im))
    return h_re[:, -1], h_im[:, -1], y


def s5_mixer(u, h0_re, h0_im, lw):
    f32 = jnp.float32
    bsz, t_len, _ = u.shape
    uf = u.astype(f32).reshape(bsz, t_len, N_SSM_GROUPS, SSM_GROUP)
    ab_re, ab_im, bb_re, bb_im = s5_discretize(lw["ssm_a_re"], lw["ssm_a_im"], lw["ssm_log_dt"],
                                               lw["ssm_b_re"], lw["ssm_b_im"])
    c_re = lw["ssm_c_re"].astype(f32)
    c_im = lw["ssm_c_im"].astype(f32)
    h0_re = h0_re.astype(f32)
    h0_im = h0_im.astype(f32)
    if t_len > CHUNK:
        n_blk = t_len // CHUNK
        ub = uf.reshape(bsz, n_blk, CHUNK, N_SSM_GROUPS, SSM_GROUP).transpose(1, 0, 2, 3, 4)

        def step(carry, u_blk):
            hr, hi = carry
            hr, hi, y_blk = s5_block(hr, hi, u_blk, ab_re, ab_im, bb_re, bb_im, c_re, c_im)
            return (hr, hi), y_blk

        (h_re, h_im), ys = lax.scan(step, (h0_re, h0_im), ub)
        y = ys.transpose(1, 0, 2, 3, 4).reshape(bsz, t_len, D_SSM)
    else:
        h_re, h_im, y = s5_block(h0_re, h0_im, uf, ab_re, ab_im, bb_re, bb_im, c_re, c_im)
        y = y.reshape(bsz, t_len, D_SSM)
    z = jax.nn.gelu(y + lw["ssm_d"].astype(f32) * u.astype(f32))
    out = z * jax.nn.sigmoid(z @ lw["glu_w"].astype(f32) + lw["glu_b"].astype(f32))
    return out.astype(u.dtype), h_re, h_im


def conv_mixer(pa, pg, conv_buf, lw):
    v = pa * jax.nn.sigmoid(pg)
    vp = jnp.concatenate([conv_buf.astype(v.dtype), v], axis=1)
    h = lax.conv_general_dilated(vp, lw["conv_w"][:, None, :].astype(v.dtype),
                                 window_strides=(1,), padding="VALID",
                                 dimension_numbers=("NWC", "WIO", "NWC"),
                                 feature_group_count=D_CONV)
    h = h + lw["conv_b"].astype(v.dtype)
    h = jax.nn.swish(layer_norm(h, lw["conv_ln_g"], lw["conv_ln_b"]))
    return h, vp[:, -(CONV_WIDTH - 1):]


def memory_kv(mem, w_k, w_v):
    bsz = mem.shape[0]
    k = (mem @ w_k).reshape(bsz, N_MEM, MEM_HEADS, MEM_HEAD_DIM)
    v = (mem @ w_v).reshape(bsz, N_MEM, MEM_HEADS, MEM_HEAD_DIM)
    return k, v


def memory_attend(x, mem_k, mem_v, w_q, w_o):
    bsz, t_len, _ = x.shape
    q = (x @ w_q).reshape(bsz, t_len, MEM_HEADS, MEM_HEAD_DIM)
    s = jnp.einsum("bthd,bmhd->bhtm", q, mem_k.astype(q.dtype),
                   preferred_element_type=jnp.float32) * (MEM_HEAD_DIM ** -0.5)
    p = jax.nn.softmax(s, axis=-1).astype(x.dtype)
    o = jnp.einsum("bhtm,bmhd->bthd", p, mem_v.astype(x.dtype)).reshape(bsz, t_len, D_MODEL)
    return o @ w_o


def encoder_layer(x, h0_re, h0_im, conv_buf, mem_k, mem_v, lw):
    proj = x @ lw["w_in"]
    u = proj[..., :D_SSM]
    pa = proj[..., D_SSM:D_SSM + D_CONV]
    pg = proj[..., D_SSM + D_CONV:]
    ya, h_re, h_im = s5_mixer(u, h0_re, h0_im, lw)
    yb, new_conv = conv_mixer(pa, pg, conv_buf, lw)
    mix = jnp.concatenate([ya, yb.astype(ya.dtype)], axis=-1) @ lw["w_out"]
    x = layer_norm(ALPHA * x + mix, lw["ln1_g"], lw["ln1_b"])
    att = memory_attend(x, mem_k, mem_v, lw["mem_w_q"], lw["mem_w_o"])
    x = layer_norm(ALPHA * x + att, lw["ln2_g"], lw["ln2_b"])
    hid = jnp.square(jax.nn.relu(x @ lw["mlp_w1"] + lw["mlp_b1"]))
    x = layer_norm(ALPHA * x + hid @ lw["mlp_w2"] + lw["mlp_b2"], lw["ln3_g"], lw["ln3_b"])
    return x, h_re, h_im, new_conv


def setup_inputs(seed: int = 0) -> dict:
    key = jax.random.key(seed)
    ks = jax.random.split(key, 40)
    f32 = jnp.float32
    nrm = lambda k, shape, s: (jax.random.normal(k, shape, f32) * s)
    L = DEPTH
    n_idx = jnp.arange(SSM_STATE, dtype=f32)
    inp = {}
    inp["x_prompt"] = nrm(ks[0], (BATCH, SEQ, D_MODEL), 1.0)
    inp["x_sample"] = nrm(ks[1], (DEC_BATCH, DEC_SEQ, D_MODEL), 1.0)
    inp["state_ssm_re"] = nrm(ks[2], (L, DEC_BATCH, N_SSM_GROUPS, SSM_STATE), 0.5)
    inp["state_ssm_im"] = nrm(ks[3], (L, DEC_BATCH, N_SSM_GROUPS, SSM_STATE), 0.5)
    inp["cache_conv"] = nrm(ks[4], (L, DEC_BATCH, CONV_WIDTH - 1, D_CONV), 1.0)
    inp["cache_mem_k"] = nrm(ks[5], (L, DEC_BATCH, N_MEM, MEM_HEADS, MEM_HEAD_DIM), 1.0)
    inp["cache_mem_v"] = nrm(ks[6], (L, DEC_BATCH, N_MEM, MEM_HEADS, MEM_HEAD_DIM), BETA)
    inp["mem_prompt"] = nrm(ks[7], (BATCH, N_MEM, D_MODEL), 1.0)
    inp["w_in"] = nrm(ks[8], (L, D_MODEL, D_IN), D_MODEL ** -0.5)
    inp["ssm_a_re"] = -0.5 + nrm(ks[9], (L, N_SSM_GROUPS, SSM_STATE), 0.01)
    inp["ssm_a_im"] = math.pi * n_idx + nrm(ks[10], (L, N_SSM_GROUPS, SSM_STATE), 0.01)
    inp["ssm_log_dt"] = jax.random.uniform(ks[11], (L, N_SSM_GROUPS), f32,
                                           minval=math.log(DT_MIN), maxval=math.log(DT_MAX))
    inp["ssm_b_re"] = nrm(ks[12], (L, N_SSM_GROUPS, SSM_STATE, SSM_GROUP), (2 * SSM_GROUP) ** -0.5)
    inp["ssm_b_im"] = nrm(ks[13], (L, N_SSM_GROUPS, SSM_STATE, SSM_GROUP), (2 * SSM_GROUP) ** -0.5)
    inp["ssm_c_re"] = nrm(ks[14], (L, N_SSM_GROUPS, SSM_GROUP, SSM_STATE), SSM_STATE ** -0.5)
    inp["ssm_c_im"] = nrm(ks[15], (L, N_SSM_GROUPS, SSM_GROUP, SSM_STATE), SSM_STATE ** -0.5)
    inp["ssm_d"] = nrm(ks[16], (L, D_SSM), 1.0)
    inp["glu_w"] = nrm(ks[17], (L, D_SSM, D_SSM), D_SSM ** -0.5)
    inp["glu_b"] = nrm(ks[18], (L, D_SSM), 0.01)
    inp["conv_w"] = nrm(ks[19], (L, CONV_WIDTH, D_CONV), CONV_WIDTH ** -0.5)
    inp["conv_b"] = nrm(ks[20], (L, D_CONV), 0.01)
    inp["conv_ln_g"] = 1.0 + nrm(ks[21], (L, D_CONV), 0.02)
    inp["conv_ln_b"] = nrm(ks[22], (L, D_CONV), 0.02)
    inp["w_out"] = nrm(ks[23], (L, D_MODEL, D_MODEL), BETA * D_MODEL ** -0.5)
    inp["ln1_g"] = 1.0 + nrm(ks[24], (L, D_MODEL), 0.02)
    inp["ln1_b"] = nrm(ks[25], (L, D_MODEL), 0.02)
    inp["mem_w_q"] = nrm(ks[26], (L, D_MODEL, D_MODEL), D_MODEL ** -0.5)
    inp["mem_w_k"] = nrm(ks[27], (L, D_MODEL, D_MODEL), D_MODEL ** -0.5)
    inp["mem_w_v"] = nrm(ks[28], (L, D_MODEL, D_MODEL), BETA * D_MODEL ** -0.5)
    inp["mem_w_o"] = nrm(ks[29], (L, D_MODEL, D_MODEL), BETA * D_MODEL ** -0.5)
    inp["ln2_g"] = 1.0 + nrm(ks[30], (L, D_MODEL), 0.02)
    inp["ln2_b"] = nrm(ks[31], (L, D_MODEL), 0.02)
    inp["mlp_w1"] = nrm(ks[32], (L, D_MODEL, D_FF), BETA * D_MODEL ** -0.5)
    inp["mlp_b1"] = nrm(ks[33], (L, D_FF), 0.01)
    inp["mlp_w2"] = nrm(ks[34], (L, D_FF, D_MODEL), BETA * D_FF ** -0.5)
    inp["mlp_b2"] = nrm(ks[35], (L, D_MODEL), 0.01)
    inp["ln3_g"] = 1.0 + nrm(ks[36], (L, D_MODEL), 0.02)
    inp["ln3_b"] = nrm(ks[37], (L, D_MODEL), 0.02)
    return inp


def reference(x_prompt, x_sample, state_ssm_re, state_ssm_im, cache_conv, cache_mem_k, cache_mem_v,
              mem_prompt, w_in, ssm_a_re, ssm_a_im, ssm_log_dt, ssm_b_re, ssm_b_im, ssm_c_re, ssm_c_im,
              ssm_d, glu_w, glu_b, conv_w, conv_b, conv_ln_g, conv_ln_b, w_out, ln1_g, ln1_b,
              mem_w_q, mem_w_k, mem_w_v, mem_w_o, ln2_g, ln2_b,
              mlp_w1, mlp_b1, mlp_w2, mlp_b2, ln3_g, ln3_b):
    yp = x_prompt
    ys = x_sample
    p_re, p_im, p_conv, p_mk, p_mv = [], [], [], [], []
    s_re, s_im, s_conv = [], [], []
    zero_h = jnp.zeros((x_prompt.shape[0], N_SSM_GROUPS, SSM_STATE), jnp.float32)
    zero_conv = jnp.zeros((x_prompt.shape[0], CONV_WIDTH - 1, D_CONV), x_prompt.dtype)
    for l in range(DEPTH):
        lw = dict(w_in=w_in[l], ssm_a_re=ssm_a_re[l], ssm_a_im=ssm_a_im[l], ssm_log_dt=ssm_log_dt[l],
                  ssm_b_re=ssm_b_re[l], ssm_b_im=ssm_b_im[l], ssm_c_re=ssm_c_re[l], ssm_c_im=ssm_c_im[l],
                  ssm_d=ssm_d[l], glu_w=glu_w[l], glu_b=glu_b[l], conv_w=conv_w[l], conv_b=conv_b[l],
                  conv_ln_g=conv_ln_g[l], conv_ln_b=conv_ln_b[l], w_out=w_out[l],
                  ln1_g=ln1_g[l], ln1_b=ln1_b[l], mem_w_q=mem_w_q[l], mem_w_o=mem_w_o[l],
                  ln2_g=ln2_g[l], ln2_b=ln2_b[l], mlp_w1=mlp_w1[l], mlp_b1=mlp_b1[l],
                  mlp_w2=mlp_w2[l], mlp_b2=mlp_b2[l], ln3_g=ln3_g[l], ln3_b=ln3_b[l])
        mk, mv = memory_kv(mem_prompt, mem_w_k[l], mem_w_v[l])
        yp, hr, hi, cb = encoder_layer(yp, zero_h, zero_h, zero_conv, mk, mv, lw)
        p_re.append(hr)
        p_im.append(hi)
        p_conv.append(cb)
        p_mk.append(mk)
        p_mv.append(mv)
        ys, hr, hi, cb = encoder_layer(ys, state_ssm_re[l], state_ssm_im[l], cache_conv[l],
                                       cache_mem_k[l], cache_mem_v[l], lw)
        s_re.append(hr)
        s_im.append(hi)
        s_conv.append(cb)
    return (yp, ys, jnp.stack(p_re), jnp.stack(p_im), jnp.stack(p_conv), jnp.stack(p_mk),
            jnp.stack(p_mv), jnp.stack(s_re), jnp.stack(s_im), jnp.stack(s_conv))
```

```python
from contextlib import ExitStack
import numpy as np
import concourse.bass as bass
import concourse.mybir as mybir
from concourse.bass_utils import run_bass_kernel_spmd

F32 = mybir.dt.float32
BF16 = mybir.dt.bfloat16
AF = mybir.ActivationFunctionType
ALU = mybir.AluOpType
AX = mybir.AxisListType

D = 1024
NCORE = 8
SEG = 4096
NPRE = 3 * SEG
T = 256
NT = SEG // T
TS = 128
LN_EPS = 1e-5
ALPHA = 2.0 ** 0.25
MAGIC = 12582912.0
TWO_PI = float(2 * np.pi)
PI = float(np.pi)

PK = {}
_o = 0
for _n, _w in [("ln1_g", 8), ("ln1_b", 8), ("ln2_g", 8), ("ln2_b", 8), ("ln3_g", 8), ("ln3_b", 8),
               ("b1", 32), ("b2", 8), ("glu_b", 4), ("ssm_d", 4), ("conv_b", 4), ("cln_g", 4), ("cln_b", 4),
               ("conv_w", 124), ("a_re", 16), ("a_im", 16), ("logdt", 16), ("tidx", 128), ("eidx", 1), ("pad", 3)]:
    PK[_n] = (_o, _o + _w)
    _o += _w
NPK = _o

SAME_ENGINE_SYNC = True
import os
NT_RUN = int(os.environ.get("K_NT", NT))
RUN_SAMPLE = bool(int(os.environ.get("K_SAMPLE", "1")))
NPB_RUN = int(os.environ.get("K_NPB", NPRE // T))
RUN_PREP = bool(int(os.environ.get("K_PREP", "1")))
RUN_KV = bool(int(os.environ.get("K_KV", "1")))
RUN_PC = os.environ.get("K_PC", "all")
KVD = os.environ.get("K_KVD", "111")
PC_ROWS = int(os.environ.get("K_PCR", "128"))
PC_COLS = int(os.environ.get("K_PCC", "1024"))
CORES = [int(x) for x in os.environ.get("K_CORES", "0,1,2,3,4,5,6,7").split(",")]


class Ev:
    __slots__ = ("eng", "sem", "value", "op", "group")

    def __init__(self, eng):
        self.eng = eng
        self.sem = None
        self.value = None
        self.op = None
        self.group = None


class Buf:
    __slots__ = ("name", "w", "r", "dma_sem", "dma_count", "group")

    def __init__(self, name, group=None):
        self.name = name
        self.w = None
        self.r = []
        self.dma_sem = None
        self.dma_count = 0
        self.group = group


class DmaGroup:
    def __init__(self, name):
        self.name = name
        self.sem = None
        self.count = 0


class Op:
    __slots__ = ("eng", "fn", "waits", "ev", "is_dma", "signals")

    def __init__(self, eng, fn, waits, ev, is_dma):
        self.eng = eng
        self.fn = fn
        self.waits = waits
        self.ev = ev
        self.is_dma = is_dma
        self.signals = is_dma


class Prog:
    ENGINES = ("pe", "act", "dve", "pool", "sp")

    def __init__(self, nc):
        self.nc = nc
        self.ops = {e: [] for e in self.ENGINES}
        self.all_ops = []
        self.dma_bufs = []
        self.groups = []
        self.final_evs = []

    def group(self, name):
        g = DmaGroup(name)
        self.groups.append(g)
        return g

    def _deps(self, ev, reads, writes):
        waits = []
        for b in reads:
            if b.w is not None:
                waits.append(b.w)
        for b in writes:
            if b.w is not None:
                waits.append(b.w)
            waits.extend(b.r)
        for b in reads:
            b.r.append(ev)
        for b in writes:
            b.w = ev
            b.r = []
        out = []
        seen = set()
        for w in waits:
            if w is ev or id(w) in seen:
                continue
            seen.add(id(w))
            out.append(w)
        return out

    def op(self, eng, fn, reads=(), writes=()):
        ev = Ev(eng)
        waits = self._deps(ev, reads, writes)
        o = Op(eng, fn, waits, ev, False)
        ev.op = o
        self.ops[eng].append(o)
        self.all_ops.append(o)
        return o

    def dma(self, eng, fn, key, reads=(), writes=(), final=False):
        ev = Ev("dma")
        waits = self._deps(ev, reads, writes)
        if key.group is not None:
            ev.group = key.group
            key.group.count += 1
        else:
            if key.dma_count == 0:
                self.dma_bufs.append(key)
            key.dma_count += 1
            ev.sem = key
            ev.value = 16 * key.dma_count
        o = Op(eng, fn, waits, ev, True)
        ev.op = o
        self.ops[eng].append(o)
        self.all_ops.append(o)
        if final:
            self.final_evs.append(ev)
        return o

    def emit(self, stack):
        nc = self.nc
        for o in self.all_ops:
            for w in o.waits:
                if w.op is not None and not w.op.is_dma:
                    if w.eng == o.eng and (o.eng == "pe" or not SAME_ENGINE_SYNC):
                        continue
                    w.op.signals = True
        esem = {}
        for e in ("pe", "act", "dve", "pool"):
            esem[e] = stack.enter_context(nc.semaphore("s_" + e))
            cnt = 0
            for o in self.ops[e]:
                if o.is_dma:
                    continue
                if o.signals:
                    cnt += 1
                    o.ev.sem = esem[e]
                    o.ev.value = cnt
        for b in self.dma_bufs:
            b.dma_sem = stack.enter_context(nc.semaphore("d_" + b.name))
        for g in self.groups:
            if g.count:
                g.sem = stack.enter_context(nc.semaphore("g_" + g.name))

        def resolve(ev):
            if ev.group is not None:
                return ev.group.sem, 16 * ev.group.count
            if isinstance(ev.sem, Buf):
                return ev.sem.dma_sem, ev.value
            return ev.sem, ev.value

        block = stack.enter_context(nc.Block())
        handles = {"pe": "tensor", "act": "scalar", "dve": "vector", "pool": "gpsimd", "sp": "sync"}
        final_evs = self.final_evs

        def make(e):
            ops = self.ops[e]

            def body(eng):
                waited = {}
                for o in ops:
                    for w in o.waits:
                        if w.eng == e and not w.op.is_dma and (e == "pe" or not SAME_ENGINE_SYNC):
                            continue
                        sem, val = resolve(w)
                        assert sem is not None and val is not None, (e, w.eng)
                        k = id(sem)
                        if waited.get(k, 0) >= val:
                            continue
                        waited[k] = val
                        eng.wait_ge(sem, val)
                    ins = o.fn(eng)
                    if o.is_dma:
                        sem, _ = resolve(o.ev)
                        ins.then_inc(sem, 16)
                    elif o.signals:
                        ins.then_inc(o.ev.sem, 1)
                if e == "sp":
                    for ev in final_evs:
                        sem, val = resolve(ev)
                        if waited.get(id(sem), 0) >= val:
                            continue
                        waited[id(sem)] = val
                        eng.wait_ge(sem, val)
            return body

        for e in self.ENGINES:
            if self.ops[e] or e == "sp":
                getattr(block, handles[e])(make(e))


class Rot:
    def __init__(self, st, nc, name, shape, dtype, n, psum=False):
        self.items = []
        for i in range(n):
            alloc = nc.psum_tensor if psum else nc.sbuf_tensor
            t = st.enter_context(alloc(f"rt_{name}{i}", shape, dtype))
            self.items.append((Buf(f"{name}{i}"), t))
        self.i = 0

    def get(self):
        it = self.items[self.i % len(self.items)]
        self.i += 1
        return it


def build_program():
    nc = bass.Bass("TRN2", target_bir_lowering=False)
    st = ExitStack()
    P = Prog(nc)

    def din(name, shape):
        return nc.dram_tensor(name, shape, F32, kind="ExternalInput").ap()

    def dout(name, shape):
        return nc.dram_tensor(name, shape, F32, kind="ExternalOutput").ap()

    def dscr(name, shape):
        return nc.dram_tensor(name, shape, BF16, kind="Internal").ap()

    xT = din("xT", [D, NPRE + SEG])
    xsT = din("xsT", [D, 32])
    h0_d = din("h0", [128, 64])
    convc_d = din("convc", [128, 240])
    kTs_d = din("kTs", [2, D, 256])
    vs_d = din("vs", [2, 256, D])
    memT_d = din("memT", [D, 256])
    w_in_d = din("w_in", [D, 1536])
    glu_w_d = din("glu_w", [512, 512])
    w_out_d = din("w_out", [D, D])
    w_q_d = din("w_q", [D, D])
    w_k_d = din("w_k", [D, D])
    w_v_d = din("w_v", [D, D])
    w_o_d = din("w_o", [D, D])
    w1_d = din("w1", [D, 4096])
    w2_d = din("w2", [4096, D])
    pk_d = din("pk", [128, NPK])
    rows_d = din("rows", [3, 2048])
    BT_d = din("BT", [2, 128, 2048])
    CT_d = din("CT", [2, 128, 2048])
    Bx_d = din("Bx", [2, 128, 512])

    yT = dout("yT", [D, SEG])
    ysT = dout("ysT", [D, 32])
    hout_d = dout("hout", [128, 32])
    convout_d = dout("convout", [128, 120])
    kout_d = dout("kout", [256, D])
    vout_d = dout("vout", [256, D])
    hsout_d = dout("hsout", [128, 64])
    convsout_d = dout("convsout", [128, 240])

    w_in_b = dscr("w_in_b", [D, 1536])
    w_out_b = dscr("w_out_b", [D, D])
    w_q_b = dscr("w_q_b", [D, D])
    w_k_b = dscr("w_k_b", [D, D])
    w_v_b = dscr("w_v_b", [D, D])
    w_o_b = dscr("w_o_b", [D, D])
    w1_b = dscr("w1_b", [D, 4096])
    w2_b = dscr("w2_b", [4096, D])

    def sb(name, shape, dt=F32):
        return st.enter_context(nc.sbuf_tensor("sb_" + name, shape, dt))

    pk = sb("pk", [128, NPK])
    cgrp = P.group("consts")
    pkB = Buf("pk", group=cgrp)
    ones_m = sb("ones_m", [128, 128], BF16)
    ones_c = sb("ones_c", [128, 128], BF16)
    ones_1 = sb("ones_1", [128, 128], BF16)
    onesB = Buf("ones")
    R0 = sb("R0", [128, 8, T])
    R1 = sb("R1", [128, 8, T])
    xnb = sb("xnb", [128, 8, T], BF16)
    R0B = [Buf(f"R0_{c}") for c in range(8)]
    R1B = [Buf(f"R1_{c}") for c in range(8)]
    xnbB = [Buf(f"xnb_{c}") for c in range(8)]
    ob, obB = xnb, xnbB
    xb = [sb(f"xb{i}", [128, 8, T], BF16) for i in range(2)]
    xbB = [Buf(f"xb{i}") for i in range(2)]
    NRING = 3
    ring = [sb(f"ring{i}", [128, 4096], BF16) for i in range(NRING)]
    ringB = [Buf(f"ring{i}") for i in range(NRING)]
    ring_i = [0]
    glu_sb = sb("glu_sb", [128, 4, 512], BF16)
    gluB = Buf("glu")
    u32 = sb("u32", [128, 4, T])
    ub = sb("ub", [128, 4, T], BF16)
    u32B = [Buf(f"u32_{c}") for c in range(4)]
    ubB = [Buf(f"ub_{c}") for c in range(4)]
    vbuf = sb("vbuf", [128, 4, 30 + T])
    vbufB = [Buf(f"vbuf_{c}") for c in range(4)]
    cacc = sb("cacc", [128, 4, T])
    caccB = [Buf(f"cacc_{c}") for c in range(4)]
    mixin = sb("mixin", [128, 8, T], BF16)
    mixB = [Buf(f"mix_{c}") for c in range(8)]
    qb, qbB = mixin, mixB
    z32 = sb("z32", [128, 4, T])
    zb = sb("zb", [128, 4, T], BF16)
    z32B = [Buf(f"z32_{c}") for c in range(4)]
    zbB = [Buf(f"zb_{c}") for c in range(4)]
    hid = sb("hid", [128, 32, T], BF16)
    hidB = [Buf(f"hid_{c}") for c in range(32)]
    kT = sb("kT", [128, 8, 256], BF16)
    kTB = Buf("kT")
    vv = sb("vv", [128, 2, D], BF16)
    vvB = Buf("vv")
    costab = sb("costab", [128, 16, TS])
    sintab = sb("sintab", [128, 16, TS])
    rtab = sb("rtab", [128, 16, TS])
    tabB = Buf("tabs")
    BreT = sb("BreT", [128, 16, 128], BF16)
    BimT = sb("BimT", [128, 16, 128], BF16)
    CreT = sb("CreT", [128, 16, 128], BF16)
    CimT = sb("CimT", [128, 16, 128], BF16)
    Pre = sb("Pre", [128, 16, 128], BF16)
    Pim = sb("Pim", [128, 16, 128], BF16)
    lhsB = Buf("ssm_lhs")
    Bxr = sb("Bxr", [128, 512])
    Bxi = sb("Bxi", [128, 512])
    BxB = Buf("Bx")
    a128r = sb("a128r", [128, 16])
    a128i = sb("a128i", [128, 16])
    a128B = Buf("a128")
    Hre = sb("Hre", [128, 16])
    Him = sb("Him", [128, 16])
    HB = [Buf(f"H_{p}") for p in range(16)]
    Hs = sb("Hs", [128, 64])
    HsB = [Buf(f"Hs_{p}") for p in range(16)]

    s16 = Rot(st, nc, "s16_", [128, T], BF16, 4)
    f32t = Rot(st, nc, "f32t_", [128, T], F32, 6)
    sst = Rot(st, nc, "sst_", [128, TS], F32, 10)
    hbf = Rot(st, nc, "hbf_", [128, TS], BF16, 8)
    tiny = Rot(st, nc, "tiny_", [128, 16], F32, 8)
    pTr = Rot(st, nc, "pT_", [128, T], BF16, 4)
    ublk = Rot(st, nc, "ublk_", [128, 512], BF16, 2)
    kvstg = Rot(st, nc, "kvstg_", [128, 512], F32, 2)
    statsb = Rot(st, nc, "stat_", [128, T], F32, 6)
    psum = Rot(st, nc, "ps", [128, 512], F32, 6, psum=True)
    ypsR = Rot(st, nc, "yps", [128, 512], F32, 2, psum=True)

    def big(i):
        src, bufs = (R0, R0B) if i < 4 else (R1, R1B)
        j = (i % 4) * 2
        return [bufs[j], bufs[j + 1]], src[:, j:j + 2, :].rearrange("p a b -> p (a b)")

    pkc = lambda name, i=0, n=1: pk[:, PK[name][0] + i: PK[name][0] + i + n]

    def tt(eng, out, a, b, op, reads, writes):
        P.op(eng, lambda e: e.tensor_tensor(out=out, in0=a, in1=b, op=op), reads, writes)

    def ts(eng, out, a, s1, s2, op0, op1, reads, writes):
        if op1 is None:
            P.op(eng, lambda e: e.tensor_scalar(out=out, in0=a, scalar1=s1, scalar2=None, op0=op0), reads, writes)
        else:
            P.op(eng, lambda e: e.tensor_scalar(out=out, in0=a, scalar1=s1, scalar2=s2, op0=op0, op1=op1), reads, writes)

    def stt(eng, out, a, s, b, op0, op1, reads, writes):
        P.op(eng, lambda e: e.scalar_tensor_tensor(out=out, in0=a, scalar=s, in1=b, op0=op0, op1=op1), reads, writes)

    def act(out, in_, func, reads, writes, scale=None, bias=None):
        kw = {}
        if scale is not None:
            kw["scale"] = scale
        if bias is not None:
            kw["bias"] = bias
        P.op("act", lambda e: e.activation(out=out, in_=in_, func=func, **kw), reads, writes)

    def cp(eng, out, in_, reads, writes):
        if eng == "act":
            act(out, in_, AF.Copy, reads, writes)
        else:
            P.op(eng, lambda e: e.tensor_copy(out=out, in_=in_), reads, writes)

    def recip(out, in_, reads, writes):
        P.op("dve", lambda e: e.reciprocal(out=out, in_=in_), reads, writes)

    def mm(out, pairs, reads, writes):
        n = len(pairs)

        def fn(e):
            ins = None
            for i, (l, r) in enumerate(pairs):
                ins = e.matmul(out, l, r, start=(i == 0), stop=(i == n - 1))
            return ins
        P.op("pe", fn, reads, writes)

    def mm1(out, l, r, start, stop, reads, writes):
        P.op("pe", lambda e: e.matmul(out, l, r, start=start, stop=stop), reads, writes)

    def dma(eng, out, in_, key, reads=(), writes=(), final=False):
        P.dma(eng, lambda e: e.dma_start(out=out, in_=in_), key, reads=reads, writes=writes, final=final)

    def memset(eng, ap, val, writes):
        P.op(eng, lambda e: e.memset(ap, val), (), writes)

    def range_reduce(out, in_, shift, reads, writes, tmp, tmpB):
        src = in_
        rd = list(reads)
        if shift != 0.0:
            ts("dve", out, in_, shift, None, ALU.add, None, reads, writes)
            src = out
            rd = list(writes)
        ts("dve", tmp, src, 1.0 / TWO_PI, MAGIC, ALU.mult, ALU.add, rd, tmpB)
        ts("dve", tmp, tmp, MAGIC, -TWO_PI, ALU.subtract, ALU.mult, tmpB, tmpB)
        tt("dve", out, src, tmp, ALU.add, rd + list(tmpB), writes)
        ts("dve", out, out, PI, -PI, ALU.min, ALU.max, writes, writes)

    dma("sp", pk[:], pk_d[:, :], pkB, writes=[pkB])
    dma("pool", glu_sb[:], glu_w_d.rearrange("(k p) n -> p k n", p=128), gluB, writes=[gluB])
    memset("dve", ones_m[:], 1.0 / 1024.0, [onesB])
    memset("dve", ones_c[:], 1.0 / 512.0, [onesB])
    memset("dve", ones_1[:], 1.0, [onesB])

    pcsB = [Buf(f"pcs{i}") for i in range(NRING)]
    pclB = [Buf(f"pcl{i}") for i in range(NRING)]

    def precast(name, dst, src, nrows, colmap=None):
        bl = [Buf(f"pcb_{name}{i}") for i in range(NRING)]
        ncols = src.shape[1]
        nk = nrows // 128
        sv = src.rearrange("(k p) n -> p k n", p=128)
        dv = dst.rearrange("(k p) n -> p k n", p=128)
        if colmap is None:
            colmap = [(c0, c0, min(1024, ncols - c0)) for c0 in range(0, ncols, 1024)]
        for (d0, s0, n) in colmap:
            kstep = max(1, min(nk, 4096 // n))
            for k0 in range(0, nk, kstep):
                i = ring_i[0] % NRING
                ring_i[0] += 1
                view = ring[i][:, 0:kstep * n].rearrange("p (a b) -> p a b", a=kstep)
                dma("pool", view, sv[:, k0:k0 + kstep, s0:s0 + n], pclB[i], writes=[ringB[i]])
                dma("sp", dv[:, k0:k0 + kstep, d0:d0 + n], view, pcsB[i], reads=[ringB[i]], writes=[bl[i]])
        return bl

    win_map = [(0, 0, 512)]
    for i in range(4):
        win_map.append((512 + 256 * i, 512 + 128 * i, 128))
        win_map.append((512 + 256 * i + 128, 1024 + 128 * i, 128))
    pc = {}
    if RUN_PC == "none":
        def precast(name, dst, src, nrows, colmap=None):
            return [Buf("pcb_" + name)]
    pc["w_in"] = precast("w_in", w_in_b, w_in_d, D, win_map)
    pc["w_k"] = precast("w_k", w_k_b, w_k_d, D)
    pc["w_v"] = precast("w_v", w_v_b, w_v_d, D)
    pc["w_out"] = precast("w_out", w_out_b, w_out_d, D)
    pc["w_q"] = precast("w_q", w_q_b, w_q_d, D)
    pc["w_o"] = precast("w_o", w_o_b, w_o_d, D)
    pc["w1"] = precast("w1", w1_b, w1_d, D)
    pc["w2"] = precast("w2", w2_b, w2_d, 4096)

    def ring_load(src_ap, a, b, pcb):
        i = ring_i[0] % NRING
        ring_i[0] += 1
        view = ring[i][:, 0:a * b].rearrange("p (a b) -> p a b", a=a)
        dma("sp", view, src_ap, ringB[i], reads=pcb, writes=[ringB[i]])
        return ringB[i], view

    def wpiece(wb, pcb, n0, n1):
        return ring_load(wb.rearrange("(k p) n -> p k n", p=128)[:, :, n0:n1], 8, n1 - n0, pcb)

    def disc(A, Bm, L, tmp, rdB, wB):
        T1, T2, T3, T4, T5, T6, T7 = tmp
        act(L, L, AF.Exp, rdB, wB)
        tt("dve", T1, A, L, ALU.mult, wB, wB)
        tt("dve", T5, Bm, L, ALU.mult, wB, wB)
        act(T6, T1, AF.Exp, wB, wB)
        range_reduce(T2, T5, 0.0, wB, wB, T7, wB)
        act(T2, T2, AF.Sin, wB, wB)
        range_reduce(T3, T5, PI / 2, wB, wB, T7, wB)
        act(T3, T3, AF.Sin, wB, wB)
        tt("dve", T3, T6, T3, ALU.mult, wB, wB)
        tt("dve", T2, T6, T2, ALU.mult, wB, wB)
        ts("dve", T3, T3, -1.0, None, ALU.add, None, wB, wB)
        tt("dve", T6, A, A, ALU.mult, wB, wB)
        tt("dve", T7, Bm, Bm, ALU.mult, wB, wB)
        tt("dve", T6, T6, T7, ALU.add, wB, wB)
        recip(T6, T6, wB, wB)
        tt("dve", L, T3, A, ALU.mult, wB, wB)
        tt("dve", T4, T2, Bm, ALU.mult, wB, wB)
        tt("dve", L, L, T4, ALU.add, wB, wB)
        tt("dve", L, L, T6, ALU.mult, wB, wB)
        tt("dve", T4, T2, A, ALU.mult, wB, wB)
        tt("dve", T3, T3, Bm, ALU.mult, wB, wB)
        tt("dve", T4, T4, T3, ALU.subtract, wB, wB)
        tt("dve", T4, T4, T6, ALU.mult, wB, wB)
        return dict(x1=T1, ang=T5, c_re=L, c_im=T4)

    if RUN_PREP:
        mB_ = [Buf("modeprep")]
        mA, mBm, mL = sb("mA", [128, 16]), sb("mBm", [128, 16]), sb("mL", [128, 16])
        mT = [sb(f"mT{i}", [128, 16]) for i in range(7)]
        cp("dve", mA[:], pkc("a_re", 0, 16), [pkB], mB_)
        cp("dve", mBm[:], pkc("a_im", 0, 16), [pkB], mB_)
        cp("dve", mL[:], pkc("logdt", 0, 16), [pkB], mB_)
        dm = disc(mA[:], mBm[:], mL[:], [t[:] for t in mT], mB_, mB_)
        m128a, m128m, mr = sb("m128a", [128, 16]), sb("m128m", [128, 16]), sb("mr", [128, 16])
        ts("dve", m128a[:], dm["ang"], 128.0, None, ALU.mult, None, mB_, mB_)
        act(m128m[:], dm["x1"], AF.Exp, mB_, mB_, scale=128.0)
        range_reduce(mT[1][:], m128a[:], 0.0, mB_, mB_, mT[6][:], mB_)
        act(mT[1][:], mT[1][:], AF.Sin, mB_, mB_)
        range_reduce(mT[2][:], m128a[:], PI / 2, mB_, mB_, mT[6][:], mB_)
        act(mT[2][:], mT[2][:], AF.Sin, mB_, mB_)
        tt("dve", a128r[:], m128m[:], mT[2][:], ALU.mult, mB_, [a128B])
        tt("dve", a128i[:], m128m[:], mT[1][:], ALU.mult, mB_ + [a128B], [a128B])
        act(mr[:], dm["x1"], AF.Exp, mB_, mB_)
        for p_ in range(16):
            tg, tg2 = sst.get(), sst.get()
            ts("dve", tg[1][:], pkc("tidx", 0, TS), dm["ang"][:, p_:p_ + 1], None, ALU.mult, None, [pkB] + mB_, [tg[0]])
            range_reduce(sintab[:, p_, :], tg[1][:], 0.0, [tg[0]], [tabB], tg2[1][:], [tg2[0]])
            act(sintab[:, p_, :], sintab[:, p_, :], AF.Sin, [tabB], [tabB])
            range_reduce(costab[:, p_, :], tg[1][:], PI / 2, [tg[0]], [tabB], tg2[1][:], [tg2[0]])
            act(costab[:, p_, :], costab[:, p_, :], AF.Sin, [tabB], [tabB])
            ts("dve", rtab[:, p_, :], pkc("tidx", 0, TS), 0.0, mr[:, p_:p_ + 1], ALU.mult, ALU.add, [pkB] + mB_, [tabB])
        bxrB, bxr_t = big(0)
        bxiB, bxi_t = big(1)
        dma("sp", bxr_t, Bx_d[0], bxrB[0], writes=bxrB)
        dma("sp", bxi_t, Bx_d[1], bxiB[0], writes=bxiB)
        for p_ in range(16):
            sl = slice(p_ * 32, p_ * 32 + 32)
            cr, ci = dm["c_re"][:, p_:p_ + 1], dm["c_im"][:, p_:p_ + 1]
            ta, tb_ = sst.get(), sst.get()
            ts("dve", ta[1][:, 0:32], bxi_t[:, sl], ci, None, ALU.mult, None, bxiB + mB_, [ta[0]])
            stt("dve", Bxr[:, sl], bxr_t[:, sl], cr, ta[1][:, 0:32], ALU.mult, ALU.subtract, bxrB + mB_ + [ta[0]], [BxB])
            ts("dve", tb_[1][:, 0:32], bxr_t[:, sl], ci, None, ALU.mult, None, bxrB + mB_, [tb_[0]])
            stt("dve", Bxi[:, sl], bxi_t[:, sl], cr, tb_[1][:, 0:32], ALU.mult, ALU.add, bxiB + mB_ + [tb_[0]], [BxB])

        rowB = R0B + R1B
        rt = [R0[:, c, :] for c in range(8)] + [R1[:, c, :] for c in range(8)]
        for blk in range(8):
            cs = slice(blk * 256, blk * 256 + 256)
            rA, rBm, rL = rt[0], rt[1], rt[2]
            for (tile_, row) in ((rA, 0), (rBm, 1), (rL, 2)):
                dma("sp", tile_, rows_d[row:row + 1, cs].partition_broadcast(128), R0B[row], writes=rowB)
            tmp = rt[3:10]
            dr = disc(rA, rBm, rL, tmp, rowB, rowB)
            btr, bti, ctr, cti = rt[10], rt[11], rt[12], rt[13]
            dma("sp", btr, BT_d[0][:, cs], R1B[2], writes=rowB)
            dma("sp", bti, BT_d[1][:, cs], R1B[3], writes=rowB)
            dma("sp", ctr, CT_d[0][:, cs], R1B[4], writes=rowB)
            dma("sp", cti, CT_d[1][:, cs], R1B[5], writes=rowB)
            sA, sB_ = tmp[1], tmp[2]
            s3, s4 = tmp[5], tmp[6]
            osl = lambda t_: t_[:, blk * 2:blk * 2 + 2, :].rearrange("p a b -> p (a b)")
            tt("dve", sA, dr["c_re"], btr, ALU.mult, rowB, rowB)
            tt("dve", sB_, dr["c_im"], bti, ALU.mult, rowB, rowB)
            tt("dve", osl(BreT), sA, sB_, ALU.subtract, rowB, [lhsB])
            tt("dve", sA, dr["c_re"], bti, ALU.mult, rowB, rowB)
            tt("dve", sB_, dr["c_im"], btr, ALU.mult, rowB, rowB)
            tt("dve", osl(BimT), sA, sB_, ALU.add, rowB + [lhsB], [lhsB])
            e_ap = pkc("eidx")
            act(sA, dr["x1"], AF.Exp, rowB + [pkB], rowB, scale=e_ap)
            ts("dve", sB_, dr["ang"], e_ap, None, ALU.mult, None, rowB + [pkB], rowB)
            range_reduce(s3, sB_, 0.0, rowB, rowB, s4, rowB)
            act(s3, s3, AF.Sin, rowB, rowB)
            tt("dve", osl(Pim), sA, s3, ALU.mult, rowB + [lhsB], [lhsB])
            range_reduce(s3, sB_, PI / 2, rowB, rowB, s4, rowB)
            act(s3, s3, AF.Sin, rowB, rowB)
            tt("dve", osl(Pre), sA, s3, ALU.mult, rowB + [lhsB], [lhsB])
            cp("dve", osl(CreT), ctr, rowB + [lhsB], [lhsB])
            ts("dve", osl(CimT), cti, -1.0, None, ALU.mult, None, rowB + [lhsB], [lhsB])

    def layer_norm(nch, srcs, srcB, ones_ap, Tn, g_name, b_name, emit_out):
        pm, pe2 = psum.get(), psum.get()
        for c in range(nch):
            s1, s2 = s16.get(), s16.get()
            act(s1[1][:, :Tn], srcs[c], AF.Copy, [srcB[c]], [s1[0]])
            act(s2[1][:, :Tn], srcs[c], AF.Square, [srcB[c]], [s2[0]])
            mm1(pm[1][:, :Tn], ones_ap, s1[1][:, :Tn], c == 0, c == nch - 1, [s1[0], onesB], [pm[0]])
            mm1(pe2[1][:, :Tn], ones_ap, s2[1][:, :Tn], c == 0, c == nch - 1, [s2[0], onesB], [pe2[0]])
        mean, var, nmr = statsb.get(), statsb.get(), statsb.get()
        cp("act", mean[1][:, :Tn], pm[1][:, :Tn], [pm[0]], [mean[0]])
        tt("dve", var[1][:, :Tn], mean[1][:, :Tn], mean[1][:, :Tn], ALU.mult, [mean[0]], [var[0]])
        tt("dve", var[1][:, :Tn], pe2[1][:, :Tn], var[1][:, :Tn], ALU.subtract, [pe2[0], var[0]], [var[0]])
        ts("dve", var[1][:, :Tn], var[1][:, :Tn], LN_EPS, None, ALU.add, None, [var[0]], [var[0]])
        act(var[1][:, :Tn], var[1][:, :Tn], AF.Sqrt, [var[0]], [var[0]])
        recip(var[1][:, :Tn], var[1][:, :Tn], [var[0]], [var[0]])
        stt("dve", nmr[1][:, :Tn], mean[1][:, :Tn], -1.0, var[1][:, :Tn], ALU.mult, ALU.mult, [mean[0], var[0]], [nmr[0]])
        for c in range(nch):
            t_ = f32t.get()
            tt("dve", t_[1][:, :Tn], srcs[c], var[1][:, :Tn], ALU.mult, [srcB[c], var[0]], [t_[0]])
            tt("pool", t_[1][:, :Tn], t_[1][:, :Tn], nmr[1][:, :Tn], ALU.add, [t_[0], nmr[0]], [t_[0]])
            emit_out(c, t_[1][:, :Tn], t_[0], pkc(g_name, c), pkc(b_name, c))

    def ln_to_resid(Tn, g_name, b_name):
        def emit_out(c, t_ap, tB, g_ap, b_ap):
            act(R1[:, c, :Tn], t_ap, AF.Identity, [tB, pkB], [R1B[c]], scale=g_ap, bias=b_ap)
            cp("pool", xnb[:, c, :Tn], R1[:, c, :Tn], [R1B[c]], [xnbB[c]])
        layer_norm(8, [R0[:, c, :Tn] for c in range(8)], R0B, ones_m[:], Tn, g_name, b_name, emit_out)

    def ssm_segment(pi, col0, L, Hr_ap, Hi_ap, HBuf, ypsum, first, last):
        c = pi // 4
        bu = psum.get()
        bre, bim = bu[1][:, 0:L], bu[1][:, 256:256 + L]
        mm1(bre, BreT[:, pi, :], ub[:, c, col0:col0 + L], True, True, [lhsB, ubB[c]], [bu[0]])
        mm1(bim, BimT[:, pi, :], ub[:, c, col0:col0 + L], True, True, [lhsB, ubB[c]], [bu[0]])
        cosA, sinA, rA = costab[:, pi, 0:L], sintab[:, pi, 0:L], rtab[:, pi, 0:L]
        cos1, sin1 = costab[:, pi, 1:2], sintab[:, pi, 1:2]
        g0, tq = tiny.get(), tiny.get()
        tt("pool", tq[1][:, 0:1], Hi_ap, sin1, ALU.mult, [HBuf, tabB], [tq[0]])
        tt("pool", tq[1][:, 1:2], Hr_ap, cos1, ALU.mult, [HBuf, tabB, tq[0]], [tq[0]])
        tt("pool", tq[1][:, 2:3], Hr_ap, sin1, ALU.mult, [HBuf, tabB, tq[0]], [tq[0]])
        tt("pool", tq[1][:, 3:4], Hi_ap, cos1, ALU.mult, [HBuf, tabB, tq[0]], [tq[0]])
        tt("pool", g0[1][:, 0:1], tq[1][:, 1:2], tq[1][:, 0:1], ALU.subtract, [tq[0]], [g0[0]])
        tt("pool", g0[1][:, 1:2], tq[1][:, 3:4], tq[1][:, 2:3], ALU.add, [tq[0], g0[0]], [g0[0]])
        t1, t2, t3, t4 = sst.get(), sst.get(), sst.get(), sst.get()
        tt("dve", t1[1][:, :L], bre, cosA, ALU.mult, [bu[0], tabB], [t1[0]])
        tt("dve", t2[1][:, :L], bim, sinA, ALU.mult, [bu[0], tabB], [t2[0]])
        tt("dve", t3[1][:, :L], bim, cosA, ALU.mult, [bu[0], tabB], [t3[0]])
        tt("dve", t4[1][:, :L], bre, sinA, ALU.mult, [bu[0], tabB], [t4[0]])
        tt("pool", t1[1][:, :L], t1[1][:, :L], t2[1][:, :L], ALU.add, [t1[0], t2[0]], [t1[0]])
        tt("pool", t3[1][:, :L], t3[1][:, :L], t4[1][:, :L], ALU.subtract, [t3[0], t4[0]], [t3[0]])
        gre, gim = sst.get(), sst.get()
        P.op("dve", lambda e: e.tensor_tensor_scan(out=gre[1][:, :L], data0=rA, data1=t1[1][:, :L], initial=g0[1][:, 0:1], op0=ALU.mult, op1=ALU.add),
             [tabB, t1[0], g0[0]], [gre[0]])
        P.op("dve", lambda e: e.tensor_tensor_scan(out=gim[1][:, :L], data0=rA, data1=t3[1][:, :L], initial=g0[1][:, 1:2], op0=ALU.mult, op1=ALU.add),
             [tabB, t3[0], g0[0]], [gim[0]])
        p1, p2, p3, p4 = sst.get(), sst.get(), sst.get(), sst.get()
        tt("pool", p1[1][:, :L], gre[1][:, :L], cosA, ALU.mult, [gre[0], tabB], [p1[0]])
        tt("pool", p2[1][:, :L], gim[1][:, :L], sinA, ALU.mult, [gim[0], tabB], [p2[0]])
        tt("pool", p3[1][:, :L], gim[1][:, :L], cosA, ALU.mult, [gim[0], tabB], [p3[0]])
        tt("pool", p4[1][:, :L], gre[1][:, :L], sinA, ALU.mult, [gre[0], tabB], [p4[0]])
        hr, hi = hbf.get(), hbf.get()
        tt("dve", hr[1][:, :L], p1[1][:, :L], p2[1][:, :L], ALU.subtract, [p1[0], p2[0]], [hr[0]])
        tt("dve", hi[1][:, :L], p3[1][:, :L], p4[1][:, :L], ALU.add, [p3[0], p4[0]], [hi[0]])
        tt("pool", Hr_ap, p1[1][:, L - 1:L], p2[1][:, L - 1:L], ALU.subtract, [p1[0], p2[0]], [HBuf])
        tt("pool", Hi_ap, p3[1][:, L - 1:L], p4[1][:, L - 1:L], ALU.add, [p3[0], p4[0], HBuf], [HBuf])
        yo = ypsum[1][:, col0:col0 + L]
        mm1(yo, CreT[:, pi, :], hr[1][:, :L], first, False, [lhsB, hr[0]], [ypsum[0]])
        mm1(yo, CimT[:, pi, :], hi[1][:, :L], False, last, [lhsB, hi[0]], [ypsum[0]])

    def conv_chunk(c, col0, L, eng):
        o = cacc[:, c, col0:col0 + L]
        base = PK["conv_w"][0] + c * 31
        ts(eng, o, vbuf[:, c, 0:L], pk[:, base:base + 1], pkc("conv_b", c), ALU.mult, ALU.add, [vbufB[c], pkB], [caccB[c]])
        for k in range(1, 31):
            stt(eng, o, vbuf[:, c, k:k + L], pk[:, base + k:base + k + 1], o, ALU.mult, ALU.add, [vbufB[c], pkB, caccB[c]], [caccB[c]])

    def run_tile(Tn, xb_t, xbB_t, x32_src, segs, conv_segs, att_segs, y_dst, is_sample=False):
        dma("sp", R0[:, :, :Tn], x32_src, R0B[0], writes=R0B)
        s0B, w0 = wpiece(w_in_b, pc["w_in"], 0, 512)
        for c in range(4):
            ps = psum.get()
            mm(ps[1][:, :Tn], [(w0[:, k, c * 128:(c + 1) * 128], xb_t[:, k, :Tn]) for k in range(8)], [s0B, xbB_t], [ps[0]])
            cp("act", u32[:, c, :Tn], ps[1][:, :Tn], [ps[0]], [u32B[c]])
            cp("dve", ub[:, c, :Tn], u32[:, c, :Tn], [u32B[c]], [ubB[c]])
        for c in range(4):
            yp = ypsR.get()
            for (col0, L, Hr_fn, Hi_fn, HB_fn) in segs:
                for q in range(4):
                    pi = c * 4 + q
                    ssm_segment(pi, col0, L, Hr_fn(pi), Hi_fn(pi), HB_fn(pi), yp, q == 0, q == 3)
            zp = f32t.get()
            stt("dve", zp[1][:, :Tn], u32[:, c, :Tn], pkc("ssm_d", c), yp[1][:, :Tn], ALU.mult, ALU.add, [u32B[c], pkB, yp[0]], [zp[0]])
            act(z32[:, c, :Tn], zp[1][:, :Tn], AF.Gelu_apprx_tanh, [zp[0]], [z32B[c]])
            cp("pool", zb[:, c, :Tn], z32[:, c, :Tn], [z32B[c]], [zbB[c]])
        for co in range(4):
            ps = psum.get()
            mm(ps[1][:, :Tn], [(glu_sb[:, k, co * 128:(co + 1) * 128], zb[:, k, :Tn]) for k in range(4)], [gluB] + zbB, [ps[0]])
            sg = f32t.get()
            act(sg[1][:, :Tn], ps[1][:, :Tn], AF.Sigmoid, [ps[0], pkB], [sg[0]], bias=pkc("glu_b", co))
            tt("dve", mixin[:, co, :Tn], z32[:, co, :Tn], sg[1][:, :Tn], ALU.mult, [z32B[co], sg[0]], [mixB[co]])
        for half in range(2):
            sB_, wv = wpiece(w_in_b, pc["w_in"], 512 + half * 512, 1024 + half * 512)
            for ci in range(2):
                c = half * 2 + ci
                pa, pg = psum.get(), psum.get()
                mm(pa[1][:, :Tn], [(wv[:, k, ci * 256:ci * 256 + 128], xb_t[:, k, :Tn]) for k in range(8)], [sB_, xbB_t], [pa[0]])
                mm(pg[1][:, :Tn], [(wv[:, k, ci * 256 + 128:ci * 256 + 256], xb_t[:, k, :Tn]) for k in range(8)], [sB_, xbB_t], [pg[0]])
                sg = f32t.get()
                act(sg[1][:, :Tn], pg[1][:, :Tn], AF.Sigmoid, [pg[0]], [sg[0]])
                eng = "dve"
                if is_sample:
                    vt = f32t.get()
                    tt("dve", vt[1][:, :Tn], pa[1][:, :Tn], sg[1][:, :Tn], ALU.mult, [pa[0], sg[0]], [vt[0]])
                    for (col0, L, halo_fn, tail_fn) in conv_segs:
                        halo_fn(c)
                        cp("pool", vbuf[:, c, 30:30 + L], vt[1][:, col0:col0 + L], [vt[0]], [vbufB[c]])
                        conv_chunk(c, col0, L, eng)
                        tail_fn(c, L)
                else:
                    (col0, L, halo_fn, tail_fn) = conv_segs[0]
                    tt("dve", vbuf[:, c, 30:30 + Tn], pa[1][:, :Tn], sg[1][:, :Tn], ALU.mult, [pa[0], sg[0]], [vbufB[c]])
                    conv_chunk(c, 0, Tn, eng)
                    if tail_fn is not None:
                        tail_fn(c, Tn)
                    cp("pool", vbuf[:, c, 0:30], vbuf[:, c, Tn:Tn + 30], [vbufB[c]], [vbufB[c]])

        def silu_out(c, t_ap, tB, g_ap, b_ap):
            act(mixin[:, 4 + c, :Tn], t_ap, AF.Silu, [tB, pkB], [mixB[4 + c]], scale=g_ap, bias=b_ap)
        layer_norm(4, [cacc[:, c, :Tn] for c in range(4)], caccB, ones_c[:], Tn, "cln_g", "cln_b", silu_out)
        for half in range(2):
            sB_, wv = wpiece(w_out_b, pc["w_out"], half * 512, half * 512 + 512)
            for ci in range(4):
                co = half * 4 + ci
                ps = psum.get()
                mm(ps[1][:, :Tn], [(wv[:, k, ci * 128:(ci + 1) * 128], mixin[:, k, :Tn]) for k in range(8)], [sB_] + mixB, [ps[0]])
                stt("dve", R0[:, co, :Tn], R0[:, co, :Tn], ALPHA, ps[1][:, :Tn], ALU.mult, ALU.add, [R0B[co], ps[0]], [R0B[co]])
        ln_to_resid(Tn, "ln1_g", "ln1_b")
        qps = []
        for half in range(2):
            sB_, wv = wpiece(w_q_b, pc["w_q"], half * 512, half * 512 + 512)
            for ci in range(4):
                ps = psum.get()
                mm(ps[1][:, :Tn], [(wv[:, k, ci * 128:(ci + 1) * 128], xnb[:, k, :Tn]) for k in range(8)], [sB_] + xnbB, [ps[0]])
                act(qb[:, half * 4 + ci, :Tn], ps[1][:, :Tn], AF.Identity, [ps[0]], [qbB[half * 4 + ci]], scale=1.0 / 16.0)
        for (col0, L, kv_loader) in att_segs:
            if kv_loader is not None:
                kv_loader()
            for h in range(4):
                pts = []
                for mc in range(2):
                    ps = psum.get()
                    mm(ps[1][:, :L], [(kT[:, h * 2 + dc, mc * 128:(mc + 1) * 128], qb[:, h * 2 + dc, col0:col0 + L]) for dc in range(2)],
                       [kTB, qbB[h * 2], qbB[h * 2 + 1]], [ps[0]])
                    pt = pTr.get()
                    act(pt[1][:, :L], ps[1][:, :L], AF.Exp, [ps[0]], [pt[0]])
                    pts.append(pt)
                ps = psum.get()
                mm(ps[1][:, :L], [(ones_1[:], pts[mc][1][:, :L]) for mc in range(2)], [onesB, pts[0][0], pts[1][0]], [ps[0]])
                rinv = f32t.get()
                recip(rinv[1][:, :L], ps[1][:, :L], [ps[0]], [rinv[0]])
                for dc in range(2):
                    po = psum.get()
                    mm(po[1][:, :L], [(vv[:, mc, h * 256 + dc * 128: h * 256 + dc * 128 + 128], pts[mc][1][:, :L]) for mc in range(2)],
                       [vvB, pts[0][0], pts[1][0]], [po[0]])
                    tt("dve", ob[:, h * 2 + dc, col0:col0 + L], po[1][:, :L], rinv[1][:, :L], ALU.mult, [po[0], rinv[0]], [obB[h * 2 + dc]])
        for half in range(2):
            sB_, wv = wpiece(w_o_b, pc["w_o"], half * 512, half * 512 + 512)
            for ci in range(4):
                co = half * 4 + ci
                ps = psum.get()
                mm(ps[1][:, :Tn], [(wv[:, k, ci * 128:(ci + 1) * 128], ob[:, k, :Tn]) for k in range(8)], [sB_] + obB, [ps[0]])
                stt("dve", R0[:, co, :Tn], R1[:, co, :Tn], ALPHA, ps[1][:, :Tn], ALU.mult, ALU.add, [R1B[co], ps[0]], [R0B[co]])
        ln_to_resid(Tn, "ln2_g", "ln2_b")
        for piece in range(8):
            sB_, wv = wpiece(w1_b, pc["w1"], piece * 512, piece * 512 + 512)
            for hc in range(4):
                hidx = piece * 4 + hc
                ps = psum.get()
                mm(ps[1][:, :Tn], [(wv[:, k, hc * 128:(hc + 1) * 128], xnb[:, k, :Tn]) for k in range(8)], [sB_] + xnbB, [ps[0]])
                rl = f32t.get()
                act(rl[1][:, :Tn], ps[1][:, :Tn], AF.Relu, [ps[0], pkB], [rl[0]], bias=pkc("b1", hidx))
                tt("pool" if hidx % 2 else "dve", hid[:, hidx, :Tn], rl[1][:, :Tn], rl[1][:, :Tn], ALU.mult, [rl[0]], [hidB[hidx]])
        w2v = w2_b.rearrange("(k p) n -> p k n", p=128)
        for cp_ in range(4):
            pss = [psum.get(), psum.get()]
            for kh in range(2):
                sB_, wv = ring_load(w2v[:, kh * 16:kh * 16 + 16, cp_ * 256:cp_ * 256 + 256], 16, 256, pc["w2"])
                for oc in range(2):
                    for k in range(16):
                        mm1(pss[oc][1][:, :Tn], wv[:, k, oc * 128:(oc + 1) * 128], hid[:, kh * 16 + k, :Tn],
                            kh == 0 and k == 0, kh == 1 and k == 15, [sB_, hidB[kh * 16 + k]], [pss[oc][0]])
            for oc in range(2):
                co = cp_ * 2 + oc
                stt("dve", R0[:, co, :Tn], R1[:, co, :Tn], ALPHA, pss[oc][1][:, :Tn], ALU.mult, ALU.add, [R1B[co], pss[oc][0]], [R0B[co]])
                ts("pool", R0[:, co, :Tn], R0[:, co, :Tn], pkc("b2", co), None, ALU.add, None, [R0B[co], pkB], [R0B[co]])
        ln_to_resid(Tn, "ln3_g", "ln3_b")
        dma("sp", y_dst, R1[:, :, :Tn], R1B[0], reads=R1B, final=True)

    xT_v = xT.rearrange("(k p) t -> p k t", p=128)
    if RUN_KV:
        memTb = xb[1]
        dma("pool", memTb[:, :, 0:256], memT_d.rearrange("(k p) m -> p k m", p=128), xbB[1], writes=[xbB[1]])
        kv_i = [4]
        for (wb, pcb, dst_d, is_k) in ((w_k_b, pc["w_k"], kout_d, True), (w_v_b, pc["w_v"], vout_d, False)):
            for half in range(2):
                sB_, wv = wpiece(wb, pcb, half * 512, half * 512 + 512)
                if is_k and KVD[0] == "1":
                    for ci in range(4):
                        ps = psum.get()
                        mm(ps[1][:, :256], [(wv[:, k, ci * 128:(ci + 1) * 128], memTb[:, k, 0:256]) for k in range(8)], [sB_, xbB[1]], [ps[0]])
                        cp("act", kT[:, half * 4 + ci, :], ps[1][:, :256], [ps[0]], [kTB])
                for mc in range(2 if KVD[1] == "1" else 0):
                    ps = psum.get()
                    if KVD[3:4] == "h":
                        mm(ps[1][:, 0:256], [(memTb[:, k, mc * 128:(mc + 1) * 128], wv[:, k, 0:256]) for k in range(8)], [sB_, xbB[1]], [ps[0]])
                        mm(ps[1][:, 256:512], [(memTb[:, k, mc * 128:(mc + 1) * 128], wv[:, k, 256:512]) for k in range(8)], [sB_, xbB[1]], [ps[0]])
                    else:
                        mm(ps[1][:, :], [(memTb[:, k, mc * 128:(mc + 1) * 128], wv[:, k, :]) for k in range(8)], [sB_, xbB[1]], [ps[0]])
                    if KVD[4:5] == "d":
                        _it = kvstg.get()
                        stgB, stg = [_it[0]], _it[1][:]
                    else:
                        stgB, stg = big(kv_i[0])
                        kv_i[0] = 4 + (kv_i[0] - 4 + 1) % 4
                    if KVD[5:6] == "s":
                        for hh in range(2):
                            cp("act", stg[:, hh * 256:(hh + 1) * 256], ps[1][:, hh * 256:(hh + 1) * 256], [ps[0]], stgB)
                            if not is_k:
                                cp("dve", vv[:, mc, half * 512 + hh * 256:half * 512 + (hh + 1) * 256], ps[1][:, hh * 256:(hh + 1) * 256], [ps[0]], [vvB])
                    elif KVD[5:6] == "n":
                        pass
                    elif KVD[5:6] == "a":
                        cp("act", stg, ps[1][:], [ps[0]], stgB)
                    elif KVD[5:6] == "v":
                        cp("dve", stg, ps[1][:], [ps[0]], stgB)
                    else:
                        cp("act", stg, ps[1][:], [ps[0]], stgB)
                        if not is_k:
                            cp("dve", vv[:, mc, half * 512:(half + 1) * 512], stg, stgB, [vvB])
                    if KVD[2] == "1":
                        dma("sp", dst_d[mc * 128:(mc + 1) * 128, half * 512:(half + 1) * 512], stg, stgB[0], reads=stgB, final=True)

    memset("dve", Hre[:], 0.0, HB)
    memset("dve", Him[:], 0.0, HB)
    winuB, winu = wpiece(w_in_b, pc["w_in"], 0, 512)
    NPB = NPRE // T
    xi = [0]

    def load_xb(col):
        i = xi[0] % 2
        xi[0] += 1
        dma("pool", xb[i][:, :, :], xT_v[:, :, col:col + T], xbB[i], writes=[xbB[i]])
        return i

    nxt = load_xb((NPB - NPB_RUN) * T)
    qi = [0]
    for pb in range(NPB - NPB_RUN, NPB):
        cur = nxt
        nxt = load_xb((pb + 1) * T)
        for blk in range(T // 128):
            pu = psum.get()
            mm(pu[1][:, :], [(xb[cur][:, k, blk * 128:(blk + 1) * 128], winu[:, k, :]) for k in range(8)], [xbB[cur], winuB], [pu[0]])
            ubk = ublk.get()
            cp("act", ubk[1][:], pu[1][:], [pu[0]], [ubk[0]])
            wr, wi = psum.get(), psum.get()

            def fn(e, ubk=ubk, wr=wr, wi=wi):
                ins = None
                for p_ in range(16):
                    e.matmul(wr[1][:, p_ * 32:(p_ + 1) * 32], Pre[:, p_, :], ubk[1][:, p_ * 32:(p_ + 1) * 32], start=True, stop=True)
                    ins = e.matmul(wi[1][:, p_ * 32:(p_ + 1) * 32], Pim[:, p_, :], ubk[1][:, p_ * 32:(p_ + 1) * 32], start=True, stop=True)
                return ins
            P.op("pe", fn, [lhsB, ubk[0]], [wr[0], wi[0]])
            base = (qi[0] % 2) * 2
            qi[0] += 1
            (q1B, q1), (q2B, q2) = big(base), big(base + 1)
            (q3B, q3), (q4B, q4) = big(4 + base), big(4 + base + 1)
            tt("dve", q1, wr[1][:], Bxr[:], ALU.mult, [wr[0], BxB], q1B)
            tt("dve", q2, wi[1][:], Bxi[:], ALU.mult, [wi[0], BxB], q2B)
            tt("dve", q3, wi[1][:], Bxr[:], ALU.mult, [wi[0], BxB], q3B)
            tt("dve", q4, wr[1][:], Bxi[:], ALU.mult, [wr[0], BxB], q4B)
            tt("pool", q1, q1, q2, ALU.subtract, q1B + q2B, q1B)
            tt("pool", q3, q3, q4, ALU.add, q3B + q4B, q3B)
            sr, si = tiny.get(), tiny.get()
            P.op("dve", (lambda sr, q1: lambda e: e.tensor_reduce(out=sr[1][:], in_=q1.rearrange("p (a b) -> p a b", a=16), axis=AX.X, op=ALU.add))(sr, q1), q1B, [sr[0]])
            P.op("dve", (lambda si, q3: lambda e: e.tensor_reduce(out=si[1][:], in_=q3.rearrange("p (a b) -> p a b", a=16), axis=AX.X, op=ALU.add))(si, q3), q3B, [si[0]])
            u1, u2, u3, u4 = tiny.get(), tiny.get(), tiny.get(), tiny.get()
            tt("pool", u1[1][:], a128r[:], Hre[:], ALU.mult, [a128B] + HB, [u1[0]])
            tt("pool", u2[1][:], a128i[:], Him[:], ALU.mult, [a128B] + HB, [u2[0]])
            tt("pool", u3[1][:], a128r[:], Him[:], ALU.mult, [a128B] + HB, [u3[0]])
            tt("pool", u4[1][:], a128i[:], Hre[:], ALU.mult, [a128B] + HB, [u4[0]])
            tt("pool", u1[1][:], u1[1][:], u2[1][:], ALU.subtract, [u1[0], u2[0]], [u1[0]])
            tt("pool", u3[1][:], u3[1][:], u4[1][:], ALU.add, [u3[0], u4[0]], [u3[0]])
            tt("pool", Hre[:], u1[1][:], sr[1][:], ALU.add, [u1[0], sr[0]], HB)
            tt("pool", Him[:], u3[1][:], si[1][:], ALU.add, [u3[0], si[0]] + HB, HB)

    hxi = xi[0] % 2
    hx, hxB = xb[hxi], xbB[hxi]
    dma("pool", hx[:, :, 0:32], xT_v[:, :, NPRE - 32:NPRE], hxB, writes=[hxB])
    for half in range(2):
        sB_, wv = wpiece(w_in_b, pc["w_in"], 512 + half * 512, 1024 + half * 512)
        for ci in range(2):
            c = half * 2 + ci
            pa, pg = psum.get(), psum.get()
            mm(pa[1][:, :32], [(wv[:, k, ci * 256:ci * 256 + 128], hx[:, k, 0:32]) for k in range(8)], [sB_, hxB], [pa[0]])
            mm(pg[1][:, :32], [(wv[:, k, ci * 256 + 128:ci * 256 + 256], hx[:, k, 0:32]) for k in range(8)], [sB_, hxB], [pg[0]])
            sg = f32t.get()
            act(sg[1][:, :32], pg[1][:, :32], AF.Sigmoid, [pg[0]], [sg[0]])
            tt("dve", vbuf[:, c, 0:30], pa[1][:, 2:32], sg[1][:, 2:32], ALU.mult, [pa[0], sg[0]], [vbufB[c]])

    p_segs = [(s * TS, TS, lambda pi: Hre[:, pi:pi + 1], lambda pi: Him[:, pi:pi + 1], lambda pi: HB[pi]) for s in range(T // TS)]
    yT_v = yT.rearrange("(k p) t -> p k t", p=128)
    cur = nxt

    def p_tail(c, L):
        dma("sp", convout_d[:, c * 30:(c + 1) * 30], vbuf[:, c, L:L + 30], vbufB[c], reads=[vbufB[c]], final=True)

    for it in range(NT_RUN):
        if it + 1 < NT_RUN:
            oth = 1 - cur
            dma("pool", xb[oth][:, :, :], xT_v[:, :, NPRE + (it + 1) * T: NPRE + (it + 2) * T], xbB[oth], writes=[xbB[oth]])
        run_tile(T, xb[cur], xbB[cur], xT_v[:, :, NPRE + it * T: NPRE + (it + 1) * T], p_segs,
                 [(0, T, None, p_tail if it == NT - 1 else None)], [(0, T, None)], yT_v[:, :, it * T:(it + 1) * T])
        cur = 1 - cur
    hst, hst2 = tiny.get(), tiny.get()
    cp("dve", hst[1][:], Hre[:], HB, [hst[0]])
    cp("dve", hst2[1][:], Him[:], HB, [hst2[0]])
    dma("sp", hout_d[:, 0:16], hst[1][:], hst[0], reads=[hst[0]], final=True)
    dma("sp", hout_d[:, 16:32], hst2[1][:], hst2[0], reads=[hst2[0]], final=True)

    if RUN_SAMPLE:
        hsinB = Buf("hs_in")
        dma("sp", Hs[:], h0_d[:, :], hsinB, writes=HsB)
        xs_v = xsT.rearrange("(k p) t -> p k t", p=128)
        dma("pool", xb[cur][:, :, 0:32], xs_v, xbB[cur], writes=[xbB[cur]])

        def s_halo(s):
            def f(c):
                dma("sp", vbuf[:, c, 0:30], convc_d[:, (c * 2 + s) * 30:(c * 2 + s) * 30 + 30], vbufB[c], writes=[vbufB[c]])
            return f

        def s_tail(s):
            def f(c, L):
                dma("sp", convsout_d[:, (c * 2 + s) * 30:(c * 2 + s) * 30 + 30], vbuf[:, c, L:L + 30], vbufB[c], reads=[vbufB[c]], final=True)
            return f

        def s_kv(s):
            def f():
                dma("pool", kT[:], kTs_d[s].rearrange("(k p) m -> p k m", p=128), kTB, writes=[kTB])
                dma("pool", vv[:], vs_d[s].rearrange("(k p) n -> p k n", p=128), vvB, writes=[vvB])
            return f

        def hs_fn(s, reim):
            return lambda pi: Hs[:, pi * 4 + s * 2 + reim: pi * 4 + s * 2 + reim + 1]

        s_segs = [(s * 16, 16, hs_fn(s, 0), hs_fn(s, 1), lambda pi: HsB[pi]) for s in range(2)]
        run_tile(32, xb[cur], xbB[cur], xs_v, s_segs,
                 [(s * 16, 16, s_halo(s), s_tail(s)) for s in range(2)],
                 [(s * 16, 16, s_kv(s)) for s in range(2)],
                 ysT.rearrange("(k p) t -> p k t", p=128), is_sample=True)
        hso = statsb.get()
        cp("dve", hso[1][:, 0:64], Hs[:], HsB, [hso[0]])
        dma("sp", hsout_d[:, :], hso[1][:, 0:64], hso[0], reads=[hso[0]], final=True)

    P.emit(st)
    st.close()
    return nc


_NC_CACHE = {}


def _mode_major(a):
    sh = a.shape
    a = a.reshape((16, 2, 64) + sh[2:])
    return np.ascontiguousarray(np.moveaxis(a, 0, 2).reshape((128, 16) + sh[2:]))


def kernel(x_prompt, x_sample, state_ssm_re, state_ssm_im, cache_conv, cache_mem_k, cache_mem_v,
           mem_prompt, w_in, ssm_a_re, ssm_a_im, ssm_log_dt, ssm_b_re, ssm_b_im, ssm_c_re, ssm_c_im,
           ssm_d, glu_w, glu_b, conv_w, conv_b, conv_ln_g, conv_ln_b, w_out, ln1_g, ln1_b,
           mem_w_q, mem_w_k, mem_w_v, mem_w_o, ln2_g, ln2_b,
           mlp_w1, mlp_b1, mlp_w2, mlp_b2, ln3_g, ln3_b):
    f = lambda a: np.ascontiguousarray(np.asarray(a, dtype=np.float32))
    x_prompt, x_sample = f(x_prompt), f(x_sample)
    col = lambda v, n: f(v).reshape(n, 128).T
    pk = np.zeros((128, NPK), np.float32)

    def put(name, arr):
        pk[:, PK[name][0]:PK[name][1]] = arr
    put("ln1_g", col(ln1_g[0], 8)); put("ln1_b", col(ln1_b[0], 8))
    put("ln2_g", col(ln2_g[0], 8)); put("ln2_b", col(ln2_b[0], 8))
    put("ln3_g", col(ln3_g[0], 8)); put("ln3_b", col(ln3_b[0], 8))
    put("b1", col(mlp_b1[0], 32)); put("b2", col(mlp_b2[0], 8))
    put("glu_b", col(glu_b[0], 4)); put("ssm_d", col(ssm_d[0], 4))
    put("conv_b", col(conv_b[0], 4)); put("cln_g", col(conv_ln_g[0], 4)); put("cln_b", col(conv_ln_b[0], 4))
    cw = f(conv_w[0]).T.reshape(4, 128, 31).transpose(1, 0, 2).reshape(128, 124)
    put("conv_w", cw)
    put("a_re", _mode_major(f(ssm_a_re[0])[:, :, None])[:, :, 0])
    put("a_im", _mode_major(f(ssm_a_im[0])[:, :, None])[:, :, 0])
    ldt = np.repeat(f(ssm_log_dt[0])[:, None], 64, axis=1)
    put("logdt", _mode_major(ldt[:, :, None])[:, :, 0])
    put("tidx", np.tile(np.arange(128, dtype=np.float32)[None, :], (128, 1)))
    put("eidx", (127.0 - np.arange(128, dtype=np.float32))[:, None])
    rows = np.stack([f(ssm_a_re[0]).reshape(-1), f(ssm_a_im[0]).reshape(-1), ldt.reshape(-1)]).astype(np.float32)
    BT = np.zeros((2, 128, 2048), np.float32)
    CT = np.zeros((2, 128, 2048), np.float32)
    Bx = np.zeros((2, 128, 512), np.float32)
    for ri, (bsrc, csrc) in enumerate(((f(ssm_b_re[0]), f(ssm_c_re[0])), (f(ssm_b_im[0]), f(ssm_c_im[0])))):
        for g in range(32):
            pi, gp, gl = g // 2, g % 2, g % 8
            BT[ri, gl * 16:(gl + 1) * 16, pi * 128 + gp * 64: pi * 128 + gp * 64 + 64] = bsrc[g].T
            CT[ri, gp * 64:(gp + 1) * 64, pi * 128 + gl * 16: pi * 128 + gl * 16 + 16] = csrc[g].T
            Bx[ri, gp * 64:(gp + 1) * 64, pi * 32 + gp * 16: pi * 32 + gp * 16 + 16] = bsrc[g]
    shared = dict(w_in=f(w_in[0]), glu_w=f(glu_w[0]), w_out=f(w_out[0]), w_q=f(mem_w_q[0]), w_k=f(mem_w_k[0]),
                  w_v=f(mem_w_v[0]), w_o=f(mem_w_o[0]), w1=f(mlp_w1[0]), w2=f(mlp_w2[0]), pk=pk, rows=rows, BT=BT, CT=CT, Bx=Bx)
    xTs = [np.ascontiguousarray(x_prompt[b].T) for b in range(2)]
    in_maps = []
    for c in CORES:
        b, j = c // 4, c % 4
        xin = np.zeros((D, NPRE + SEG), np.float32)
        npre = j * SEG
        xin[:, NPRE - npre: NPRE + SEG] = xTs[b][:, 0:(j + 1) * SEG]
        ss = [2 * c, 2 * c + 1]
        xs = np.concatenate([x_sample[s].T for s in ss], axis=1)
        h0 = np.zeros((128, 16, 2, 2), np.float32)
        cc = np.zeros((128, 4, 2, 30), np.float32)
        for si, s in enumerate(ss):
            h0[:, :, si, 0] = _mode_major(f(state_ssm_re[0, s])[:, :, None])[:, :, 0]
            h0[:, :, si, 1] = _mode_major(f(state_ssm_im[0, s])[:, :, None])[:, :, 0]
            cc[:, :, si, :] = f(cache_conv[0, s]).T.reshape(4, 128, 30).transpose(1, 0, 2)
        kTs = np.stack([f(cache_mem_k[0, s]).reshape(256, D).T for s in ss])
        vs = np.stack([f(cache_mem_v[0, s]).reshape(256, D) for s in ss])
        m = dict(shared)
        m.update(xT=xin, xsT=np.ascontiguousarray(xs), h0=h0.reshape(128, 64), convc=cc.reshape(128, 240),
                 kTs=np.ascontiguousarray(kTs), vs=np.ascontiguousarray(vs), memT=np.ascontiguousarray(f(mem_prompt[b]).T))
        in_maps.append(m)
    if "nc" not in _NC_CACHE:
        _NC_CACHE["nc"] = build_program()
    res = run_bass_kernel_spmd(_NC_CACHE["nc"], in_maps, core_ids=list(range(len(CORES))))
    R = {c: res.results[i] for i, c in enumerate(CORES)}

    def unmode(a):
        return a.reshape(2, 64, 16).transpose(2, 0, 1).reshape(32, 64)
    y_prompt = np.zeros((2, 16384, D), np.float32)
    y_sample = np.zeros((16, 16, D), np.float32)
    p_re = np.zeros((1, 2, 32, 64), np.float32)
    p_im = np.zeros((1, 2, 32, 64), np.float32)
    p_conv = np.zeros((1, 2, 30, 512), np.float32)
    p_mk = np.zeros((1, 2, 256, 4, 256), np.float32)
    p_mv = np.zeros((1, 2, 256, 4, 256), np.float32)
    s_re = np.zeros((1, 16, 32, 64), np.float32)
    s_im = np.zeros((1, 16, 32, 64), np.float32)
    s_conv = np.zeros((1, 16, 30, 512), np.float32)
    for c in CORES:
        b, j = c // 4, c % 4
        r = R[c]
        y_prompt[b, j * SEG:(j + 1) * SEG, :] = r["yT"].T
        ys = r["ysT"].T
        hs = r["hsout"].reshape(128, 16, 2, 2)
        cs = r["convsout"].reshape(128, 4, 2, 30)
        for si in range(2):
            s = 2 * c + si
            y_sample[s] = ys[si * 16:(si + 1) * 16]
            s_re[0, s] = unmode(hs[:, :, si, 0])
            s_im[0, s] = unmode(hs[:, :, si, 1])
            s_conv[0, s] = cs[:, :, si, :].transpose(1, 0, 2).reshape(512, 30).T
        if j == 3:
            ho = r["hout"].reshape(128, 2, 16)
            p_re[0, b] = unmode(ho[:, 0, :])
            p_im[0, b] = unmode(ho[:, 1, :])
            p_conv[0, b] = r["convout"].reshape(128, 4, 30).transpose(1, 0, 2).reshape(512, 30).T
        if j == 0:
            p_mk[0, b] = r["kout"].reshape(256, 4, 256)
            p_mv[0, b] = r["vout"].reshape(256, 4, 256)
    return (y_prompt, y_sample, p_re, p_im, p_conv, p_mk, p_mv, s_re, s_im, s_conv)
```

```python
import os
from contextlib import ExitStack
import numpy as np
import concourse.bass as bass
import concourse.mybir as mybir
from concourse.bass_utils import run_bass_kernel_spmd

F32 = mybir.dt.float32
BF16 = mybir.dt.bfloat16
AF = mybir.ActivationFunctionType
ALU = mybir.AluOpType
AX = mybir.AxisListType

D = 1024
NCORE = 8
SEG = 4096
NPRE = 3 * SEG
T = 256
NT = SEG // T
TS = 128
LN_EPS = 1e-5
ALPHA = 2.0 ** 0.25
MAGIC = 12582912.0
TWO_PI = float(2 * np.pi)
PI = float(np.pi)

PK = {}
_o = 0
for _n, _w in [("ln1_g", 8), ("ln1_b", 8), ("ln2_g", 8), ("ln2_b", 8), ("ln3_g", 8), ("ln3_b", 8),
               ("b1", 32), ("b2", 8), ("glu_b", 4), ("ssm_d", 4), ("conv_b", 4), ("cln_g", 4), ("cln_b", 4),
               ("conv_w", 124), ("a_re", 16), ("a_im", 16), ("logdt", 16), ("tidx", 128), ("eidx", 1), ("pad", 3)]:
    PK[_n] = (_o, _o + _w)
    _o += _w
NPK = _o

SYNC_SAME = {e: (e in os.environ.get("K_SYNC", "act,dve,pool").split(",")) for e in ("act", "dve", "pool", "pe", "sp")}
NT_RUN = int(os.environ.get("K_NT", NT))
NGST = int(os.environ.get("K_NGST", "8"))
PIPE = bool(int(os.environ.get("K_PIPE", "1")))
GRP = bool(int(os.environ.get("K_GRP", "1")))
RUN_SAMPLE = bool(int(os.environ.get("K_SAMPLE", "1")))
NPB_RUN = int(os.environ.get("K_NPB", NPRE // T))
RUN_PREP = bool(int(os.environ.get("K_PREP", "1")))
RUN_KV = bool(int(os.environ.get("K_KV", "1")))
RUN_PC = os.environ.get("K_PC", "all")
KVD = os.environ.get("K_KVD", "111")
PC_ROWS = int(os.environ.get("K_PCR", "128"))
PC_COLS = int(os.environ.get("K_PCC", "1024"))
CORES = [int(x) for x in os.environ.get("K_CORES", "0,1,2,3,4,5,6,7").split(",")]


class Ev:
    __slots__ = ("eng", "sem", "value", "op", "group")

    def __init__(self, eng):
        self.eng = eng
        self.sem = None
        self.value = None
        self.op = None
        self.group = None


class Buf:
    __slots__ = ("name", "w", "r", "dma_sem", "dma_count", "group")

    def __init__(self, name, group=None):
        self.name = name
        self.w = None
        self.r = []
        self.dma_sem = None
        self.dma_count = 0
        self.group = group


class DmaGroup:
    def __init__(self, name):
        self.name = name
        self.sem = None
        self.count = 0


class Op:
    __slots__ = ("eng", "fn", "waits", "ev", "is_dma", "signals")

    def __init__(self, eng, fn, waits, ev, is_dma):
        self.eng = eng
        self.fn = fn
        self.waits = waits
        self.ev = ev
        self.is_dma = is_dma
        self.signals = is_dma


class Prog:
    ENGINES = ("pe", "act", "dve", "pool", "sp")

    def __init__(self, nc):
        self.nc = nc
        self.ops = {e: [] for e in self.ENGINES}
        self.all_ops = []
        self.dma_bufs = []
        self.groups = []
        self.final_evs = []

    def group(self, name):
        g = DmaGroup(name)
        self.groups.append(g)
        return g

    def _deps(self, ev, reads, writes):
        waits = []
        for b in reads:
            if b.w is not None:
                waits.append(b.w)
        for b in writes:
            if b.w is not None:
                waits.append(b.w)
            waits.extend(b.r)
        for b in reads:
            b.r.append(ev)
        for b in writes:
            b.w = ev
            b.r = []
        out = []
        seen = set()
        for w in waits:
            if w is ev or id(w) in seen:
                continue
            seen.add(id(w))
            out.append(w)
        return out

    def op(self, eng, fn, reads=(), writes=()):
        ev = Ev(eng)
        waits = self._deps(ev, reads, writes)
        o = Op(eng, fn, waits, ev, False)
        ev.op = o
        self.ops[eng].append(o)
        self.all_ops.append(o)
        return o

    def dma(self, eng, fn, key, reads=(), writes=(), final=False):
        ev = Ev("dma")
        waits = self._deps(ev, reads, writes)
        if key.group is not None:
            ev.group = key.group
            key.group.count += 1
        else:
            if key.dma_count == 0:
                self.dma_bufs.append(key)
            key.dma_count += 1
            ev.sem = key
            ev.value = 16 * key.dma_count
        o = Op(eng, fn, waits, ev, True)
        ev.op = o
        self.ops[eng].append(o)
        self.all_ops.append(o)
        if final:
            self.final_evs.append(ev)
        return o

    def emit(self, stack):
        nc = self.nc
        for o in self.all_ops:
            for w in o.waits:
                if w.op is not None and not w.op.is_dma:
                    if w.eng == o.eng and not SYNC_SAME[o.eng]:
                        continue
                    w.op.signals = True
        esem = {}
        for e in ("pe", "act", "dve", "pool"):
            esem[e] = stack.enter_context(nc.semaphore("s_" + e))
            cnt = 0
            for o in self.ops[e]:
                if o.is_dma:
                    continue
                if o.signals:
                    cnt += 1
                    o.ev.sem = esem[e]
                    o.ev.value = cnt
        for b in self.dma_bufs:
            b.dma_sem = stack.enter_context(nc.semaphore("d_" + b.name))
        for g in self.groups:
            if g.count:
                g.sem = stack.enter_context(nc.semaphore("g_" + g.name))

        def resolve(ev):
            if ev.group is not None:
                return ev.group.sem, 16 * ev.group.count
            if isinstance(ev.sem, Buf):
                return ev.sem.dma_sem, ev.value
            return ev.sem, ev.value

        block = stack.enter_context(nc.Block())
        handles = {"pe": "tensor", "act": "scalar", "dve": "vector", "pool": "gpsimd", "sp": "sync"}
        final_evs = self.final_evs

        def make(e):
            ops = self.ops[e]

            def body(eng):
                waited = {}
                for o in ops:
                    for w in o.waits:
                        if w.eng == e and not w.op.is_dma and not SYNC_SAME[e]:
                            continue
                        sem, val = resolve(w)
                        assert sem is not None and val is not None, (e, w.eng)
                        k = id(sem)
                        if waited.get(k, 0) >= val:
                            continue
                        waited[k] = val
                        eng.wait_ge(sem, val)
                    ins = o.fn(eng)
                    if o.is_dma:
                        sem, _ = resolve(o.ev)
                        ins.then_inc(sem, 16)
                    elif o.signals:
                        ins.then_inc(o.ev.sem, 1)
                if e == "sp":
                    for ev in final_evs:
                        sem, val = resolve(ev)
                        if waited.get(id(sem), 0) >= val:
                            continue
                        waited[id(sem)] = val
                        eng.wait_ge(sem, val)
            return body

        for e in self.ENGINES:
            if self.ops[e] or e == "sp":
                getattr(block, handles[e])(make(e))


class Rot:
    def __init__(self, st, nc, name, shape, dtype, n, psum=False):
        self.items = []
        for i in range(n):
            alloc = nc.psum_tensor if psum else nc.sbuf_tensor
            t = st.enter_context(alloc(f"rt_{name}{i}", shape, dtype))
            self.items.append((Buf(f"{name}{i}"), t))
        self.i = 0

    def get(self):
        it = self.items[self.i % len(self.items)]
        self.i += 1
        return it


def build_program():
    nc = bass.Bass("TRN2", target_bir_lowering=False)
    st = ExitStack()
    P = Prog(nc)

    def din(name, shape):
        return nc.dram_tensor(name, shape, F32, kind="ExternalInput").ap()

    def dout(name, shape):
        return nc.dram_tensor(name, shape, F32, kind="ExternalOutput").ap()

    def dscr(name, shape):
        return nc.dram_tensor(name, shape, BF16, kind="Internal").ap()

    xT = din("xT", [D, NPRE + SEG])
    xsT = din("xsT", [D, 32])
    h0_d = din("h0", [128, 64])
    convc_d = din("convc", [128, 240])
    kTs_d = din("kTs", [2, D, 256])
    vs_d = din("vs", [2, 256, D])
    memT_d = din("memT", [D, 256])
    w_in_d = din("w_in", [D, 1536])
    glu_w_d = din("glu_w", [512, 512])
    w_out_d = din("w_out", [D, D])
    w_q_d = din("w_q", [D, D])
    w_k_d = din("w_k", [D, D])
    w_v_d = din("w_v", [D, D])
    w_o_d = din("w_o", [D, D])
    w1_d = din("w1", [D, 4096])
    w2_d = din("w2", [4096, D])
    pk_d = din("pk", [128, NPK])
    rows_d = din("rows", [3, 2048])
    BT_d = din("BT", [2, 128, 2048])
    CT_d = din("CT", [2, 128, 2048])
    Bx_d = din("Bx", [2, 128, 512])

    yT = dout("yT", [D, SEG])
    ysT = dout("ysT", [D, 32])
    hout_d = dout("hout", [128, 32])
    convout_d = dout("convout", [128, 120])
    kout_d = dout("kout", [256, D])
    vout_d = dout("vout", [256, D])
    hsout_d = dout("hsout", [128, 64])
    convsout_d = dout("convsout", [128, 240])

    w_in_b = dscr("w_in_b", [D, 1536])
    w_out_b = dscr("w_out_b", [D, D])
    w_q_b = dscr("w_q_b", [D, D])
    w_k_b = dscr("w_k_b", [D, D])
    w_v_b = dscr("w_v_b", [D, D])
    w_o_b = dscr("w_o_b", [D, D])
    w1_b = dscr("w1_b", [D, 4096])
    w2_b = dscr("w2_b", [4096, D])

    def sb(name, shape, dt=F32):
        return st.enter_context(nc.sbuf_tensor("sb_" + name, shape, dt))

    pk = sb("pk", [128, NPK])
    cgrp = P.group("consts")
    pkB = Buf("pk", group=cgrp)
    ones_m = sb("ones_m", [128, 128], BF16)
    ones_c = sb("ones_c", [128, 128], BF16)
    ones_1 = sb("ones_1", [128, 128], BF16)
    onesB = Buf("ones")
    R0 = sb("R0", [128, 8, T])
    R1 = sb("R1", [128, 8, T])
    xnb = sb("xnb", [128, 8, T], BF16)
    R0B = [Buf(f"R0_{c}") for c in range(8)]
    R1B = [Buf(f"R1_{c}") for c in range(8)]
    xnbB = [Buf(f"xnb_{c}") for c in range(8)]
    ob, obB = xnb, xnbB
    xb = [sb(f"xb{i}", [128, 8, T], BF16) for i in range(2)]
    xbB = [Buf(f"xb{i}") for i in range(2)]
    NRING = 3
    ring = [sb(f"ring{i}", [128, 4096], BF16) for i in range(NRING)]
    ringB = [Buf(f"ring{i}") for i in range(NRING)]
    ring_i = [0]
    fring = sb("fring", [128, 4096], BF16)
    fringB = Buf("fring")
    glu_sb = sb("glu_sb", [128, 4, 512], BF16)
    gluB = Buf("glu")
    u32 = sb("u32", [128, 4, T])
    ub = sb("ub", [128, 4, T], BF16)
    u32B = [Buf(f"u32_{c}") for c in range(4)]
    ubB = [Buf(f"ub_{c}") for c in range(4)]
    vbuf = sb("vbuf", [128, 4, 30 + T])
    vbufB = [Buf(f"vbuf_{c}") for c in range(4)]
    cacc = sb("cacc", [128, 4, T])
    caccB = [Buf(f"cacc_{c}") for c in range(4)]
    mixin = sb("mixin", [128, 8, T], BF16)
    mixB = [Buf(f"mix_{c}") for c in range(8)]
    z32 = sb("z32", [128, 4, T])
    zb = sb("zb", [128, 4, T], BF16)
    z32B = [Buf(f"z32_{c}") for c in range(4)]
    zbB = [Buf(f"zb_{c}") for c in range(4)]
    hid = sb("hid", [128, 32, T], BF16)
    hidB = [Buf(f"hid_{c}") for c in range(32)]
    qb, qbB = hid, hidB
    kT = sb("kT", [128, 8, 256], BF16)
    kTB = Buf("kT")
    vv = sb("vv", [128, 2, D], BF16)
    vvB = Buf("vv")
    costab = sb("costab", [128, 16, TS])
    sintab = sb("sintab", [128, 16, TS])
    rtab = sb("rtab", [128, 16, TS])
    tabB = Buf("tabs")
    BreT = sb("BreT", [128, 16, 128], BF16)
    BimT = sb("BimT", [128, 16, 128], BF16)
    CreT = sb("CreT", [128, 16, 128], BF16)
    CimT = sb("CimT", [128, 16, 128], BF16)
    Pre = sb("Pre", [128, 16, 128], BF16)
    Pim = sb("Pim", [128, 16, 128], BF16)
    lhsB = Buf("ssm_lhs")
    Bxr = sb("Bxr", [128, 512])
    Bxi = sb("Bxi", [128, 512])
    BxB = Buf("Bx")
    a128r = sb("a128r", [128, 16])
    a128i = sb("a128i", [128, 16])
    a128B = Buf("a128")
    Hre = sb("Hre", [128, 16])
    Him = sb("Him", [128, 16])
    HB = [Buf(f"H_{p}") for p in range(16)]
    Hs = sb("Hs", [128, 64])
    HsB = [Buf(f"Hs_{p}") for p in range(16)]

    s16 = Rot(st, nc, "s16_", [128, T], BF16, 4)
    f32t = Rot(st, nc, "f32t_", [128, T], F32, 5)
    sst = Rot(st, nc, "sst_", [128, 32 if GRP else TS], F32, 10)
    hbf = Rot(st, nc, "hbf_", [128, 32 if GRP else TS], BF16, 4)
    gst = Rot(st, nc, "gst_", [128, 512], F32, NGST)
    ghb = Rot(st, nc, "ghb_", [128, 512], BF16, 4)
    tiny = Rot(st, nc, "tiny_", [128, 16], F32, 8)
    pTr = Rot(st, nc, "pT_", [128, T], BF16, 4)
    ublk = Rot(st, nc, "ublk_", [128, 512], BF16, 2)
    statsb = Rot(st, nc, "stat_", [128, T], F32, 6)
    psum = Rot(st, nc, "ps", [128, 512], F32, 3, psum=True)
    ypsR = Rot(st, nc, "yps", [128, 512], F32, 1, psum=True)
    buR = Rot(st, nc, "bups", [128, 512], F32, 2, psum=True)
    lnR = Rot(st, nc, "lnps", [128, 512], F32, 2, psum=True)

    def big(i):
        src, bufs = (R0, R0B) if i < 4 else (R1, R1B)
        j = (i % 4) * 2
        return [bufs[j], bufs[j + 1]], src[:, j:j + 2, :].rearrange("p a b -> p (a b)")

    pkc = lambda name, i=0, n=1: pk[:, PK[name][0] + i: PK[name][0] + i + n]

    def tt(eng, out, a, b, op, reads, writes):
        P.op(eng, lambda e: e.tensor_tensor(out=out, in0=a, in1=b, op=op), reads, writes)

    def ts(eng, out, a, s1, s2, op0, op1, reads, writes):
        if op1 is None:
            P.op(eng, lambda e: e.tensor_scalar(out=out, in0=a, scalar1=s1, scalar2=None, op0=op0), reads, writes)
        else:
            P.op(eng, lambda e: e.tensor_scalar(out=out, in0=a, scalar1=s1, scalar2=s2, op0=op0, op1=op1), reads, writes)

    def stt(eng, out, a, s, b, op0, op1, reads, writes):
        P.op(eng, lambda e: e.scalar_tensor_tensor(out=out, in0=a, scalar=s, in1=b, op0=op0, op1=op1), reads, writes)

    def act(out, in_, func, reads, writes, scale=None, bias=None):
        kw = {}
        if scale is not None:
            kw["scale"] = scale
        if bias is not None:
            kw["bias"] = bias
        P.op("act", lambda e: e.activation(out=out, in_=in_, func=func, **kw), reads, writes)

    def cp(eng, out, in_, reads, writes):
        if eng == "act":
            act(out, in_, AF.Copy, reads, writes)
        else:
            P.op(eng, lambda e: e.tensor_copy(out=out, in_=in_), reads, writes)

    def recip(out, in_, reads, writes):
        P.op("dve", lambda e: e.reciprocal(out=out, in_=in_), reads, writes)

    def mm(out, pairs, reads, writes):
        n = len(pairs)

        def fn(e):
            ins = None
            for i, (l, r) in enumerate(pairs):
                ins = e.matmul(out, l, r, start=(i == 0), stop=(i == n - 1))
            return ins
        P.op("pe", fn, reads, writes)

    def mm1(out, l, r, start, stop, reads, writes):
        P.op("pe", lambda e: e.matmul(out, l, r, start=start, stop=stop), reads, writes)

    def dma(eng, out, in_, key, reads=(), writes=(), final=False):
        P.dma(eng, lambda e: e.dma_start(out=out, in_=in_), key, reads=reads, writes=writes, final=final)

    def memset(eng, ap, val, writes):
        P.op(eng, lambda e: e.memset(ap, val), (), writes)

    def range_reduce(out, in_, shift, reads, writes, tmp, tmpB):
        src = in_
        rd = list(reads)
        if shift != 0.0:
            ts("dve", out, in_, shift, None, ALU.add, None, reads, writes)
            src = out
            rd = list(writes)
        ts("dve", tmp, src, 1.0 / TWO_PI, MAGIC, ALU.mult, ALU.add, rd, tmpB)
        ts("dve", tmp, tmp, MAGIC, -TWO_PI, ALU.subtract, ALU.mult, tmpB, tmpB)
        tt("dve", out, src, tmp, ALU.add, rd + list(tmpB), writes)
        ts("dve", out, out, PI, -PI, ALU.min, ALU.max, writes, writes)

    dma("sp", pk[:], pk_d[:, :], pkB, writes=[pkB])
    dma("pool", glu_sb[:], glu_w_d.rearrange("(k p) n -> p k n", p=128), gluB, writes=[gluB])
    memset("dve", ones_m[:], 1.0 / 1024.0, [onesB])
    memset("dve", ones_c[:], 1.0 / 512.0, [onesB])
    memset("dve", ones_1[:], 1.0, [onesB])

    pcsB = [Buf(f"pcs{i}") for i in range(NRING)]
    pclB = [Buf(f"pcl{i}") for i in range(NRING)]

    def precast(name, dst, src, nrows, colmap=None):
        bl = [Buf(f"pcb_{name}{i}") for i in range(NRING)]
        ncols = src.shape[1]
        nk = nrows // 128
        sv = src.rearrange("(k p) n -> p k n", p=128)
        dv = dst.rearrange("(k p) n -> p k n", p=128)
        if colmap is None:
            colmap = [(c0, c0, min(1024, ncols - c0)) for c0 in range(0, ncols, 1024)]
        for (d0, s0, n) in colmap:
            kstep = max(1, min(nk, 4096 // n))
            for k0 in range(0, nk, kstep):
                i = ring_i[0] % NRING
                ring_i[0] += 1
                view = ring[i][:, 0:kstep * n].rearrange("p (a b) -> p a b", a=kstep)
                dma("pool", view, sv[:, k0:k0 + kstep, s0:s0 + n], pclB[i], writes=[ringB[i]])
                dma("sp", dv[:, k0:k0 + kstep, d0:d0 + n], view, pcsB[i], reads=[ringB[i]], writes=[bl[i]])
        return bl

    win_map = [(0, 0, 512)]
    for i in range(4):
        win_map.append((512 + 256 * i, 512 + 128 * i, 128))
        win_map.append((512 + 256 * i + 128, 1024 + 128 * i, 128))
    pc = {}
    if RUN_PC == "none":
        def precast(name, dst, src, nrows, colmap=None):
            return [Buf("pcb_" + name)]
    pc["w_in"] = precast("w_in", w_in_b, w_in_d, D, win_map)
    pc["w_k"] = precast("w_k", w_k_b, w_k_d, D)
    pc["w_v"] = precast("w_v", w_v_b, w_v_d, D)
    pc["w_out"] = precast("w_out", w_out_b, w_out_d, D)
    pc["w_q"] = precast("w_q", w_q_b, w_q_d, D)
    pc["w_o"] = precast("w_o", w_o_b, w_o_d, D)
    pc["w1"] = precast("w1", w1_b, w1_d, D)
    pc["w2"] = precast("w2", w2_b, w2_d, 4096)

    def ring_load(src_ap, a, b, pcb):
        i = ring_i[0] % NRING
        ring_i[0] += 1
        view = ring[i][:, 0:a * b].rearrange("p (a b) -> p a b", a=a)
        dma("sp", view, src_ap, ringB[i], reads=pcb, writes=[ringB[i]])
        return ringB[i], view

    def wpiece(wb, pcb, n0, n1):
        return ring_load(wb.rearrange("(k p) n -> p k n", p=128)[:, :, n0:n1], 8, n1 - n0, pcb)

    def fpiece(wb, pcb, n0, n1):
        view = fring[:, 0:8 * (n1 - n0)].rearrange("p (a b) -> p a b", a=8)
        dma("sp", view, wb.rearrange("(k p) n -> p k n", p=128)[:, :, n0:n1], fringB, reads=pcb, writes=[fringB])
        return fringB, view

    def disc(A, Bm, L, tmp, rdB, wB):
        T1, T2, T3, T4, T5, T6, T7 = tmp
        act(L, L, AF.Exp, rdB, wB)
        tt("dve", T1, A, L, ALU.mult, wB, wB)
        tt("dve", T5, Bm, L, ALU.mult, wB, wB)
        act(T6, T1, AF.Exp, wB, wB)
        range_reduce(T2, T5, 0.0, wB, wB, T7, wB)
        act(T2, T2, AF.Sin, wB, wB)
        range_reduce(T3, T5, PI / 2, wB, wB, T7, wB)
        act(T3, T3, AF.Sin, wB, wB)
        tt("dve", T3, T6, T3, ALU.mult, wB, wB)
        tt("dve", T2, T6, T2, ALU.mult, wB, wB)
        ts("dve", T3, T3, -1.0, None, ALU.add, None, wB, wB)
        tt("dve", T6, A, A, ALU.mult, wB, wB)
        tt("dve", T7, Bm, Bm, ALU.mult, wB, wB)
        tt("dve", T6, T6, T7, ALU.add, wB, wB)
        recip(T6, T6, wB, wB)
        tt("dve", L, T3, A, ALU.mult, wB, wB)
        tt("dve", T4, T2, Bm, ALU.mult, wB, wB)
        tt("dve", L, L, T4, ALU.add, wB, wB)
        tt("dve", L, L, T6, ALU.mult, wB, wB)
        tt("dve", T4, T2, A, ALU.mult, wB, wB)
        tt("dve", T3, T3, Bm, ALU.mult, wB, wB)
        tt("dve", T4, T4, T3, ALU.subtract, wB, wB)
        tt("dve", T4, T4, T6, ALU.mult, wB, wB)
        return dict(x1=T1, ang=T5, c_re=L, c_im=T4)

    if RUN_PREP:
        mB_ = [Buf("modeprep")]
        mA, mBm, mL = sb("mA", [128, 16]), sb("mBm", [128, 16]), sb("mL", [128, 16])
        mT = [sb(f"mT{i}", [128, 16]) for i in range(7)]
        cp("dve", mA[:], pkc("a_re", 0, 16), [pkB], mB_)
        cp("dve", mBm[:], pkc("a_im", 0, 16), [pkB], mB_)
        cp("dve", mL[:], pkc("logdt", 0, 16), [pkB], mB_)
        dm = disc(mA[:], mBm[:], mL[:], [t[:] for t in mT], mB_, mB_)
        m128a, m128m, mr = sb("m128a", [128, 16]), sb("m128m", [128, 16]), sb("mr", [128, 16])
        ts("dve", m128a[:], dm["ang"], 128.0, None, ALU.mult, None, mB_, mB_)
        act(m128m[:], dm["x1"], AF.Exp, mB_, mB_, scale=128.0)
        range_reduce(mT[1][:], m128a[:], 0.0, mB_, mB_, mT[6][:], mB_)
        act(mT[1][:], mT[1][:], AF.Sin, mB_, mB_)
        range_reduce(mT[2][:], m128a[:], PI / 2, mB_, mB_, mT[6][:], mB_)
        act(mT[2][:], mT[2][:], AF.Sin, mB_, mB_)
        tt("dve", a128r[:], m128m[:], mT[2][:], ALU.mult, mB_, [a128B])
        tt("dve", a128i[:], m128m[:], mT[1][:], ALU.mult, mB_ + [a128B], [a128B])
        act(mr[:], dm["x1"], AF.Exp, mB_, mB_)
        for p_ in range(16):
            tg, tg2 = gst.get(), gst.get()
            tg = (tg[0], tg[1][:, 0:TS])
            tg2 = (tg2[0], tg2[1][:, 0:TS])
            ts("dve", tg[1][:], pkc("tidx", 0, TS), dm["ang"][:, p_:p_ + 1], None, ALU.mult, None, [pkB] + mB_, [tg[0]])
            range_reduce(sintab[:, p_, :], tg[1][:], 0.0, [tg[0]], [tabB], tg2[1][:], [tg2[0]])
            act(sintab[:, p_, :], sintab[:, p_, :], AF.Sin, [tabB], [tabB])
            range_reduce(costab[:, p_, :], tg[1][:], PI / 2, [tg[0]], [tabB], tg2[1][:], [tg2[0]])
            act(costab[:, p_, :], costab[:, p_, :], AF.Sin, [tabB], [tabB])
            ts("dve", rtab[:, p_, :], pkc("tidx", 0, TS), 0.0, mr[:, p_:p_ + 1], ALU.mult, ALU.add, [pkB] + mB_, [tabB])
            memset("dve", rtab[:, p_, 0:1], 0.0, [tabB])
        bxrB, bxr_t = big(0)
        bxiB, bxi_t = big(1)
        dma("sp", bxr_t, Bx_d[0], bxrB[0], writes=bxrB)
        dma("sp", bxi_t, Bx_d[1], bxiB[0], writes=bxiB)
        for p_ in range(16):
            sl = slice(p_ * 32, p_ * 32 + 32)
            cr, ci = dm["c_re"][:, p_:p_ + 1], dm["c_im"][:, p_:p_ + 1]
            ta, tb_ = gst.get(), gst.get()
            ts("dve", ta[1][:, 0:32], bxi_t[:, sl], ci, None, ALU.mult, None, bxiB + mB_, [ta[0]])
            stt("dve", Bxr[:, sl], bxr_t[:, sl], cr, ta[1][:, 0:32], ALU.mult, ALU.subtract, bxrB + mB_ + [ta[0]], [BxB])
            ts("dve", tb_[1][:, 0:32], bxr_t[:, sl], ci, None, ALU.mult, None, bxrB + mB_, [tb_[0]])
            stt("dve", Bxi[:, sl], bxi_t[:, sl], cr, tb_[1][:, 0:32], ALU.mult, ALU.add, bxiB + mB_ + [tb_[0]], [BxB])

        rowB = R0B + R1B
        rt = [R0[:, c, :] for c in range(8)] + [R1[:, c, :] for c in range(8)]
        for blk in range(8):
            cs = slice(blk * 256, blk * 256 + 256)
            rA, rBm, rL = rt[0], rt[1], rt[2]
            for (tile_, row) in ((rA, 0), (rBm, 1), (rL, 2)):
                dma("sp", tile_, rows_d[row:row + 1, cs].partition_broadcast(128), R0B[row], writes=rowB)
            tmp = rt[3:10]
            dr = disc(rA, rBm, rL, tmp, rowB, rowB)
            btr, bti, ctr, cti = rt[10], rt[11], rt[12], rt[13]
            dma("sp", btr, BT_d[0][:, cs], R1B[2], writes=rowB)
            dma("sp", bti, BT_d[1][:, cs], R1B[3], writes=rowB)
            dma("sp", ctr, CT_d[0][:, cs], R1B[4], writes=rowB)
            dma("sp", cti, CT_d[1][:, cs], R1B[5], writes=rowB)
            sA, sB_ = tmp[1], tmp[2]
            s3, s4 = tmp[5], tmp[6]
            osl = lambda t_: t_[:, blk * 2:blk * 2 + 2, :].rearrange("p a b -> p (a b)")
            tt("dve", sA, dr["c_re"], btr, ALU.mult, rowB, rowB)
            tt("dve", sB_, dr["c_im"], bti, ALU.mult, rowB, rowB)
            tt("dve", osl(BreT), sA, sB_, ALU.subtract, rowB, [lhsB])
            tt("dve", sA, dr["c_re"], bti, ALU.mult, rowB, rowB)
            tt("dve", sB_, dr["c_im"], btr, ALU.mult, rowB, rowB)
            tt("dve", osl(BimT), sA, sB_, ALU.add, rowB + [lhsB], [lhsB])
            e_ap = pkc("eidx")
            act(sA, dr["x1"], AF.Exp, rowB + [pkB], rowB, scale=e_ap)
            ts("dve", sB_, dr["ang"], e_ap, None, ALU.mult, None, rowB + [pkB], rowB)
            range_reduce(s3, sB_, 0.0, rowB, rowB, s4, rowB)
            act(s3, s3, AF.Sin, rowB, rowB)
            tt("dve", osl(Pim), sA, s3, ALU.mult, rowB + [lhsB], [lhsB])
            range_reduce(s3, sB_, PI / 2, rowB, rowB, s4, rowB)
            act(s3, s3, AF.Sin, rowB, rowB)
            tt("dve", osl(Pre), sA, s3, ALU.mult, rowB + [lhsB], [lhsB])
            cp("dve", osl(CreT), ctr, rowB + [lhsB], [lhsB])
            ts("dve", osl(CimT), cti, -1.0, None, ALU.mult, None, rowB + [lhsB], [lhsB])

    def layer_norm(nch, srcs, srcB, ones_ap, Tn, g_name, b_name, emit_out):
        pm, pe2 = lnR.get(), lnR.get()
        for c in range(nch):
            s1, s2 = s16.get(), s16.get()
            act(s1[1][:, :Tn], srcs[c], AF.Copy, [srcB[c]], [s1[0]])
            act(s2[1][:, :Tn], srcs[c], AF.Square, [srcB[c]], [s2[0]])
            mm1(pm[1][:, :Tn], ones_ap, s1[1][:, :Tn], c == 0, c == nch - 1, [s1[0], onesB], [pm[0]])
            mm1(pe2[1][:, :Tn], ones_ap, s2[1][:, :Tn], c == 0, c == nch - 1, [s2[0], onesB], [pe2[0]])
        mean, var, nmr = statsb.get(), statsb.get(), statsb.get()
        cp("act", mean[1][:, :Tn], pm[1][:, :Tn], [pm[0]], [mean[0]])
        tt("dve", var[1][:, :Tn], mean[1][:, :Tn], mean[1][:, :Tn], ALU.mult, [mean[0]], [var[0]])
        tt("dve", var[1][:, :Tn], pe2[1][:, :Tn], var[1][:, :Tn], ALU.subtract, [pe2[0], var[0]], [var[0]])
        ts("dve", var[1][:, :Tn], var[1][:, :Tn], LN_EPS, None, ALU.add, None, [var[0]], [var[0]])
        act(var[1][:, :Tn], var[1][:, :Tn], AF.Sqrt, [var[0]], [var[0]])
        recip(var[1][:, :Tn], var[1][:, :Tn], [var[0]], [var[0]])
        stt("dve", nmr[1][:, :Tn], mean[1][:, :Tn], -1.0, var[1][:, :Tn], ALU.mult, ALU.mult, [mean[0], var[0]], [nmr[0]])
        for c in range(nch):
            t_ = f32t.get()
            tt("dve", t_[1][:, :Tn], srcs[c], var[1][:, :Tn], ALU.mult, [srcB[c], var[0]], [t_[0]])
            tt("pool", t_[1][:, :Tn], t_[1][:, :Tn], nmr[1][:, :Tn], ALU.add, [t_[0], nmr[0]], [t_[0]])
            emit_out(c, t_[1][:, :Tn], t_[0], pkc(g_name, c), pkc(b_name, c))

    def ln_to_resid(Tn, g_name, b_name):
        def emit_out(c, t_ap, tB, g_ap, b_ap):
            act(R1[:, c, :Tn], t_ap, AF.Identity, [tB, pkB], [R1B[c]], scale=g_ap, bias=b_ap)
            cp("pool", xnb[:, c, :Tn], R1[:, c, :Tn], [R1B[c]], [xnbB[c]])
        layer_norm(8, [R0[:, c, :Tn] for c in range(8)], R0B, ones_m[:], Tn, g_name, b_name, emit_out)

    def ssm_segment(pi, col0, L, Hr_ap, Hi_ap, HBuf, ypsum, first, last):
        c = pi // 4
        bu = psum.get()
        bre, bim = bu[1][:, 0:L], bu[1][:, 256:256 + L]
        mm1(bre, BreT[:, pi, :], ub[:, c, col0:col0 + L], True, True, [lhsB, ubB[c]], [bu[0]])
        mm1(bim, BimT[:, pi, :], ub[:, c, col0:col0 + L], True, True, [lhsB, ubB[c]], [bu[0]])
        cosA, sinA, rA = costab[:, pi, 0:L], sintab[:, pi, 0:L], rtab[:, pi, 0:L]
        cos1, sin1 = costab[:, pi, 1:2], sintab[:, pi, 1:2]
        g0, tq = tiny.get(), tiny.get()
        tt("pool", tq[1][:, 0:1], Hi_ap, sin1, ALU.mult, [HBuf, tabB], [tq[0]])
        tt("pool", tq[1][:, 1:2], Hr_ap, cos1, ALU.mult, [HBuf, tabB, tq[0]], [tq[0]])
        tt("pool", tq[1][:, 2:3], Hr_ap, sin1, ALU.mult, [HBuf, tabB, tq[0]], [tq[0]])
        tt("pool", tq[1][:, 3:4], Hi_ap, cos1, ALU.mult, [HBuf, tabB, tq[0]], [tq[0]])
        tt("pool", g0[1][:, 0:1], tq[1][:, 1:2], tq[1][:, 0:1], ALU.subtract, [tq[0]], [g0[0]])
        tt("pool", g0[1][:, 1:2], tq[1][:, 3:4], tq[1][:, 2:3], ALU.add, [tq[0], g0[0]], [g0[0]])
        t1, t2, t3, t4 = sst.get(), sst.get(), sst.get(), sst.get()
        tt("dve", t1[1][:, :L], bre, cosA, ALU.mult, [bu[0], tabB], [t1[0]])
        tt("dve", t2[1][:, :L], bim, sinA, ALU.mult, [bu[0], tabB], [t2[0]])
        tt("dve", t3[1][:, :L], bim, cosA, ALU.mult, [bu[0], tabB], [t3[0]])
        tt("dve", t4[1][:, :L], bre, sinA, ALU.mult, [bu[0], tabB], [t4[0]])
        tt("pool", t1[1][:, :L], t1[1][:, :L], t2[1][:, :L], ALU.add, [t1[0], t2[0]], [t1[0]])
        tt("pool", t3[1][:, :L], t3[1][:, :L], t4[1][:, :L], ALU.subtract, [t3[0], t4[0]], [t3[0]])
        r1 = rtab[:, pi, 1:2]
        tt("pool", g0[1][:, 2:3], g0[1][:, 0:1], r1, ALU.mult, [g0[0], tabB], [g0[0]])
        tt("pool", g0[1][:, 3:4], g0[1][:, 1:2], r1, ALU.mult, [g0[0], tabB], [g0[0]])
        tt("pool", t1[1][:, 0:1], t1[1][:, 0:1], g0[1][:, 2:3], ALU.add, [t1[0], g0[0]], [t1[0]])
        tt("pool", t3[1][:, 0:1], t3[1][:, 0:1], g0[1][:, 3:4], ALU.add, [t3[0], g0[0]], [t3[0]])
        gre, gim = sst.get(), sst.get()
        P.op("dve", lambda e: e.tensor_tensor_scan(out=gre[1][:, :L], data0=rA, data1=t1[1][:, :L], initial=0.0, op0=ALU.mult, op1=ALU.add),
             [tabB, t1[0]], [gre[0]])
        P.op("dve", lambda e: e.tensor_tensor_scan(out=gim[1][:, :L], data0=rA, data1=t3[1][:, :L], initial=0.0, op0=ALU.mult, op1=ALU.add),
             [tabB, t3[0]], [gim[0]])
        p1, p2, p3, p4 = sst.get(), sst.get(), sst.get(), sst.get()
        tt("pool", p1[1][:, :L], gre[1][:, :L], cosA, ALU.mult, [gre[0], tabB], [p1[0]])
        tt("pool", p2[1][:, :L], gim[1][:, :L], sinA, ALU.mult, [gim[0], tabB], [p2[0]])
        tt("pool", p3[1][:, :L], gim[1][:, :L], cosA, ALU.mult, [gim[0], tabB], [p3[0]])
        tt("pool", p4[1][:, :L], gre[1][:, :L], sinA, ALU.mult, [gre[0], tabB], [p4[0]])
        hr, hi = hbf.get(), hbf.get()
        tt("dve", hr[1][:, :L], p1[1][:, :L], p2[1][:, :L], ALU.subtract, [p1[0], p2[0]], [hr[0]])
        tt("dve", hi[1][:, :L], p3[1][:, :L], p4[1][:, :L], ALU.add, [p3[0], p4[0]], [hi[0]])
        tt("pool", Hr_ap, p1[1][:, L - 1:L], p2[1][:, L - 1:L], ALU.subtract, [p1[0], p2[0]], [HBuf])
        tt("pool", Hi_ap, p3[1][:, L - 1:L], p4[1][:, L - 1:L], ALU.add, [p3[0], p4[0], HBuf], [HBuf])
        yo = ypsum[1][:, col0:col0 + L]
        mm1(yo, CreT[:, pi, :], hr[1][:, :L], first, False, [lhsB, hr[0]], [ypsum[0]])
        mm1(yo, CimT[:, pi, :], hi[1][:, :L], False, last, [lhsB, hi[0]], [ypsum[0]])

    def conv_chunk(c, col0, L, eng):
        o = cacc[:, c, col0:col0 + L]
        base = PK["conv_w"][0] + c * 31
        ts(eng, o, vbuf[:, c, 0:L], pk[:, base:base + 1], pkc("conv_b", c), ALU.mult, ALU.add, [vbufB[c], pkB], [caccB[c]])
        for k in range(1, 31):
            stt(eng, o, vbuf[:, c, k:k + L], pk[:, base + k:base + k + 1], o, ALU.mult, ALU.add, [vbufB[c], pkB, caccB[c]], [caccB[c]])

    def v3(t_):
        return t_.rearrange("p (a b) -> p a b", a=4)

    def ssm_group(c, col0, yp, filler):
        ps4 = slice(4 * c, 4 * c + 4)
        HBs = HB[4 * c:4 * c + 4]
        bR, bI = buR.get(), buR.get()

        def fn(e):
            ins = None
            for q in range(4):
                e.matmul(bR[1][:, q * 128:(q + 1) * 128], BreT[:, 4 * c + q, :], ub[:, c, col0:col0 + 128], start=True, stop=True)
                ins = e.matmul(bI[1][:, q * 128:(q + 1) * 128], BimT[:, 4 * c + q, :], ub[:, c, col0:col0 + 128], start=True, stop=True)
            return ins
        P.op("pe", fn, [lhsB, ubB[c]], [bR[0], bI[0]])
        cosG, sinG = costab[:, ps4, :], sintab[:, ps4, :]
        rG = rtab[:, ps4, :].rearrange("p a b -> p (a b)")
        cos1, sin1, r1 = costab[:, ps4, 1], sintab[:, ps4, 1], rtab[:, ps4, 1]
        Hr, Hi = Hre[:, ps4], Him[:, ps4]
        tq, g0 = tiny.get(), tiny.get()
        tt("pool", tq[1][:, 0:4], Hi, sin1, ALU.mult, HBs + [tabB], [tq[0]])
        tt("pool", tq[1][:, 4:8], Hr, cos1, ALU.mult, HBs + [tabB, tq[0]], [tq[0]])
        tt("pool", tq[1][:, 8:12], Hr, sin1, ALU.mult, HBs + [tabB, tq[0]], [tq[0]])
        tt("pool", tq[1][:, 12:16], Hi, cos1, ALU.mult, HBs + [tabB, tq[0]], [tq[0]])
        tt("pool", g0[1][:, 0:4], tq[1][:, 4:8], tq[1][:, 0:4], ALU.subtract, [tq[0]], [g0[0]])
        tt("pool", g0[1][:, 4:8], tq[1][:, 12:16], tq[1][:, 8:12], ALU.add, [tq[0], g0[0]], [g0[0]])
        tt("pool", g0[1][:, 8:12], g0[1][:, 0:4], r1, ALU.mult, [g0[0], tabB], [g0[0]])
        tt("pool", g0[1][:, 12:16], g0[1][:, 4:8], r1, ALU.mult, [g0[0], tabB], [g0[0]])
        filler(3)
        yield
        A, B_, C, D_ = gst.get(), gst.get(), gst.get(), gst.get()
        tt("dve", v3(A[1][:]), v3(bR[1][:]), cosG, ALU.mult, [bR[0], tabB], [A[0]])
        tt("dve", v3(B_[1][:]), v3(bI[1][:]), sinG, ALU.mult, [bI[0], tabB], [B_[0]])
        filler(2)
        tt("dve", v3(C[1][:]), v3(bI[1][:]), cosG, ALU.mult, [bI[0], tabB], [C[0]])
        tt("dve", v3(D_[1][:]), v3(bR[1][:]), sinG, ALU.mult, [bR[0], tabB], [D_[0]])
        filler(2)
        yield
        tt("pool", A[1][:], A[1][:], B_[1][:], ALU.add, [A[0], B_[0]], [A[0]])
        tt("pool", v3(A[1][:])[:, :, 0], v3(A[1][:])[:, :, 0], g0[1][:, 8:12], ALU.add, [A[0], g0[0]], [A[0]])
        tt("pool", C[1][:], C[1][:], D_[1][:], ALU.subtract, [C[0], D_[0]], [C[0]])
        tt("pool", v3(C[1][:])[:, :, 0], v3(C[1][:])[:, :, 0], g0[1][:, 12:16], ALU.add, [C[0], g0[0]], [C[0]])
        filler(4)
        yield
        GR, GI = gst.get(), gst.get()
        P.op("dve", lambda e: e.tensor_tensor_scan(out=GR[1][:], data0=rG, data1=A[1][:], initial=0.0, op0=ALU.mult, op1=ALU.add),
             [tabB, A[0]], [GR[0]])
        filler(2)
        P.op("dve", lambda e: e.tensor_tensor_scan(out=GI[1][:], data0=rG, data1=C[1][:], initial=0.0, op0=ALU.mult, op1=ALU.add),
             [tabB, C[0]], [GI[0]])
        filler(2)
        yield
        tt("pool", v3(B_[1][:]), v3(GR[1][:]), cosG, ALU.mult, [GR[0], tabB], [B_[0]])
        tt("pool", v3(D_[1][:]), v3(GI[1][:]), sinG, ALU.mult, [GI[0], tabB], [D_[0]])
        tt("pool", v3(A[1][:]), v3(GI[1][:]), cosG, ALU.mult, [GI[0], tabB], [A[0]])
        tt("pool", v3(C[1][:]), v3(GR[1][:]), sinG, ALU.mult, [GR[0], tabB], [C[0]])
        filler(4)
        yield
        hr, hi = ghb.get(), ghb.get()
        tt("dve", hr[1][:], B_[1][:], D_[1][:], ALU.subtract, [B_[0], D_[0]], [hr[0]])
        filler(1)
        tt("dve", hi[1][:], A[1][:], C[1][:], ALU.add, [A[0], C[0]], [hi[0]])
        filler(1)
        tt("pool", Hr, v3(B_[1][:])[:, :, 127], v3(D_[1][:])[:, :, 127], ALU.subtract, [B_[0], D_[0]], HBs)
        tt("pool", Hi, v3(A[1][:])[:, :, 127], v3(C[1][:])[:, :, 127], ALU.add, [A[0], C[0]] + HBs, HBs)
        yo = yp[1][:, col0:col0 + 128]

        def fn2(e):
            ins = None
            for q in range(4):
                e.matmul(yo, CreT[:, 4 * c + q, :], hr[1][:, q * 128:(q + 1) * 128], start=(q == 0), stop=False)
                ins = e.matmul(yo, CimT[:, 4 * c + q, :], hi[1][:, q * 128:(q + 1) * 128], start=False, stop=(q == 3))
            return ins
        P.op("pe", fn2, [lhsB, hr[0], hi[0]], [yp[0]])
        yield

    def front_p(xb_t, xbB_t, tail_fn):
        Tn = T
        s0B, w0 = fpiece(w_in_b, pc["w_in"], 0, 512)
        for c in range(4):
            ps = psum.get()
            mm(ps[1][:, :Tn], [(w0[:, k, c * 128:(c + 1) * 128], xb_t[:, k, :Tn]) for k in range(8)], [s0B, xbB_t], [ps[0]])
            cp("act", u32[:, c, :Tn], ps[1][:, :Tn], [ps[0]], [u32B[c]])
            cp("dve", ub[:, c, :Tn], u32[:, c, :Tn], [u32B[c]], [ubB[c]])
            yield
        for half in range(2):
            sB_, wv = fpiece(w_in_b, pc["w_in"], 512 + half * 512, 1024 + half * 512)
            for ci in range(2):
                c = half * 2 + ci
                pa, pg = psum.get(), psum.get()
                mm(pa[1][:, :Tn], [(wv[:, k, ci * 256:ci * 256 + 128], xb_t[:, k, :Tn]) for k in range(8)], [sB_, xbB_t], [pa[0]])
                mm(pg[1][:, :Tn], [(wv[:, k, ci * 256 + 128:ci * 256 + 256], xb_t[:, k, :Tn]) for k in range(8)], [sB_, xbB_t], [pg[0]])
                sg = f32t.get()
                act(sg[1][:, :Tn], pg[1][:, :Tn], AF.Sigmoid, [pg[0]], [sg[0]])
                tt("dve", vbuf[:, c, 30:30 + Tn], pa[1][:, :Tn], sg[1][:, :Tn], ALU.mult, [pa[0], sg[0]], [vbufB[c]])
                yield
        taps = []
        for k in range(31):
            for c in range(4):
                taps.append((c, k))
        tap_i = [0]

        def filler(n):
            for _ in range(n):
                if tap_i[0] >= len(taps):
                    return
                c, k = taps[tap_i[0]]
                tap_i[0] += 1
                o = cacc[:, c, 0:Tn]
                base = PK["conv_w"][0] + c * 31
                if k == 0:
                    ts("dve", o, vbuf[:, c, 0:Tn], pk[:, base:base + 1], pkc("conv_b", c), ALU.mult, ALU.add, [vbufB[c], pkB], [caccB[c]])
                else:
                    stt("dve", o, vbuf[:, c, k:k + Tn], pk[:, base + k:base + k + 1], o, ALU.mult, ALU.add, [vbufB[c], pkB, caccB[c]], [caccB[c]])

        for c in range(4):
            yp = ypsR.get()
            for sg_ in range(T // TS):
                if GRP:
                    for _ in ssm_group(c, sg_ * TS, yp, filler):
                        yield
                else:
                    for q in range(4):
                        pi = c * 4 + q
                        ssm_segment(pi, sg_ * TS, TS, Hre[:, pi:pi + 1], Him[:, pi:pi + 1], HB[pi], yp, q == 0, q == 3)
                        filler(8)
                        yield
            zp = f32t.get()
            stt("dve", zp[1][:, :Tn], u32[:, c, :Tn], pkc("ssm_d", c), yp[1][:, :Tn], ALU.mult, ALU.add, [u32B[c], pkB, yp[0]], [zp[0]])
            act(z32[:, c, :Tn], zp[1][:, :Tn], AF.Gelu_apprx_tanh, [zp[0]], [z32B[c]])
            cp("pool", zb[:, c, :Tn], z32[:, c, :Tn], [z32B[c]], [zbB[c]])
            yield
        while tap_i[0] < len(taps):
            filler(4)
            yield
        for co in range(4):
            ps = psum.get()
            mm(ps[1][:, :Tn], [(glu_sb[:, k, co * 128:(co + 1) * 128], zb[:, k, :Tn]) for k in range(4)], [gluB] + zbB, [ps[0]])
            sg = f32t.get()
            act(sg[1][:, :Tn], ps[1][:, :Tn], AF.Sigmoid, [ps[0], pkB], [sg[0]], bias=pkc("glu_b", co))
            tt("dve", mixin[:, co, :Tn], z32[:, co, :Tn], sg[1][:, :Tn], ALU.mult, [z32B[co], sg[0]], [mixB[co]])
            yield
        for c in range(4):
            if tail_fn is not None:
                tail_fn(c, Tn)
            cp("pool", vbuf[:, c, 0:30], vbuf[:, c, Tn:Tn + 30], [vbufB[c]], [vbufB[c]])

        def silu_out(c, t_ap, tB, g_ap, b_ap):
            act(mixin[:, 4 + c, :Tn], t_ap, AF.Silu, [tB, pkB], [mixB[4 + c]], scale=g_ap, bias=b_ap)
        layer_norm(4, [cacc[:, c, :Tn] for c in range(4)], caccB, ones_c[:], Tn, "cln_g", "cln_b", silu_out)
        yield

    def drain(g):
        for _ in g:
            pass

    def merge(ga, gb):
        da = db = False
        while not (da and db):
            if not da:
                try:
                    next(ga)
                except StopIteration:
                    da = True
            if not db:
                try:
                    next(gb)
                except StopIteration:
                    db = True

    def run_tile(Tn, xb_t, xbB_t, x32_src, segs, conv_segs, att_segs, y_dst, is_sample=False, part="all"):
        if part in ("all", "front"):
            s0B, w0 = wpiece(w_in_b, pc["w_in"], 0, 512)
            for c in range(4):
                ps = psum.get()
                mm(ps[1][:, :Tn], [(w0[:, k, c * 128:(c + 1) * 128], xb_t[:, k, :Tn]) for k in range(8)], [s0B, xbB_t], [ps[0]])
                cp("act", u32[:, c, :Tn], ps[1][:, :Tn], [ps[0]], [u32B[c]])
                cp("dve", ub[:, c, :Tn], u32[:, c, :Tn], [u32B[c]], [ubB[c]])
            for c in range(4):
                yp = ypsR.get()
                for (col0, L, Hr_fn, Hi_fn, HB_fn) in segs:
                    for q in range(4):
                        pi = c * 4 + q
                        ssm_segment(pi, col0, L, Hr_fn(pi), Hi_fn(pi), HB_fn(pi), yp, q == 0, q == 3)
                zp = f32t.get()
                stt("dve", zp[1][:, :Tn], u32[:, c, :Tn], pkc("ssm_d", c), yp[1][:, :Tn], ALU.mult, ALU.add, [u32B[c], pkB, yp[0]], [zp[0]])
                act(z32[:, c, :Tn], zp[1][:, :Tn], AF.Gelu_apprx_tanh, [zp[0]], [z32B[c]])
                cp("pool", zb[:, c, :Tn], z32[:, c, :Tn], [z32B[c]], [zbB[c]])
            for co in range(4):
                ps = psum.get()
                mm(ps[1][:, :Tn], [(glu_sb[:, k, co * 128:(co + 1) * 128], zb[:, k, :Tn]) for k in range(4)], [gluB] + zbB, [ps[0]])
                sg = f32t.get()
                act(sg[1][:, :Tn], ps[1][:, :Tn], AF.Sigmoid, [ps[0], pkB], [sg[0]], bias=pkc("glu_b", co))
                tt("dve", mixin[:, co, :Tn], z32[:, co, :Tn], sg[1][:, :Tn], ALU.mult, [z32B[co], sg[0]], [mixB[co]])
            for half in range(2):
                sB_, wv = wpiece(w_in_b, pc["w_in"], 512 + half * 512, 1024 + half * 512)
                for ci in range(2):
                    c = half * 2 + ci
                    pa, pg = psum.get(), psum.get()
                    mm(pa[1][:, :Tn], [(wv[:, k, ci * 256:ci * 256 + 128], xb_t[:, k, :Tn]) for k in range(8)], [sB_, xbB_t], [pa[0]])
                    mm(pg[1][:, :Tn], [(wv[:, k, ci * 256 + 128:ci * 256 + 256], xb_t[:, k, :Tn]) for k in range(8)], [sB_, xbB_t], [pg[0]])
                    sg = f32t.get()
                    act(sg[1][:, :Tn], pg[1][:, :Tn], AF.Sigmoid, [pg[0]], [sg[0]])
                    eng = "dve"
                    if is_sample:
                        vt = f32t.get()
                        tt("dve", vt[1][:, :Tn], pa[1][:, :Tn], sg[1][:, :Tn], ALU.mult, [pa[0], sg[0]], [vt[0]])
                        for (col0, L, halo_fn, tail_fn) in conv_segs:
                            halo_fn(c)
                            cp("pool", vbuf[:, c, 30:30 + L], vt[1][:, col0:col0 + L], [vt[0]], [vbufB[c]])
                            conv_chunk(c, col0, L, eng)
                            tail_fn(c, L)
                    else:
                        (col0, L, halo_fn, tail_fn) = conv_segs[0]
                        tt("dve", vbuf[:, c, 30:30 + Tn], pa[1][:, :Tn], sg[1][:, :Tn], ALU.mult, [pa[0], sg[0]], [vbufB[c]])
                        conv_chunk(c, 0, Tn, eng)
                        if tail_fn is not None:
                            tail_fn(c, Tn)
                        cp("pool", vbuf[:, c, 0:30], vbuf[:, c, Tn:Tn + 30], [vbufB[c]], [vbufB[c]])

            def silu_out(c, t_ap, tB, g_ap, b_ap):
                act(mixin[:, 4 + c, :Tn], t_ap, AF.Silu, [tB, pkB], [mixB[4 + c]], scale=g_ap, bias=b_ap)
            layer_norm(4, [cacc[:, c, :Tn] for c in range(4)], caccB, ones_c[:], Tn, "cln_g", "cln_b", silu_out)
        if part in ("all", "back"):
            dma("sp", R0[:, :, :Tn], x32_src, R0B[0], writes=R0B)
            for half in range(2):
                sB_, wv = wpiece(w_out_b, pc["w_out"], half * 512, half * 512 + 512)
                for ci in range(4):
                    co = half * 4 + ci
                    yield
                    ps = psum.get()
                    mm(ps[1][:, :Tn], [(wv[:, k, ci * 128:(ci + 1) * 128], mixin[:, k, :Tn]) for k in range(8)], [sB_] + mixB, [ps[0]])
                    stt("dve", R0[:, co, :Tn], R0[:, co, :Tn], ALPHA, ps[1][:, :Tn], ALU.mult, ALU.add, [R0B[co], ps[0]], [R0B[co]])
            ln_to_resid(Tn, "ln1_g", "ln1_b")
            qps = []
            for half in range(2):
                sB_, wv = wpiece(w_q_b, pc["w_q"], half * 512, half * 512 + 512)
                for ci in range(4):
                    yield
                    ps = psum.get()
                    mm(ps[1][:, :Tn], [(wv[:, k, ci * 128:(ci + 1) * 128], xnb[:, k, :Tn]) for k in range(8)], [sB_] + xnbB, [ps[0]])
                    act(qb[:, half * 4 + ci, :Tn], ps[1][:, :Tn], AF.Identity, [ps[0]], [qbB[half * 4 + ci]], scale=1.0 / 16.0)
            for (col0, L, kv_loader) in att_segs:
                if kv_loader is not None:
                    kv_loader()
                for h in range(4):
                    pts = []
                    for mc in range(2):
                        yield
                        ps = psum.get()
                        mm(ps[1][:, :L], [(kT[:, h * 2 + dc, mc * 128:(mc + 1) * 128], qb[:, h * 2 + dc, col0:col0 + L]) for dc in range(2)],
                           [kTB, qbB[h * 2], qbB[h * 2 + 1]], [ps[0]])
                        pt = pTr.get()
                        act(pt[1][:, :L], ps[1][:, :L], AF.Exp, [ps[0]], [pt[0]])
                        pts.append(pt)
                    yield
                    ps = psum.get()
                    mm(ps[1][:, :L], [(ones_1[:], pts[mc][1][:, :L]) for mc in range(2)], [onesB, pts[0][0], pts[1][0]], [ps[0]])
                    rinv = f32t.get()
                    recip(rinv[1][:, :L], ps[1][:, :L], [ps[0]], [rinv[0]])
                    for dc in range(2):
                        po = psum.get()
                        mm(po[1][:, :L], [(vv[:, mc, h * 256 + dc * 128: h * 256 + dc * 128 + 128], pts[mc][1][:, :L]) for mc in range(2)],
                           [vvB, pts[0][0], pts[1][0]], [po[0]])
                        tt("dve", ob[:, h * 2 + dc, col0:col0 + L], po[1][:, :L], rinv[1][:, :L], ALU.mult, [po[0], rinv[0]], [obB[h * 2 + dc]])
            for half in range(2):
                sB_, wv = wpiece(w_o_b, pc["w_o"], half * 512, half * 512 + 512)
                for ci in range(4):
                    co = half * 4 + ci
                    yield
                    ps = psum.get()
                    mm(ps[1][:, :Tn], [(wv[:, k, ci * 128:(ci + 1) * 128], ob[:, k, :Tn]) for k in range(8)], [sB_] + obB, [ps[0]])
                    stt("dve", R0[:, co, :Tn], R1[:, co, :Tn], ALPHA, ps[1][:, :Tn], ALU.mult, ALU.add, [R1B[co], ps[0]], [R0B[co]])
            ln_to_resid(Tn, "ln2_g", "ln2_b")
            for piece in range(8):
                sB_, wv = wpiece(w1_b, pc["w1"], piece * 512, piece * 512 + 512)
                for hc in range(4):
                    hidx = piece * 4 + hc
                    yield
                    ps = psum.get()
                    mm(ps[1][:, :Tn], [(wv[:, k, hc * 128:(hc + 1) * 128], xnb[:, k, :Tn]) for k in range(8)], [sB_] + xnbB, [ps[0]])
                    rl = f32t.get()
                    act(rl[1][:, :Tn], ps[1][:, :Tn], AF.Relu, [ps[0], pkB], [rl[0]], bias=pkc("b1", hidx))
                    tt("pool" if hidx % 2 else "dve", hid[:, hidx, :Tn], rl[1][:, :Tn], rl[1][:, :Tn], ALU.mult, [rl[0]], [hidB[hidx]])
            w2v = w2_b.rearrange("(k p) n -> p k n", p=128)
            for cp_ in range(4):
                yield
                pss = [psum.get(), psum.get()]
                for kh in range(2):
                    sB_, wv = ring_load(w2v[:, kh * 16:kh * 16 + 16, cp_ * 256:cp_ * 256 + 256], 16, 256, pc["w2"])
                    for oc in range(2):
                        for k in range(16):
                            mm1(pss[oc][1][:, :Tn], wv[:, k, oc * 128:(oc + 1) * 128], hid[:, kh * 16 + k, :Tn],
                                kh == 0 and k == 0, kh == 1 and k == 15, [sB_, hidB[kh * 16 + k]], [pss[oc][0]])
                for oc in range(2):
                    co = cp_ * 2 + oc
                    stt("dve", R0[:, co, :Tn], R1[:, co, :Tn], ALPHA, pss[oc][1][:, :Tn], ALU.mult, ALU.add, [R1B[co], pss[oc][0]], [R0B[co]])
                    ts("pool", R0[:, co, :Tn], R0[:, co, :Tn], pkc("b2", co), None, ALU.add, None, [R0B[co], pkB], [R0B[co]])
            ln_to_resid(Tn, "ln3_g", "ln3_b")
            dma("sp", y_dst, R1[:, :, :Tn], R1B[0], reads=R1B, final=True)

    xT_v = xT.rearrange("(k p) t -> p k t", p=128)
    if RUN_KV:
        memTb = xb[1]
        dma("pool", memTb[:, :, 0:256], memT_d.rearrange("(k p) m -> p k m", p=128), xbB[1], writes=[xbB[1]])
        kv_i = [4]
        for (wb, pcb, dst_d, is_k) in ((w_k_b, pc["w_k"], kout_d, True), (w_v_b, pc["w_v"], vout_d, False)):
            for half in range(2):
                sB_, wv = wpiece(wb, pcb, half * 512, half * 512 + 512)
                if is_k and KVD[0] == "1":
                    for ci in range(4):
                        ps = psum.get()
                        mm(ps[1][:, :256], [(wv[:, k, ci * 128:(ci + 1) * 128], memTb[:, k, 0:256]) for k in range(8)], [sB_, xbB[1]], [ps[0]])
                        cp("act", kT[:, half * 4 + ci, :], ps[1][:, :256], [ps[0]], [kTB])
                for mc in range(2 if KVD[1] == "1" else 0):
                    ps = psum.get()
                    if KVD[3:4] == "h":
                        mm(ps[1][:, 0:256], [(memTb[:, k, mc * 128:(mc + 1) * 128], wv[:, k, 0:256]) for k in range(8)], [sB_, xbB[1]], [ps[0]])
                        mm(ps[1][:, 256:512], [(memTb[:, k, mc * 128:(mc + 1) * 128], wv[:, k, 256:512]) for k in range(8)], [sB_, xbB[1]], [ps[0]])
                    else:
                        mm(ps[1][:, :], [(memTb[:, k, mc * 128:(mc + 1) * 128], wv[:, k, :]) for k in range(8)], [sB_, xbB[1]], [ps[0]])
                    stgB, stg = big(kv_i[0])
                    kv_i[0] = 4 + (kv_i[0] - 4 + 1) % 4
                    if KVD[5:6] == "s":
                        for hh in range(2):
                            cp("act", stg[:, hh * 256:(hh + 1) * 256], ps[1][:, hh * 256:(hh + 1) * 256], [ps[0]], stgB)
                            if not is_k:
                                cp("dve", vv[:, mc, half * 512 + hh * 256:half * 512 + (hh + 1) * 256], ps[1][:, hh * 256:(hh + 1) * 256], [ps[0]], [vvB])
                    elif KVD[5:6] == "n":
                        pass
                    elif KVD[5:6] == "a":
                        cp("act", stg, ps[1][:], [ps[0]], stgB)
                    elif KVD[5:6] == "v":
                        cp("dve", stg, ps[1][:], [ps[0]], stgB)
                    else:
                        cp("act", stg, ps[1][:], [ps[0]], stgB)
                        if not is_k:
                            cp("dve", vv[:, mc, half * 512:(half + 1) * 512], stg, stgB, [vvB])
                    if KVD[2] == "1":
                        dma("sp", dst_d[mc * 128:(mc + 1) * 128, half * 512:(half + 1) * 512], stg, stgB[0], reads=stgB, final=True)

    memset("dve", Hre[:], 0.0, HB)
    memset("dve", Him[:], 0.0, HB)
    winuB, winu = wpiece(w_in_b, pc["w_in"], 0, 512)
    NPB = NPRE // T
    xi = [0]

    def load_xb(col):
        i = xi[0] % 2
        xi[0] += 1
        dma("pool", xb[i][:, :, :], xT_v[:, :, col:col + T], xbB[i], writes=[xbB[i]])
        return i

    nxt = load_xb((NPB - NPB_RUN) * T)
    qi = [0]
    for pb in range(NPB - NPB_RUN, NPB):
        cur = nxt
        nxt = load_xb((pb + 1) * T)
        for blk in range(T // 128):
            pu = psum.get()
            mm(pu[1][:, :], [(xb[cur][:, k, blk * 128:(blk + 1) * 128], winu[:, k, :]) for k in range(8)], [xbB[cur], winuB], [pu[0]])
            ubk = ublk.get()
            cp("act", ubk[1][:], pu[1][:], [pu[0]], [ubk[0]])
            wr, wi = psum.get(), psum.get()

            def fn(e, ubk=ubk, wr=wr, wi=wi):
                ins = None
                for p_ in range(16):
                    e.matmul(wr[1][:, p_ * 32:(p_ + 1) * 32], Pre[:, p_, :], ubk[1][:, p_ * 32:(p_ + 1) * 32], start=True, stop=True)
                    ins = e.matmul(wi[1][:, p_ * 32:(p_ + 1) * 32], Pim[:, p_, :], ubk[1][:, p_ * 32:(p_ + 1) * 32], start=True, stop=True)
                return ins
            P.op("pe", fn, [lhsB, ubk[0]], [wr[0], wi[0]])
            base = (qi[0] % 2) * 2
            qi[0] += 1
            (q1B, q1), (q2B, q2) = big(base), big(base + 1)
            (q3B, q3), (q4B, q4) = big(4 + base), big(4 + base + 1)
            tt("dve", q1, wr[1][:], Bxr[:], ALU.mult, [wr[0], BxB], q1B)
            tt("dve", q2, wi[1][:], Bxi[:], ALU.mult, [wi[0], BxB], q2B)
            tt("dve", q3, wi[1][:], Bxr[:], ALU.mult, [wi[0], BxB], q3B)
            tt("dve", q4, wr[1][:], Bxi[:], ALU.mult, [wr[0], BxB], q4B)
            tt("pool", q1, q1, q2, ALU.subtract, q1B + q2B, q1B)
            tt("pool", q3, q3, q4, ALU.add, q3B + q4B, q3B)
            sr, si = tiny.get(), tiny.get()
            P.op("dve", (lambda sr, q1: lambda e: e.tensor_reduce(out=sr[1][:], in_=q1.rearrange("p (a b) -> p a b", a=16), axis=AX.X, op=ALU.add))(sr, q1), q1B, [sr[0]])
            P.op("dve", (lambda si, q3: lambda e: e.tensor_reduce(out=si[1][:], in_=q3.rearrange("p (a b) -> p a b", a=16), axis=AX.X, op=ALU.add))(si, q3), q3B, [si[0]])
            u1, u2, u3, u4 = tiny.get(), tiny.get(), tiny.get(), tiny.get()
            tt("pool", u1[1][:], a128r[:], Hre[:], ALU.mult, [a128B] + HB, [u1[0]])
            tt("pool", u2[1][:], a128i[:], Him[:], ALU.mult, [a128B] + HB, [u2[0]])
            tt("pool", u3[1][:], a128r[:], Him[:], ALU.mult, [a128B] + HB, [u3[0]])
            tt("pool", u4[1][:], a128i[:], Hre[:], ALU.mult, [a128B] + HB, [u4[0]])
            tt("pool", u1[1][:], u1[1][:], u2[1][:], ALU.subtract, [u1[0], u2[0]], [u1[0]])
            tt("pool", u3[1][:], u3[1][:], u4[1][:], ALU.add, [u3[0], u4[0]], [u3[0]])
            tt("pool", Hre[:], u1[1][:], sr[1][:], ALU.add, [u1[0], sr[0]], HB)
            tt("pool", Him[:], u3[1][:], si[1][:], ALU.add, [u3[0], si[0]] + HB, HB)

    hxi = xi[0] % 2
    hx, hxB = xb[hxi], xbB[hxi]
    dma("pool", hx[:, :, 0:32], xT_v[:, :, NPRE - 32:NPRE], hxB, writes=[hxB])
    for half in range(2):
        sB_, wv = wpiece(w_in_b, pc["w_in"], 512 + half * 512, 1024 + half * 512)
        for ci in range(2):
            c = half * 2 + ci
            pa, pg = psum.get(), psum.get()
            mm(pa[1][:, :32], [(wv[:, k, ci * 256:ci * 256 + 128], hx[:, k, 0:32]) for k in range(8)], [sB_, hxB], [pa[0]])
            mm(pg[1][:, :32], [(wv[:, k, ci * 256 + 128:ci * 256 + 256], hx[:, k, 0:32]) for k in range(8)], [sB_, hxB], [pg[0]])
            sg = f32t.get()
            act(sg[1][:, :32], pg[1][:, :32], AF.Sigmoid, [pg[0]], [sg[0]])
            tt("dve", vbuf[:, c, 0:30], pa[1][:, 2:32], sg[1][:, 2:32], ALU.mult, [pa[0], sg[0]], [vbufB[c]])

    p_segs = [(s * TS, TS, lambda pi: Hre[:, pi:pi + 1], lambda pi: Him[:, pi:pi + 1], lambda pi: HB[pi]) for s in range(T // TS)]
    yT_v = yT.rearrange("(k p) t -> p k t", p=128)
    cur = nxt

    def p_tail(c, L):
        dma("sp", convout_d[:, c * 30:(c + 1) * 30], vbuf[:, c, L:L + 30], vbufB[c], reads=[vbufB[c]], final=True)

    def xload(it, buf_i):
        dma("pool", xb[buf_i][:, :, :], xT_v[:, :, NPRE + it * T: NPRE + (it + 1) * T], xbB[buf_i], writes=[xbB[buf_i]])

    if NT_RUN > 0:
        if NT_RUN > 1:
            xload(1, 1 - cur)
        drain(front_p(xb[cur], xbB[cur], p_tail if NT_RUN == 1 and NT == 1 else None))
    for it in range(NT_RUN):
        back = run_tile(T, xb[cur], xbB[cur], xT_v[:, :, NPRE + it * T: NPRE + (it + 1) * T], p_segs,
                        [(0, T, None, None)], [(0, T, None)], yT_v[:, :, it * T:(it + 1) * T], part="back")
        if it + 1 < NT_RUN:
            nb = 1 - cur
            fr = front_p(xb[nb], xbB[nb], p_tail if it + 1 == NT - 1 else None)
            if it + 2 < NT_RUN:
                xload(it + 2, cur)
            if PIPE:
                merge(back, fr)
            else:
                drain(back)
                drain(fr)
        else:
            drain(back)
        cur = 1 - cur
    hst, hst2 = tiny.get(), tiny.get()
    cp("dve", hst[1][:], Hre[:], HB, [hst[0]])
    cp("dve", hst2[1][:], Him[:], HB, [hst2[0]])
    dma("sp", hout_d[:, 0:16], hst[1][:], hst[0], reads=[hst[0]], final=True)
    dma("sp", hout_d[:, 16:32], hst2[1][:], hst2[0], reads=[hst2[0]], final=True)

    if RUN_SAMPLE:
        hsinB = Buf("hs_in")
        dma("sp", Hs[:], h0_d[:, :], hsinB, writes=HsB)
        xs_v = xsT.rearrange("(k p) t -> p k t", p=128)
        dma("pool", xb[cur][:, :, 0:32], xs_v, xbB[cur], writes=[xbB[cur]])

        def s_halo(s):
            def f(c):
                dma("sp", vbuf[:, c, 0:30], convc_d[:, (c * 2 + s) * 30:(c * 2 + s) * 30 + 30], vbufB[c], writes=[vbufB[c]])
            return f

        def s_tail(s):
            def f(c, L):
                dma("sp", convsout_d[:, (c * 2 + s) * 30:(c * 2 + s) * 30 + 30], vbuf[:, c, L:L + 30], vbufB[c], reads=[vbufB[c]], final=True)
            return f

        def s_kv(s):
            def f():
                dma("pool", kT[:], kTs_d[s].rearrange("(k p) m -> p k m", p=128), kTB, writes=[kTB])
                dma("pool", vv[:], vs_d[s].rearrange("(k p) n -> p k n", p=128), vvB, writes=[vvB])
            return f

        def hs_fn(s, reim):
            return lambda pi: Hs[:, pi * 4 + s * 2 + reim: pi * 4 + s * 2 + reim + 1]

        s_segs = [(s * 16, 16, hs_fn(s, 0), hs_fn(s, 1), lambda pi: HsB[pi]) for s in range(2)]
        drain(run_tile(32, xb[cur], xbB[cur], xs_v, s_segs,
                       [(s * 16, 16, s_halo(s), s_tail(s)) for s in range(2)],
                       [(s * 16, 16, s_kv(s)) for s in range(2)],
                       ysT.rearrange("(k p) t -> p k t", p=128), is_sample=True))
        hso = statsb.get()
        cp("dve", hso[1][:, 0:64], Hs[:], HsB, [hso[0]])
        dma("sp", hsout_d[:, :], hso[1][:, 0:64], hso[0], reads=[hso[0]], final=True)

    P.emit(st)
    st.close()
    return nc


_NC_CACHE = {}


def _mode_major(a):
    sh = a.shape
    a = a.reshape((16, 2, 64) + sh[2:])
    return np.ascontiguousarray(np.moveaxis(a, 0, 2).reshape((128, 16) + sh[2:]))


def kernel(x_prompt, x_sample, state_ssm_re, state_ssm_im, cache_conv, cache_mem_k, cache_mem_v,
           mem_prompt, w_in, ssm_a_re, ssm_a_im, ssm_log_dt, ssm_b_re, ssm_b_im, ssm_c_re, ssm_c_im,
           ssm_d, glu_w, glu_b, conv_w, conv_b, conv_ln_g, conv_ln_b, w_out, ln1_g, ln1_b,
           mem_w_q, mem_w_k, mem_w_v, mem_w_o, ln2_g, ln2_b,
           mlp_w1, mlp_b1, mlp_w2, mlp_b2, ln3_g, ln3_b):
    f = lambda a: np.ascontiguousarray(np.asarray(a, dtype=np.float32))
    x_prompt, x_sample = f(x_prompt), f(x_sample)
    col = lambda v, n: f(v).reshape(n, 128).T
    pk = np.zeros((128, NPK), np.float32)

    def put(name, arr):
        pk[:, PK[name][0]:PK[name][1]] = arr
    put("ln1_g", col(ln1_g[0], 8)); put("ln1_b", col(ln1_b[0], 8))
    put("ln2_g", col(ln2_g[0], 8)); put("ln2_b", col(ln2_b[0], 8))
    put("ln3_g", col(ln3_g[0], 8)); put("ln3_b", col(ln3_b[0], 8))
    put("b1", col(mlp_b1[0], 32)); put("b2", col(mlp_b2[0], 8))
    put("glu_b", col(glu_b[0], 4)); put("ssm_d", col(ssm_d[0], 4))
    put("conv_b", col(conv_b[0], 4)); put("cln_g", col(conv_ln_g[0], 4)); put("cln_b", col(conv_ln_b[0], 4))
    cw = f(conv_w[0]).T.reshape(4, 128, 31).transpose(1, 0, 2).reshape(128, 124)
    put("conv_w", cw)
    put("a_re", _mode_major(f(ssm_a_re[0])[:, :, None])[:, :, 0])
    put("a_im", _mode_major(f(ssm_a_im[0])[:, :, None])[:, :, 0])
    ldt = np.repeat(f(ssm_log_dt[0])[:, None], 64, axis=1)
    put("logdt", _mode_major(ldt[:, :, None])[:, :, 0])
    put("tidx", np.tile(np.arange(128, dtype=np.float32)[None, :], (128, 1)))
    put("eidx", (127.0 - np.arange(128, dtype=np.float32))[:, None])
    rows = np.stack([f(ssm_a_re[0]).reshape(-1), f(ssm_a_im[0]).reshape(-1), ldt.reshape(-1)]).astype(np.float32)
    BT = np.zeros((2, 128, 2048), np.float32)
    CT = np.zeros((2, 128, 2048), np.float32)
    Bx = np.zeros((2, 128, 512), np.float32)
    for ri, (bsrc, csrc) in enumerate(((f(ssm_b_re[0]), f(ssm_c_re[0])), (f(ssm_b_im[0]), f(ssm_c_im[0])))):
        for g in range(32):
            pi, gp, gl = g // 2, g % 2, g % 8
            BT[ri, gl * 16:(gl + 1) * 16, pi * 128 + gp * 64: pi * 128 + gp * 64 + 64] = bsrc[g].T
            CT[ri, gp * 64:(gp + 1) * 64, pi * 128 + gl * 16: pi * 128 + gl * 16 + 16] = csrc[g].T
            Bx[ri, gp * 64:(gp + 1) * 64, pi * 32 + gp * 16: pi * 32 + gp * 16 + 16] = bsrc[g]
    shared = dict(w_in=f(w_in[0]), glu_w=f(glu_w[0]), w_out=f(w_out[0]), w_q=f(mem_w_q[0]), w_k=f(mem_w_k[0]),
                  w_v=f(mem_w_v[0]), w_o=f(mem_w_o[0]), w1=f(mlp_w1[0]), w2=f(mlp_w2[0]), pk=pk, rows=rows, BT=BT, CT=CT, Bx=Bx)
    xTs = [np.ascontiguousarray(x_prompt[b].T) for b in range(2)]
    in_maps = []
    for c in CORES:
        b, j = c // 4, c % 4
        xin = np.zeros((D, NPRE + SEG), np.float32)
        npre = j * SEG
        xin[:, NPRE - npre: NPRE + SEG] = xTs[b][:, 0:(j + 1) * SEG]
        ss = [2 * c, 2 * c + 1]
        xs = np.concatenate([x_sample[s].T for s in ss], axis=1)
        h0 = np.zeros((128, 16, 2, 2), np.float32)
        cc = np.zeros((128, 4, 2, 30), np.float32)
        for si, s in enumerate(ss):
            h0[:, :, si, 0] = _mode_major(f(state_ssm_re[0, s])[:, :, None])[:, :, 0]
            h0[:, :, si, 1] = _mode_major(f(state_ssm_im[0, s])[:, :, None])[:, :, 0]
            cc[:, :, si, :] = f(cache_conv[0, s]).T.reshape(4, 128, 30).transpose(1, 0, 2)
        kTs = np.stack([f(cache_mem_k[0, s]).reshape(256, D).T for s in ss])
        vs = np.stack([f(cache_mem_v[0, s]).reshape(256, D) for s in ss])
        m = dict(shared)
        m.update(xT=xin, xsT=np.ascontiguousarray(xs), h0=h0.reshape(128, 64), convc=cc.reshape(128, 240),
                 kTs=np.ascontiguousarray(kTs), vs=np.ascontiguousarray(vs), memT=np.ascontiguousarray(f(mem_prompt[b]).T))
        in_maps.append(m)
    if "nc" not in _NC_CACHE:
        _NC_CACHE["nc"] = build_program()
    res = run_bass_kernel_spmd(_NC_CACHE["nc"], in_maps, core_ids=list(range(len(CORES))))
    R = {c: res.results[i] for i, c in enumerate(CORES)}

    def unmode(a):
        return a.reshape(2, 64, 16).transpose(2, 0, 1).reshape(32, 64)
    y_prompt = np.zeros((2, 16384, D), np.float32)
    y_sample = np.zeros((16, 16, D), np.float32)
    p_re = np.zeros((1, 2, 32, 64), np.float32)
    p_im = np.zeros((1, 2, 32, 64), np.float32)
    p_conv = np.zeros((1, 2, 30, 512), np.float32)
    p_mk = np.zeros((1, 2, 256, 4, 256), np.float32)
    p_mv = np.zeros((1, 2, 256, 4, 256), np.float32)
    s_re = np.zeros((1, 16, 32, 64), np.float32)
    s_im = np.zeros((1, 16, 32, 64), np.float32)
    s_conv = np.zeros((1, 16, 30, 512), np.float32)
    for c in CORES:
        b, j = c // 4, c % 4
        r = R[c]
        y_prompt[b, j * SEG:(j + 1) * SEG, :] = r["yT"].T
        ys = r["ysT"].T
        hs = r["hsout"].reshape(128, 16, 2, 2)
        cs = r["convsout"].reshape(128, 4, 2, 30)
        for si in range(2):
            s = 2 * c + si
            y_sample[s] = ys[si * 16:(si + 1) * 16]
            s_re[0, s] = unmode(hs[:, :, si, 0])
            s_im[0, s] = unmode(hs[:, :, si, 1])
            s_conv[0, s] = cs[:, :, si, :].transpose(1, 0, 2).reshape(512, 30).T
        if j == 3:
            ho = r["hout"].reshape(128, 2, 16)
            p_re[0, b] = unmode(ho[:, 0, :])
            p_im[0, b] = unmode(ho[:, 1, :])
            p_conv[0, b] = r["convout"].reshape(128, 4, 30).transpose(1, 0, 2).reshape(512, 30).T
        if j == 0:
            p_mk[0, b] = r["kout"].reshape(256, 4, 256)
            p_mv[0, b] = r["vout"].reshape(256, 4, 256)
    return (y_prompt, y_sample, p_re, p_im, p_conv, p_mk, p_mv, s_re, s_im, s_conv)
```

```python
import os
from contextlib import ExitStack
import numpy as np
import concourse.bass as bass
import concourse.mybir as mybir
from concourse.bass_utils import run_bass_kernel_spmd

F32 = mybir.dt.float32
BF16 = mybir.dt.bfloat16
AF = mybir.ActivationFunctionType
ALU = mybir.AluOpType
AX = mybir.AxisListType

D = 1024
NCORE = 8
SEG = 4096
NPRE = 3 * SEG
T = 256
NT = SEG // T
TS = 128
LN_EPS = 1e-5
ALPHA = 2.0 ** 0.25
MAGIC = 12582912.0
TWO_PI = float(2 * np.pi)
PI = float(np.pi)

PK = {}
_o = 0
for _n, _w in [("ln1_g", 8), ("ln1_b", 8), ("ln2_g", 8), ("ln2_b", 8), ("ln3_g", 8), ("ln3_b", 8),
               ("b1", 32), ("b2", 8), ("glu_b", 4), ("ssm_d", 4), ("conv_b", 4), ("cln_g", 4), ("cln_b", 4),
               ("conv_w", 124), ("a_re", 16), ("a_im", 16), ("logdt", 16), ("tidx", 128), ("eidx", 1), ("pad", 3)]:
    PK[_n] = (_o, _o + _w)
    _o += _w
NPK = _o

SYNC_SAME = {e: (e in os.environ.get("K_SYNC", "act,dve,pool").split(",")) for e in ("act", "dve", "pool", "pe", "sp")}
NT_RUN = int(os.environ.get("K_NT", NT))
NGST = int(os.environ.get("K_NGST", "8"))
PIPE = bool(int(os.environ.get("K_PIPE", "1")))
GRP = bool(int(os.environ.get("K_GRP", "1")))
RUN_SAMPLE = bool(int(os.environ.get("K_SAMPLE", "1")))
NPB_RUN = int(os.environ.get("K_NPB", NPRE // T))
RUN_PREP = bool(int(os.environ.get("K_PREP", "1")))
RUN_KV = bool(int(os.environ.get("K_KV", "1")))
RUN_PC = os.environ.get("K_PC", "all")
KVD = os.environ.get("K_KVD", "111")
PC_ROWS = int(os.environ.get("K_PCR", "128"))
PC_COLS = int(os.environ.get("K_PCC", "1024"))
CORES = [int(x) for x in os.environ.get("K_CORES", "0,1,2,3,4,5,6,7").split(",")]


class Ev:
    __slots__ = ("eng", "sem", "value", "op", "group")

    def __init__(self, eng):
        self.eng = eng
        self.sem = None
        self.value = None
        self.op = None
        self.group = None


class Buf:
    __slots__ = ("name", "w", "r", "dma_sem", "dma_count", "group")

    def __init__(self, name, group=None):
        self.name = name
        self.w = None
        self.r = []
        self.dma_sem = None
        self.dma_count = 0
        self.group = group


class DmaGroup:
    def __init__(self, name):
        self.name = name
        self.sem = None
        self.count = 0


class Op:
    __slots__ = ("eng", "fn", "waits", "ev", "is_dma", "signals")

    def __init__(self, eng, fn, waits, ev, is_dma):
        self.eng = eng
        self.fn = fn
        self.waits = waits
        self.ev = ev
        self.is_dma = is_dma
        self.signals = is_dma


class Prog:
    ENGINES = ("pe", "act", "dve", "pool", "sp")

    def __init__(self, nc):
        self.nc = nc
        self.ops = {e: [] for e in self.ENGINES}
        self.all_ops = []
        self.dma_bufs = []
        self.groups = []
        self.final_evs = []

    def group(self, name):
        g = DmaGroup(name)
        self.groups.append(g)
        return g

    def _deps(self, ev, reads, writes):
        waits = []
        for b in reads:
            if b.w is not None:
                waits.append(b.w)
        for b in writes:
            if b.w is not None:
                waits.append(b.w)
            waits.extend(b.r)
        for b in reads:
            b.r.append(ev)
        for b in writes:
            b.w = ev
            b.r = []
        out = []
        seen = set()
        for w in waits:
            if w is ev or id(w) in seen:
                continue
            seen.add(id(w))
            out.append(w)
        return out

    def op(self, eng, fn, reads=(), writes=()):
        ev = Ev(eng)
        waits = self._deps(ev, reads, writes)
        o = Op(eng, fn, waits, ev, False)
        ev.op = o
        self.ops[eng].append(o)
        self.all_ops.append(o)
        return o

    def dma(self, eng, fn, key, reads=(), writes=(), final=False):
        ev = Ev("dma")
        waits = self._deps(ev, reads, writes)
        if key.group is not None:
            ev.group = key.group
            key.group.count += 1
        else:
            if key.dma_count == 0:
                self.dma_bufs.append(key)
            key.dma_count += 1
            ev.sem = key
            ev.value = 16 * key.dma_count
        o = Op(eng, fn, waits, ev, True)
        ev.op = o
        self.ops[eng].append(o)
        self.all_ops.append(o)
        if final:
            self.final_evs.append(ev)
        return o

    def emit(self, stack):
        nc = self.nc
        for o in self.all_ops:
            for w in o.waits:
                if w.op is not None and not w.op.is_dma:
                    if w.eng == o.eng and not SYNC_SAME[o.eng]:
                        continue
                    w.op.signals = True
        esem = {}
        for e in ("pe", "act", "dve", "pool"):
            esem[e] = stack.enter_context(nc.semaphore("s_" + e))
            cnt = 0
            for o in self.ops[e]:
                if o.is_dma:
                    continue
                if o.signals:
                    cnt += 1
                    o.ev.sem = esem[e]
                    o.ev.value = cnt
        for b in self.dma_bufs:
            b.dma_sem = stack.enter_context(nc.semaphore("d_" + b.name))
        for g in self.groups:
            if g.count:
                g.sem = stack.enter_context(nc.semaphore("g_" + g.name))

        def resolve(ev):
            if ev.group is not None:
                return ev.group.sem, 16 * ev.group.count
            if isinstance(ev.sem, Buf):
                return ev.sem.dma_sem, ev.value
            return ev.sem, ev.value

        block = stack.enter_context(nc.Block())
        handles = {"pe": "tensor", "act": "scalar", "dve": "vector", "pool": "gpsimd", "sp": "sync"}
        final_evs = self.final_evs

        def make(e):
            ops = self.ops[e]

            def body(eng):
                waited = {}
                for o in ops:
                    for w in o.waits:
                        if w.eng == e and not w.op.is_dma and not SYNC_SAME[e]:
                            continue
                        sem, val = resolve(w)
                        assert sem is not None and val is not None, (e, w.eng)
                        k = id(sem)
                        if waited.get(k, 0) >= val:
                            continue
                        waited[k] = val
                        eng.wait_ge(sem, val)
                    ins = o.fn(eng)
                    if o.is_dma:
                        sem, _ = resolve(o.ev)
                        ins.then_inc(sem, 16)
                    elif o.signals:
                        ins.then_inc(o.ev.sem, 1)
                if e == "sp":
                    for ev in final_evs:
                        sem, val = resolve(ev)
                        if waited.get(id(sem), 0) >= val:
                            continue
                        waited[id(sem)] = val
                        eng.wait_ge(sem, val)
            return body

        for e in self.ENGINES:
            if self.ops[e] or e == "sp":
                getattr(block, handles[e])(make(e))


class Rot:
    def __init__(self, st, nc, name, shape, dtype, n, psum=False):
        self.items = []
        for i in range(n):
            alloc = nc.psum_tensor if psum else nc.sbuf_tensor
            t = st.enter_context(alloc(f"rt_{name}{i}", shape, dtype))
            self.items.append((Buf(f"{name}{i}"), t))
        self.i = 0

    def get(self):
        it = self.items[self.i % len(self.items)]
        self.i += 1
        return it


def build_program():
    nc = bass.Bass("TRN2", target_bir_lowering=False)
    st = ExitStack()
    P = Prog(nc)

    def din(name, shape):
        return nc.dram_tensor(name, shape, F32, kind="ExternalInput").ap()

    def dout(name, shape):
        return nc.dram_tensor(name, shape, F32, kind="ExternalOutput").ap()

    def dscr(name, shape):
        return nc.dram_tensor(name, shape, BF16, kind="Internal").ap()

    xT = din("xT", [D, NPRE + SEG])
    xsT = din("xsT", [D, 32])
    h0_d = din("h0", [128, 64])
    convc_d = din("convc", [128, 240])
    kTs_d = din("kTs", [2, D, 256])
    vs_d = din("vs", [2, 256, D])
    memT_d = din("memT", [D, 256])
    w_in_d = din("w_in", [D, 1536])
    glu_w_d = din("glu_w", [512, 512])
    w_out_d = din("w_out", [D, D])
    w_q_d = din("w_q", [D, D])
    w_k_d = din("w_k", [D, D])
    w_v_d = din("w_v", [D, D])
    w_o_d = din("w_o", [D, D])
    w1_d = din("w1", [D, 4096])
    w2_d = din("w2", [4096, D])
    pk_d = din("pk", [128, NPK])
    rows_d = din("rows", [3, 2048])
    BT_d = din("BT", [2, 128, 2048])
    CT_d = din("CT", [2, 128, 2048])
    Bx_d = din("Bx", [2, 128, 512])

    yT = dout("yT", [D, SEG])
    ysT = dout("ysT", [D, 32])
    hout_d = dout("hout", [128, 32])
    convout_d = dout("convout", [128, 120])
    kout_d = dout("kout", [256, D])
    vout_d = dout("vout", [256, D])
    hsout_d = dout("hsout", [128, 64])
    convsout_d = dout("convsout", [128, 240])

    w_in_b = dscr("w_in_b", [D, 1536])
    w_out_b = dscr("w_out_b", [D, D])
    w_q_b = dscr("w_q_b", [D, D])
    w_k_b = dscr("w_k_b", [D, D])
    w_v_b = dscr("w_v_b", [D, D])
    w_o_b = dscr("w_o_b", [D, D])
    w1_b = dscr("w1_b", [D, 4096])
    w2_b = dscr("w2_b", [4096, D])

    def sb(name, shape, dt=F32):
        return st.enter_context(nc.sbuf_tensor("sb_" + name, shape, dt))

    pk = sb("pk", [128, NPK])
    cgrp = P.group("consts")
    pkB = Buf("pk", group=cgrp)
    ones_m = sb("ones_m", [128, 128], BF16)
    ones_c = sb("ones_c", [128, 128], BF16)
    ones_1 = sb("ones_1", [128, 128], BF16)
    onesB = Buf("ones")
    R0 = sb("R0", [128, 8, T])
    R1 = sb("R1", [128, 8, T])
    xnb = sb("xnb", [128, 8, T], BF16)
    R0B = [Buf(f"R0_{c}") for c in range(8)]
    R1B = [Buf(f"R1_{c}") for c in range(8)]
    xnbB = [Buf(f"xnb_{c}") for c in range(8)]
    ob, obB = xnb, xnbB
    xb = [sb(f"xb{i}", [128, 8, T], BF16) for i in range(2)]
    xbB = [Buf(f"xb{i}") for i in range(2)]
    NRING = 3
    ring = [sb(f"ring{i}", [128, 4096], BF16) for i in range(NRING)]
    ringB = [Buf(f"ring{i}") for i in range(NRING)]
    ring_i = [0]
    fring = sb("fring", [128, 4096], BF16)
    fringB = Buf("fring")
    glu_sb = sb("glu_sb", [128, 4, 512], BF16)
    gluB = Buf("glu")
    u32 = sb("u32", [128, 4, T])
    ub = sb("ub", [128, 4, T], BF16)
    u32B = [Buf(f"u32_{c}") for c in range(4)]
    ubB = [Buf(f"ub_{c}") for c in range(4)]
    vbuf = sb("vbuf", [128, 4, 30 + T])
    vbufB = [Buf(f"vbuf_{c}") for c in range(4)]
    cacc = sb("cacc", [128, 4, T])
    caccB = [Buf(f"cacc_{c}") for c in range(4)]
    mixin = sb("mixin", [128, 8, T], BF16)
    mixB = [Buf(f"mix_{c}") for c in range(8)]
    z32 = sb("z32", [128, 4, T])
    zb = sb("zb", [128, 4, T], BF16)
    z32B = [Buf(f"z32_{c}") for c in range(4)]
    zbB = [Buf(f"zb_{c}") for c in range(4)]
    hid = sb("hid", [128, 32, T], BF16)
    hidB = [Buf(f"hid_{c}") for c in range(32)]
    qb, qbB = hid, hidB
    kT = sb("kT", [128, 8, 256], BF16)
    kTB = Buf("kT")
    vv = sb("vv", [128, 2, D], BF16)
    vvB = Buf("vv")
    costab = sb("costab", [128, 16, TS])
    sintab = sb("sintab", [128, 16, TS])
    rtab = sb("rtab", [128, 16, TS])
    tabB = Buf("tabs")
    BreT = sb("BreT", [128, 16, 128], BF16)
    BimT = sb("BimT", [128, 16, 128], BF16)
    CreT = sb("CreT", [128, 16, 128], BF16)
    CimT = sb("CimT", [128, 16, 128], BF16)
    Pre = sb("Pre", [128, 16, 128], BF16)
    Pim = sb("Pim", [128, 16, 128], BF16)
    lhsB = Buf("ssm_lhs")
    Bxr = sb("Bxr", [128, 512])
    Bxi = sb("Bxi", [128, 512])
    BxB = Buf("Bx")
    a128r = sb("a128r", [128, 16])
    a128i = sb("a128i", [128, 16])
    a128B = Buf("a128")
    Hre = sb("Hre", [128, 16])
    Him = sb("Him", [128, 16])
    HB = [Buf(f"H_{p}") for p in range(16)]
    Hs = sb("Hs", [128, 64])
    HsB = [Buf(f"Hs_{p}") for p in range(16)]

    s16 = Rot(st, nc, "s16_", [128, T], BF16, 4)
    f32t = Rot(st, nc, "f32t_", [128, T], F32, 5)
    sst = Rot(st, nc, "sst_", [128, 32 if GRP else TS], F32, 10)
    hbf = Rot(st, nc, "hbf_", [128, 32 if GRP else TS], BF16, 4)
    gst = Rot(st, nc, "gst_", [128, 512], F32, NGST)
    ghb = Rot(st, nc, "ghb_", [128, 512], BF16, 4)
    tiny = Rot(st, nc, "tiny_", [128, 16], F32, 8)
    pTr = Rot(st, nc, "pT_", [128, T], BF16, 4)
    ublk = Rot(st, nc, "ublk_", [128, 512], BF16, 2)
    statsb = Rot(st, nc, "stat_", [128, T], F32, 6)
    psum = Rot(st, nc, "ps", [128, 512], F32, 3, psum=True)
    ypsR = Rot(st, nc, "yps", [128, 512], F32, 1, psum=True)
    buR = Rot(st, nc, "bups", [128, 512], F32, 2, psum=True)
    lnR = Rot(st, nc, "lnps", [128, 512], F32, 2, psum=True)

    def big(i):
        src, bufs = (R0, R0B) if i < 4 else (R1, R1B)
        j = (i % 4) * 2
        return [bufs[j], bufs[j + 1]], src[:, j:j + 2, :].rearrange("p a b -> p (a b)")

    pkc = lambda name, i=0, n=1: pk[:, PK[name][0] + i: PK[name][0] + i + n]

    def tt(eng, out, a, b, op, reads, writes):
        P.op(eng, lambda e: e.tensor_tensor(out=out, in0=a, in1=b, op=op), reads, writes)

    def ts(eng, out, a, s1, s2, op0, op1, reads, writes):
        if op1 is None:
            P.op(eng, lambda e: e.tensor_scalar(out=out, in0=a, scalar1=s1, scalar2=None, op0=op0), reads, writes)
        else:
            P.op(eng, lambda e: e.tensor_scalar(out=out, in0=a, scalar1=s1, scalar2=s2, op0=op0, op1=op1), reads, writes)

    def stt(eng, out, a, s, b, op0, op1, reads, writes):
        P.op(eng, lambda e: e.scalar_tensor_tensor(out=out, in0=a, scalar=s, in1=b, op0=op0, op1=op1), reads, writes)

    def act(out, in_, func, reads, writes, scale=None, bias=None):
        kw = {}
        if scale is not None:
            kw["scale"] = scale
        if bias is not None:
            kw["bias"] = bias
        P.op("act", lambda e: e.activation(out=out, in_=in_, func=func, **kw), reads, writes)

    def cp(eng, out, in_, reads, writes):
        if eng == "act":
            act(out, in_, AF.Copy, reads, writes)
        else:
            P.op(eng, lambda e: e.tensor_copy(out=out, in_=in_), reads, writes)

    def recip(out, in_, reads, writes):
        P.op("dve", lambda e: e.reciprocal(out=out, in_=in_), reads, writes)

    def mm(out, pairs, reads, writes):
        n = len(pairs)

        def fn(e):
            ins = None
            for i, (l, r) in enumerate(pairs):
                ins = e.matmul(out, l, r, start=(i == 0), stop=(i == n - 1))
            return ins
        P.op("pe", fn, reads, writes)

    def mm1(out, l, r, start, stop, reads, writes):
        P.op("pe", lambda e: e.matmul(out, l, r, start=start, stop=stop), reads, writes)

    def dma(eng, out, in_, key, reads=(), writes=(), final=False):
        P.dma(eng, lambda e: e.dma_start(out=out, in_=in_), key, reads=reads, writes=writes, final=final)

    def memset(eng, ap, val, writes):
        P.op(eng, lambda e: e.memset(ap, val), (), writes)

    def range_reduce(out, in_, shift, reads, writes, tmp, tmpB):
        src = in_
        rd = list(reads)
        if shift != 0.0:
            ts("dve", out, in_, shift, None, ALU.add, None, reads, writes)
            src = out
            rd = list(writes)
        ts("dve", tmp, src, 1.0 / TWO_PI, MAGIC, ALU.mult, ALU.add, rd, tmpB)
        ts("dve", tmp, tmp, MAGIC, -TWO_PI, ALU.subtract, ALU.mult, tmpB, tmpB)
        tt("dve", out, src, tmp, ALU.add, rd + list(tmpB), writes)
        ts("dve", out, out, PI, -PI, ALU.min, ALU.max, writes, writes)

    dma("sp", pk[:], pk_d[:, :], pkB, writes=[pkB])
    dma("pool", glu_sb[:], glu_w_d.rearrange("(k p) n -> p k n", p=128), gluB, writes=[gluB])
    memset("dve", ones_m[:], 1.0 / 1024.0, [onesB])
    memset("dve", ones_c[:], 1.0 / 512.0, [onesB])
    memset("dve", ones_1[:], 1.0, [onesB])


    pcsB = [Buf(f"pcs{i}") for i in range(NRING)]
    pclB = [Buf(f"pcl{i}") for i in range(NRING)]

    def precast_gen(bl, dst, src, nrows, colmap=None):
        ncols = src.shape[1]
        nk = nrows // 128
        sv = src.rearrange("(k p) n -> p k n", p=128)
        dv = dst.rearrange("(k p) n -> p k n", p=128)
        if colmap is None:
            colmap = [(c0, c0, min(1024, ncols - c0)) for c0 in range(0, ncols, 1024)]
        for (d0, s0, n) in colmap:
            kstep = max(1, min(nk, 4096 // n))
            for k0 in range(0, nk, kstep):
                i = ring_i[0] % NRING
                ring_i[0] += 1
                view = ring[i][:, 0:kstep * n].rearrange("p (a b) -> p a b", a=kstep)
                dma("pool", view, sv[:, k0:k0 + kstep, s0:s0 + n], pclB[i], writes=[ringB[i]])
                dma("sp", dv[:, k0:k0 + kstep, d0:d0 + n], view, pcsB[i], reads=[ringB[i]], writes=[bl[i]])
                yield

    def precast(name, dst, src, nrows, colmap=None):
        bl = [Buf(f"pcb_{name}{i}") for i in range(NRING)]
        for _ in precast_gen(bl, dst, src, nrows, colmap):
            pass
        return bl

    def precast_later(name, dst, src, nrows):
        bl = [Buf(f"pcb_{name}{i}") for i in range(NRING)]
        late_pc.append(precast_gen(bl, dst, src, nrows))
        return bl

    late_pc = []

    win_map = [(0, 0, 512)]
    for i in range(4):
        win_map.append((512 + 256 * i, 512 + 128 * i, 128))
        win_map.append((512 + 256 * i + 128, 1024 + 128 * i, 128))
    pc = {}
    if RUN_PC == "none":
        def precast(name, dst, src, nrows, colmap=None):
            return [Buf("pcb_" + name)]
        precast_later = lambda name, dst, src, nrows: [Buf("pcb_" + name)]
    pc["w_in"] = precast("w_in", w_in_b, w_in_d, D, win_map)
    pc["w_k"] = precast("w_k", w_k_b, w_k_d, D)
    pc["w_v"] = precast("w_v", w_v_b, w_v_d, D)
    pc["w_out"] = precast_later("w_out", w_out_b, w_out_d, D)
    pc["w_q"] = precast_later("w_q", w_q_b, w_q_d, D)
    pc["w_o"] = precast_later("w_o", w_o_b, w_o_d, D)
    pc["w1"] = precast_later("w1", w1_b, w1_d, D)
    pc["w2"] = precast_later("w2", w2_b, w2_d, 4096)

    def ring_load(src_ap, a, b, pcb):
        i = ring_i[0] % NRING
        ring_i[0] += 1
        view = ring[i][:, 0:a * b].rearrange("p (a b) -> p a b", a=a)
        dma("sp", view, src_ap, ringB[i], reads=pcb, writes=[ringB[i]])
        return ringB[i], view

    def wpiece(wb, pcb, n0, n1):
        return ring_load(wb.rearrange("(k p) n -> p k n", p=128)[:, :, n0:n1], 8, n1 - n0, pcb)

    def fpiece(wb, pcb, n0, n1):
        view = fring[:, 0:8 * (n1 - n0)].rearrange("p (a b) -> p a b", a=8)
        dma("sp", view, wb.rearrange("(k p) n -> p k n", p=128)[:, :, n0:n1], fringB, reads=pcb, writes=[fringB])
        return fringB, view

    def disc(A, Bm, L, tmp, rdB, wB):
        T1, T2, T3, T4, T5, T6, T7 = tmp
        act(L, L, AF.Exp, rdB, wB)
        tt("dve", T1, A, L, ALU.mult, wB, wB)
        tt("dve", T5, Bm, L, ALU.mult, wB, wB)
        act(T6, T1, AF.Exp, wB, wB)
        range_reduce(T2, T5, 0.0, wB, wB, T7, wB)
        act(T2, T2, AF.Sin, wB, wB)
        range_reduce(T3, T5, PI / 2, wB, wB, T7, wB)
        act(T3, T3, AF.Sin, wB, wB)
        tt("dve", T3, T6, T3, ALU.mult, wB, wB)
        tt("dve", T2, T6, T2, ALU.mult, wB, wB)
        ts("dve", T3, T3, -1.0, None, ALU.add, None, wB, wB)
        tt("dve", T6, A, A, ALU.mult, wB, wB)
        tt("dve", T7, Bm, Bm, ALU.mult, wB, wB)
        tt("dve", T6, T6, T7, ALU.add, wB, wB)
        recip(T6, T6, wB, wB)
        tt("dve", L, T3, A, ALU.mult, wB, wB)
        tt("dve", T4, T2, Bm, ALU.mult, wB, wB)
        tt("dve", L, L, T4, ALU.add, wB, wB)
        tt("dve", L, L, T6, ALU.mult, wB, wB)
        tt("dve", T4, T2, A, ALU.mult, wB, wB)
        tt("dve", T3, T3, Bm, ALU.mult, wB, wB)
        tt("dve", T4, T4, T3, ALU.subtract, wB, wB)
        tt("dve", T4, T4, T6, ALU.mult, wB, wB)
        return dict(x1=T1, ang=T5, c_re=L, c_im=T4)

    if RUN_PREP:
        mB_ = [Buf("modeprep")]
        mA, mBm, mL = sb("mA", [128, 16]), sb("mBm", [128, 16]), sb("mL", [128, 16])
        mT = [sb(f"mT{i}", [128, 16]) for i in range(7)]
        cp("dve", mA[:], pkc("a_re", 0, 16), [pkB], mB_)
        cp("dve", mBm[:], pkc("a_im", 0, 16), [pkB], mB_)
        cp("dve", mL[:], pkc("logdt", 0, 16), [pkB], mB_)
        dm = disc(mA[:], mBm[:], mL[:], [t[:] for t in mT], mB_, mB_)
        m128a, m128m, mr = sb("m128a", [128, 16]), sb("m128m", [128, 16]), sb("mr", [128, 16])
        ts("dve", m128a[:], dm["ang"], 128.0, None, ALU.mult, None, mB_, mB_)
        act(m128m[:], dm["x1"], AF.Exp, mB_, mB_, scale=128.0)
        range_reduce(mT[1][:], m128a[:], 0.0, mB_, mB_, mT[6][:], mB_)
        act(mT[1][:], mT[1][:], AF.Sin, mB_, mB_)
        range_reduce(mT[2][:], m128a[:], PI / 2, mB_, mB_, mT[6][:], mB_)
        act(mT[2][:], mT[2][:], AF.Sin, mB_, mB_)
        tt("dve", a128r[:], m128m[:], mT[2][:], ALU.mult, mB_, [a128B])
        tt("dve", a128i[:], m128m[:], mT[1][:], ALU.mult, mB_ + [a128B], [a128B])
        act(mr[:], dm["x1"], AF.Exp, mB_, mB_)
        for p_ in range(16):
            tg, tg2 = gst.get(), gst.get()
            tg = (tg[0], tg[1][:, 0:TS])
            tg2 = (tg2[0], tg2[1][:, 0:TS])
            ts("dve", tg[1][:], pkc("tidx", 0, TS), dm["ang"][:, p_:p_ + 1], None, ALU.mult, None, [pkB] + mB_, [tg[0]])
            range_reduce(sintab[:, p_, :], tg[1][:], 0.0, [tg[0]], [tabB], tg2[1][:], [tg2[0]])
            act(sintab[:, p_, :], sintab[:, p_, :], AF.Sin, [tabB], [tabB])
            range_reduce(costab[:, p_, :], tg[1][:], PI / 2, [tg[0]], [tabB], tg2[1][:], [tg2[0]])
            act(costab[:, p_, :], costab[:, p_, :], AF.Sin, [tabB], [tabB])
            ts("dve", rtab[:, p_, :], pkc("tidx", 0, TS), 0.0, mr[:, p_:p_ + 1], ALU.mult, ALU.add, [pkB] + mB_, [tabB])
            memset("dve", rtab[:, p_, 0:1], 0.0, [tabB])
        bxrB, bxr_t = big(0)
        bxiB, bxi_t = big(1)
        dma("sp", bxr_t, Bx_d[0], bxrB[0], writes=bxrB)
        dma("sp", bxi_t, Bx_d[1], bxiB[0], writes=bxiB)
        for p_ in range(16):
            sl = slice(p_ * 32, p_ * 32 + 32)
            cr, ci = dm["c_re"][:, p_:p_ + 1], dm["c_im"][:, p_:p_ + 1]
            ta, tb_ = gst.get(), gst.get()
            ts("dve", ta[1][:, 0:32], bxi_t[:, sl], ci, None, ALU.mult, None, bxiB + mB_, [ta[0]])
            stt("dve", Bxr[:, sl], bxr_t[:, sl], cr, ta[1][:, 0:32], ALU.mult, ALU.subtract, bxrB + mB_ + [ta[0]], [BxB])
            ts("dve", tb_[1][:, 0:32], bxr_t[:, sl], ci, None, ALU.mult, None, bxrB + mB_, [tb_[0]])
            stt("dve", Bxi[:, sl], bxi_t[:, sl], cr, tb_[1][:, 0:32], ALU.mult, ALU.add, bxiB + mB_ + [tb_[0]], [BxB])

        rowB = R0B + R1B
        rt = [R0[:, c, :] for c in range(8)] + [R1[:, c, :] for c in range(8)]
        for blk in range(8):
            cs = slice(blk * 256, blk * 256 + 256)
            rA, rBm, rL = rt[0], rt[1], rt[2]
            for (tile_, row) in ((rA, 0), (rBm, 1), (rL, 2)):
                dma("sp", tile_, rows_d[row:row + 1, cs].partition_broadcast(128), R0B[row], writes=rowB)
            tmp = rt[3:10]
            dr = disc(rA, rBm, rL, tmp, rowB, rowB)
            btr, bti, ctr, cti = rt[10], rt[11], rt[12], rt[13]
            dma("sp", btr, BT_d[0][:, cs], R1B[2], writes=rowB)
            dma("sp", bti, BT_d[1][:, cs], R1B[3], writes=rowB)
            dma("sp", ctr, CT_d[0][:, cs], R1B[4], writes=rowB)
            dma("sp", cti, CT_d[1][:, cs], R1B[5], writes=rowB)
            sA, sB_ = tmp[1], tmp[2]
            s3, s4 = tmp[5], tmp[6]
            osl = lambda t_: t_[:, blk * 2:blk * 2 + 2, :].rearrange("p a b -> p (a b)")
            tt("dve", sA, dr["c_re"], btr, ALU.mult, rowB, rowB)
            tt("dve", sB_, dr["c_im"], bti, ALU.mult, rowB, rowB)
            tt("dve", osl(BreT), sA, sB_, ALU.subtract, rowB, [lhsB])
            tt("dve", sA, dr["c_re"], bti, ALU.mult, rowB, rowB)
            tt("dve", sB_, dr["c_im"], btr, ALU.mult, rowB, rowB)
            tt("dve", osl(BimT), sA, sB_, ALU.add, rowB + [lhsB], [lhsB])
            e_ap = pkc("eidx")
            act(sA, dr["x1"], AF.Exp, rowB + [pkB], rowB, scale=e_ap)
            ts("dve", sB_, dr["ang"], e_ap, None, ALU.mult, None, rowB + [pkB], rowB)
            range_reduce(s3, sB_, 0.0, rowB, rowB, s4, rowB)
            act(s3, s3, AF.Sin, rowB, rowB)
            tt("dve", osl(Pim), sA, s3, ALU.mult, rowB + [lhsB], [lhsB])
            range_reduce(s3, sB_, PI / 2, rowB, rowB, s4, rowB)
            act(s3, s3, AF.Sin, rowB, rowB)
            tt("dve", osl(Pre), sA, s3, ALU.mult, rowB + [lhsB], [lhsB])
            cp("dve", osl(CreT), ctr, rowB + [lhsB], [lhsB])
            ts("dve", osl(CimT), cti, -1.0, None, ALU.mult, None, rowB + [lhsB], [lhsB])

    def layer_norm(nch, srcs, srcB, ones_ap, Tn, g_name, b_name, emit_out):
        pm, pe2 = lnR.get(), lnR.get()
        for c in range(nch):
            s1, s2 = s16.get(), s16.get()
            act(s1[1][:, :Tn], srcs[c], AF.Copy, [srcB[c]], [s1[0]])
            act(s2[1][:, :Tn], srcs[c], AF.Square, [srcB[c]], [s2[0]])
            mm1(pm[1][:, :Tn], ones_ap, s1[1][:, :Tn], c == 0, c == nch - 1, [s1[0], onesB], [pm[0]])
            mm1(pe2[1][:, :Tn], ones_ap, s2[1][:, :Tn], c == 0, c == nch - 1, [s2[0], onesB], [pe2[0]])
        mean, var, nmr = statsb.get(), statsb.get(), statsb.get()
        cp("act", mean[1][:, :Tn], pm[1][:, :Tn], [pm[0]], [mean[0]])
        tt("dve", var[1][:, :Tn], mean[1][:, :Tn], mean[1][:, :Tn], ALU.mult, [mean[0]], [var[0]])
        tt("dve", var[1][:, :Tn], pe2[1][:, :Tn], var[1][:, :Tn], ALU.subtract, [pe2[0], var[0]], [var[0]])
        ts("dve", var[1][:, :Tn], var[1][:, :Tn], LN_EPS, None, ALU.add, None, [var[0]], [var[0]])
        act(var[1][:, :Tn], var[1][:, :Tn], AF.Sqrt, [var[0]], [var[0]])
        recip(var[1][:, :Tn], var[1][:, :Tn], [var[0]], [var[0]])
        stt("dve", nmr[1][:, :Tn], mean[1][:, :Tn], -1.0, var[1][:, :Tn], ALU.mult, ALU.mult, [mean[0], var[0]], [nmr[0]])
        for c in range(nch):
            t_ = f32t.get()
            tt("dve", t_[1][:, :Tn], srcs[c], var[1][:, :Tn], ALU.mult, [srcB[c], var[0]], [t_[0]])
            tt("pool", t_[1][:, :Tn], t_[1][:, :Tn], nmr[1][:, :Tn], ALU.add, [t_[0], nmr[0]], [t_[0]])
            emit_out(c, t_[1][:, :Tn], t_[0], pkc(g_name, c), pkc(b_name, c))

    def ln_to_resid(Tn, g_name, b_name):
        def emit_out(c, t_ap, tB, g_ap, b_ap):
            act(R1[:, c, :Tn], t_ap, AF.Identity, [tB, pkB], [R1B[c]], scale=g_ap, bias=b_ap)
            cp("pool", xnb[:, c, :Tn], R1[:, c, :Tn], [R1B[c]], [xnbB[c]])
        layer_norm(8, [R0[:, c, :Tn] for c in range(8)], R0B, ones_m[:], Tn, g_name, b_name, emit_out)

    def ssm_segment(pi, col0, L, Hr_ap, Hi_ap, HBuf, ypsum, first, last):
        c = pi // 4
        bu = psum.get()
        bre, bim = bu[1][:, 0:L], bu[1][:, 256:256 + L]
        mm1(bre, BreT[:, pi, :], ub[:, c, col0:col0 + L], True, True, [lhsB, ubB[c]], [bu[0]])
        mm1(bim, BimT[:, pi, :], ub[:, c, col0:col0 + L], True, True, [lhsB, ubB[c]], [bu[0]])
        cosA, sinA, rA = costab[:, pi, 0:L], sintab[:, pi, 0:L], rtab[:, pi, 0:L]
        cos1, sin1 = costab[:, pi, 1:2], sintab[:, pi, 1:2]
        g0, tq = tiny.get(), tiny.get()
        tt("pool", tq[1][:, 0:1], Hi_ap, sin1, ALU.mult, [HBuf, tabB], [tq[0]])
        tt("pool", tq[1][:, 1:2], Hr_ap, cos1, ALU.mult, [HBuf, tabB, tq[0]], [tq[0]])
        tt("pool", tq[1][:, 2:3], Hr_ap, sin1, ALU.mult, [HBuf, tabB, tq[0]], [tq[0]])
        tt("pool", tq[1][:, 3:4], Hi_ap, cos1, ALU.mult, [HBuf, tabB, tq[0]], [tq[0]])
        tt("pool", g0[1][:, 0:1], tq[1][:, 1:2], tq[1][:, 0:1], ALU.subtract, [tq[0]], [g0[0]])
        tt("pool", g0[1][:, 1:2], tq[1][:, 3:4], tq[1][:, 2:3], ALU.add, [tq[0], g0[0]], [g0[0]])
        t1, t2, t3, t4 = sst.get(), sst.get(), sst.get(), sst.get()
        tt("dve", t1[1][:, :L], bre, cosA, ALU.mult, [bu[0], tabB], [t1[0]])
        tt("dve", t2[1][:, :L], bim, sinA, ALU.mult, [bu[0], tabB], [t2[0]])
        tt("dve", t3[1][:, :L], bim, cosA, ALU.mult, [bu[0], tabB], [t3[0]])
        tt("dve", t4[1][:, :L], bre, sinA, ALU.mult, [bu[0], tabB], [t4[0]])
        tt("pool", t1[1][:, :L], t1[1][:, :L], t2[1][:, :L], ALU.add, [t1[0], t2[0]], [t1[0]])
        tt("pool", t3[1][:, :L], t3[1][:, :L], t4[1][:, :L], ALU.subtract, [t3[0], t4[0]], [t3[0]])
        r1 = rtab[:, pi, 1:2]
        tt("pool", g0[1][:, 2:3], g0[1][:, 0:1], r1, ALU.mult, [g0[0], tabB], [g0[0]])
        tt("pool", g0[1][:, 3:4], g0[1][:, 1:2], r1, ALU.mult, [g0[0], tabB], [g0[0]])
        tt("pool", t1[1][:, 0:1], t1[1][:, 0:1], g0[1][:, 2:3], ALU.add, [t1[0], g0[0]], [t1[0]])
        tt("pool", t3[1][:, 0:1], t3[1][:, 0:1], g0[1][:, 3:4], ALU.add, [t3[0], g0[0]], [t3[0]])
        gre, gim = sst.get(), sst.get()
        P.op("dve", lambda e: e.tensor_tensor_scan(out=gre[1][:, :L], data0=rA, data1=t1[1][:, :L], initial=0.0, op0=ALU.mult, op1=ALU.add),
             [tabB, t1[0]], [gre[0]])
        P.op("dve", lambda e: e.tensor_tensor_scan(out=gim[1][:, :L], data0=rA, data1=t3[1][:, :L], initial=0.0, op0=ALU.mult, op1=ALU.add),
             [tabB, t3[0]], [gim[0]])
        p1, p2, p3, p4 = sst.get(), sst.get(), sst.get(), sst.get()
        tt("pool", p1[1][:, :L], gre[1][:, :L], cosA, ALU.mult, [gre[0], tabB], [p1[0]])
        tt("pool", p2[1][:, :L], gim[1][:, :L], sinA, ALU.mult, [gim[0], tabB], [p2[0]])
        tt("pool", p3[1][:, :L], gim[1][:, :L], cosA, ALU.mult, [gim[0], tabB], [p3[0]])
        tt("pool", p4[1][:, :L], gre[1][:, :L], sinA, ALU.mult, [gre[0], tabB], [p4[0]])
        hr, hi = hbf.get(), hbf.get()
        tt("dve", hr[1][:, :L], p1[1][:, :L], p2[1][:, :L], ALU.subtract, [p1[0], p2[0]], [hr[0]])
        tt("dve", hi[1][:, :L], p3[1][:, :L], p4[1][:, :L], ALU.add, [p3[0], p4[0]], [hi[0]])
        tt("pool", Hr_ap, p1[1][:, L - 1:L], p2[1][:, L - 1:L], ALU.subtract, [p1[0], p2[0]], [HBuf])
        tt("pool", Hi_ap, p3[1][:, L - 1:L], p4[1][:, L - 1:L], ALU.add, [p3[0], p4[0], HBuf], [HBuf])
        yo = ypsum[1][:, col0:col0 + L]
        mm1(yo, CreT[:, pi, :], hr[1][:, :L], first, False, [lhsB, hr[0]], [ypsum[0]])
        mm1(yo, CimT[:, pi, :], hi[1][:, :L], False, last, [lhsB, hi[0]], [ypsum[0]])

    def conv_chunk(c, col0, L, eng):
        o = cacc[:, c, col0:col0 + L]
        base = PK["conv_w"][0] + c * 31
        ts(eng, o, vbuf[:, c, 0:L], pk[:, base:base + 1], pkc("conv_b", c), ALU.mult, ALU.add, [vbufB[c], pkB], [caccB[c]])
        for k in range(1, 31):
            stt(eng, o, vbuf[:, c, k:k + L], pk[:, base + k:base + k + 1], o, ALU.mult, ALU.add, [vbufB[c], pkB, caccB[c]], [caccB[c]])

    def v3(t_):
        return t_.rearrange("p (a b) -> p a b", a=4)

    def ssm_group(c, col0, yp, filler):
        ps4 = slice(4 * c, 4 * c + 4)
        HBs = HB[4 * c:4 * c + 4]
        bR, bI = buR.get(), buR.get()

        def fn(e):
            ins = None
            for q in range(4):
                e.matmul(bR[1][:, q * 128:(q + 1) * 128], BreT[:, 4 * c + q, :], ub[:, c, col0:col0 + 128], start=True, stop=True)
                ins = e.matmul(bI[1][:, q * 128:(q + 1) * 128], BimT[:, 4 * c + q, :], ub[:, c, col0:col0 + 128], start=True, stop=True)
            return ins
        P.op("pe", fn, [lhsB, ubB[c]], [bR[0], bI[0]])
        cosG, sinG = costab[:, ps4, :], sintab[:, ps4, :]
        rG = rtab[:, ps4, :].rearrange("p a b -> p (a b)")
        cos1, sin1, r1 = costab[:, ps4, 1], sintab[:, ps4, 1], rtab[:, ps4, 1]
        Hr, Hi = Hre[:, ps4], Him[:, ps4]
        tq, g0 = tiny.get(), tiny.get()
        tt("pool", tq[1][:, 0:4], Hi, sin1, ALU.mult, HBs + [tabB], [tq[0]])
        tt("pool", tq[1][:, 4:8], Hr, cos1, ALU.mult, HBs + [tabB, tq[0]], [tq[0]])
        tt("pool", tq[1][:, 8:12], Hr, sin1, ALU.mult, HBs + [tabB, tq[0]], [tq[0]])
        tt("pool", tq[1][:, 12:16], Hi, cos1, ALU.mult, HBs + [tabB, tq[0]], [tq[0]])
        tt("pool", g0[1][:, 0:4], tq[1][:, 4:8], tq[1][:, 0:4], ALU.subtract, [tq[0]], [g0[0]])
        tt("pool", g0[1][:, 4:8], tq[1][:, 12:16], tq[1][:, 8:12], ALU.add, [tq[0], g0[0]], [g0[0]])
        tt("pool", g0[1][:, 8:12], g0[1][:, 0:4], r1, ALU.mult, [g0[0], tabB], [g0[0]])
        tt("pool", g0[1][:, 12:16], g0[1][:, 4:8], r1, ALU.mult, [g0[0], tabB], [g0[0]])
        filler(3)
        yield
        A, B_, C, D_ = gst.get(), gst.get(), gst.get(), gst.get()
        tt("dve", v3(A[1][:]), v3(bR[1][:]), cosG, ALU.mult, [bR[0], tabB], [A[0]])
        tt("dve", v3(B_[1][:]), v3(bI[1][:]), sinG, ALU.mult, [bI[0], tabB], [B_[0]])
        filler(2)
        tt("dve", v3(C[1][:]), v3(bI[1][:]), cosG, ALU.mult, [bI[0], tabB], [C[0]])
        tt("dve", v3(D_[1][:]), v3(bR[1][:]), sinG, ALU.mult, [bR[0], tabB], [D_[0]])
        filler(2)
        yield
        tt("pool", A[1][:], A[1][:], B_[1][:], ALU.add, [A[0], B_[0]], [A[0]])
        tt("pool", v3(A[1][:])[:, :, 0], v3(A[1][:])[:, :, 0], g0[1][:, 8:12], ALU.add, [A[0], g0[0]], [A[0]])
        tt("pool", C[1][:], C[1][:], D_[1][:], ALU.subtract, [C[0], D_[0]], [C[0]])
        tt("pool", v3(C[1][:])[:, :, 0], v3(C[1][:])[:, :, 0], g0[1][:, 12:16], ALU.add, [C[0], g0[0]], [C[0]])
        filler(4)
        yield
        GR, GI = gst.get(), gst.get()
        P.op("dve", lambda e: e.tensor_tensor_scan(out=GR[1][:], data0=rG, data1=A[1][:], initial=0.0, op0=ALU.mult, op1=ALU.add),
             [tabB, A[0]], [GR[0]])
        filler(2)
        P.op("dve", lambda e: e.tensor_tensor_scan(out=GI[1][:], data0=rG, data1=C[1][:], initial=0.0, op0=ALU.mult, op1=ALU.add),
             [tabB, C[0]], [GI[0]])
        filler(2)
        yield
        tt("pool", v3(B_[1][:]), v3(GR[1][:]), cosG, ALU.mult, [GR[0], tabB], [B_[0]])
        tt("pool", v3(D_[1][:]), v3(GI[1][:]), sinG, ALU.mult, [GI[0], tabB], [D_[0]])
        tt("pool", v3(A[1][:]), v3(GI[1][:]), cosG, ALU.mult, [GI[0], tabB], [A[0]])
        tt("pool", v3(C[1][:]), v3(GR[1][:]), sinG, ALU.mult, [GR[0], tabB], [C[0]])
        filler(4)
        yield
        hr, hi = ghb.get(), ghb.get()
        tt("dve", hr[1][:], B_[1][:], D_[1][:], ALU.subtract, [B_[0], D_[0]], [hr[0]])
        filler(1)
        tt("dve", hi[1][:], A[1][:], C[1][:], ALU.add, [A[0], C[0]], [hi[0]])
        filler(1)
        tt("pool", Hr, v3(B_[1][:])[:, :, 127], v3(D_[1][:])[:, :, 127], ALU.subtract, [B_[0], D_[0]], HBs)
        tt("pool", Hi, v3(A[1][:])[:, :, 127], v3(C[1][:])[:, :, 127], ALU.add, [A[0], C[0]] + HBs, HBs)
        yo = yp[1][:, col0:col0 + 128]

        def fn2(e):
            ins = None
            for q in range(4):
                e.matmul(yo, CreT[:, 4 * c + q, :], hr[1][:, q * 128:(q + 1) * 128], start=(q == 0), stop=False)
                ins = e.matmul(yo, CimT[:, 4 * c + q, :], hi[1][:, q * 128:(q + 1) * 128], start=False, stop=(q == 3))
            return ins
        P.op("pe", fn2, [lhsB, hr[0], hi[0]], [yp[0]])
        yield

    def front_p(xb_t, xbB_t, tail_fn):
        Tn = T
        s0B, w0 = fpiece(w_in_b, pc["w_in"], 0, 512)
        for c in range(4):
            ps = psum.get()
            mm(ps[1][:, :Tn], [(w0[:, k, c * 128:(c + 1) * 128], xb_t[:, k, :Tn]) for k in range(8)], [s0B, xbB_t], [ps[0]])
            cp("act", u32[:, c, :Tn], ps[1][:, :Tn], [ps[0]], [u32B[c]])
            cp("dve", ub[:, c, :Tn], u32[:, c, :Tn], [u32B[c]], [ubB[c]])
            yield
        for half in range(2):
            sB_, wv = fpiece(w_in_b, pc["w_in"], 512 + half * 512, 1024 + half * 512)
            for ci in range(2):
                c = half * 2 + ci
                pa, pg = psum.get(), psum.get()
                mm(pa[1][:, :Tn], [(wv[:, k, ci * 256:ci * 256 + 128], xb_t[:, k, :Tn]) for k in range(8)], [sB_, xbB_t], [pa[0]])
                mm(pg[1][:, :Tn], [(wv[:, k, ci * 256 + 128:ci * 256 + 256], xb_t[:, k, :Tn]) for k in range(8)], [sB_, xbB_t], [pg[0]])
                sg = f32t.get()
                act(sg[1][:, :Tn], pg[1][:, :Tn], AF.Sigmoid, [pg[0]], [sg[0]])
                tt("dve", vbuf[:, c, 30:30 + Tn], pa[1][:, :Tn], sg[1][:, :Tn], ALU.mult, [pa[0], sg[0]], [vbufB[c]])
                yield
        taps = []
        for k in range(31):
            for c in range(4):
                taps.append((c, k))
        tap_i = [0]

        def filler(n):
            for _ in range(n):
                if tap_i[0] >= len(taps):
                    return
                c, k = taps[tap_i[0]]
                tap_i[0] += 1
                o = cacc[:, c, 0:Tn]
                base = PK["conv_w"][0] + c * 31
                if k == 0:
                    ts("dve", o, vbuf[:, c, 0:Tn], pk[:, base:base + 1], pkc("conv_b", c), ALU.mult, ALU.add, [vbufB[c], pkB], [caccB[c]])
                else:
                    stt("dve", o, vbuf[:, c, k:k + Tn], pk[:, base + k:base + k + 1], o, ALU.mult, ALU.add, [vbufB[c], pkB, caccB[c]], [caccB[c]])

        for c in range(4):
            yp = ypsR.get()
            for sg_ in range(T // TS):
                if GRP:
                    for _ in ssm_group(c, sg_ * TS, yp, filler):
                        yield
                else:
                    for q in range(4):
                        pi = c * 4 + q
                        ssm_segment(pi, sg_ * TS, TS, Hre[:, pi:pi + 1], Him[:, pi:pi + 1], HB[pi], yp, q == 0, q == 3)
                        filler(8)
                        yield
            zp = f32t.get()
            stt("dve", zp[1][:, :Tn], u32[:, c, :Tn], pkc("ssm_d", c), yp[1][:, :Tn], ALU.mult, ALU.add, [u32B[c], pkB, yp[0]], [zp[0]])
            act(z32[:, c, :Tn], zp[1][:, :Tn], AF.Gelu_apprx_tanh, [zp[0]], [z32B[c]])
            cp("pool", zb[:, c, :Tn], z32[:, c, :Tn], [z32B[c]], [zbB[c]])
            yield
        while tap_i[0] < len(taps):
            filler(4)
            yield
        for co in range(4):
            ps = psum.get()
            mm(ps[1][:, :Tn], [(glu_sb[:, k, co * 128:(co + 1) * 128], zb[:, k, :Tn]) for k in range(4)], [gluB] + zbB, [ps[0]])
            sg = f32t.get()
            act(sg[1][:, :Tn], ps[1][:, :Tn], AF.Sigmoid, [ps[0], pkB], [sg[0]], bias=pkc("glu_b", co))
            tt("dve", mixin[:, co, :Tn], z32[:, co, :Tn], sg[1][:, :Tn], ALU.mult, [z32B[co], sg[0]], [mixB[co]])
            yield
        for c in range(4):
            if tail_fn is not None:
                tail_fn(c, Tn)
            cp("pool", vbuf[:, c, 0:30], vbuf[:, c, Tn:Tn + 30], [vbufB[c]], [vbufB[c]])

        def silu_out(c, t_ap, tB, g_ap, b_ap):
            act(mixin[:, 4 + c, :Tn], t_ap, AF.Silu, [tB, pkB], [mixB[4 + c]], scale=g_ap, bias=b_ap)
        layer_norm(4, [cacc[:, c, :Tn] for c in range(4)], caccB, ones_c[:], Tn, "cln_g", "cln_b", silu_out)
        yield

    def drain(g):
        for _ in g:
            pass

    def merge(ga, gb):
        da = db = False
        while not (da and db):
            if not da:
                try:
                    next(ga)
                except StopIteration:
                    da = True
            if not db:
                try:
                    next(gb)
                except StopIteration:
                    db = True

    def run_tile(Tn, xb_t, xbB_t, x32_src, segs, conv_segs, att_segs, y_dst, is_sample=False, part="all"):
        if part in ("all", "front"):
            s0B, w0 = fpiece(w_in_b, pc["w_in"], 0, 512)
            for c in range(4):
                ps = psum.get()
                mm(ps[1][:, :Tn], [(w0[:, k, c * 128:(c + 1) * 128], xb_t[:, k, :Tn]) for k in range(8)], [s0B, xbB_t], [ps[0]])
                cp("act", u32[:, c, :Tn], ps[1][:, :Tn], [ps[0]], [u32B[c]])
                cp("dve", ub[:, c, :Tn], u32[:, c, :Tn], [u32B[c]], [ubB[c]])
            for c in range(4):
                yp = ypsR.get()
                for (col0, L, Hr_fn, Hi_fn, HB_fn) in segs:
                    for q in range(4):
                        pi = c * 4 + q
                        ssm_segment(pi, col0, L, Hr_fn(pi), Hi_fn(pi), HB_fn(pi), yp, q == 0, q == 3)
                    yield
                zp = f32t.get()
                stt("dve", zp[1][:, :Tn], u32[:, c, :Tn], pkc("ssm_d", c), yp[1][:, :Tn], ALU.mult, ALU.add, [u32B[c], pkB, yp[0]], [zp[0]])
                act(z32[:, c, :Tn], zp[1][:, :Tn], AF.Gelu_apprx_tanh, [zp[0]], [z32B[c]])
                cp("pool", zb[:, c, :Tn], z32[:, c, :Tn], [z32B[c]], [zbB[c]])
            for co in range(4):
                ps = psum.get()
                mm(ps[1][:, :Tn], [(glu_sb[:, k, co * 128:(co + 1) * 128], zb[:, k, :Tn]) for k in range(4)], [gluB] + zbB, [ps[0]])
                sg = f32t.get()
                act(sg[1][:, :Tn], ps[1][:, :Tn], AF.Sigmoid, [ps[0], pkB], [sg[0]], bias=pkc("glu_b", co))
                tt("dve", mixin[:, co, :Tn], z32[:, co, :Tn], sg[1][:, :Tn], ALU.mult, [z32B[co], sg[0]], [mixB[co]])
            for half in range(2):
                sB_, wv = fpiece(w_in_b, pc["w_in"], 512 + half * 512, 1024 + half * 512)
                for ci in range(2):
                    c = half * 2 + ci
                    pa, pg = psum.get(), psum.get()
                    mm(pa[1][:, :Tn], [(wv[:, k, ci * 256:ci * 256 + 128], xb_t[:, k, :Tn]) for k in range(8)], [sB_, xbB_t], [pa[0]])
                    mm(pg[1][:, :Tn], [(wv[:, k, ci * 256 + 128:ci * 256 + 256], xb_t[:, k, :Tn]) for k in range(8)], [sB_, xbB_t], [pg[0]])
                    sg = f32t.get()
                    act(sg[1][:, :Tn], pg[1][:, :Tn], AF.Sigmoid, [pg[0]], [sg[0]])
                    eng = "dve"
                    if is_sample:
                        vt = f32t.get()
                        tt("dve", vt[1][:, :Tn], pa[1][:, :Tn], sg[1][:, :Tn], ALU.mult, [pa[0], sg[0]], [vt[0]])
                        for (col0, L, halo_fn, tail_fn) in conv_segs:
                            halo_fn(c)
                            cp("pool", vbuf[:, c, 30:30 + L], vt[1][:, col0:col0 + L], [vt[0]], [vbufB[c]])
                            conv_chunk(c, col0, L, eng)
                            tail_fn(c, L)
                    else:
                        (col0, L, halo_fn, tail_fn) = conv_segs[0]
                        tt("dve", vbuf[:, c, 30:30 + Tn], pa[1][:, :Tn], sg[1][:, :Tn], ALU.mult, [pa[0], sg[0]], [vbufB[c]])
                        conv_chunk(c, 0, Tn, eng)
                        if tail_fn is not None:
                            tail_fn(c, Tn)
                        cp("pool", vbuf[:, c, 0:30], vbuf[:, c, Tn:Tn + 30], [vbufB[c]], [vbufB[c]])

            def silu_out(c, t_ap, tB, g_ap, b_ap):
                act(mixin[:, 4 + c, :Tn], t_ap, AF.Silu, [tB, pkB], [mixB[4 + c]], scale=g_ap, bias=b_ap)
            layer_norm(4, [cacc[:, c, :Tn] for c in range(4)], caccB, ones_c[:], Tn, "cln_g", "cln_b", silu_out)
        if part in ("all", "back"):
            dma("sp", R0[:, :, :Tn], x32_src, R0B[0], writes=R0B)
            for half in range(2):
                sB_, wv = wpiece(w_out_b, pc["w_out"], half * 512, half * 512 + 512)
                for ci in range(4):
                    co = half * 4 + ci
                    yield
                    ps = psum.get()
                    mm(ps[1][:, :Tn], [(wv[:, k, ci * 128:(ci + 1) * 128], mixin[:, k, :Tn]) for k in range(8)], [sB_] + mixB, [ps[0]])
                    stt("dve", R0[:, co, :Tn], R0[:, co, :Tn], ALPHA, ps[1][:, :Tn], ALU.mult, ALU.add, [R0B[co], ps[0]], [R0B[co]])
            ln_to_resid(Tn, "ln1_g", "ln1_b")
            qps = []
            for half in range(2):
                sB_, wv = wpiece(w_q_b, pc["w_q"], half * 512, half * 512 + 512)
                for ci in range(4):
                    yield
                    ps = psum.get()
                    mm(ps[1][:, :Tn], [(wv[:, k, ci * 128:(ci + 1) * 128], xnb[:, k, :Tn]) for k in range(8)], [sB_] + xnbB, [ps[0]])
                    act(qb[:, half * 4 + ci, :Tn], ps[1][:, :Tn], AF.Identity, [ps[0]], [qbB[half * 4 + ci]], scale=1.0 / 16.0)
            for (col0, L, kv_loader) in att_segs:
                if kv_loader is not None:
                    kv_loader()
                for h in range(4):
                    pts = []
                    for mc in range(2):
                        yield
                        ps = psum.get()
                        mm(ps[1][:, :L], [(kT[:, h * 2 + dc, mc * 128:(mc + 1) * 128], qb[:, h * 2 + dc, col0:col0 + L]) for dc in range(2)],
                           [kTB, qbB[h * 2], qbB[h * 2 + 1]], [ps[0]])
                        pt = pTr.get()
                        act(pt[1][:, :L], ps[1][:, :L], AF.Exp, [ps[0]], [pt[0]])
                        pts.append(pt)
                    yield
                    ps = psum.get()
                    mm(ps[1][:, :L], [(ones_1[:], pts[mc][1][:, :L]) for mc in range(2)], [onesB, pts[0][0], pts[1][0]], [ps[0]])
                    rinv = f32t.get()
                    recip(rinv[1][:, :L], ps[1][:, :L], [ps[0]], [rinv[0]])
                    for dc in range(2):
                        po = psum.get()
                        mm(po[1][:, :L], [(vv[:, mc, h * 256 + dc * 128: h * 256 + dc * 128 + 128], pts[mc][1][:, :L]) for mc in range(2)],
                           [vvB, pts[0][0], pts[1][0]], [po[0]])
                        tt("dve", ob[:, h * 2 + dc, col0:col0 + L], po[1][:, :L], rinv[1][:, :L], ALU.mult, [po[0], rinv[0]], [obB[h * 2 + dc]])
            for half in range(2):
                sB_, wv = wpiece(w_o_b, pc["w_o"], half * 512, half * 512 + 512)
                for ci in range(4):
                    co = half * 4 + ci
                    yield
                    ps = psum.get()
                    mm(ps[1][:, :Tn], [(wv[:, k, ci * 128:(ci + 1) * 128], ob[:, k, :Tn]) for k in range(8)], [sB_] + obB, [ps[0]])
                    stt("dve", R0[:, co, :Tn], R1[:, co, :Tn], ALPHA, ps[1][:, :Tn], ALU.mult, ALU.add, [R1B[co], ps[0]], [R0B[co]])
            ln_to_resid(Tn, "ln2_g", "ln2_b")
            for piece in range(8):
                sB_, wv = wpiece(w1_b, pc["w1"], piece * 512, piece * 512 + 512)
                for hc in range(4):
                    hidx = piece * 4 + hc
                    yield
                    ps = psum.get()
                    mm(ps[1][:, :Tn], [(wv[:, k, hc * 128:(hc + 1) * 128], xnb[:, k, :Tn]) for k in range(8)], [sB_] + xnbB, [ps[0]])
                    rl = f32t.get()
                    act(rl[1][:, :Tn], ps[1][:, :Tn], AF.Relu, [ps[0], pkB], [rl[0]], bias=pkc("b1", hidx))
                    tt("pool" if hidx % 2 else "dve", hid[:, hidx, :Tn], rl[1][:, :Tn], rl[1][:, :Tn], ALU.mult, [rl[0]], [hidB[hidx]])
            w2v = w2_b.rearrange("(k p) n -> p k n", p=128)
            for cp_ in range(4):
                yield
                pss = [psum.get(), psum.get()]
                for kh in range(2):
                    sB_, wv = ring_load(w2v[:, kh * 16:kh * 16 + 16, cp_ * 256:cp_ * 256 + 256], 16, 256, pc["w2"])
                    for oc in range(2):
                        for k in range(16):
                            mm1(pss[oc][1][:, :Tn], wv[:, k, oc * 128:(oc + 1) * 128], hid[:, kh * 16 + k, :Tn],
                                kh == 0 and k == 0, kh == 1 and k == 15, [sB_, hidB[kh * 16 + k]], [pss[oc][0]])
                for oc in range(2):
                    co = cp_ * 2 + oc
                    stt("dve", R0[:, co, :Tn], R1[:, co, :Tn], ALPHA, pss[oc][1][:, :Tn], ALU.mult, ALU.add, [R1B[co], pss[oc][0]], [R0B[co]])
                    ts("pool", R0[:, co, :Tn], R0[:, co, :Tn], pkc("b2", co), None, ALU.add, None, [R0B[co], pkB], [R0B[co]])
            ln_to_resid(Tn, "ln3_g", "ln3_b")
            dma("sp", y_dst, R1[:, :, :Tn], R1B[0], reads=R1B, final=True)

    xT_v = xT.rearrange("(k p) t -> p k t", p=128)
    if RUN_KV:
        memTb = xb[1]
        dma("pool", memTb[:, :, 0:256], memT_d.rearrange("(k p) m -> p k m", p=128), xbB[1], writes=[xbB[1]])
        kv_i = [4]
        for (wb, pcb, dst_d, is_k) in ((w_k_b, pc["w_k"], kout_d, True), (w_v_b, pc["w_v"], vout_d, False)):
            for half in range(2):
                sB_, wv = wpiece(wb, pcb, half * 512, half * 512 + 512)
                if is_k and KVD[0] == "1":
                    for ci in range(4):
                        ps = psum.get()
                        mm(ps[1][:, :256], [(wv[:, k, ci * 128:(ci + 1) * 128], memTb[:, k, 0:256]) for k in range(8)], [sB_, xbB[1]], [ps[0]])
                        cp("act", kT[:, half * 4 + ci, :], ps[1][:, :256], [ps[0]], [kTB])
                for mc in range(2 if KVD[1] == "1" else 0):
                    ps = psum.get()
                    if KVD[3:4] == "h":
                        mm(ps[1][:, 0:256], [(memTb[:, k, mc * 128:(mc + 1) * 128], wv[:, k, 0:256]) for k in range(8)], [sB_, xbB[1]], [ps[0]])
                        mm(ps[1][:, 256:512], [(memTb[:, k, mc * 128:(mc + 1) * 128], wv[:, k, 256:512]) for k in range(8)], [sB_, xbB[1]], [ps[0]])
                    else:
                        mm(ps[1][:, :], [(memTb[:, k, mc * 128:(mc + 1) * 128], wv[:, k, :]) for k in range(8)], [sB_, xbB[1]], [ps[0]])
                    stgB, stg = big(kv_i[0])
                    kv_i[0] = 4 + (kv_i[0] - 4 + 1) % 4
                    if KVD[5:6] == "s":
                        for hh in range(2):
                            cp("act", stg[:, hh * 256:(hh + 1) * 256], ps[1][:, hh * 256:(hh + 1) * 256], [ps[0]], stgB)
                            if not is_k:
                                cp("dve", vv[:, mc, half * 512 + hh * 256:half * 512 + (hh + 1) * 256], ps[1][:, hh * 256:(hh + 1) * 256], [ps[0]], [vvB])
                    elif KVD[5:6] == "n":
                        pass
                    elif KVD[5:6] == "a":
                        cp("act", stg, ps[1][:], [ps[0]], stgB)
                    elif KVD[5:6] == "v":
                        cp("dve", stg, ps[1][:], [ps[0]], stgB)
                    else:
                        cp("act", stg, ps[1][:], [ps[0]], stgB)
                        if not is_k:
                            cp("dve", vv[:, mc, half * 512:(half + 1) * 512], stg, stgB, [vvB])
                    if KVD[2] == "1":
                        dma("sp", dst_d[mc * 128:(mc + 1) * 128, half * 512:(half + 1) * 512], stg, stgB[0], reads=stgB, final=True)

    memset("dve", Hre[:], 0.0, HB)
    memset("dve", Him[:], 0.0, HB)
    winuB, winu = fpiece(w_in_b, pc["w_in"], 0, 512)
    NPB = NPRE // T
    xi = [0]

    def load_xb(col):
        i = xi[0] % 2
        xi[0] += 1
        dma("pool", xb[i][:, :, :], xT_v[:, :, col:col + T], xbB[i], writes=[xbB[i]])
        return i

    nxt = load_xb((NPB - NPB_RUN) * T)
    qi = [0]
    def step_late_pc():
        while late_pc:
            try:
                next(late_pc[0])
                return
            except StopIteration:
                late_pc.pop(0)

    for pb in range(NPB - NPB_RUN, NPB):
        cur = nxt
        step_late_pc()
        nxt = load_xb((pb + 1) * T)
        for blk in range(T // 128):
            pu = psum.get()
            mm(pu[1][:, :], [(xb[cur][:, k, blk * 128:(blk + 1) * 128], winu[:, k, :]) for k in range(8)], [xbB[cur], winuB], [pu[0]])
            ubk = ublk.get()
            cp("act", ubk[1][:], pu[1][:], [pu[0]], [ubk[0]])
            wr, wi = psum.get(), psum.get()

            def fn(e, ubk=ubk, wr=wr, wi=wi):
                ins = None
                for p_ in range(16):
                    e.matmul(wr[1][:, p_ * 32:(p_ + 1) * 32], Pre[:, p_, :], ubk[1][:, p_ * 32:(p_ + 1) * 32], start=True, stop=True)
                    ins = e.matmul(wi[1][:, p_ * 32:(p_ + 1) * 32], Pim[:, p_, :], ubk[1][:, p_ * 32:(p_ + 1) * 32], start=True, stop=True)
                return ins
            P.op("pe", fn, [lhsB, ubk[0]], [wr[0], wi[0]])
            base = (qi[0] % 2) * 2
            qi[0] += 1
            (q1B, q1), (q2B, q2) = big(base), big(base + 1)
            (q3B, q3), (q4B, q4) = big(4 + base), big(4 + base + 1)
            tt("dve", q1, wr[1][:], Bxr[:], ALU.mult, [wr[0], BxB], q1B)
            tt("dve", q2, wi[1][:], Bxi[:], ALU.mult, [wi[0], BxB], q2B)
            tt("dve", q3, wi[1][:], Bxr[:], ALU.mult, [wi[0], BxB], q3B)
            tt("dve", q4, wr[1][:], Bxi[:], ALU.mult, [wr[0], BxB], q4B)
            tt("pool", q1, q1, q2, ALU.subtract, q1B + q2B, q1B)
            tt("pool", q3, q3, q4, ALU.add, q3B + q4B, q3B)
            sr, si = tiny.get(), tiny.get()
            P.op("dve", (lambda sr, q1: lambda e: e.tensor_reduce(out=sr[1][:], in_=q1.rearrange("p (a b) -> p a b", a=16), axis=AX.X, op=ALU.add))(sr, q1), q1B, [sr[0]])
            P.op("dve", (lambda si, q3: lambda e: e.tensor_reduce(out=si[1][:], in_=q3.rearrange("p (a b) -> p a b", a=16), axis=AX.X, op=ALU.add))(si, q3), q3B, [si[0]])
            u1, u2, u3, u4 = tiny.get(), tiny.get(), tiny.get(), tiny.get()
            tt("pool", u1[1][:], a128r[:], Hre[:], ALU.mult, [a128B] + HB, [u1[0]])
            tt("pool", u2[1][:], a128i[:], Him[:], ALU.mult, [a128B] + HB, [u2[0]])
            tt("pool", u3[1][:], a128r[:], Him[:], ALU.mult, [a128B] + HB, [u3[0]])
            tt("pool", u4[1][:], a128i[:], Hre[:], ALU.mult, [a128B] + HB, [u4[0]])
            tt("pool", u1[1][:], u1[1][:], u2[1][:], ALU.subtract, [u1[0], u2[0]], [u1[0]])
            tt("pool", u3[1][:], u3[1][:], u4[1][:], ALU.add, [u3[0], u4[0]], [u3[0]])
            tt("pool", Hre[:], u1[1][:], sr[1][:], ALU.add, [u1[0], sr[0]], HB)
            tt("pool", Him[:], u3[1][:], si[1][:], ALU.add, [u3[0], si[0]] + HB, HB)

    for g_ in late_pc:
        for _ in g_:
            pass
    hxi = xi[0] % 2
    hx, hxB = xb[hxi], xbB[hxi]
    dma("pool", hx[:, :, 0:32], xT_v[:, :, NPRE - 32:NPRE], hxB, writes=[hxB])
    for half in range(2):
        sB_, wv = wpiece(w_in_b, pc["w_in"], 512 + half * 512, 1024 + half * 512)
        for ci in range(2):
            c = half * 2 + ci
            pa, pg = psum.get(), psum.get()
            mm(pa[1][:, :32], [(wv[:, k, ci * 256:ci * 256 + 128], hx[:, k, 0:32]) for k in range(8)], [sB_, hxB], [pa[0]])
            mm(pg[1][:, :32], [(wv[:, k, ci * 256 + 128:ci * 256 + 256], hx[:, k, 0:32]) for k in range(8)], [sB_, hxB], [pg[0]])
            sg = f32t.get()
            act(sg[1][:, :32], pg[1][:, :32], AF.Sigmoid, [pg[0]], [sg[0]])
            tt("dve", vbuf[:, c, 0:30], pa[1][:, 2:32], sg[1][:, 2:32], ALU.mult, [pa[0], sg[0]], [vbufB[c]])

    xs_v = xsT.rearrange("(k p) t -> p k t", p=128)
    hsinB = Buf("hs_in")

    def s_halo(s_):
        def f(c):
            dma("sp", vbuf[:, c, 0:30], convc_d[:, (c * 2 + s_) * 30:(c * 2 + s_) * 30 + 30], vbufB[c], writes=[vbufB[c]])
        return f

    def s_tail(s_):
        def f(c, L):
            dma("sp", convsout_d[:, (c * 2 + s_) * 30:(c * 2 + s_) * 30 + 30], vbuf[:, c, L:L + 30], vbufB[c], reads=[vbufB[c]], final=True)
        return f

    def s_kv(s_):
        def f():
            dma("pool", kT[:], kTs_d[s_].rearrange("(k p) m -> p k m", p=128), kTB, writes=[kTB])
            dma("pool", vv[:], vs_d[s_].rearrange("(k p) n -> p k n", p=128), vvB, writes=[vvB])
        return f

    def hs_fn(s_, reim):
        return lambda pi: Hs[:, pi * 4 + s_ * 2 + reim: pi * 4 + s_ * 2 + reim + 1]

    s_segs = [(s_ * 16, 16, hs_fn(s_, 0), hs_fn(s_, 1), lambda pi: HsB[pi]) for s_ in range(2)]

    def sample_gen(part, bi):
        if part in ("all", "front"):
            dma("sp", Hs[:], h0_d[:, :], hsinB, writes=HsB)
            dma("pool", xb[bi][:, :, 0:32], xs_v, xbB[bi], writes=[xbB[bi]])
        for _ in run_tile(32, xb[bi], xbB[bi], xs_v, s_segs,
                          [(s_ * 16, 16, s_halo(s_), s_tail(s_)) for s_ in range(2)],
                          [(s_ * 16, 16, s_kv(s_)) for s_ in range(2)],
                          ysT.rearrange("(k p) t -> p k t", p=128), is_sample=True, part=part):
            yield
        if part in ("all", "back"):
            hso = statsb.get()
            cp("dve", hso[1][:, 0:64], Hs[:], HsB, [hso[0]])
            dma("sp", hsout_d[:, :], hso[1][:, 0:64], hso[0], reads=[hso[0]], final=True)

    p_segs = [(s * TS, TS, lambda pi: Hre[:, pi:pi + 1], lambda pi: Him[:, pi:pi + 1], lambda pi: HB[pi]) for s in range(T // TS)]
    yT_v = yT.rearrange("(k p) t -> p k t", p=128)
    cur = nxt

    def p_tail(c, L):
        dma("sp", convout_d[:, c * 30:(c + 1) * 30], vbuf[:, c, L:L + 30], vbufB[c], reads=[vbufB[c]], final=True)

    def xload(it, buf_i):
        dma("pool", xb[buf_i][:, :, :], xT_v[:, :, NPRE + it * T: NPRE + (it + 1) * T], xbB[buf_i], writes=[xbB[buf_i]])

    if NT_RUN > 0:
        if NT_RUN > 1:
            xload(1, 1 - cur)
        drain(front_p(xb[cur], xbB[cur], p_tail if NT_RUN == 1 and NT == 1 else None))
    for it in range(NT_RUN):
        back = run_tile(T, xb[cur], xbB[cur], xT_v[:, :, NPRE + it * T: NPRE + (it + 1) * T], p_segs,
                        [(0, T, None, None)], [(0, T, None)], yT_v[:, :, it * T:(it + 1) * T], part="back")
        if it + 1 < NT_RUN:
            nb = 1 - cur
            fr = front_p(xb[nb], xbB[nb], p_tail if it + 1 == NT - 1 else None)
            if it + 2 < NT_RUN:
                xload(it + 2, cur)
            if PIPE:
                merge(back, fr)
            else:
                drain(back)
                drain(fr)
        else:
            if RUN_SAMPLE and PIPE:
                merge(back, sample_gen("front", 1 - cur))
            else:
                drain(back)
        cur = 1 - cur
    hst, hst2 = tiny.get(), tiny.get()
    cp("dve", hst[1][:], Hre[:], HB, [hst[0]])
    cp("dve", hst2[1][:], Him[:], HB, [hst2[0]])
    dma("sp", hout_d[:, 0:16], hst[1][:], hst[0], reads=[hst[0]], final=True)
    dma("sp", hout_d[:, 16:32], hst2[1][:], hst2[0], reads=[hst2[0]], final=True)

    if RUN_SAMPLE:
        if PIPE and NT_RUN > 0:
            drain(sample_gen("back", cur))
        else:
            drain(sample_gen("all", cur))

    P.emit(st)
    st.close()
    return nc


_NC_CACHE = {}


def _mode_major(a):
    sh = a.shape
    a = a.reshape((16, 2, 64) + sh[2:])
    return np.ascontiguousarray(np.moveaxis(a, 0, 2).reshape((128, 16) + sh[2:]))


def kernel(x_prompt, x_sample, state_ssm_re, state_ssm_im, cache_conv, cache_mem_k, cache_mem_v,
           mem_prompt, w_in, ssm_a_re, ssm_a_im, ssm_log_dt, ssm_b_re, ssm_b_im, ssm_c_re, ssm_c_im,
           ssm_d, glu_w, glu_b, conv_w, conv_b, conv_ln_g, conv_ln_b, w_out, ln1_g, ln1_b,
           mem_w_q, mem_w_k, mem_w_v, mem_w_o, ln2_g, ln2_b,
           mlp_w1, mlp_b1, mlp_w2, mlp_b2, ln3_g, ln3_b):
    f = lambda a: np.ascontiguousarray(np.asarray(a, dtype=np.float32))
    x_prompt, x_sample = f(x_prompt), f(x_sample)
    col = lambda v, n: f(v).reshape(n, 128).T
    pk = np.zeros((128, NPK), np.float32)

    def put(name, arr):
        pk[:, PK[name][0]:PK[name][1]] = arr
    put("ln1_g", col(ln1_g[0], 8)); put("ln1_b", col(ln1_b[0], 8))
    put("ln2_g", col(ln2_g[0], 8)); put("ln2_b", col(ln2_b[0], 8))
    put("ln3_g", col(ln3_g[0], 8)); put("ln3_b", col(ln3_b[0], 8))
    put("b1", col(mlp_b1[0], 32)); put("b2", col(mlp_b2[0], 8))
    put("glu_b", col(glu_b[0], 4)); put("ssm_d", col(ssm_d[0], 4))
    put("conv_b", col(conv_b[0], 4)); put("cln_g", col(conv_ln_g[0], 4)); put("cln_b", col(conv_ln_b[0], 4))
    cw = f(conv_w[0]).T.reshape(4, 128, 31).transpose(1, 0, 2).reshape(128, 124)
    put("conv_w", cw)
    put("a_re", _mode_major(f(ssm_a_re[0])[:, :, None])[:, :, 0])
    put("a_im", _mode_major(f(ssm_a_im[0])[:, :, None])[:, :, 0])
    ldt = np.repeat(f(ssm_log_dt[0])[:, None], 64, axis=1)
    put("logdt", _mode_major(ldt[:, :, None])[:, :, 0])
    put("tidx", np.tile(np.arange(128, dtype=np.float32)[None, :], (128, 1)))
    put("eidx", (127.0 - np.arange(128, dtype=np.float32))[:, None])
    rows = np.stack([f(ssm_a_re[0]).reshape(-1), f(ssm_a_im[0]).reshape(-1), ldt.reshape(-1)]).astype(np.float32)
    BT = np.zeros((2, 128, 2048), np.float32)
    CT = np.zeros((2, 128, 2048), np.float32)
    Bx = np.zeros((2, 128, 512), np.float32)
    for ri, (bsrc, csrc) in enumerate(((f(ssm_b_re[0]), f(ssm_c_re[0])), (f(ssm_b_im[0]), f(ssm_c_im[0])))):
        for g in range(32):
            pi, gp, gl = g // 2, g % 2, g % 8
            BT[ri, gl * 16:(gl + 1) * 16, pi * 128 + gp * 64: pi * 128 + gp * 64 + 64] = bsrc[g].T
            CT[ri, gp * 64:(gp + 1) * 64, pi * 128 + gl * 16: pi * 128 + gl * 16 + 16] = csrc[g].T
            Bx[ri, gp * 64:(gp + 1) * 64, pi * 32 + gp * 16: pi * 32 + gp * 16 + 16] = bsrc[g]
    shared = dict(w_in=f(w_in[0]), glu_w=f(glu_w[0]), w_out=f(w_out[0]), w_q=f(mem_w_q[0]), w_k=f(mem_w_k[0]),
                  w_v=f(mem_w_v[0]), w_o=f(mem_w_o[0]), w1=f(mlp_w1[0]), w2=f(mlp_w2[0]), pk=pk, rows=rows, BT=BT, CT=CT, Bx=Bx)
    xTs = [np.ascontiguousarray(x_prompt[b].T) for b in range(2)]
    in_maps = []
    for c in CORES:
        b, j = c // 4, c % 4
        xin = np.zeros((D, NPRE + SEG), np.float32)
        npre = j * SEG
        xin[:, NPRE - npre: NPRE + SEG] = xTs[b][:, 0:(j + 1) * SEG]
        ss = [2 * c, 2 * c + 1]
        xs = np.concatenate([x_sample[s].T for s in ss], axis=1)
        h0 = np.zeros((128, 16, 2, 2), np.float32)
        cc = np.zeros((128, 4, 2, 30), np.float32)
        for si, s in enumerate(ss):
            h0[:, :, si, 0] = _mode_major(f(state_ssm_re[0, s])[:, :, None])[:, :, 0]
            h0[:, :, si, 1] = _mode_major(f(state_ssm_im[0, s])[:, :, None])[:, :, 0]
            cc[:, :, si, :] = f(cache_conv[0, s]).T.reshape(4, 128, 30).transpose(1, 0, 2)
        kTs = np.stack([f(cache_mem_k[0, s]).reshape(256, D).T for s in ss])
        vs = np.stack([f(cache_mem_v[0, s]).reshape(256, D) for s in ss])
        m = dict(shared)
        m.update(xT=xin, xsT=np.ascontiguousarray(xs), h0=h0.reshape(128, 64), convc=cc.reshape(128, 240),
                 kTs=np.ascontiguousarray(kTs), vs=np.ascontiguousarray(vs), memT=np.ascontiguousarray(f(mem_prompt[b]).T))
        in_maps.append(m)
    if "nc" not in _NC_CACHE:
        _NC_CACHE["nc"] = build_program()
    res = run_bass_kernel_spmd(_NC_CACHE["nc"], in_maps, core_ids=list(range(len(CORES))))
    R = {c: res.results[i] for i, c in enumerate(CORES)}

    def unmode(a):
        return a.reshape(2, 64, 16).transpose(2, 0, 1).reshape(32, 64)
    y_prompt = np.zeros((2, 16384, D), np.float32)
    y_sample = np.zeros((16, 16, D), np.float32)
    p_re = np.zeros((1, 2, 32, 64), np.float32)
    p_im = np.zeros((1, 2, 32, 64), np.float32)
    p_conv = np.zeros((1, 2, 30, 512), np.float32)
    p_mk = np.zeros((1, 2, 256, 4, 256), np.float32)
    p_mv = np.zeros((1, 2, 256, 4, 256), np.float32)
    s_re = np.zeros((1, 16, 32, 64), np.float32)
    s_im = np.zeros((1, 16, 32, 64), np.float32)
    s_conv = np.zeros((1, 16, 30, 512), np.float32)
    for c in CORES:
        b, j = c // 4, c % 4
        r = R[c]
        y_prompt[b, j * SEG:(j + 1) * SEG, :] = r["yT"].T
        ys = r["ysT"].T
        hs = r["hsout"].reshape(128, 16, 2, 2)
        cs = r["convsout"].reshape(128, 4, 2, 30)
        for si in range(2):
            s = 2 * c + si
            y_sample[s] = ys[si * 16:(si + 1) * 16]
            s_re[0, s] = unmode(hs[:, :, si, 0])
            s_im[0, s] = unmode(hs[:, :, si, 1])
            s_conv[0, s] = cs[:, :, si, :].transpose(1, 0, 2).reshape(512, 30).T
        if j == 3:
            ho = r["hout"].reshape(128, 2, 16)
            p_re[0, b] = unmode(ho[:, 0, :])
            p_im[0, b] = unmode(ho[:, 1, :])
            p_conv[0, b] = r["convout"].reshape(128, 4, 30).transpose(1, 0, 2).reshape(512, 30).T
        if j == 0:
            p_mk[0, b] = r["kout"].reshape(256, 4, 256)
            p_mv[0, b] = r["vout"].reshape(256, 4, 256)
    return (y_prompt, y_sample, p_re, p_im, p_conv, p_mk, p_mv, s_re, s_im, s_conv)
```

```python
import os
from contextlib import ExitStack
import numpy as np
import concourse.bass as bass
import concourse.mybir as mybir
from concourse.bass_utils import run_bass_kernel_spmd

F32 = mybir.dt.float32
BF16 = mybir.dt.bfloat16
AF = mybir.ActivationFunctionType
ALU = mybir.AluOpType
AX = mybir.AxisListType

D = 1024
NCORE = 8
SEG = 4096
NPRE = 3 * SEG
T = 256
NT = SEG // T
TS = 128
LN_EPS = 1e-5
ALPHA = 2.0 ** 0.25
MAGIC = 12582912.0
TWO_PI = float(2 * np.pi)
PI = float(np.pi)

PK = {}
_o = 0
for _n, _w in [("ln1_g", 8), ("ln1_b", 8), ("ln2_g", 8), ("ln2_b", 8), ("ln3_g", 8), ("ln3_b", 8),
               ("b1", 32), ("b2", 8), ("glu_b", 4), ("ssm_d", 4), ("conv_b", 4), ("cln_g", 4), ("cln_b", 4),
               ("conv_w", 124), ("a_re", 16), ("a_im", 16), ("logdt", 16), ("tidx", 128), ("eidx", 1), ("pad", 3)]:
    PK[_n] = (_o, _o + _w)
    _o += _w
NPK = _o

SYNC_SAME = {e: (e in os.environ.get("K_SYNC", "act,dve,pool").split(",")) for e in ("act", "dve", "pool", "pe", "sp")}
NT_RUN = int(os.environ.get("K_NT", NT))
NGST = int(os.environ.get("K_NGST", "8"))
PIPE = bool(int(os.environ.get("K_PIPE", "1")))
GRP = bool(int(os.environ.get("K_GRP", "1")))
SSM_POST = os.environ.get("K_SSMPOST", "dve")
CONV_ENG = os.environ.get("K_CONV", "dve")
SQ_ENG = os.environ.get("K_SQ", "act")
SQ_MOD = int(os.environ.get("K_SQMOD", "2"))
CONV_POOL_CH = int(os.environ.get("K_CPC", "0"))
PF_BANKS = bool(int(os.environ.get("K_PFB", "1")))
RUN_SAMPLE = bool(int(os.environ.get("K_SAMPLE", "1")))
NPB_RUN = int(os.environ.get("K_NPB", NPRE // T))
RUN_PREP = bool(int(os.environ.get("K_PREP", "1")))
RUN_KV = bool(int(os.environ.get("K_KV", "1")))
RUN_PC = os.environ.get("K_PC", "all")
KVD = os.environ.get("K_KVD", "111")
PC_ROWS = int(os.environ.get("K_PCR", "128"))
PC_COLS = int(os.environ.get("K_PCC", "1024"))
CORES = [int(x) for x in os.environ.get("K_CORES", "0,1,2,3,4,5,6,7").split(",")]


class Ev:
    __slots__ = ("eng", "sem", "value", "op", "group")

    def __init__(self, eng):
        self.eng = eng
        self.sem = None
        self.value = None
        self.op = None
        self.group = None


class Buf:
    __slots__ = ("name", "w", "r", "dma_sem", "dma_count", "group")

    def __init__(self, name, group=None):
        self.name = name
        self.w = None
        self.r = []
        self.dma_sem = None
        self.dma_count = 0
        self.group = group


class DmaGroup:
    def __init__(self, name):
        self.name = name
        self.sem = None
        self.count = 0


class Op:
    __slots__ = ("eng", "fn", "waits", "ev", "is_dma", "signals")

    def __init__(self, eng, fn, waits, ev, is_dma):
        self.eng = eng
        self.fn = fn
        self.waits = waits
        self.ev = ev
        self.is_dma = is_dma
        self.signals = is_dma


class Prog:
    ENGINES = ("pe", "act", "dve", "pool", "sp")

    def __init__(self, nc):
        self.nc = nc
        self.ops = {e: [] for e in self.ENGINES}
        self.all_ops = []
        self.dma_bufs = []
        self.groups = []
        self.final_evs = []

    def group(self, name):
        g = DmaGroup(name)
        self.groups.append(g)
        return g

    def _deps(self, ev, reads, writes):
        waits = []
        for b in reads:
            if b.w is not None:
                waits.append(b.w)
        for b in writes:
            if b.w is not None:
                waits.append(b.w)
            waits.extend(b.r)
        for b in reads:
            b.r.append(ev)
        for b in writes:
            b.w = ev
            b.r = []
        out = []
        seen = set()
        for w in waits:
            if w is ev or id(w) in seen:
                continue
            seen.add(id(w))
            out.append(w)
        return out

    def op(self, eng, fn, reads=(), writes=()):
        ev = Ev(eng)
        waits = self._deps(ev, reads, writes)
        o = Op(eng, fn, waits, ev, False)
        ev.op = o
        self.ops[eng].append(o)
        self.all_ops.append(o)
        return o

    def dma(self, eng, fn, key, reads=(), writes=(), final=False):
        ev = Ev("dma")
        waits = self._deps(ev, reads, writes)
        if key.group is not None:
            ev.group = key.group
            key.group.count += 1
        else:
            if key.dma_count == 0:
                self.dma_bufs.append(key)
            key.dma_count += 1
            ev.sem = key
            ev.value = 16 * key.dma_count
        o = Op(eng, fn, waits, ev, True)
        ev.op = o
        self.ops[eng].append(o)
        self.all_ops.append(o)
        if final:
            self.final_evs.append(ev)
        return o

    def emit(self, stack):
        nc = self.nc
        for o in self.all_ops:
            for w in o.waits:
                if w.op is not None and not w.op.is_dma:
                    if w.eng == o.eng and not SYNC_SAME[o.eng]:
                        continue
                    w.op.signals = True
        esem = {}
        for e in ("pe", "act", "dve", "pool"):
            esem[e] = stack.enter_context(nc.semaphore("s_" + e))
            cnt = 0
            for o in self.ops[e]:
                if o.is_dma:
                    continue
                if o.signals:
                    cnt += 1
                    o.ev.sem = esem[e]
                    o.ev.value = cnt
        for b in self.dma_bufs:
            b.dma_sem = stack.enter_context(nc.semaphore("d_" + b.name))
        for g in self.groups:
            if g.count:
                g.sem = stack.enter_context(nc.semaphore("g_" + g.name))

        def resolve(ev):
            if ev.group is not None:
                return ev.group.sem, 16 * ev.group.count
            if isinstance(ev.sem, Buf):
                return ev.sem.dma_sem, ev.value
            return ev.sem, ev.value

        block = stack.enter_context(nc.Block())
        handles = {"pe": "tensor", "act": "scalar", "dve": "vector", "pool": "gpsimd", "sp": "sync"}
        final_evs = self.final_evs

        def make(e):
            ops = self.ops[e]

            def body(eng):
                waited = {}
                for o in ops:
                    for w in o.waits:
                        if w.eng == e and not w.op.is_dma and not SYNC_SAME[e]:
                            continue
                        sem, val = resolve(w)
                        assert sem is not None and val is not None, (e, w.eng)
                        k = id(sem)
                        if waited.get(k, 0) >= val:
                            continue
                        waited[k] = val
                        eng.wait_ge(sem, val)
                    ins = o.fn(eng)
                    if o.is_dma:
                        sem, _ = resolve(o.ev)
                        ins.then_inc(sem, 16)
                    elif o.signals:
                        ins.then_inc(o.ev.sem, 1)
                if e == "sp":
                    for ev in final_evs:
                        sem, val = resolve(ev)
                        if waited.get(id(sem), 0) >= val:
                            continue
                        waited[id(sem)] = val
                        eng.wait_ge(sem, val)
            return body

        for e in self.ENGINES:
            if self.ops[e] or e == "sp":
                getattr(block, handles[e])(make(e))


class Rot:
    def __init__(self, st, nc, name, shape, dtype, n, psum=False):
        self.items = []
        for i in range(n):
            alloc = nc.psum_tensor if psum else nc.sbuf_tensor
            t = st.enter_context(alloc(f"rt_{name}{i}", shape, dtype))
            self.items.append((Buf(f"{name}{i}"), t))
        self.i = 0

    def get(self):
        it = self.items[self.i % len(self.items)]
        self.i += 1
        return it


def build_program():
    nc = bass.Bass("TRN2", target_bir_lowering=False)
    st = ExitStack()
    P = Prog(nc)

    def din(name, shape):
        return nc.dram_tensor(name, shape, F32, kind="ExternalInput").ap()

    def dout(name, shape):
        return nc.dram_tensor(name, shape, F32, kind="ExternalOutput").ap()

    def dscr(name, shape):
        return nc.dram_tensor(name, shape, BF16, kind="Internal").ap()

    xT = din("xT", [D, NPRE + SEG])
    xsT = din("xsT", [D, 32])
    h0_d = din("h0", [128, 64])
    convc_d = din("convc", [128, 240])
    kTs_d = din("kTs", [2, D, 256])
    vs_d = din("vs", [2, 256, D])
    memT_d = din("memT", [D, 256])
    w_in_d = din("w_in", [D, 1536])
    glu_w_d = din("glu_w", [512, 512])
    w_out_d = din("w_out", [D, D])
    w_q_d = din("w_q", [D, D])
    w_k_d = din("w_k", [D, D])
    w_v_d = din("w_v", [D, D])
    w_o_d = din("w_o", [D, D])
    w1_d = din("w1", [D, 4096])
    w2_d = din("w2", [4096, D])
    pk_d = din("pk", [128, NPK])
    rows_d = din("rows", [3, 2048])
    BT_d = din("BT", [2, 128, 2048])
    CT_d = din("CT", [2, 128, 2048])
    Bx_d = din("Bx", [2, 128, 512])

    yT = dout("yT", [D, SEG])
    ysT = dout("ysT", [D, 32])
    hout_d = dout("hout", [128, 32])
    convout_d = dout("convout", [128, 120])
    kout_d = dout("kout", [256, D])
    vout_d = dout("vout", [256, D])
    hsout_d = dout("hsout", [128, 64])
    convsout_d = dout("convsout", [128, 240])

    w_in_b = dscr("w_in_b", [D, 1536])
    w_out_b = dscr("w_out_b", [D, D])
    w_q_b = dscr("w_q_b", [D, D])
    w_k_b = dscr("w_k_b", [D, D])
    w_v_b = dscr("w_v_b", [D, D])
    w_o_b = dscr("w_o_b", [D, D])
    w1_b = dscr("w1_b", [D, 4096])
    w2_b = dscr("w2_b", [4096, D])

    def sb(name, shape, dt=F32):
        return st.enter_context(nc.sbuf_tensor("sb_" + name, shape, dt))

    pk = sb("pk", [128, NPK])
    cgrp = P.group("consts")
    pkB = Buf("pk", group=cgrp)
    ones_m = sb("ones_m", [128, 128], BF16)
    ones_c = sb("ones_c", [128, 128], BF16)
    ones_1 = sb("ones_1", [128, 128], BF16)
    onesB = Buf("ones")
    R0 = sb("R0", [128, 8, T])
    R1 = sb("R1", [128, 8, T])
    xnb = sb("xnb", [128, 8, T], BF16)
    R0B = [Buf(f"R0_{c}") for c in range(8)]
    R1B = [Buf(f"R1_{c}") for c in range(8)]
    xnbB = [Buf(f"xnb_{c}") for c in range(8)]
    ob, obB = xnb, xnbB
    xb = [sb(f"xb{i}", [128, 8, T], BF16) for i in range(2)]
    xbB = [Buf(f"xb{i}") for i in range(2)]
    NRING = 3
    ring = [sb(f"ring{i}", [128, 4096], BF16) for i in range(NRING)]
    ringB = [Buf(f"ring{i}") for i in range(NRING)]
    ring_i = [0]
    fring = sb("fring", [128, 4096], BF16)
    fringB = Buf("fring")
    glu_sb = sb("glu_sb", [128, 4, 512], BF16)
    gluB = Buf("glu")
    u32 = sb("u32", [128, 4, T])
    ub = sb("ub", [128, 4, T], BF16)
    u32B = [Buf(f"u32_{c}") for c in range(4)]
    ubB = [Buf(f"ub_{c}") for c in range(4)]
    vbuf = sb("vbuf", [128, 4, 30 + T])
    vbufB = [Buf(f"vbuf_{c}") for c in range(4)]
    cacc = sb("cacc", [128, 4, T])
    caccB = [Buf(f"cacc_{c}") for c in range(4)]
    mixin = sb("mixin", [128, 8, T], BF16)
    mixB = [Buf(f"mix_{c}") for c in range(8)]
    z32 = sb("z32", [128, 4, T])
    zb = sb("zb", [128, 4, T], BF16)
    z32B = [Buf(f"z32_{c}") for c in range(4)]
    zbB = [Buf(f"zb_{c}") for c in range(4)]
    hid = sb("hid", [128, 32, T], BF16)
    hidB = [Buf(f"hid_{c}") for c in range(32)]
    qb, qbB = hid, hidB
    kT = sb("kT", [128, 8, 256], BF16)
    kTB = Buf("kT")
    vv = sb("vv", [128, 2, D], BF16)
    vvB = Buf("vv")
    costab = sb("costab", [128, 16, TS])
    sintab = sb("sintab", [128, 16, TS])
    rtab = sb("rtab", [128, 16, TS])
    tabB = Buf("tabs")
    BreT = sb("BreT", [128, 16, 128], BF16)
    BimT = sb("BimT", [128, 16, 128], BF16)
    CreT = sb("CreT", [128, 16, 128], BF16)
    CimT = sb("CimT", [128, 16, 128], BF16)
    Pre = sb("Pre", [128, 16, 128], BF16)
    Pim = sb("Pim", [128, 16, 128], BF16)
    lhsB = Buf("ssm_lhs")
    Bxr = sb("Bxr", [128, 512])
    Bxi = sb("Bxi", [128, 512])
    BxB = Buf("Bx")
    a128r = sb("a128r", [128, 16])
    a128i = sb("a128i", [128, 16])
    a128B = Buf("a128")
    Hre = sb("Hre", [128, 16])
    Him = sb("Him", [128, 16])
    HB = [Buf(f"H_{p}") for p in range(16)]
    Hs = sb("Hs", [128, 64])
    HsB = [Buf(f"Hs_{p}") for p in range(16)]

    s16 = Rot(st, nc, "s16_", [128, T], BF16, 4)
    f32t = Rot(st, nc, "f32t_", [128, T], F32, 5)
    sst = Rot(st, nc, "sst_", [128, 32 if GRP else TS], F32, 10)
    hbf = Rot(st, nc, "hbf_", [128, 32 if GRP else TS], BF16, 4)
    gst = Rot(st, nc, "gst_", [128, 512], F32, NGST)
    ctmp = Rot(st, nc, "ctmp_", [128, T], F32, 2)
    ghb = Rot(st, nc, "ghb_", [128, 512], BF16, 4)
    tiny = Rot(st, nc, "tiny_", [128, 16], F32, 8)
    pTr = Rot(st, nc, "pT_", [128, T], BF16, 4)
    statsb = Rot(st, nc, "stat_", [128, T], F32, 5)
    psum = Rot(st, nc, "ps", [128, 512], F32, 3, psum=True)
    ypsR = Rot(st, nc, "yps", [128, 512], F32, 1, psum=True)
    buR = Rot(st, nc, "bups", [128, 512], F32, 2, psum=True)
    lnR = Rot(st, nc, "lnps", [128, 512], F32, 2, psum=True)

    def big(i):
        src, bufs = (R0, R0B) if i < 4 else (R1, R1B)
        j = (i % 4) * 2
        return [bufs[j], bufs[j + 1]], src[:, j:j + 2, :].rearrange("p a b -> p (a b)")

    pkc = lambda name, i=0, n=1: pk[:, PK[name][0] + i: PK[name][0] + i + n]

    def tt(eng, out, a, b, op, reads, writes):
        P.op(eng, lambda e: e.tensor_tensor(out=out, in0=a, in1=b, op=op), reads, writes)

    def ts(eng, out, a, s1, s2, op0, op1, reads, writes):
        if op1 is None:
            P.op(eng, lambda e: e.tensor_scalar(out=out, in0=a, scalar1=s1, scalar2=None, op0=op0), reads, writes)
        else:
            P.op(eng, lambda e: e.tensor_scalar(out=out, in0=a, scalar1=s1, scalar2=s2, op0=op0, op1=op1), reads, writes)

    def stt(eng, out, a, s, b, op0, op1, reads, writes):
        P.op(eng, lambda e: e.scalar_tensor_tensor(out=out, in0=a, scalar=s, in1=b, op0=op0, op1=op1), reads, writes)

    def act(out, in_, func, reads, writes, scale=None, bias=None):
        kw = {}
        if scale is not None:
            kw["scale"] = scale
        if bias is not None:
            kw["bias"] = bias
        P.op("act", lambda e: e.activation(out=out, in_=in_, func=func, **kw), reads, writes)

    def cp(eng, out, in_, reads, writes):
        if eng == "act":
            act(out, in_, AF.Copy, reads, writes)
        else:
            P.op(eng, lambda e: e.tensor_copy(out=out, in_=in_), reads, writes)

    def recip(out, in_, reads, writes):
        P.op("dve", lambda e: e.reciprocal(out=out, in_=in_), reads, writes)

    def mm(out, pairs, reads, writes):
        n = len(pairs)

        def fn(e):
            ins = None
            for i, (l, r) in enumerate(pairs):
                ins = e.matmul(out, l, r, start=(i == 0), stop=(i == n - 1))
            return ins
        P.op("pe", fn, reads, writes)

    def mm1(out, l, r, start, stop, reads, writes):
        P.op("pe", lambda e: e.matmul(out, l, r, start=start, stop=stop), reads, writes)

    def dma(eng, out, in_, key, reads=(), writes=(), final=False):
        P.dma(eng, lambda e: e.dma_start(out=out, in_=in_), key, reads=reads, writes=writes, final=final)

    def memset(eng, ap, val, writes):
        P.op(eng, lambda e: e.memset(ap, val), (), writes)

    def range_reduce(out, in_, shift, reads, writes, tmp, tmpB):
        src = in_
        rd = list(reads)
        if shift != 0.0:
            ts("dve", out, in_, shift, None, ALU.add, None, reads, writes)
            src = out
            rd = list(writes)
        ts("dve", tmp, src, 1.0 / TWO_PI, MAGIC, ALU.mult, ALU.add, rd, tmpB)
        ts("dve", tmp, tmp, MAGIC, -TWO_PI, ALU.subtract, ALU.mult, tmpB, tmpB)
        tt("dve", out, src, tmp, ALU.add, rd + list(tmpB), writes)
        ts("dve", out, out, PI, -PI, ALU.min, ALU.max, writes, writes)

    dma("sp", pk[:], pk_d[:, :], pkB, writes=[pkB])
    dma("pool", glu_sb[:], glu_w_d.rearrange("(k p) n -> p k n", p=128), gluB, writes=[gluB])
    memset("dve", ones_m[:], 1.0 / 1024.0, [onesB])
    memset("dve", ones_c[:], 1.0 / 512.0, [onesB])
    memset("dve", ones_1[:], 1.0, [onesB])


    pcsB = [Buf(f"pcs{i}") for i in range(NRING)]
    pclB = [Buf(f"pcl{i}") for i in range(NRING)]

    def precast_gen(bl, dst, src, nrows, colmap=None):
        ncols = src.shape[1]
        nk = nrows // 128
        sv = src.rearrange("(k p) n -> p k n", p=128)
        dv = dst.rearrange("(k p) n -> p k n", p=128)
        if colmap is None:
            colmap = [(c0, c0, min(1024, ncols - c0)) for c0 in range(0, ncols, 1024)]
        for (d0, s0, n) in colmap:
            kstep = max(1, min(nk, 4096 // n))
            for k0 in range(0, nk, kstep):
                i = ring_i[0] % NRING
                ring_i[0] += 1
                view = ring[i][:, 0:kstep * n].rearrange("p (a b) -> p a b", a=kstep)
                dma("pool", view, sv[:, k0:k0 + kstep, s0:s0 + n], pclB[i], writes=[ringB[i]])
                dma("sp", dv[:, k0:k0 + kstep, d0:d0 + n], view, pcsB[i], reads=[ringB[i]], writes=[bl[i]])
                yield

    def precast(name, dst, src, nrows, colmap=None):
        bl = [Buf(f"pcb_{name}{i}") for i in range(NRING)]
        for _ in precast_gen(bl, dst, src, nrows, colmap):
            pass
        return bl

    def precast_later(name, dst, src, nrows):
        bl = [Buf(f"pcb_{name}{i}") for i in range(NRING)]
        late_pc.append(precast_gen(bl, dst, src, nrows))
        return bl

    late_pc = []

    win_map = [(0, 0, 512)]
    for i in range(4):
        win_map.append((512 + 256 * i, 512 + 128 * i, 128))
        win_map.append((512 + 256 * i + 128, 1024 + 128 * i, 128))
    pc = {}
    if RUN_PC == "none":
        def precast(name, dst, src, nrows, colmap=None):
            return [Buf("pcb_" + name)]
        precast_later = lambda name, dst, src, nrows: [Buf("pcb_" + name)]
    pc["w_in"] = precast("w_in", w_in_b, w_in_d, D, win_map)
    pc["w_k"] = precast("w_k", w_k_b, w_k_d, D)
    pc["w_v"] = precast("w_v", w_v_b, w_v_d, D)
    pc["w_out"] = precast_later("w_out", w_out_b, w_out_d, D)
    pc["w_q"] = precast_later("w_q", w_q_b, w_q_d, D)
    pc["w_o"] = precast_later("w_o", w_o_b, w_o_d, D)
    pc["w1"] = precast_later("w1", w1_b, w1_d, D)
    pc["w2"] = precast_later("w2", w2_b, w2_d, 4096)

    def ring_load(src_ap, a, b, pcb):
        i = ring_i[0] % NRING
        ring_i[0] += 1
        view = ring[i][:, 0:a * b].rearrange("p (a b) -> p a b", a=a)
        dma("sp", view, src_ap, ringB[i], reads=pcb, writes=[ringB[i]])
        return ringB[i], view

    def wpiece(wb, pcb, n0, n1):
        return ring_load(wb.rearrange("(k p) n -> p k n", p=128)[:, :, n0:n1], 8, n1 - n0, pcb)

    def fpiece(wb, pcb, n0, n1):
        view = fring[:, 0:8 * (n1 - n0)].rearrange("p (a b) -> p a b", a=8)
        dma("sp", view, wb.rearrange("(k p) n -> p k n", p=128)[:, :, n0:n1], fringB, reads=pcb, writes=[fringB])
        return fringB, view

    def disc(A, Bm, L, tmp, rdB, wB):
        T1, T2, T3, T4, T5, T6, T7 = tmp
        act(L, L, AF.Exp, rdB, wB)
        tt("dve", T1, A, L, ALU.mult, wB, wB)
        tt("dve", T5, Bm, L, ALU.mult, wB, wB)
        act(T6, T1, AF.Exp, wB, wB)
        range_reduce(T2, T5, 0.0, wB, wB, T7, wB)
        act(T2, T2, AF.Sin, wB, wB)
        range_reduce(T3, T5, PI / 2, wB, wB, T7, wB)
        act(T3, T3, AF.Sin, wB, wB)
        tt("dve", T3, T6, T3, ALU.mult, wB, wB)
        tt("dve", T2, T6, T2, ALU.mult, wB, wB)
        ts("dve", T3, T3, -1.0, None, ALU.add, None, wB, wB)
        tt("dve", T6, A, A, ALU.mult, wB, wB)
        tt("dve", T7, Bm, Bm, ALU.mult, wB, wB)
        tt("dve", T6, T6, T7, ALU.add, wB, wB)
        recip(T6, T6, wB, wB)
        tt("dve", L, T3, A, ALU.mult, wB, wB)
        tt("dve", T4, T2, Bm, ALU.mult, wB, wB)
        tt("dve", L, L, T4, ALU.add, wB, wB)
        tt("dve", L, L, T6, ALU.mult, wB, wB)
        tt("dve", T4, T2, A, ALU.mult, wB, wB)
        tt("dve", T3, T3, Bm, ALU.mult, wB, wB)
        tt("dve", T4, T4, T3, ALU.subtract, wB, wB)
        tt("dve", T4, T4, T6, ALU.mult, wB, wB)
        return dict(x1=T1, ang=T5, c_re=L, c_im=T4)

    if RUN_PREP:
        mB_ = [Buf("modeprep")]
        mA, mBm, mL = sb("mA", [128, 16]), sb("mBm", [128, 16]), sb("mL", [128, 16])
        mT = [sb(f"mT{i}", [128, 16]) for i in range(7)]
        cp("dve", mA[:], pkc("a_re", 0, 16), [pkB], mB_)
        cp("dve", mBm[:], pkc("a_im", 0, 16), [pkB], mB_)
        cp("dve", mL[:], pkc("logdt", 0, 16), [pkB], mB_)
        dm = disc(mA[:], mBm[:], mL[:], [t[:] for t in mT], mB_, mB_)
        m128a, m128m, mr = sb("m128a", [128, 16]), sb("m128m", [128, 16]), sb("mr", [128, 16])
        ts("dve", m128a[:], dm["ang"], 128.0, None, ALU.mult, None, mB_, mB_)
        act(m128m[:], dm["x1"], AF.Exp, mB_, mB_, scale=128.0)
        range_reduce(mT[1][:], m128a[:], 0.0, mB_, mB_, mT[6][:], mB_)
        act(mT[1][:], mT[1][:], AF.Sin, mB_, mB_)
        range_reduce(mT[2][:], m128a[:], PI / 2, mB_, mB_, mT[6][:], mB_)
        act(mT[2][:], mT[2][:], AF.Sin, mB_, mB_)
        tt("dve", a128r[:], m128m[:], mT[2][:], ALU.mult, mB_, [a128B])
        tt("dve", a128i[:], m128m[:], mT[1][:], ALU.mult, mB_ + [a128B], [a128B])
        act(mr[:], dm["x1"], AF.Exp, mB_, mB_)
        for p_ in range(16):
            tg, tg2 = gst.get(), gst.get()
            tg = (tg[0], tg[1][:, 0:TS])
            tg2 = (tg2[0], tg2[1][:, 0:TS])
            ts("dve", tg[1][:], pkc("tidx", 0, TS), dm["ang"][:, p_:p_ + 1], None, ALU.mult, None, [pkB] + mB_, [tg[0]])
            range_reduce(sintab[:, p_, :], tg[1][:], 0.0, [tg[0]], [tabB], tg2[1][:], [tg2[0]])
            act(sintab[:, p_, :], sintab[:, p_, :], AF.Sin, [tabB], [tabB])
            range_reduce(costab[:, p_, :], tg[1][:], PI / 2, [tg[0]], [tabB], tg2[1][:], [tg2[0]])
            act(costab[:, p_, :], costab[:, p_, :], AF.Sin, [tabB], [tabB])
            ts("dve", rtab[:, p_, :], pkc("tidx", 0, TS), 0.0, mr[:, p_:p_ + 1], ALU.mult, ALU.add, [pkB] + mB_, [tabB])
            memset("dve", rtab[:, p_, 0:1], 0.0, [tabB])
        bxrB, bxr_t = big(0)
        bxiB, bxi_t = big(1)
        dma("sp", bxr_t, Bx_d[0], bxrB[0], writes=bxrB)
        dma("sp", bxi_t, Bx_d[1], bxiB[0], writes=bxiB)
        for p_ in range(16):
            sl = slice(p_ * 32, p_ * 32 + 32)
            cr, ci = dm["c_re"][:, p_:p_ + 1], dm["c_im"][:, p_:p_ + 1]
            ta, tb_ = gst.get(), gst.get()
            ts("dve", ta[1][:, 0:32], bxi_t[:, sl], ci, None, ALU.mult, None, bxiB + mB_, [ta[0]])
            stt("dve", Bxr[:, sl], bxr_t[:, sl], cr, ta[1][:, 0:32], ALU.mult, ALU.subtract, bxrB + mB_ + [ta[0]], [BxB])
            ts("dve", tb_[1][:, 0:32], bxr_t[:, sl], ci, None, ALU.mult, None, bxrB + mB_, [tb_[0]])
            stt("dve", Bxi[:, sl], bxi_t[:, sl], cr, tb_[1][:, 0:32], ALU.mult, ALU.add, bxiB + mB_ + [tb_[0]], [BxB])

        rt = [R0[:, c, :] for c in range(8)] + [R1[:, c, :] for c in range(8)]
        rtB = R0B + R1B
        tmp_tiles = []
        tmpB = []
        for gi_ in range(4):
            gb_, gt_ = gst.items[gi_]
            tmpB.append(gb_)
            tmp_tiles += [gt_[:, 0:256], gt_[:, 256:512]]
        for blk in range(8):
            cs = slice(blk * 256, blk * 256 + 256)
            base_ = (blk % 2) * 7
            inT = rt[base_:base_ + 7]
            inB = rtB[base_:base_ + 7]
            rA, rBm, rL, btr, bti, ctr, cti = inT
            srcs_ = [rows_d[0:1, cs].partition_broadcast(128), rows_d[1:2, cs].partition_broadcast(128),
                     rows_d[2:3, cs].partition_broadcast(128), BT_d[0][:, cs], BT_d[1][:, cs], CT_d[0][:, cs], CT_d[1][:, cs]]
            for k_ in range(7):
                dma("sp", inT[k_], srcs_[k_], inB[k_], writes=[inB[k_]])
            rowB = inB + tmpB
            tmp = tmp_tiles[0:7]
            dr = disc(rA, rBm, rL, tmp, rowB, rowB)
            sA, sB_ = tmp[1], tmp[2]
            s3, s4 = tmp[5], tmp[6]
            osl = lambda t_: t_[:, blk * 2:blk * 2 + 2, :].rearrange("p a b -> p (a b)")
            tt("dve", sA, dr["c_re"], btr, ALU.mult, rowB, rowB)
            tt("dve", sB_, dr["c_im"], bti, ALU.mult, rowB, rowB)
            tt("dve", osl(BreT), sA, sB_, ALU.subtract, rowB, [lhsB])
            tt("dve", sA, dr["c_re"], bti, ALU.mult, rowB, rowB)
            tt("dve", sB_, dr["c_im"], btr, ALU.mult, rowB, rowB)
            tt("dve", osl(BimT), sA, sB_, ALU.add, rowB + [lhsB], [lhsB])
            e_ap = pkc("eidx")
            act(sA, dr["x1"], AF.Exp, rowB + [pkB], rowB, scale=e_ap)
            ts("dve", sB_, dr["ang"], e_ap, None, ALU.mult, None, rowB + [pkB], rowB)
            range_reduce(s3, sB_, 0.0, rowB, rowB, s4, rowB)
            act(s3, s3, AF.Sin, rowB, rowB)
            tt("dve", osl(Pim), sA, s3, ALU.mult, rowB + [lhsB], [lhsB])
            range_reduce(s3, sB_, PI / 2, rowB, rowB, s4, rowB)
            act(s3, s3, AF.Sin, rowB, rowB)
            tt("dve", osl(Pre), sA, s3, ALU.mult, rowB + [lhsB], [lhsB])
            cp("dve", osl(CreT), ctr, rowB + [lhsB], [lhsB])
            ts("dve", osl(CimT), cti, -1.0, None, ALU.mult, None, rowB + [lhsB], [lhsB])

    def layer_norm(nch, srcs, srcB, ones_ap, Tn, g_name, b_name, emit_out):
        pm, pe2 = lnR.get(), lnR.get()
        for c in range(nch):
            s1, s2 = s16.get(), s16.get()
            act(s1[1][:, :Tn], srcs[c], AF.Copy, [srcB[c]], [s1[0]])
            act(s2[1][:, :Tn], srcs[c], AF.Square, [srcB[c]], [s2[0]])
            mm1(pm[1][:, :Tn], ones_ap, s1[1][:, :Tn], c == 0, c == nch - 1, [s1[0], onesB], [pm[0]])
            mm1(pe2[1][:, :Tn], ones_ap, s2[1][:, :Tn], c == 0, c == nch - 1, [s2[0], onesB], [pe2[0]])
        mean, var, nmr = statsb.get(), statsb.get(), statsb.get()
        cp("act", mean[1][:, :Tn], pm[1][:, :Tn], [pm[0]], [mean[0]])
        tt("dve", var[1][:, :Tn], mean[1][:, :Tn], mean[1][:, :Tn], ALU.mult, [mean[0]], [var[0]])
        tt("dve", var[1][:, :Tn], pe2[1][:, :Tn], var[1][:, :Tn], ALU.subtract, [pe2[0], var[0]], [var[0]])
        ts("dve", var[1][:, :Tn], var[1][:, :Tn], LN_EPS, None, ALU.add, None, [var[0]], [var[0]])
        act(var[1][:, :Tn], var[1][:, :Tn], AF.Sqrt, [var[0]], [var[0]])
        recip(var[1][:, :Tn], var[1][:, :Tn], [var[0]], [var[0]])
        stt("dve", nmr[1][:, :Tn], mean[1][:, :Tn], -1.0, var[1][:, :Tn], ALU.mult, ALU.mult, [mean[0], var[0]], [nmr[0]])
        for c in range(nch):
            t_ = f32t.get()
            tt("dve", t_[1][:, :Tn], srcs[c], var[1][:, :Tn], ALU.mult, [srcB[c], var[0]], [t_[0]])
            tt("dve", t_[1][:, :Tn], t_[1][:, :Tn], nmr[1][:, :Tn], ALU.add, [t_[0], nmr[0]], [t_[0]])
            emit_out(c, t_[1][:, :Tn], t_[0], pkc(g_name, c), pkc(b_name, c))

    def ln_to_resid(Tn, g_name, b_name):
        def emit_out(c, t_ap, tB, g_ap, b_ap):
            act(R1[:, c, :Tn], t_ap, AF.Identity, [tB, pkB], [R1B[c]], scale=g_ap, bias=b_ap)
            act(xnb[:, c, :Tn], t_ap, AF.Identity, [tB, pkB], [xnbB[c]], scale=g_ap, bias=b_ap)
        layer_norm(8, [R0[:, c, :Tn] for c in range(8)], R0B, ones_m[:], Tn, g_name, b_name, emit_out)

    def ssm_segment(pi, col0, L, Hr_ap, Hi_ap, HBuf, ypsum, first, last):
        c = pi // 4
        bu = psum.get()
        bre, bim = bu[1][:, 0:L], bu[1][:, 256:256 + L]
        mm1(bre, BreT[:, pi, :], ub[:, c, col0:col0 + L], True, True, [lhsB, ubB[c]], [bu[0]])
        mm1(bim, BimT[:, pi, :], ub[:, c, col0:col0 + L], True, True, [lhsB, ubB[c]], [bu[0]])
        cosA, sinA, rA = costab[:, pi, 0:L], sintab[:, pi, 0:L], rtab[:, pi, 0:L]
        cos1, sin1 = costab[:, pi, 1:2], sintab[:, pi, 1:2]
        g0, tq = tiny.get(), tiny.get()
        tt("dve", tq[1][:, 0:1], Hi_ap, sin1, ALU.mult, [HBuf, tabB], [tq[0]])
        tt("dve", tq[1][:, 1:2], Hr_ap, cos1, ALU.mult, [HBuf, tabB, tq[0]], [tq[0]])
        tt("dve", tq[1][:, 2:3], Hr_ap, sin1, ALU.mult, [HBuf, tabB, tq[0]], [tq[0]])
        tt("dve", tq[1][:, 3:4], Hi_ap, cos1, ALU.mult, [HBuf, tabB, tq[0]], [tq[0]])
        tt("dve", g0[1][:, 0:1], tq[1][:, 1:2], tq[1][:, 0:1], ALU.subtract, [tq[0]], [g0[0]])
        tt("dve", g0[1][:, 1:2], tq[1][:, 3:4], tq[1][:, 2:3], ALU.add, [tq[0], g0[0]], [g0[0]])
        t1, t2, t3, t4 = sst.get(), sst.get(), sst.get(), sst.get()
        tt("dve", t1[1][:, :L], bre, cosA, ALU.mult, [bu[0], tabB], [t1[0]])
        tt("dve", t2[1][:, :L], bim, sinA, ALU.mult, [bu[0], tabB], [t2[0]])
        tt("dve", t3[1][:, :L], bim, cosA, ALU.mult, [bu[0], tabB], [t3[0]])
        tt("dve", t4[1][:, :L], bre, sinA, ALU.mult, [bu[0], tabB], [t4[0]])
        tt("dve", t1[1][:, :L], t1[1][:, :L], t2[1][:, :L], ALU.add, [t1[0], t2[0]], [t1[0]])
        tt("dve", t3[1][:, :L], t3[1][:, :L], t4[1][:, :L], ALU.subtract, [t3[0], t4[0]], [t3[0]])
        r1 = rtab[:, pi, 1:2]
        tt("dve", g0[1][:, 2:3], g0[1][:, 0:1], r1, ALU.mult, [g0[0], tabB], [g0[0]])
        tt("dve", g0[1][:, 3:4], g0[1][:, 1:2], r1, ALU.mult, [g0[0], tabB], [g0[0]])
        tt("dve", t1[1][:, 0:1], t1[1][:, 0:1], g0[1][:, 2:3], ALU.add, [t1[0], g0[0]], [t1[0]])
        tt("dve", t3[1][:, 0:1], t3[1][:, 0:1], g0[1][:, 3:4], ALU.add, [t3[0], g0[0]], [t3[0]])
        gre, gim = sst.get(), sst.get()
        P.op("dve", lambda e: e.tensor_tensor_scan(out=gre[1][:, :L], data0=rA, data1=t1[1][:, :L], initial=0.0, op0=ALU.mult, op1=ALU.add),
             [tabB, t1[0]], [gre[0]])
        P.op("dve", lambda e: e.tensor_tensor_scan(out=gim[1][:, :L], data0=rA, data1=t3[1][:, :L], initial=0.0, op0=ALU.mult, op1=ALU.add),
             [tabB, t3[0]], [gim[0]])
        p1, p2, p3, p4 = sst.get(), sst.get(), sst.get(), sst.get()
        tt("dve", p1[1][:, :L], gre[1][:, :L], cosA, ALU.mult, [gre[0], tabB], [p1[0]])
        tt("dve", p2[1][:, :L], gim[1][:, :L], sinA, ALU.mult, [gim[0], tabB], [p2[0]])
        tt("dve", p3[1][:, :L], gim[1][:, :L], cosA, ALU.mult, [gim[0], tabB], [p3[0]])
        tt("dve", p4[1][:, :L], gre[1][:, :L], sinA, ALU.mult, [gre[0], tabB], [p4[0]])
        hr, hi = hbf.get(), hbf.get()
        tt("dve", hr[1][:, :L], p1[1][:, :L], p2[1][:, :L], ALU.subtract, [p1[0], p2[0]], [hr[0]])
        tt("dve", hi[1][:, :L], p3[1][:, :L], p4[1][:, :L], ALU.add, [p3[0], p4[0]], [hi[0]])
        tt("dve", Hr_ap, p1[1][:, L - 1:L], p2[1][:, L - 1:L], ALU.subtract, [p1[0], p2[0]], [HBuf])
        tt("dve", Hi_ap, p3[1][:, L - 1:L], p4[1][:, L - 1:L], ALU.add, [p3[0], p4[0], HBuf], [HBuf])
        yo = ypsum[1][:, col0:col0 + L]
        mm1(yo, CreT[:, pi, :], hr[1][:, :L], first, False, [lhsB, hr[0]], [ypsum[0]])
        mm1(yo, CimT[:, pi, :], hi[1][:, :L], False, last, [lhsB, hi[0]], [ypsum[0]])

    def conv_chunk(c, col0, L, eng):
        o = cacc[:, c, col0:col0 + L]
        base = PK["conv_w"][0] + c * 31
        ts(eng, o, vbuf[:, c, 0:L], pk[:, base:base + 1], pkc("conv_b", c), ALU.mult, ALU.add, [vbufB[c], pkB], [caccB[c]])
        for k in range(1, 31):
            stt(eng, o, vbuf[:, c, k:k + L], pk[:, base + k:base + k + 1], o, ALU.mult, ALU.add, [vbufB[c], pkB, caccB[c]], [caccB[c]])

    def v3(t_):
        return t_.rearrange("p (a b) -> p a b", a=4)

    def ssm_group(c, col0, yp, filler):
        ps4 = slice(4 * c, 4 * c + 4)
        HBs = HB[4 * c:4 * c + 4]
        bR, bI = buR.get(), buR.get()

        def fn(e):
            ins = None
            for q in range(4):
                e.matmul(bR[1][:, q * 128:(q + 1) * 128], BreT[:, 4 * c + q, :], ub[:, c, col0:col0 + 128], start=True, stop=True)
                ins = e.matmul(bI[1][:, q * 128:(q + 1) * 128], BimT[:, 4 * c + q, :], ub[:, c, col0:col0 + 128], start=True, stop=True)
            return ins
        P.op("pe", fn, [lhsB, ubB[c]], [bR[0], bI[0]])
        cosG, sinG = costab[:, ps4, :], sintab[:, ps4, :]
        rG = rtab[:, ps4, :].rearrange("p a b -> p (a b)")
        cos1, sin1, r1 = costab[:, ps4, 1], sintab[:, ps4, 1], rtab[:, ps4, 1]
        Hr, Hi = Hre[:, ps4], Him[:, ps4]
        tq, g0 = tiny.get(), tiny.get()
        tt("dve", tq[1][:, 0:4], Hi, sin1, ALU.mult, HBs + [tabB], [tq[0]])
        tt("dve", tq[1][:, 4:8], Hr, cos1, ALU.mult, HBs + [tabB, tq[0]], [tq[0]])
        tt("dve", tq[1][:, 8:12], Hr, sin1, ALU.mult, HBs + [tabB, tq[0]], [tq[0]])
        tt("dve", tq[1][:, 12:16], Hi, cos1, ALU.mult, HBs + [tabB, tq[0]], [tq[0]])
        tt("dve", g0[1][:, 0:4], tq[1][:, 4:8], tq[1][:, 0:4], ALU.subtract, [tq[0]], [g0[0]])
        tt("dve", g0[1][:, 4:8], tq[1][:, 12:16], tq[1][:, 8:12], ALU.add, [tq[0], g0[0]], [g0[0]])
        tt("dve", g0[1][:, 8:12], g0[1][:, 0:4], r1, ALU.mult, [g0[0], tabB], [g0[0]])
        tt("dve", g0[1][:, 12:16], g0[1][:, 4:8], r1, ALU.mult, [g0[0], tabB], [g0[0]])
        filler(3)
        yield
        A, B_, C, D_ = gst.get(), gst.get(), gst.get(), gst.get()
        tt("dve", v3(A[1][:]), v3(bR[1][:]), cosG, ALU.mult, [bR[0], tabB], [A[0]])
        tt("dve", v3(B_[1][:]), v3(bI[1][:]), sinG, ALU.mult, [bI[0], tabB], [B_[0]])
        filler(2)
        tt("dve", v3(C[1][:]), v3(bI[1][:]), cosG, ALU.mult, [bI[0], tabB], [C[0]])
        tt("dve", v3(D_[1][:]), v3(bR[1][:]), sinG, ALU.mult, [bR[0], tabB], [D_[0]])
        filler(2)
        yield
        tt(SSM_POST, A[1][:], A[1][:], B_[1][:], ALU.add, [A[0], B_[0]], [A[0]])
        tt(SSM_POST, v3(A[1][:])[:, :, 0], v3(A[1][:])[:, :, 0], g0[1][:, 8:12], ALU.add, [A[0], g0[0]], [A[0]])
        tt(SSM_POST, C[1][:], C[1][:], D_[1][:], ALU.subtract, [C[0], D_[0]], [C[0]])
        tt(SSM_POST, v3(C[1][:])[:, :, 0], v3(C[1][:])[:, :, 0], g0[1][:, 12:16], ALU.add, [C[0], g0[0]], [C[0]])
        filler(4)
        yield
        GR, GI = gst.get(), gst.get()
        P.op("dve", lambda e: e.tensor_tensor_scan(out=GR[1][:], data0=rG, data1=A[1][:], initial=0.0, op0=ALU.mult, op1=ALU.add),
             [tabB, A[0]], [GR[0]])
        filler(2)
        P.op("dve", lambda e: e.tensor_tensor_scan(out=GI[1][:], data0=rG, data1=C[1][:], initial=0.0, op0=ALU.mult, op1=ALU.add),
             [tabB, C[0]], [GI[0]])
        filler(2)
        yield
        tt(SSM_POST, v3(B_[1][:]), v3(GR[1][:]), cosG, ALU.mult, [GR[0], tabB], [B_[0]])
        tt(SSM_POST, v3(D_[1][:]), v3(GI[1][:]), sinG, ALU.mult, [GI[0], tabB], [D_[0]])
        tt(SSM_POST, v3(A[1][:]), v3(GI[1][:]), cosG, ALU.mult, [GI[0], tabB], [A[0]])
        tt(SSM_POST, v3(C[1][:]), v3(GR[1][:]), sinG, ALU.mult, [GR[0], tabB], [C[0]])
        filler(4)
        yield
        hr, hi = ghb.get(), ghb.get()
        tt("dve", hr[1][:], B_[1][:], D_[1][:], ALU.subtract, [B_[0], D_[0]], [hr[0]])
        filler(1)
        tt("dve", hi[1][:], A[1][:], C[1][:], ALU.add, [A[0], C[0]], [hi[0]])
        filler(1)
        tt("dve", Hr, v3(B_[1][:])[:, :, 127], v3(D_[1][:])[:, :, 127], ALU.subtract, [B_[0], D_[0]], HBs)
        tt("dve", Hi, v3(A[1][:])[:, :, 127], v3(C[1][:])[:, :, 127], ALU.add, [A[0], C[0]] + HBs, HBs)
        yo = yp[1][:, col0:col0 + 128]

        def fn2(e):
            ins = None
            for q in range(4):
                e.matmul(yo, CreT[:, 4 * c + q, :], hr[1][:, q * 128:(q + 1) * 128], start=(q == 0), stop=False)
                ins = e.matmul(yo, CimT[:, 4 * c + q, :], hi[1][:, q * 128:(q + 1) * 128], start=False, stop=(q == 3))
            return ins
        P.op("pe", fn2, [lhsB, hr[0], hi[0]], [yp[0]])
        yield

    def front_p(xb_t, xbB_t, tail_fn):
        Tn = T
        s0B, w0 = fpiece(w_in_b, pc["w_in"], 0, 512)
        for c in range(4):
            ps = psum.get()
            mm(ps[1][:, :Tn], [(w0[:, k, c * 128:(c + 1) * 128], xb_t[:, k, :Tn]) for k in range(8)], [s0B, xbB_t], [ps[0]])
            cp("act", u32[:, c, :Tn], ps[1][:, :Tn], [ps[0]], [u32B[c]])
            cp("act", ub[:, c, :Tn], u32[:, c, :Tn], [u32B[c]], [ubB[c]])
            yield
        for half in range(2):
            sB_, wv = fpiece(w_in_b, pc["w_in"], 512 + half * 512, 1024 + half * 512)
            for ci in range(2):
                c = half * 2 + ci
                pa, pg = psum.get(), psum.get()
                mm(pa[1][:, :Tn], [(wv[:, k, ci * 256:ci * 256 + 128], xb_t[:, k, :Tn]) for k in range(8)], [sB_, xbB_t], [pa[0]])
                mm(pg[1][:, :Tn], [(wv[:, k, ci * 256 + 128:ci * 256 + 256], xb_t[:, k, :Tn]) for k in range(8)], [sB_, xbB_t], [pg[0]])
                sg = f32t.get()
                act(sg[1][:, :Tn], pg[1][:, :Tn], AF.Sigmoid, [pg[0]], [sg[0]])
                tt("dve", vbuf[:, c, 30:30 + Tn], pa[1][:, :Tn], sg[1][:, :Tn], ALU.mult, [pa[0], sg[0]], [vbufB[c]])
                yield
        taps = []
        for k in range(31):
            for c in range(4):
                taps.append((c, k))
        tap_i = [0]

        def filler(n):
            for _ in range(n):
                if tap_i[0] >= len(taps):
                    return
                c, k = taps[tap_i[0]]
                tap_i[0] += 1
                o = cacc[:, c, 0:Tn]
                base = PK["conv_w"][0] + c * 31
                ceng = "pool" if c >= 4 - CONV_POOL_CH else "dve"
                if k == 0:
                    ts(ceng, o, vbuf[:, c, 0:Tn], pk[:, base:base + 1], pkc("conv_b", c), ALU.mult, ALU.add, [vbufB[c], pkB], [caccB[c]])
                elif ceng == "dve":
                    stt("dve", o, vbuf[:, c, k:k + Tn], pk[:, base + k:base + k + 1], o, ALU.mult, ALU.add, [vbufB[c], pkB, caccB[c]], [caccB[c]])
                else:
                    tmp_ = ctmp.get()
                    ts("pool", tmp_[1][:], vbuf[:, c, k:k + Tn], pk[:, base + k:base + k + 1], None, ALU.mult, None, [vbufB[c], pkB], [tmp_[0]])
                    tt("pool", o, o, tmp_[1][:], ALU.add, [caccB[c], tmp_[0]], [caccB[c]])

        for c in range(4):
            yp = ypsR.get()
            for sg_ in range(T // TS):
                if GRP:
                    for _ in ssm_group(c, sg_ * TS, yp, filler):
                        yield
                else:
                    for q in range(4):
                        pi = c * 4 + q
                        ssm_segment(pi, sg_ * TS, TS, Hre[:, pi:pi + 1], Him[:, pi:pi + 1], HB[pi], yp, q == 0, q == 3)
                        filler(8)
                        yield
            zp = f32t.get()
            stt("dve", zp[1][:, :Tn], u32[:, c, :Tn], pkc("ssm_d", c), yp[1][:, :Tn], ALU.mult, ALU.add, [u32B[c], pkB, yp[0]], [zp[0]])
            act(z32[:, c, :Tn], zp[1][:, :Tn], AF.Gelu_apprx_tanh, [zp[0]], [z32B[c]])
            act(zb[:, c, :Tn], zp[1][:, :Tn], AF.Gelu_apprx_tanh, [zp[0]], [zbB[c]])
            yield
        while tap_i[0] < len(taps):
            filler(4)
            yield
        for co in range(4):
            ps = psum.get()
            mm(ps[1][:, :Tn], [(glu_sb[:, k, co * 128:(co + 1) * 128], zb[:, k, :Tn]) for k in range(4)], [gluB] + zbB, [ps[0]])
            sg = f32t.get()
            act(sg[1][:, :Tn], ps[1][:, :Tn], AF.Sigmoid, [ps[0], pkB], [sg[0]], bias=pkc("glu_b", co))
            tt("dve", mixin[:, co, :Tn], z32[:, co, :Tn], sg[1][:, :Tn], ALU.mult, [z32B[co], sg[0]], [mixB[co]])
            yield
        for c in range(4):
            if tail_fn is not None:
                tail_fn(c, Tn)
            cp("pool", vbuf[:, c, 0:30], vbuf[:, c, Tn:Tn + 30], [vbufB[c]], [vbufB[c]])

        def silu_out(c, t_ap, tB, g_ap, b_ap):
            act(mixin[:, 4 + c, :Tn], t_ap, AF.Silu, [tB, pkB], [mixB[4 + c]], scale=g_ap, bias=b_ap)
        layer_norm(4, [cacc[:, c, :Tn] for c in range(4)], caccB, ones_c[:], Tn, "cln_g", "cln_b", silu_out)
        yield

    def drain(g):
        for _ in g:
            pass

    def merge(ga, gb):
        da = db = False
        while not (da and db):
            if not da:
                try:
                    next(ga)
                except StopIteration:
                    da = True
            if not db:
                try:
                    next(gb)
                except StopIteration:
                    db = True

    def run_tile(Tn, xb_t, xbB_t, x32_src, segs, conv_segs, att_segs, y_dst, is_sample=False, part="all"):
        if part in ("all", "front"):
            s0B, w0 = fpiece(w_in_b, pc["w_in"], 0, 512)
            for c in range(4):
                ps = psum.get()
                mm(ps[1][:, :Tn], [(w0[:, k, c * 128:(c + 1) * 128], xb_t[:, k, :Tn]) for k in range(8)], [s0B, xbB_t], [ps[0]])
                cp("act", u32[:, c, :Tn], ps[1][:, :Tn], [ps[0]], [u32B[c]])
                cp("dve", ub[:, c, :Tn], u32[:, c, :Tn], [u32B[c]], [ubB[c]])
            for c in range(4):
                yp = ypsR.get()
                for (col0, L, Hr_fn, Hi_fn, HB_fn) in segs:
                    for q in range(4):
                        pi = c * 4 + q
                        ssm_segment(pi, col0, L, Hr_fn(pi), Hi_fn(pi), HB_fn(pi), yp, q == 0, q == 3)
                    yield
                zp = f32t.get()
                stt("dve", zp[1][:, :Tn], u32[:, c, :Tn], pkc("ssm_d", c), yp[1][:, :Tn], ALU.mult, ALU.add, [u32B[c], pkB, yp[0]], [zp[0]])
                act(z32[:, c, :Tn], zp[1][:, :Tn], AF.Gelu_apprx_tanh, [zp[0]], [z32B[c]])
                act(zb[:, c, :Tn], zp[1][:, :Tn], AF.Gelu_apprx_tanh, [zp[0]], [zbB[c]])
            for co in range(4):
                ps = psum.get()
                mm(ps[1][:, :Tn], [(glu_sb[:, k, co * 128:(co + 1) * 128], zb[:, k, :Tn]) for k in range(4)], [gluB] + zbB, [ps[0]])
                sg = f32t.get()
                act(sg[1][:, :Tn], ps[1][:, :Tn], AF.Sigmoid, [ps[0], pkB], [sg[0]], bias=pkc("glu_b", co))
                tt("dve", mixin[:, co, :Tn], z32[:, co, :Tn], sg[1][:, :Tn], ALU.mult, [z32B[co], sg[0]], [mixB[co]])
            for half in range(2):
                sB_, wv = fpiece(w_in_b, pc["w_in"], 512 + half * 512, 1024 + half * 512)
                for ci in range(2):
                    c = half * 2 + ci
                    pa, pg = psum.get(), psum.get()
                    mm(pa[1][:, :Tn], [(wv[:, k, ci * 256:ci * 256 + 128], xb_t[:, k, :Tn]) for k in range(8)], [sB_, xbB_t], [pa[0]])
                    mm(pg[1][:, :Tn], [(wv[:, k, ci * 256 + 128:ci * 256 + 256], xb_t[:, k, :Tn]) for k in range(8)], [sB_, xbB_t], [pg[0]])
                    sg = f32t.get()
                    act(sg[1][:, :Tn], pg[1][:, :Tn], AF.Sigmoid, [pg[0]], [sg[0]])
                    eng = "dve"
                    if is_sample:
                        vt = f32t.get()
                        tt("dve", vt[1][:, :Tn], pa[1][:, :Tn], sg[1][:, :Tn], ALU.mult, [pa[0], sg[0]], [vt[0]])
                        for (col0, L, halo_fn, tail_fn) in conv_segs:
                            halo_fn(c)
                            cp("pool", vbuf[:, c, 30:30 + L], vt[1][:, col0:col0 + L], [vt[0]], [vbufB[c]])
                            conv_chunk(c, col0, L, eng)
                            tail_fn(c, L)
                    else:
                        (col0, L, halo_fn, tail_fn) = conv_segs[0]
                        tt("dve", vbuf[:, c, 30:30 + Tn], pa[1][:, :Tn], sg[1][:, :Tn], ALU.mult, [pa[0], sg[0]], [vbufB[c]])
                        conv_chunk(c, 0, Tn, eng)
                        if tail_fn is not None:
                            tail_fn(c, Tn)
                        cp("pool", vbuf[:, c, 0:30], vbuf[:, c, Tn:Tn + 30], [vbufB[c]], [vbufB[c]])

            def silu_out(c, t_ap, tB, g_ap, b_ap):
                act(mixin[:, 4 + c, :Tn], t_ap, AF.Silu, [tB, pkB], [mixB[4 + c]], scale=g_ap, bias=b_ap)
            layer_norm(4, [cacc[:, c, :Tn] for c in range(4)], caccB, ones_c[:], Tn, "cln_g", "cln_b", silu_out)
        if part in ("all", "back"):
            dma("sp", R0[:, :, :Tn], x32_src, R0B[0], writes=R0B)
            for half in range(2):
                sB_, wv = wpiece(w_out_b, pc["w_out"], half * 512, half * 512 + 512)
                for ci in range(4):
                    co = half * 4 + ci
                    yield
                    ps = psum.get()
                    mm(ps[1][:, :Tn], [(wv[:, k, ci * 128:(ci + 1) * 128], mixin[:, k, :Tn]) for k in range(8)], [sB_] + mixB, [ps[0]])
                    stt("dve", R0[:, co, :Tn], R0[:, co, :Tn], ALPHA, ps[1][:, :Tn], ALU.mult, ALU.add, [R0B[co], ps[0]], [R0B[co]])
            ln_to_resid(Tn, "ln1_g", "ln1_b")
            qps = []
            for half in range(2):
                sB_, wv = wpiece(w_q_b, pc["w_q"], half * 512, half * 512 + 512)
                for ci in range(4):
                    yield
                    ps = psum.get()
                    mm(ps[1][:, :Tn], [(wv[:, k, ci * 128:(ci + 1) * 128], xnb[:, k, :Tn]) for k in range(8)], [sB_] + xnbB, [ps[0]])
                    act(qb[:, half * 4 + ci, :Tn], ps[1][:, :Tn], AF.Identity, [ps[0]], [qbB[half * 4 + ci]], scale=1.0 / 16.0)
            for (col0, L, kv_loader) in att_segs:
                if kv_loader is not None:
                    kv_loader()
                for h in range(4):
                    pts = []
                    for mc in range(2):
                        yield
                        ps = psum.get()
                        mm(ps[1][:, :L], [(kT[:, h * 2 + dc, mc * 128:(mc + 1) * 128], qb[:, h * 2 + dc, col0:col0 + L]) for dc in range(2)],
                           [kTB, qbB[h * 2], qbB[h * 2 + 1]], [ps[0]])
                        pt = pTr.get()
                        act(pt[1][:, :L], ps[1][:, :L], AF.Exp, [ps[0]], [pt[0]])
                        pts.append(pt)
                    yield
                    ps = psum.get()
                    mm(ps[1][:, :L], [(ones_1[:], pts[mc][1][:, :L]) for mc in range(2)], [onesB, pts[0][0], pts[1][0]], [ps[0]])
                    rinv = f32t.get()
                    recip(rinv[1][:, :L], ps[1][:, :L], [ps[0]], [rinv[0]])
                    for dc in range(2):
                        po = psum.get()
                        mm(po[1][:, :L], [(vv[:, mc, h * 256 + dc * 128: h * 256 + dc * 128 + 128], pts[mc][1][:, :L]) for mc in range(2)],
                           [vvB, pts[0][0], pts[1][0]], [po[0]])
                        tt("dve", ob[:, h * 2 + dc, col0:col0 + L], po[1][:, :L], rinv[1][:, :L], ALU.mult, [po[0], rinv[0]], [obB[h * 2 + dc]])
            for half in range(2):
                sB_, wv = wpiece(w_o_b, pc["w_o"], half * 512, half * 512 + 512)
                for ci in range(4):
                    co = half * 4 + ci
                    yield
                    ps = psum.get()
                    mm(ps[1][:, :Tn], [(wv[:, k, ci * 128:(ci + 1) * 128], ob[:, k, :Tn]) for k in range(8)], [sB_] + obB, [ps[0]])
                    stt("dve", R0[:, co, :Tn], R1[:, co, :Tn], ALPHA, ps[1][:, :Tn], ALU.mult, ALU.add, [R1B[co], ps[0]], [R0B[co]])
            ln_to_resid(Tn, "ln2_g", "ln2_b")
            for piece in range(8):
                sB_, wv = wpiece(w1_b, pc["w1"], piece * 512, piece * 512 + 512)
                for hc in range(4):
                    hidx = piece * 4 + hc
                    yield
                    ps = psum.get()
                    mm(ps[1][:, :Tn], [(wv[:, k, hc * 128:(hc + 1) * 128], xnb[:, k, :Tn]) for k in range(8)], [sB_] + xnbB, [ps[0]])
                    rl = f32t.get()
                    act(rl[1][:, :Tn], ps[1][:, :Tn], AF.Relu, [ps[0], pkB], [rl[0]], bias=pkc("b1", hidx))
                    if SQ_ENG == "act":
                        act(hid[:, hidx, :Tn], rl[1][:, :Tn], AF.Square, [rl[0]], [hidB[hidx]])
                    else:
                        tt("dve", hid[:, hidx, :Tn], rl[1][:, :Tn], rl[1][:, :Tn], ALU.mult, [rl[0]], [hidB[hidx]])
            w2v = w2_b.rearrange("(k p) n -> p k n", p=128)
            for cp_ in range(4):
                yield
                pss = [psum.get(), psum.get()]
                for kh in range(2):
                    sB_, wv = ring_load(w2v[:, kh * 16:kh * 16 + 16, cp_ * 256:cp_ * 256 + 256], 16, 256, pc["w2"])
                    for oc in range(2):
                        for k in range(16):
                            mm1(pss[oc][1][:, :Tn], wv[:, k, oc * 128:(oc + 1) * 128], hid[:, kh * 16 + k, :Tn],
                                kh == 0 and k == 0, kh == 1 and k == 15, [sB_, hidB[kh * 16 + k]], [pss[oc][0]])
                for oc in range(2):
                    co = cp_ * 2 + oc
                    stt("dve", R0[:, co, :Tn], R1[:, co, :Tn], ALPHA, pss[oc][1][:, :Tn], ALU.mult, ALU.add, [R1B[co], pss[oc][0]], [R0B[co]])
                    act(R0[:, co, :Tn], R0[:, co, :Tn], AF.Identity, [R0B[co], pkB], [R0B[co]], bias=pkc("b2", co))
            ln_to_resid(Tn, "ln3_g", "ln3_b")
            dma("sp", y_dst, R1[:, :, :Tn], R1B[0], reads=R1B, final=True)

    xT_v = xT.rearrange("(k p) t -> p k t", p=128)
    if RUN_KV:
        memTb = xb[1]
        dma("pool", memTb[:, :, 0:256], memT_d.rearrange("(k p) m -> p k m", p=128), xbB[1], writes=[xbB[1]])
        kv_i = [4]
        for (wb, pcb, dst_d, is_k) in ((w_k_b, pc["w_k"], kout_d, True), (w_v_b, pc["w_v"], vout_d, False)):
            for half in range(2):
                sB_, wv = wpiece(wb, pcb, half * 512, half * 512 + 512)
                if is_k and KVD[0] == "1":
                    for ci in range(4):
                        ps = psum.get()
                        mm(ps[1][:, :256], [(wv[:, k, ci * 128:(ci + 1) * 128], memTb[:, k, 0:256]) for k in range(8)], [sB_, xbB[1]], [ps[0]])
                        cp("act", kT[:, half * 4 + ci, :], ps[1][:, :256], [ps[0]], [kTB])
                for mc in range(2 if KVD[1] == "1" else 0):
                    ps = psum.get()
                    if KVD[3:4] == "h":
                        mm(ps[1][:, 0:256], [(memTb[:, k, mc * 128:(mc + 1) * 128], wv[:, k, 0:256]) for k in range(8)], [sB_, xbB[1]], [ps[0]])
                        mm(ps[1][:, 256:512], [(memTb[:, k, mc * 128:(mc + 1) * 128], wv[:, k, 256:512]) for k in range(8)], [sB_, xbB[1]], [ps[0]])
                    else:
                        mm(ps[1][:, :], [(memTb[:, k, mc * 128:(mc + 1) * 128], wv[:, k, :]) for k in range(8)], [sB_, xbB[1]], [ps[0]])
                    stgB, stg = big(kv_i[0])
                    kv_i[0] = 4 + (kv_i[0] - 4 + 1) % 4
                    if KVD[5:6] == "s":
                        for hh in range(2):
                            cp("act", stg[:, hh * 256:(hh + 1) * 256], ps[1][:, hh * 256:(hh + 1) * 256], [ps[0]], stgB)
                            if not is_k:
                                cp("dve", vv[:, mc, half * 512 + hh * 256:half * 512 + (hh + 1) * 256], ps[1][:, hh * 256:(hh + 1) * 256], [ps[0]], [vvB])
                    elif KVD[5:6] == "n":
                        pass
                    elif KVD[5:6] == "a":
                        cp("act", stg, ps[1][:], [ps[0]], stgB)
                    elif KVD[5:6] == "v":
                        cp("dve", stg, ps[1][:], [ps[0]], stgB)
                    else:
                        cp("act", stg, ps[1][:], [ps[0]], stgB)
                        if not is_k:
                            cp("dve", vv[:, mc, half * 512:(half + 1) * 512], stg, stgB, [vvB])
                    if KVD[2] == "1":
                        dma("sp", dst_d[mc * 128:(mc + 1) * 128, half * 512:(half + 1) * 512], stg, stgB[0], reads=stgB, final=True)

    memset("dve", Hre[:], 0.0, HB)
    memset("dve", Him[:], 0.0, HB)
    winuB, winu = fpiece(w_in_b, pc["w_in"], 0, 512)
    NPB = NPRE // T
    xi = [0]

    def load_xb(col):
        i = xi[0] % 2
        xi[0] += 1
        dma("pool", xb[i][:, :, :], xT_v[:, :, col:col + T], xbB[i], writes=[xbB[i]])
        return i

    nxt = load_xb((NPB - NPB_RUN) * T)
    qi = [0]
    pf_i = [0]
    def step_late_pc():
        while late_pc:
            try:
                next(late_pc[0])
                return
            except StopIteration:
                late_pc.pop(0)

    for pb in range(NPB - NPB_RUN, NPB):
        cur = nxt
        step_late_pc()
        nxt = load_xb((pb + 1) * T)
        for blk in range(T // 128):
            pu = psum.get()
            mm(pu[1][:, :], [(xb[cur][:, k, blk * 128:(blk + 1) * 128], winu[:, k, :]) for k in range(8)], [xbB[cur], winuB], [pu[0]])
            ubk = ghb.get()
            cp("act", ubk[1][:], pu[1][:], [pu[0]], [ubk[0]])
            if PF_BANKS:
                pfl = buR.items + lnR.items
                wr, wi = pfl[(pf_i[0] % 2) * 2], pfl[(pf_i[0] % 2) * 2 + 1]
                pf_i[0] += 1
            else:
                wr, wi = psum.get(), psum.get()

            def fn(e, ubk=ubk, wr=wr, wi=wi):
                ins = None
                for p_ in range(16):
                    e.matmul(wr[1][:, p_ * 32:(p_ + 1) * 32], Pre[:, p_, :], ubk[1][:, p_ * 32:(p_ + 1) * 32], start=True, stop=True)
                    ins = e.matmul(wi[1][:, p_ * 32:(p_ + 1) * 32], Pim[:, p_, :], ubk[1][:, p_ * 32:(p_ + 1) * 32], start=True, stop=True)
                return ins
            P.op("pe", fn, [lhsB, ubk[0]], [wr[0], wi[0]])
            base = (qi[0] % 2) * 2
            qi[0] += 1
            (q1B, q1), (q2B, q2) = big(base), big(base + 1)
            (q3B, q3), (q4B, q4) = big(4 + base), big(4 + base + 1)
            tt("dve", q1, wr[1][:], Bxr[:], ALU.mult, [wr[0], BxB], q1B)
            tt("dve", q2, wi[1][:], Bxi[:], ALU.mult, [wi[0], BxB], q2B)
            tt("dve", q3, wi[1][:], Bxr[:], ALU.mult, [wi[0], BxB], q3B)
            tt("dve", q4, wr[1][:], Bxi[:], ALU.mult, [wr[0], BxB], q4B)
            tt(SSM_POST, q1, q1, q2, ALU.subtract, q1B + q2B, q1B)
            tt(SSM_POST, q3, q3, q4, ALU.add, q3B + q4B, q3B)
            sr, si = tiny.get(), tiny.get()
            P.op("dve", (lambda sr, q1: lambda e: e.tensor_reduce(out=sr[1][:], in_=q1.rearrange("p (a b) -> p a b", a=16), axis=AX.X, op=ALU.add))(sr, q1), q1B, [sr[0]])
            P.op("dve", (lambda si, q3: lambda e: e.tensor_reduce(out=si[1][:], in_=q3.rearrange("p (a b) -> p a b", a=16), axis=AX.X, op=ALU.add))(si, q3), q3B, [si[0]])
            u1, u2, u3, u4 = tiny.get(), tiny.get(), tiny.get(), tiny.get()
            tt("dve", u1[1][:], a128r[:], Hre[:], ALU.mult, [a128B] + HB, [u1[0]])
            tt("dve", u2[1][:], a128i[:], Him[:], ALU.mult, [a128B] + HB, [u2[0]])
            tt("dve", u3[1][:], a128r[:], Him[:], ALU.mult, [a128B] + HB, [u3[0]])
            tt("dve", u4[1][:], a128i[:], Hre[:], ALU.mult, [a128B] + HB, [u4[0]])
            tt("dve", u1[1][:], u1[1][:], u2[1][:], ALU.subtract, [u1[0], u2[0]], [u1[0]])
            tt("dve", u3[1][:], u3[1][:], u4[1][:], ALU.add, [u3[0], u4[0]], [u3[0]])
            tt("dve", Hre[:], u1[1][:], sr[1][:], ALU.add, [u1[0], sr[0]], HB)
            tt("dve", Him[:], u3[1][:], si[1][:], ALU.add, [u3[0], si[0]] + HB, HB)

    for g_ in late_pc:
        for _ in g_:
            pass
    hxi = xi[0] % 2
    hx, hxB = xb[hxi], xbB[hxi]
    dma("pool", hx[:, :, 0:32], xT_v[:, :, NPRE - 32:NPRE], hxB, writes=[hxB])
    for half in range(2):
        sB_, wv = wpiece(w_in_b, pc["w_in"], 512 + half * 512, 1024 + half * 512)
        for ci in range(2):
            c = half * 2 + ci
            pa, pg = psum.get(), psum.get()
            mm(pa[1][:, :32], [(wv[:, k, ci * 256:ci * 256 + 128], hx[:, k, 0:32]) for k in range(8)], [sB_, hxB], [pa[0]])
            mm(pg[1][:, :32], [(wv[:, k, ci * 256 + 128:ci * 256 + 256], hx[:, k, 0:32]) for k in range(8)], [sB_, hxB], [pg[0]])
            sg = f32t.get()
            act(sg[1][:, :32], pg[1][:, :32], AF.Sigmoid, [pg[0]], [sg[0]])
            tt("dve", vbuf[:, c, 0:30], pa[1][:, 2:32], sg[1][:, 2:32], ALU.mult, [pa[0], sg[0]], [vbufB[c]])

    xs_v = xsT.rearrange("(k p) t -> p k t", p=128)
    hsinB = Buf("hs_in")

    def s_halo(s_):
        def f(c):
            dma("sp", vbuf[:, c, 0:30], convc_d[:, (c * 2 + s_) * 30:(c * 2 + s_) * 30 + 30], vbufB[c], writes=[vbufB[c]])
        return f

    def s_tail(s_):
        def f(c, L):
            dma("sp", convsout_d[:, (c * 2 + s_) * 30:(c * 2 + s_) * 30 + 30], vbuf[:, c, L:L + 30], vbufB[c], reads=[vbufB[c]], final=True)
        return f

    def s_kv(s_):
        def f():
            dma("pool", kT[:], kTs_d[s_].rearrange("(k p) m -> p k m", p=128), kTB, writes=[kTB])
            dma("pool", vv[:], vs_d[s_].rearrange("(k p) n -> p k n", p=128), vvB, writes=[vvB])
        return f

    def hs_fn(s_, reim):
        return lambda pi: Hs[:, pi * 4 + s_ * 2 + reim: pi * 4 + s_ * 2 + reim + 1]

    s_segs = [(s_ * 16, 16, hs_fn(s_, 0), hs_fn(s_, 1), lambda pi: HsB[pi]) for s_ in range(2)]

    def sample_gen(part, bi):
        if part in ("all", "front"):
            dma("sp", Hs[:], h0_d[:, :], hsinB, writes=HsB)
            dma("pool", xb[bi][:, :, 0:32], xs_v, xbB[bi], writes=[xbB[bi]])
        for _ in run_tile(32, xb[bi], xbB[bi], xs_v, s_segs,
                          [(s_ * 16, 16, s_halo(s_), s_tail(s_)) for s_ in range(2)],
                          [(s_ * 16, 16, s_kv(s_)) for s_ in range(2)],
                          ysT.rearrange("(k p) t -> p k t", p=128), is_sample=True, part=part):
            yield
        if part in ("all", "back"):
            hso = statsb.get()
            cp("dve", hso[1][:, 0:64], Hs[:], HsB, [hso[0]])
            dma("sp", hsout_d[:, :], hso[1][:, 0:64], hso[0], reads=[hso[0]], final=True)

    p_segs = [(s * TS, TS, lambda pi: Hre[:, pi:pi + 1], lambda pi: Him[:, pi:pi + 1], lambda pi: HB[pi]) for s in range(T // TS)]
    yT_v = yT.rearrange("(k p) t -> p k t", p=128)
    cur = nxt

    def p_tail(c, L):
        dma("sp", convout_d[:, c * 30:(c + 1) * 30], vbuf[:, c, L:L + 30], vbufB[c], reads=[vbufB[c]], final=True)

    def xload(it, buf_i):
        dma("pool", xb[buf_i][:, :, :], xT_v[:, :, NPRE + it * T: NPRE + (it + 1) * T], xbB[buf_i], writes=[xbB[buf_i]])

    if NT_RUN > 0:
        if NT_RUN > 1:
            xload(1, 1 - cur)
        drain(front_p(xb[cur], xbB[cur], p_tail if NT_RUN == 1 and NT == 1 else None))
    for it in range(NT_RUN):
        back = run_tile(T, xb[cur], xbB[cur], xT_v[:, :, NPRE + it * T: NPRE + (it + 1) * T], p_segs,
                        [(0, T, None, None)], [(0, T, None)], yT_v[:, :, it * T:(it + 1) * T], part="back")
        if it + 1 < NT_RUN:
            nb = 1 - cur
            fr = front_p(xb[nb], xbB[nb], p_tail if it + 1 == NT - 1 else None)
            if it + 2 < NT_RUN:
                xload(it + 2, cur)
            if PIPE:
                merge(back, fr)
            else:
                drain(back)
                drain(fr)
        else:
            if RUN_SAMPLE and PIPE:
                merge(back, sample_gen("front", 1 - cur))
            else:
                drain(back)
        cur = 1 - cur
    hst, hst2 = tiny.get(), tiny.get()
    cp("dve", hst[1][:], Hre[:], HB, [hst[0]])
    cp("dve", hst2[1][:], Him[:], HB, [hst2[0]])
    dma("sp", hout_d[:, 0:16], hst[1][:], hst[0], reads=[hst[0]], final=True)
    dma("sp", hout_d[:, 16:32], hst2[1][:], hst2[0], reads=[hst2[0]], final=True)

    if RUN_SAMPLE:
        if PIPE and NT_RUN > 0:
            drain(sample_gen("back", cur))
        else:
            drain(sample_gen("all", cur))

    P.emit(st)
    st.close()
    return nc


_NC_CACHE = {}


def _mode_major(a):
    sh = a.shape
    a = a.reshape((16, 2, 64) + sh[2:])
    return np.ascontiguousarray(np.moveaxis(a, 0, 2).reshape((128, 16) + sh[2:]))


def kernel(x_prompt, x_sample, state_ssm_re, state_ssm_im, cache_conv, cache_mem_k, cache_mem_v,
           mem_prompt, w_in, ssm_a_re, ssm_a_im, ssm_log_dt, ssm_b_re, ssm_b_im, ssm_c_re, ssm_c_im,
           ssm_d, glu_w, glu_b, conv_w, conv_b, conv_ln_g, conv_ln_b, w_out, ln1_g, ln1_b,
           mem_w_q, mem_w_k, mem_w_v, mem_w_o, ln2_g, ln2_b,
           mlp_w1, mlp_b1, mlp_w2, mlp_b2, ln3_g, ln3_b):
    f = lambda a: np.ascontiguousarray(np.asarray(a, dtype=np.float32))
    x_prompt, x_sample = f(x_prompt), f(x_sample)
    col = lambda v, n: f(v).reshape(n, 128).T
    pk = np.zeros((128, NPK), np.float32)

    def put(name, arr):
        pk[:, PK[name][0]:PK[name][1]] = arr
    put("ln1_g", col(ln1_g[0], 8)); put("ln1_b", col(ln1_b[0], 8))
    put("ln2_g", col(ln2_g[0], 8)); put("ln2_b", col(ln2_b[0], 8))
    put("ln3_g", col(ln3_g[0], 8)); put("ln3_b", col(ln3_b[0], 8))
    put("b1", col(mlp_b1[0], 32)); put("b2", col(mlp_b2[0], 8))
    put("glu_b", col(glu_b[0], 4)); put("ssm_d", col(ssm_d[0], 4))
    put("conv_b", col(conv_b[0], 4)); put("cln_g", col(conv_ln_g[0], 4)); put("cln_b", col(conv_ln_b[0], 4))
    cw = f(conv_w[0]).T.reshape(4, 128, 31).transpose(1, 0, 2).reshape(128, 124)
    put("conv_w", cw)
    put("a_re", _mode_major(f(ssm_a_re[0])[:, :, None])[:, :, 0])
    put("a_im", _mode_major(f(ssm_a_im[0])[:, :, None])[:, :, 0])
    ldt = np.repeat(f(ssm_log_dt[0])[:, None], 64, axis=1)
    put("logdt", _mode_major(ldt[:, :, None])[:, :, 0])
    put("tidx", np.tile(np.arange(128, dtype=np.float32)[None, :], (128, 1)))
    put("eidx", (127.0 - np.arange(128, dtype=np.float32))[:, None])
    rows = np.stack([f(ssm_a_re[0]).reshape(-1), f(ssm_a_im[0]).reshape(-1), ldt.reshape(-1)]).astype(np.float32)
    BT = np.zeros((2, 128, 2048), np.float32)
    CT = np.zeros((2, 128, 2048), np.float32)
    Bx = np.zeros((2, 128, 512), np.float32)
    for ri, (bsrc, csrc) in enumerate(((f(ssm_b_re[0]), f(ssm_c_re[0])), (f(ssm_b_im[0]), f(ssm_c_im[0])))):
        for g in range(32):
            pi, gp, gl = g // 2, g % 2, g % 8
            BT[ri, gl * 16:(gl + 1) * 16, pi * 128 + gp * 64: pi * 128 + gp * 64 + 64] = bsrc[g].T
            CT[ri, gp * 64:(gp + 1) * 64, pi * 128 + gl * 16: pi * 128 + gl * 16 + 16] = csrc[g].T
            Bx[ri, gp * 64:(gp + 1) * 64, pi * 32 + gp * 16: pi * 32 + gp * 16 + 16] = bsrc[g]
    shared = dict(w_in=f(w_in[0]), glu_w=f(glu_w[0]), w_out=f(w_out[0]), w_q=f(mem_w_q[0]), w_k=f(mem_w_k[0]),
                  w_v=f(mem_w_v[0]), w_o=f(mem_w_o[0]), w1=f(mlp_w1[0]), w2=f(mlp_w2[0]), pk=pk, rows=rows, BT=BT, CT=CT, Bx=Bx)
    xTs = [np.ascontiguousarray(x_prompt[b].T) for b in range(2)]
    in_maps = []
    for c in CORES:
        b, j = c // 4, c % 4
        xin = np.zeros((D, NPRE + SEG), np.float32)
        npre = j * SEG
        xin[:, NPRE - npre: NPRE + SEG] = xTs[b][:, 0:(j + 1) * SEG]
        ss = [2 * c, 2 * c + 1]
        xs = np.concatenate([x_sample[s].T for s in ss], axis=1)
        h0 = np.zeros((128, 16, 2, 2), np.float32)
        cc = np.zeros((128, 4, 2, 30), np.float32)
        for si, s in enumerate(ss):
            h0[:, :, si, 0] = _mode_major(f(state_ssm_re[0, s])[:, :, None])[:, :, 0]
            h0[:, :, si, 1] = _mode_major(f(state_ssm_im[0, s])[:, :, None])[:, :, 0]
            cc[:, :, si, :] = f(cache_conv[0, s]).T.reshape(4, 128, 30).transpose(1, 0, 2)
        kTs = np.stack([f(cache_mem_k[0, s]).reshape(256, D).T for s in ss])
        vs = np.stack([f(cache_mem_v[0, s]).reshape(256, D) for s in ss])
        m = dict(shared)
        m.update(xT=xin, xsT=np.ascontiguousarray(xs), h0=h0.reshape(128, 64), convc=cc.reshape(128, 240),
                 kTs=np.ascontiguousarray(kTs), vs=np.ascontiguousarray(vs), memT=np.ascontiguousarray(f(mem_prompt[b]).T))
        in_maps.append(m)
    if "nc" not in _NC_CACHE:
        _NC_CACHE["nc"] = build_program()
    res = run_bass_kernel_spmd(_NC_CACHE["nc"], in_maps, core_ids=list(range(len(CORES))))
    R = {c: res.results[i] for i, c in enumerate(CORES)}

    def unmode(a):
        return a.reshape(2, 64, 16).transpose(2, 0, 1).reshape(32, 64)
    y_prompt = np.zeros((2, 16384, D), np.float32)
    y_sample = np.zeros((16, 16, D), np.float32)
    p_re = np.zeros((1, 2, 32, 64), np.float32)
    p_im = np.zeros((1, 2, 32, 64), np.float32)
    p_conv = np.zeros((1, 2, 30, 512), np.float32)
    p_mk = np.zeros((1, 2, 256, 4, 256), np.float32)
    p_mv = np.zeros((1, 2, 256, 4, 256), np.float32)
    s_re = np.zeros((1, 16, 32, 64), np.float32)
    s_im = np.zeros((1, 16, 32, 64), np.float32)
    s_conv = np.zeros((1, 16, 30, 512), np.float32)
    for c in CORES:
        b, j = c // 4, c % 4
        r = R[c]
        y_prompt[b, j * SEG:(j + 1) * SEG, :] = r["yT"].T
        ys = r["ysT"].T
        hs = r["hsout"].reshape(128, 16, 2, 2)
        cs = r["convsout"].reshape(128, 4, 2, 30)
        for si in range(2):
            s = 2 * c + si
            y_sample[s] = ys[si * 16:(si + 1) * 16]
            s_re[0, s] = unmode(hs[:, :, si, 0])
            s_im[0, s] = unmode(hs[:, :, si, 1])
            s_conv[0, s] = cs[:, :, si, :].transpose(1, 0, 2).reshape(512, 30).T
        if j == 3:
            ho = r["hout"].reshape(128, 2, 16)
            p_re[0, b] = unmode(ho[:, 0, :])
            p_im[0, b] = unmode(ho[:, 1, :])
            p_conv[0, b] = r["convout"].reshape(128, 4, 30).transpose(1, 0, 2).reshape(512, 30).T
        if j == 0:
            p_mk[0, b] = r["kout"].reshape(256, 4, 256)
            p_mv[0, b] = r["vout"].reshape(256, 4, 256)
    return (y_prompt, y_sample, p_re, p_im, p_conv, p_mk, p_mv, s_re, s_im, s_conv)
```

```python
import os
from contextlib import ExitStack
import numpy as np
import concourse.bass as bass
import concourse.mybir as mybir
from concourse.bass_utils import run_bass_kernel_spmd

F32 = mybir.dt.float32
BF16 = mybir.dt.bfloat16
AF = mybir.ActivationFunctionType
ALU = mybir.AluOpType
AX = mybir.AxisListType

D = 1024
NCORE = 8
SEG = 4096
NPRE = 3 * SEG
T = 256
NT = SEG // T
TS = 128
LN_EPS = 1e-5
ALPHA = 2.0 ** 0.25
MAGIC = 12582912.0
TWO_PI = float(2 * np.pi)
PI = float(np.pi)

PK = {}
_o = 0
for _n, _w in [("ln1_g", 8), ("ln1_b", 8), ("ln2_g", 8), ("ln2_b", 8), ("ln3_g", 8), ("ln3_b", 8),
               ("b1", 32), ("b2", 8), ("glu_b", 4), ("ssm_d", 4), ("conv_b", 4), ("cln_g", 4), ("cln_b", 4),
               ("conv_w", 124), ("a_re", 16), ("a_im", 16), ("logdt", 16), ("tidx", 128), ("eidx", 1), ("pad", 3)]:
    PK[_n] = (_o, _o + _w)
    _o += _w
NPK = _o

SYNC_SAME = {e: (e in os.environ.get("K_SYNC", "act,dve,pool").split(",")) for e in ("act", "dve", "pool", "pe", "sp")}
NT_RUN = int(os.environ.get("K_NT", NT))
NGST = int(os.environ.get("K_NGST", "8"))
PIPE = bool(int(os.environ.get("K_PIPE", "1")))
GRP = bool(int(os.environ.get("K_GRP", "1")))
DELAY_Y = bool(int(os.environ.get("K_DY", "1")))
MRA = int(os.environ.get("K_MRA", "2"))
MRB = int(os.environ.get("K_MRB", "1"))
SSM_POST = os.environ.get("K_SSMPOST", "dve")
CONV_ENG = os.environ.get("K_CONV", "dve")
SQ_ENG = os.environ.get("K_SQ", "act")
SQ_MOD = int(os.environ.get("K_SQMOD", "2"))
CONV_POOL_CH = int(os.environ.get("K_CPC", "0"))
PF_BANKS = bool(int(os.environ.get("K_PFB", "1")))
RUN_SAMPLE = bool(int(os.environ.get("K_SAMPLE", "1")))
NPB_RUN = int(os.environ.get("K_NPB", NPRE // T))
RUN_PREP = bool(int(os.environ.get("K_PREP", "1")))
RUN_KV = bool(int(os.environ.get("K_KV", "1")))
RUN_PC = os.environ.get("K_PC", "all")
KVD = os.environ.get("K_KVD", "111")
PC_ROWS = int(os.environ.get("K_PCR", "128"))
PC_COLS = int(os.environ.get("K_PCC", "1024"))
CORES = [int(x) for x in os.environ.get("K_CORES", "0,1,2,3,4,5,6,7").split(",")]


class Ev:
    __slots__ = ("eng", "sem", "value", "op", "group")

    def __init__(self, eng):
        self.eng = eng
        self.sem = None
        self.value = None
        self.op = None
        self.group = None


class Buf:
    __slots__ = ("name", "w", "r", "dma_sem", "dma_count", "group")

    def __init__(self, name, group=None):
        self.name = name
        self.w = None
        self.r = []
        self.dma_sem = None
        self.dma_count = 0
        self.group = group


class DmaGroup:
    def __init__(self, name):
        self.name = name
        self.sem = None
        self.count = 0


class Op:
    __slots__ = ("eng", "fn", "waits", "ev", "is_dma", "signals")

    def __init__(self, eng, fn, waits, ev, is_dma):
        self.eng = eng
        self.fn = fn
        self.waits = waits
        self.ev = ev
        self.is_dma = is_dma
        self.signals = is_dma


class Prog:
    ENGINES = ("pe", "act", "dve", "pool", "sp")

    def __init__(self, nc):
        self.nc = nc
        self.ops = {e: [] for e in self.ENGINES}
        self.all_ops = []
        self.dma_bufs = []
        self.groups = []
        self.final_evs = []

    def group(self, name):
        g = DmaGroup(name)
        self.groups.append(g)
        return g

    def _deps(self, ev, reads, writes):
        waits = []
        for b in reads:
            if b.w is not None:
                waits.append(b.w)
        for b in writes:
            if b.w is not None:
                waits.append(b.w)
            waits.extend(b.r)
        for b in reads:
            b.r.append(ev)
        for b in writes:
            b.w = ev
            b.r = []
        out = []
        seen = set()
        for w in waits:
            if w is ev or id(w) in seen:
                continue
            seen.add(id(w))
            out.append(w)
        return out

    def op(self, eng, fn, reads=(), writes=()):
        ev = Ev(eng)
        waits = self._deps(ev, reads, writes)
        o = Op(eng, fn, waits, ev, False)
        ev.op = o
        self.ops[eng].append(o)
        self.all_ops.append(o)
        return o

    def dma(self, eng, fn, key, reads=(), writes=(), final=False):
        ev = Ev("dma")
        waits = self._deps(ev, reads, writes)
        if key.group is not None:
            ev.group = key.group
            key.group.count += 1
        else:
            if key.dma_count == 0:
                self.dma_bufs.append(key)
            key.dma_count += 1
            ev.sem = key
            ev.value = 16 * key.dma_count
        o = Op(eng, fn, waits, ev, True)
        ev.op = o
        self.ops[eng].append(o)
        self.all_ops.append(o)
        if final:
            self.final_evs.append(ev)
        return o

    def emit(self, stack):
        nc = self.nc
        for o in self.all_ops:
            for w in o.waits:
                if w.op is not None and not w.op.is_dma:
                    if w.eng == o.eng and not SYNC_SAME[o.eng]:
                        continue
                    w.op.signals = True
        esem = {}
        for e in ("pe", "act", "dve", "pool"):
            esem[e] = stack.enter_context(nc.semaphore("s_" + e))
            cnt = 0
            for o in self.ops[e]:
                if o.is_dma:
                    continue
                if o.signals:
                    cnt += 1
                    o.ev.sem = esem[e]
                    o.ev.value = cnt
        for b in self.dma_bufs:
            b.dma_sem = stack.enter_context(nc.semaphore("d_" + b.name))
        for g in self.groups:
            if g.count:
                g.sem = stack.enter_context(nc.semaphore("g_" + g.name))

        def resolve(ev):
            if ev.group is not None:
                return ev.group.sem, 16 * ev.group.count
            if isinstance(ev.sem, Buf):
                return ev.sem.dma_sem, ev.value
            return ev.sem, ev.value

        block = stack.enter_context(nc.Block())
        handles = {"pe": "tensor", "act": "scalar", "dve": "vector", "pool": "gpsimd", "sp": "sync"}
        final_evs = self.final_evs

        def make(e):
            ops = self.ops[e]

            def body(eng):
                waited = {}
                for o in ops:
                    for w in o.waits:
                        if w.eng == e and not w.op.is_dma and not SYNC_SAME[e]:
                            continue
                        sem, val = resolve(w)
                        assert sem is not None and val is not None, (e, w.eng)
                        k = id(sem)
                        if waited.get(k, 0) >= val:
                            continue
                        waited[k] = val
                        eng.wait_ge(sem, val)
                    ins = o.fn(eng)
                    if o.is_dma:
                        sem, _ = resolve(o.ev)
                        ins.then_inc(sem, 16)
                    elif o.signals:
                        ins.then_inc(o.ev.sem, 1)
                if e == "sp":
                    for ev in final_evs:
                        sem, val = resolve(ev)
                        if waited.get(id(sem), 0) >= val:
                            continue
                        waited[id(sem)] = val
                        eng.wait_ge(sem, val)
            return body

        for e in self.ENGINES:
            if self.ops[e] or e == "sp":
                getattr(block, handles[e])(make(e))


class Rot:
    def __init__(self, st, nc, name, shape, dtype, n, psum=False):
        self.items = []
        for i in range(n):
            alloc = nc.psum_tensor if psum else nc.sbuf_tensor
            t = st.enter_context(alloc(f"rt_{name}{i}", shape, dtype))
            self.items.append((Buf(f"{name}{i}"), t))
        self.i = 0

    def get(self):
        it = self.items[self.i % len(self.items)]
        self.i += 1
        return it


def build_program():
    nc = bass.Bass("TRN2", target_bir_lowering=False)
    st = ExitStack()
    P = Prog(nc)

    def din(name, shape):
        return nc.dram_tensor(name, shape, F32, kind="ExternalInput").ap()

    def dout(name, shape):
        return nc.dram_tensor(name, shape, F32, kind="ExternalOutput").ap()

    def dscr(name, shape):
        return nc.dram_tensor(name, shape, BF16, kind="Internal").ap()

    xT = din("xT", [D, NPRE + SEG])
    xsT = din("xsT", [D, 32])
    h0_d = din("h0", [128, 64])
    convc_d = din("convc", [128, 240])
    kTs_d = din("kTs", [2, D, 256])
    vs_d = din("vs", [2, 256, D])
    memT_d = din("memT", [D, 256])
    w_in_d = din("w_in", [D, 1536])
    glu_w_d = din("glu_w", [512, 512])
    w_out_d = din("w_out", [D, D])
    w_q_d = din("w_q", [D, D])
    w_k_d = din("w_k", [D, D])
    w_v_d = din("w_v", [D, D])
    w_o_d = din("w_o", [D, D])
    w1_d = din("w1", [D, 4096])
    w2_d = din("w2", [4096, D])
    pk_d = din("pk", [128, NPK])
    rows_d = din("rows", [3, 2048])
    BT_d = din("BT", [2, 128, 2048])
    CT_d = din("CT", [2, 128, 2048])
    Bx_d = din("Bx", [2, 128, 512])

    yT = dout("yT", [D, SEG])
    ysT = dout("ysT", [D, 32])
    hout_d = dout("hout", [128, 32])
    convout_d = dout("convout", [128, 120])
    kout_d = dout("kout", [256, D])
    vout_d = dout("vout", [256, D])
    hsout_d = dout("hsout", [128, 64])
    convsout_d = dout("convsout", [128, 240])

    w_in_b = dscr("w_in_b", [D, 1536])
    w_out_b = dscr("w_out_b", [D, D])
    w_q_b = dscr("w_q_b", [D, D])
    w_k_b = dscr("w_k_b", [D, D])
    w_v_b = dscr("w_v_b", [D, D])
    w_o_b = dscr("w_o_b", [D, D])
    w1_b = dscr("w1_b", [D, 4096])
    w2_b = dscr("w2_b", [4096, D])

    def sb(name, shape, dt=F32):
        return st.enter_context(nc.sbuf_tensor("sb_" + name, shape, dt))

    pk = sb("pk", [128, NPK])
    cgrp = P.group("consts")
    pkB = Buf("pk", group=cgrp)
    ones_m = sb("ones_m", [128, 128], BF16)
    ones_c = sb("ones_c", [128, 128], BF16)
    ones_1 = sb("ones_1", [128, 128], BF16)
    onesB = Buf("ones")
    R0 = sb("R0", [128, 8, T])
    R1 = sb("R1", [128, 8, T])
    xnb = sb("xnb", [128, 8, T], BF16)
    R0B = [Buf(f"R0_{c}") for c in range(8)]
    R1B = [Buf(f"R1_{c}") for c in range(8)]
    xnbB = [Buf(f"xnb_{c}") for c in range(8)]
    ob, obB = xnb, xnbB
    xb = [sb(f"xb{i}", [128, 8, T], BF16) for i in range(2)]
    xbB = [Buf(f"xb{i}") for i in range(2)]
    NRING = 3
    ring = [sb(f"ring{i}", [128, 4096], BF16) for i in range(NRING)]
    ringB = [Buf(f"ring{i}") for i in range(NRING)]
    ring_i = [0]
    fring = sb("fring", [128, 4096], BF16)
    fringB = Buf("fring")
    glu_sb = sb("glu_sb", [128, 4, 512], BF16)
    gluB = Buf("glu")
    u32 = sb("u32", [128, 4, T])
    ub = sb("ub", [128, 4, T], BF16)
    u32B = [Buf(f"u32_{c}") for c in range(4)]
    ubB = [Buf(f"ub_{c}") for c in range(4)]
    vbuf = sb("vbuf", [128, 4, 30 + T])
    vbufB = [Buf(f"vbuf_{c}") for c in range(4)]
    cacc = sb("cacc", [128, 4, T])
    caccB = [Buf(f"cacc_{c}") for c in range(4)]
    mixin = sb("mixin", [128, 8, T], BF16)
    mixB = [Buf(f"mix_{c}") for c in range(8)]
    z32 = sb("z32", [128, 4, T])
    zb = sb("zb", [128, 4, T], BF16)
    z32B = [Buf(f"z32_{c}") for c in range(4)]
    zbB = [Buf(f"zb_{c}") for c in range(4)]
    hid = sb("hid", [128, 32, T], BF16)
    hidB = [Buf(f"hid_{c}") for c in range(32)]
    qb, qbB = hid, hidB
    kT = sb("kT", [128, 8, 256], BF16)
    kTB = Buf("kT")
    vv = sb("vv", [128, 2, D], BF16)
    vvB = Buf("vv")
    costab = sb("costab", [128, 16, TS])
    sintab = sb("sintab", [128, 16, TS])
    rtab = sb("rtab", [128, 16, TS])
    tabB = Buf("tabs")
    BreT = sb("BreT", [128, 16, 128], BF16)
    BimT = sb("BimT", [128, 16, 128], BF16)
    CreT = sb("CreT", [128, 16, 128], BF16)
    CimT = sb("CimT", [128, 16, 128], BF16)
    Pre = sb("Pre", [128, 16, 128], BF16)
    Pim = sb("Pim", [128, 16, 128], BF16)
    lhsB = Buf("ssm_lhs")
    Bxr = sb("Bxr", [128, 512])
    Bxi = sb("Bxi", [128, 512])
    BxB = Buf("Bx")
    a128r = sb("a128r", [128, 16])
    a128i = sb("a128i", [128, 16])
    a128B = Buf("a128")
    Hre = sb("Hre", [128, 16])
    Him = sb("Him", [128, 16])
    HB = [Buf(f"H_{p}") for p in range(16)]
    Hs = sb("Hs", [128, 64])
    HsB = [Buf(f"Hs_{p}") for p in range(16)]

    s16 = Rot(st, nc, "s16_", [128, T], BF16, 4)
    f32t = Rot(st, nc, "f32t_", [128, T], F32, 5)
    sst = Rot(st, nc, "sst_", [128, 32 if GRP else TS], F32, 10)
    hbf = Rot(st, nc, "hbf_", [128, 32 if GRP else TS], BF16, 4)
    gst = Rot(st, nc, "gst_", [128, 512], F32, NGST)
    ctmp = Rot(st, nc, "ctmp_", [128, T], F32, 2)
    ghb = Rot(st, nc, "ghb_", [128, 512], BF16, 4)
    tiny = Rot(st, nc, "tiny_", [128, 16], F32, 8)
    pTr = Rot(st, nc, "pT_", [128, T], BF16, 4)
    statsb = Rot(st, nc, "stat_", [128, T], F32, 5)
    psum = Rot(st, nc, "ps", [128, 512], F32, 3, psum=True)
    ypsR = Rot(st, nc, "yps", [128, 512], F32, 1, psum=True)
    buR = Rot(st, nc, "bups", [128, 512], F32, 2, psum=True)
    lnR = Rot(st, nc, "lnps", [128, 512], F32, 2, psum=True)

    def big(i):
        src, bufs = (R0, R0B) if i < 4 else (R1, R1B)
        j = (i % 4) * 2
        return [bufs[j], bufs[j + 1]], src[:, j:j + 2, :].rearrange("p a b -> p (a b)")

    pkc = lambda name, i=0, n=1: pk[:, PK[name][0] + i: PK[name][0] + i + n]

    def tt(eng, out, a, b, op, reads, writes):
        P.op(eng, lambda e: e.tensor_tensor(out=out, in0=a, in1=b, op=op), reads, writes)

    def ts(eng, out, a, s1, s2, op0, op1, reads, writes):
        if op1 is None:
            P.op(eng, lambda e: e.tensor_scalar(out=out, in0=a, scalar1=s1, scalar2=None, op0=op0), reads, writes)
        else:
            P.op(eng, lambda e: e.tensor_scalar(out=out, in0=a, scalar1=s1, scalar2=s2, op0=op0, op1=op1), reads, writes)

    def stt(eng, out, a, s, b, op0, op1, reads, writes):
        P.op(eng, lambda e: e.scalar_tensor_tensor(out=out, in0=a, scalar=s, in1=b, op0=op0, op1=op1), reads, writes)

    def act(out, in_, func, reads, writes, scale=None, bias=None):
        kw = {}
        if scale is not None:
            kw["scale"] = scale
        if bias is not None:
            kw["bias"] = bias
        P.op("act", lambda e: e.activation(out=out, in_=in_, func=func, **kw), reads, writes)

    def cp(eng, out, in_, reads, writes):
        if eng == "act":
            act(out, in_, AF.Copy, reads, writes)
        else:
            P.op(eng, lambda e: e.tensor_copy(out=out, in_=in_), reads, writes)

    def recip(out, in_, reads, writes):
        P.op("dve", lambda e: e.reciprocal(out=out, in_=in_), reads, writes)

    def mm(out, pairs, reads, writes):
        n = len(pairs)

        def fn(e):
            ins = None
            for i, (l, r) in enumerate(pairs):
                ins = e.matmul(out, l, r, start=(i == 0), stop=(i == n - 1))
            return ins
        P.op("pe", fn, reads, writes)

    def mm1(out, l, r, start, stop, reads, writes):
        P.op("pe", lambda e: e.matmul(out, l, r, start=start, stop=stop), reads, writes)

    def dma(eng, out, in_, key, reads=(), writes=(), final=False):
        P.dma(eng, lambda e: e.dma_start(out=out, in_=in_), key, reads=reads, writes=writes, final=final)

    def memset(eng, ap, val, writes):
        P.op(eng, lambda e: e.memset(ap, val), (), writes)

    def range_reduce(out, in_, shift, reads, writes, tmp, tmpB):
        src = in_
        rd = list(reads)
        if shift != 0.0:
            ts("dve", out, in_, shift, None, ALU.add, None, reads, writes)
            src = out
            rd = list(writes)
        ts("dve", tmp, src, 1.0 / TWO_PI, MAGIC, ALU.mult, ALU.add, rd, tmpB)
        ts("dve", tmp, tmp, MAGIC, -TWO_PI, ALU.subtract, ALU.mult, tmpB, tmpB)
        tt("dve", out, src, tmp, ALU.add, rd + list(tmpB), writes)
        ts("dve", out, out, PI, -PI, ALU.min, ALU.max, writes, writes)

    dma("sp", pk[:], pk_d[:, :], pkB, writes=[pkB])
    dma("pool", glu_sb[:], glu_w_d.rearrange("(k p) n -> p k n", p=128), gluB, writes=[gluB])
    memset("dve", ones_m[:], 1.0 / 1024.0, [onesB])
    memset("dve", ones_c[:], 1.0 / 512.0, [onesB])
    memset("dve", ones_1[:], 1.0, [onesB])


    pcsB = [Buf(f"pcs{i}") for i in range(NRING)]
    pclB = [Buf(f"pcl{i}") for i in range(NRING)]

    def precast_gen(bl, dst, src, nrows, colmap=None):
        ncols = src.shape[1]
        nk = nrows // 128
        sv = src.rearrange("(k p) n -> p k n", p=128)
        dv = dst.rearrange("(k p) n -> p k n", p=128)
        if colmap is None:
            colmap = [(c0, c0, min(1024, ncols - c0)) for c0 in range(0, ncols, 1024)]
        for (d0, s0, n) in colmap:
            kstep = max(1, min(nk, 4096 // n))
            for k0 in range(0, nk, kstep):
                i = ring_i[0] % NRING
                ring_i[0] += 1
                view = ring[i][:, 0:kstep * n].rearrange("p (a b) -> p a b", a=kstep)
                dma("pool", view, sv[:, k0:k0 + kstep, s0:s0 + n], pclB[i], writes=[ringB[i]])
                dma("sp", dv[:, k0:k0 + kstep, d0:d0 + n], view, pcsB[i], reads=[ringB[i]], writes=[bl[i]])
                yield

    def precast(name, dst, src, nrows, colmap=None):
        bl = [Buf(f"pcb_{name}{i}") for i in range(NRING)]
        for _ in precast_gen(bl, dst, src, nrows, colmap):
            pass
        return bl

    def precast_later(name, dst, src, nrows):
        bl = [Buf(f"pcb_{name}{i}") for i in range(NRING)]
        late_pc.append(precast_gen(bl, dst, src, nrows))
        return bl

    late_pc = []

    win_map = [(0, 0, 512)]
    for i in range(4):
        win_map.append((512 + 256 * i, 512 + 128 * i, 128))
        win_map.append((512 + 256 * i + 128, 1024 + 128 * i, 128))
    pc = {}
    if RUN_PC == "none":
        def precast(name, dst, src, nrows, colmap=None):
            return [Buf("pcb_" + name)]
        precast_later = lambda name, dst, src, nrows: [Buf("pcb_" + name)]
    pc["w_in"] = precast("w_in", w_in_b, w_in_d, D, win_map)
    pc["w_k"] = precast("w_k", w_k_b, w_k_d, D)
    pc["w_v"] = precast("w_v", w_v_b, w_v_d, D)
    pc["w_out"] = precast_later("w_out", w_out_b, w_out_d, D)
    pc["w_q"] = precast_later("w_q", w_q_b, w_q_d, D)
    pc["w_o"] = precast_later("w_o", w_o_b, w_o_d, D)
    pc["w1"] = precast_later("w1", w1_b, w1_d, D)
    pc["w2"] = precast_later("w2", w2_b, w2_d, 4096)

    def ring_load(src_ap, a, b, pcb):
        i = ring_i[0] % NRING
        ring_i[0] += 1
        view = ring[i][:, 0:a * b].rearrange("p (a b) -> p a b", a=a)
        dma("sp", view, src_ap, ringB[i], reads=pcb, writes=[ringB[i]])
        return ringB[i], view

    def wpiece(wb, pcb, n0, n1):
        return ring_load(wb.rearrange("(k p) n -> p k n", p=128)[:, :, n0:n1], 8, n1 - n0, pcb)

    def fpiece(wb, pcb, n0, n1):
        view = fring[:, 0:8 * (n1 - n0)].rearrange("p (a b) -> p a b", a=8)
        dma("sp", view, wb.rearrange("(k p) n -> p k n", p=128)[:, :, n0:n1], fringB, reads=pcb, writes=[fringB])
        return fringB, view

    def disc(A, Bm, L, tmp, rdB, wB):
        T1, T2, T3, T4, T5, T6, T7 = tmp
        act(L, L, AF.Exp, rdB, wB)
        tt("dve", T1, A, L, ALU.mult, wB, wB)
        tt("dve", T5, Bm, L, ALU.mult, wB, wB)
        act(T6, T1, AF.Exp, wB, wB)
        range_reduce(T2, T5, 0.0, wB, wB, T7, wB)
        act(T2, T2, AF.Sin, wB, wB)
        range_reduce(T3, T5, PI / 2, wB, wB, T7, wB)
        act(T3, T3, AF.Sin, wB, wB)
        tt("dve", T3, T6, T3, ALU.mult, wB, wB)
        tt("dve", T2, T6, T2, ALU.mult, wB, wB)
        ts("dve", T3, T3, -1.0, None, ALU.add, None, wB, wB)
        tt("dve", T6, A, A, ALU.mult, wB, wB)
        tt("dve", T7, Bm, Bm, ALU.mult, wB, wB)
        tt("dve", T6, T6, T7, ALU.add, wB, wB)
        recip(T6, T6, wB, wB)
        tt("dve", L, T3, A, ALU.mult, wB, wB)
        tt("dve", T4, T2, Bm, ALU.mult, wB, wB)
        tt("dve", L, L, T4, ALU.add, wB, wB)
        tt("dve", L, L, T6, ALU.mult, wB, wB)
        tt("dve", T4, T2, A, ALU.mult, wB, wB)
        tt("dve", T3, T3, Bm, ALU.mult, wB, wB)
        tt("dve", T4, T4, T3, ALU.subtract, wB, wB)
        tt("dve", T4, T4, T6, ALU.mult, wB, wB)
        return dict(x1=T1, ang=T5, c_re=L, c_im=T4)

    if RUN_PREP:
        mB_ = [Buf("modeprep")]
        mA, mBm, mL = sb("mA", [128, 16]), sb("mBm", [128, 16]), sb("mL", [128, 16])
        mT = [sb(f"mT{i}", [128, 16]) for i in range(7)]
        cp("dve", mA[:], pkc("a_re", 0, 16), [pkB], mB_)
        cp("dve", mBm[:], pkc("a_im", 0, 16), [pkB], mB_)
        cp("dve", mL[:], pkc("logdt", 0, 16), [pkB], mB_)
        dm = disc(mA[:], mBm[:], mL[:], [t[:] for t in mT], mB_, mB_)
        m128a, m128m, mr = sb("m128a", [128, 16]), sb("m128m", [128, 16]), sb("mr", [128, 16])
        ts("dve", m128a[:], dm["ang"], 128.0, None, ALU.mult, None, mB_, mB_)
        act(m128m[:], dm["x1"], AF.Exp, mB_, mB_, scale=128.0)
        range_reduce(mT[1][:], m128a[:], 0.0, mB_, mB_, mT[6][:], mB_)
        act(mT[1][:], mT[1][:], AF.Sin, mB_, mB_)
        range_reduce(mT[2][:], m128a[:], PI / 2, mB_, mB_, mT[6][:], mB_)
        act(mT[2][:], mT[2][:], AF.Sin, mB_, mB_)
        tt("dve", a128r[:], m128m[:], mT[2][:], ALU.mult, mB_, [a128B])
        tt("dve", a128i[:], m128m[:], mT[1][:], ALU.mult, mB_ + [a128B], [a128B])
        act(mr[:], dm["x1"], AF.Exp, mB_, mB_)
        for p_ in range(16):
            tg, tg2 = gst.get(), gst.get()
            tg = (tg[0], tg[1][:, 0:TS])
            tg2 = (tg2[0], tg2[1][:, 0:TS])
            ts("dve", tg[1][:], pkc("tidx", 0, TS), dm["ang"][:, p_:p_ + 1], None, ALU.mult, None, [pkB] + mB_, [tg[0]])
            range_reduce(sintab[:, p_, :], tg[1][:], 0.0, [tg[0]], [tabB], tg2[1][:], [tg2[0]])
            act(sintab[:, p_, :], sintab[:, p_, :], AF.Sin, [tabB], [tabB])
            range_reduce(costab[:, p_, :], tg[1][:], PI / 2, [tg[0]], [tabB], tg2[1][:], [tg2[0]])
            act(costab[:, p_, :], costab[:, p_, :], AF.Sin, [tabB], [tabB])
            ts("dve", rtab[:, p_, :], pkc("tidx", 0, TS), 0.0, mr[:, p_:p_ + 1], ALU.mult, ALU.add, [pkB] + mB_, [tabB])
            memset("dve", rtab[:, p_, 0:1], 0.0, [tabB])
        bxrB, bxr_t = big(0)
        bxiB, bxi_t = big(1)
        dma("sp", bxr_t, Bx_d[0], bxrB[0], writes=bxrB)
        dma("sp", bxi_t, Bx_d[1], bxiB[0], writes=bxiB)
        for p_ in range(16):
            sl = slice(p_ * 32, p_ * 32 + 32)
            cr, ci = dm["c_re"][:, p_:p_ + 1], dm["c_im"][:, p_:p_ + 1]
            ta, tb_ = gst.get(), gst.get()
            ts("dve", ta[1][:, 0:32], bxi_t[:, sl], ci, None, ALU.mult, None, bxiB + mB_, [ta[0]])
            stt("dve", Bxr[:, sl], bxr_t[:, sl], cr, ta[1][:, 0:32], ALU.mult, ALU.subtract, bxrB + mB_ + [ta[0]], [BxB])
            ts("dve", tb_[1][:, 0:32], bxr_t[:, sl], ci, None, ALU.mult, None, bxrB + mB_, [tb_[0]])
            stt("dve", Bxi[:, sl], bxi_t[:, sl], cr, tb_[1][:, 0:32], ALU.mult, ALU.add, bxiB + mB_ + [tb_[0]], [BxB])

        rt = [R0[:, c, :] for c in range(8)] + [R1[:, c, :] for c in range(8)]
        rtB = R0B + R1B
        tmp_tiles = []
        tmpB = []
        for gi_ in range(4):
            gb_, gt_ = gst.items[gi_]
            tmpB.append(gb_)
            tmp_tiles += [gt_[:, 0:256], gt_[:, 256:512]]
        for blk in range(8):
            cs = slice(blk * 256, blk * 256 + 256)
            base_ = (blk % 2) * 7
            inT = rt[base_:base_ + 7]
            inB = rtB[base_:base_ + 7]
            rA, rBm, rL, btr, bti, ctr, cti = inT
            srcs_ = [rows_d[0:1, cs].partition_broadcast(128), rows_d[1:2, cs].partition_broadcast(128),
                     rows_d[2:3, cs].partition_broadcast(128), BT_d[0][:, cs], BT_d[1][:, cs], CT_d[0][:, cs], CT_d[1][:, cs]]
            for k_ in range(7):
                dma("sp", inT[k_], srcs_[k_], inB[k_], writes=[inB[k_]])
            rowB = inB + tmpB
            tmp = tmp_tiles[0:7]
            dr = disc(rA, rBm, rL, tmp, rowB, rowB)
            sA, sB_ = tmp[1], tmp[2]
            s3, s4 = tmp[5], tmp[6]
            osl = lambda t_: t_[:, blk * 2:blk * 2 + 2, :].rearrange("p a b -> p (a b)")
            tt("dve", sA, dr["c_re"], btr, ALU.mult, rowB, rowB)
            tt("dve", sB_, dr["c_im"], bti, ALU.mult, rowB, rowB)
            tt("dve", osl(BreT), sA, sB_, ALU.subtract, rowB, [lhsB])
            tt("dve", sA, dr["c_re"], bti, ALU.mult, rowB, rowB)
            tt("dve", sB_, dr["c_im"], btr, ALU.mult, rowB, rowB)
            tt("dve", osl(BimT), sA, sB_, ALU.add, rowB + [lhsB], [lhsB])
            e_ap = pkc("eidx")
            act(sA, dr["x1"], AF.Exp, rowB + [pkB], rowB, scale=e_ap)
            ts("dve", sB_, dr["ang"], e_ap, None, ALU.mult, None, rowB + [pkB], rowB)
            range_reduce(s3, sB_, 0.0, rowB, rowB, s4, rowB)
            act(s3, s3, AF.Sin, rowB, rowB)
            tt("dve", osl(Pim), sA, s3, ALU.mult, rowB + [lhsB], [lhsB])
            range_reduce(s3, sB_, PI / 2, rowB, rowB, s4, rowB)
            act(s3, s3, AF.Sin, rowB, rowB)
            tt("dve", osl(Pre), sA, s3, ALU.mult, rowB + [lhsB], [lhsB])
            cp("dve", osl(CreT), ctr, rowB + [lhsB], [lhsB])
            ts("dve", osl(CimT), cti, -1.0, None, ALU.mult, None, rowB + [lhsB], [lhsB])

    def layer_norm(nch, srcs, srcB, ones_ap, Tn, g_name, b_name, emit_out):
        pm, pe2 = lnR.get(), lnR.get()
        for c in range(nch):
            s1, s2 = s16.get(), s16.get()
            act(s1[1][:, :Tn], srcs[c], AF.Copy, [srcB[c]], [s1[0]])
            act(s2[1][:, :Tn], srcs[c], AF.Square, [srcB[c]], [s2[0]])
            mm1(pm[1][:, :Tn], ones_ap, s1[1][:, :Tn], c == 0, c == nch - 1, [s1[0], onesB], [pm[0]])
            mm1(pe2[1][:, :Tn], ones_ap, s2[1][:, :Tn], c == 0, c == nch - 1, [s2[0], onesB], [pe2[0]])
        mean, var, nmr = statsb.get(), statsb.get(), statsb.get()
        cp("act", mean[1][:, :Tn], pm[1][:, :Tn], [pm[0]], [mean[0]])
        tt("dve", var[1][:, :Tn], mean[1][:, :Tn], mean[1][:, :Tn], ALU.mult, [mean[0]], [var[0]])
        tt("dve", var[1][:, :Tn], pe2[1][:, :Tn], var[1][:, :Tn], ALU.subtract, [pe2[0], var[0]], [var[0]])
        ts("dve", var[1][:, :Tn], var[1][:, :Tn], LN_EPS, None, ALU.add, None, [var[0]], [var[0]])
        act(var[1][:, :Tn], var[1][:, :Tn], AF.Sqrt, [var[0]], [var[0]])
        recip(var[1][:, :Tn], var[1][:, :Tn], [var[0]], [var[0]])
        stt("dve", nmr[1][:, :Tn], mean[1][:, :Tn], -1.0, var[1][:, :Tn], ALU.mult, ALU.mult, [mean[0], var[0]], [nmr[0]])
        for c in range(nch):
            t_ = f32t.get()
            tt("dve", t_[1][:, :Tn], srcs[c], var[1][:, :Tn], ALU.mult, [srcB[c], var[0]], [t_[0]])
            tt("dve", t_[1][:, :Tn], t_[1][:, :Tn], nmr[1][:, :Tn], ALU.add, [t_[0], nmr[0]], [t_[0]])
            emit_out(c, t_[1][:, :Tn], t_[0], pkc(g_name, c), pkc(b_name, c))

    def ln_to_resid(Tn, g_name, b_name):
        def emit_out(c, t_ap, tB, g_ap, b_ap):
            act(R1[:, c, :Tn], t_ap, AF.Identity, [tB, pkB], [R1B[c]], scale=g_ap, bias=b_ap)
            act(xnb[:, c, :Tn], t_ap, AF.Identity, [tB, pkB], [xnbB[c]], scale=g_ap, bias=b_ap)
        layer_norm(8, [R0[:, c, :Tn] for c in range(8)], R0B, ones_m[:], Tn, g_name, b_name, emit_out)

    def ssm_segment(pi, col0, L, Hr_ap, Hi_ap, HBuf, ypsum, first, last):
        c = pi // 4
        bu = psum.get()
        bre, bim = bu[1][:, 0:L], bu[1][:, 256:256 + L]
        mm1(bre, BreT[:, pi, :], ub[:, c, col0:col0 + L], True, True, [lhsB, ubB[c]], [bu[0]])
        mm1(bim, BimT[:, pi, :], ub[:, c, col0:col0 + L], True, True, [lhsB, ubB[c]], [bu[0]])
        cosA, sinA, rA = costab[:, pi, 0:L], sintab[:, pi, 0:L], rtab[:, pi, 0:L]
        cos1, sin1 = costab[:, pi, 1:2], sintab[:, pi, 1:2]
        g0, tq = tiny.get(), tiny.get()
        tt("dve", tq[1][:, 0:1], Hi_ap, sin1, ALU.mult, [HBuf, tabB], [tq[0]])
        tt("dve", tq[1][:, 1:2], Hr_ap, cos1, ALU.mult, [HBuf, tabB, tq[0]], [tq[0]])
        tt("dve", tq[1][:, 2:3], Hr_ap, sin1, ALU.mult, [HBuf, tabB, tq[0]], [tq[0]])
        tt("dve", tq[1][:, 3:4], Hi_ap, cos1, ALU.mult, [HBuf, tabB, tq[0]], [tq[0]])
        tt("dve", g0[1][:, 0:1], tq[1][:, 1:2], tq[1][:, 0:1], ALU.subtract, [tq[0]], [g0[0]])
        tt("dve", g0[1][:, 1:2], tq[1][:, 3:4], tq[1][:, 2:3], ALU.add, [tq[0], g0[0]], [g0[0]])
        t1, t2, t3, t4 = sst.get(), sst.get(), sst.get(), sst.get()
        tt("dve", t1[1][:, :L], bre, cosA, ALU.mult, [bu[0], tabB], [t1[0]])
        tt("dve", t2[1][:, :L], bim, sinA, ALU.mult, [bu[0], tabB], [t2[0]])
        tt("dve", t3[1][:, :L], bim, cosA, ALU.mult, [bu[0], tabB], [t3[0]])
        tt("dve", t4[1][:, :L], bre, sinA, ALU.mult, [bu[0], tabB], [t4[0]])
        tt("dve", t1[1][:, :L], t1[1][:, :L], t2[1][:, :L], ALU.add, [t1[0], t2[0]], [t1[0]])
        tt("dve", t3[1][:, :L], t3[1][:, :L], t4[1][:, :L], ALU.subtract, [t3[0], t4[0]], [t3[0]])
        r1 = rtab[:, pi, 1:2]
        tt("dve", g0[1][:, 2:3], g0[1][:, 0:1], r1, ALU.mult, [g0[0], tabB], [g0[0]])
        tt("dve", g0[1][:, 3:4], g0[1][:, 1:2], r1, ALU.mult, [g0[0], tabB], [g0[0]])
        tt("dve", t1[1][:, 0:1], t1[1][:, 0:1], g0[1][:, 2:3], ALU.add, [t1[0], g0[0]], [t1[0]])
        tt("dve", t3[1][:, 0:1], t3[1][:, 0:1], g0[1][:, 3:4], ALU.add, [t3[0], g0[0]], [t3[0]])
        gre, gim = sst.get(), sst.get()
        P.op("dve", lambda e: e.tensor_tensor_scan(out=gre[1][:, :L], data0=rA, data1=t1[1][:, :L], initial=0.0, op0=ALU.mult, op1=ALU.add),
             [tabB, t1[0]], [gre[0]])
        P.op("dve", lambda e: e.tensor_tensor_scan(out=gim[1][:, :L], data0=rA, data1=t3[1][:, :L], initial=0.0, op0=ALU.mult, op1=ALU.add),
             [tabB, t3[0]], [gim[0]])
        p1, p2, p3, p4 = sst.get(), sst.get(), sst.get(), sst.get()
        tt("dve", p1[1][:, :L], gre[1][:, :L], cosA, ALU.mult, [gre[0], tabB], [p1[0]])
        tt("dve", p2[1][:, :L], gim[1][:, :L], sinA, ALU.mult, [gim[0], tabB], [p2[0]])
        tt("dve", p3[1][:, :L], gim[1][:, :L], cosA, ALU.mult, [gim[0], tabB], [p3[0]])
        tt("dve", p4[1][:, :L], gre[1][:, :L], sinA, ALU.mult, [gre[0], tabB], [p4[0]])
        hr, hi = hbf.get(), hbf.get()
        tt("dve", hr[1][:, :L], p1[1][:, :L], p2[1][:, :L], ALU.subtract, [p1[0], p2[0]], [hr[0]])
        tt("dve", hi[1][:, :L], p3[1][:, :L], p4[1][:, :L], ALU.add, [p3[0], p4[0]], [hi[0]])
        tt("dve", Hr_ap, p1[1][:, L - 1:L], p2[1][:, L - 1:L], ALU.subtract, [p1[0], p2[0]], [HBuf])
        tt("dve", Hi_ap, p3[1][:, L - 1:L], p4[1][:, L - 1:L], ALU.add, [p3[0], p4[0], HBuf], [HBuf])
        yo = ypsum[1][:, col0:col0 + L]
        mm1(yo, CreT[:, pi, :], hr[1][:, :L], first, False, [lhsB, hr[0]], [ypsum[0]])
        mm1(yo, CimT[:, pi, :], hi[1][:, :L], False, last, [lhsB, hi[0]], [ypsum[0]])

    def conv_chunk(c, col0, L, eng):
        o = cacc[:, c, col0:col0 + L]
        base = PK["conv_w"][0] + c * 31
        ts(eng, o, vbuf[:, c, 0:L], pk[:, base:base + 1], pkc("conv_b", c), ALU.mult, ALU.add, [vbufB[c], pkB], [caccB[c]])
        for k in range(1, 31):
            stt(eng, o, vbuf[:, c, k:k + L], pk[:, base + k:base + k + 1], o, ALU.mult, ALU.add, [vbufB[c], pkB, caccB[c]], [caccB[c]])

    def v3(t_):
        return t_.rearrange("p (a b) -> p a b", a=4)

    def ssm_s0(c, col0):
        bR, bI = buR.get(), buR.get()

        def fn(e):
            ins = None
            for q in range(4):
                e.matmul(bR[1][:, q * 128:(q + 1) * 128], BreT[:, 4 * c + q, :], ub[:, c, col0:col0 + 128], start=True, stop=True)
                ins = e.matmul(bI[1][:, q * 128:(q + 1) * 128], BimT[:, 4 * c + q, :], ub[:, c, col0:col0 + 128], start=True, stop=True)
            return ins
        P.op("pe", fn, [lhsB, ubB[c]], [bR[0], bI[0]])
        return bR, bI

    def ssm_group(c, col0, yp, filler, bu, after_s2):
        ps4 = slice(4 * c, 4 * c + 4)
        HBs = HB[4 * c:4 * c + 4]
        bR, bI = bu
        cosG, sinG = costab[:, ps4, :], sintab[:, ps4, :]
        rG = rtab[:, ps4, :].rearrange("p a b -> p (a b)")
        cos1, sin1, r1 = costab[:, ps4, 1], sintab[:, ps4, 1], rtab[:, ps4, 1]
        Hr, Hi = Hre[:, ps4], Him[:, ps4]
        tq, g0 = tiny.get(), tiny.get()
        tt("dve", tq[1][:, 0:4], Hi, sin1, ALU.mult, HBs + [tabB], [tq[0]])
        tt("dve", tq[1][:, 4:8], Hr, cos1, ALU.mult, HBs + [tabB, tq[0]], [tq[0]])
        tt("dve", tq[1][:, 8:12], Hr, sin1, ALU.mult, HBs + [tabB, tq[0]], [tq[0]])
        tt("dve", tq[1][:, 12:16], Hi, cos1, ALU.mult, HBs + [tabB, tq[0]], [tq[0]])
        tt("dve", g0[1][:, 0:4], tq[1][:, 4:8], tq[1][:, 0:4], ALU.subtract, [tq[0]], [g0[0]])
        tt("dve", g0[1][:, 4:8], tq[1][:, 12:16], tq[1][:, 8:12], ALU.add, [tq[0], g0[0]], [g0[0]])
        tt("dve", g0[1][:, 8:12], g0[1][:, 0:4], r1, ALU.mult, [g0[0], tabB], [g0[0]])
        tt("dve", g0[1][:, 12:16], g0[1][:, 4:8], r1, ALU.mult, [g0[0], tabB], [g0[0]])
        filler(3)
        yield
        A, B_, C, D_ = gst.get(), gst.get(), gst.get(), gst.get()
        tt("dve", v3(A[1][:]), v3(bR[1][:]), cosG, ALU.mult, [bR[0], tabB], [A[0]])
        tt("dve", v3(B_[1][:]), v3(bI[1][:]), sinG, ALU.mult, [bI[0], tabB], [B_[0]])
        filler(2)
        tt("dve", v3(C[1][:]), v3(bI[1][:]), cosG, ALU.mult, [bI[0], tabB], [C[0]])
        tt("dve", v3(D_[1][:]), v3(bR[1][:]), sinG, ALU.mult, [bR[0], tabB], [D_[0]])
        after_s2()
        filler(2)
        yield
        tt(SSM_POST, A[1][:], A[1][:], B_[1][:], ALU.add, [A[0], B_[0]], [A[0]])
        tt(SSM_POST, v3(A[1][:])[:, :, 0], v3(A[1][:])[:, :, 0], g0[1][:, 8:12], ALU.add, [A[0], g0[0]], [A[0]])
        tt(SSM_POST, C[1][:], C[1][:], D_[1][:], ALU.subtract, [C[0], D_[0]], [C[0]])
        tt(SSM_POST, v3(C[1][:])[:, :, 0], v3(C[1][:])[:, :, 0], g0[1][:, 12:16], ALU.add, [C[0], g0[0]], [C[0]])
        filler(4)
        yield
        GR, GI = gst.get(), gst.get()
        P.op("dve", lambda e: e.tensor_tensor_scan(out=GR[1][:], data0=rG, data1=A[1][:], initial=0.0, op0=ALU.mult, op1=ALU.add),
             [tabB, A[0]], [GR[0]])
        filler(2)
        P.op("dve", lambda e: e.tensor_tensor_scan(out=GI[1][:], data0=rG, data1=C[1][:], initial=0.0, op0=ALU.mult, op1=ALU.add),
             [tabB, C[0]], [GI[0]])
        filler(2)
        yield
        tt(SSM_POST, v3(B_[1][:]), v3(GR[1][:]), cosG, ALU.mult, [GR[0], tabB], [B_[0]])
        tt(SSM_POST, v3(D_[1][:]), v3(GI[1][:]), sinG, ALU.mult, [GI[0], tabB], [D_[0]])
        tt(SSM_POST, v3(A[1][:]), v3(GI[1][:]), cosG, ALU.mult, [GI[0], tabB], [A[0]])
        tt(SSM_POST, v3(C[1][:]), v3(GR[1][:]), sinG, ALU.mult, [GR[0], tabB], [C[0]])
        filler(4)
        yield
        hr, hi = ghb.get(), ghb.get()
        tt("dve", hr[1][:], B_[1][:], D_[1][:], ALU.subtract, [B_[0], D_[0]], [hr[0]])
        filler(1)
        tt("dve", hi[1][:], A[1][:], C[1][:], ALU.add, [A[0], C[0]], [hi[0]])
        filler(1)
        tt("dve", Hr, v3(B_[1][:])[:, :, 127], v3(D_[1][:])[:, :, 127], ALU.subtract, [B_[0], D_[0]], HBs)
        tt("dve", Hi, v3(A[1][:])[:, :, 127], v3(C[1][:])[:, :, 127], ALU.add, [A[0], C[0]] + HBs, HBs)
        yo = yp[1][:, col0:col0 + 128]

        def fn2(e):
            ins = None
            for q in range(4):
                e.matmul(yo, CreT[:, 4 * c + q, :], hr[1][:, q * 128:(q + 1) * 128], start=(q == 0), stop=False)
                ins = e.matmul(yo, CimT[:, 4 * c + q, :], hi[1][:, q * 128:(q + 1) * 128], start=False, stop=(q == 3))
            return ins
        if DELAY_Y:
            pending_y.append(lambda: P.op("pe", fn2, [lhsB, hr[0], hi[0]], [yp[0]]))
        else:
            P.op("pe", fn2, [lhsB, hr[0], hi[0]], [yp[0]])
        yield

    pending_y = []

    def flush_y():
        while pending_y:
            pending_y.pop(0)()

    def front_p(xb_t, xbB_t, tail_fn):
        Tn = T
        s0B, w0 = pre_win[0] if pre_win[0] is not None else fpiece(w_in_b, pc["w_in"], 0, 512)
        pre_win[0] = None
        for c in range(4):
            ps = psum.get()
            mm(ps[1][:, :Tn], [(w0[:, k, c * 128:(c + 1) * 128], xb_t[:, k, :Tn]) for k in range(8)], [s0B, xbB_t], [ps[0]])
            cp("act", u32[:, c, :Tn], ps[1][:, :Tn], [ps[0]], [u32B[c]])
            cp("act", ub[:, c, :Tn], u32[:, c, :Tn], [u32B[c]], [ubB[c]])
            yield
        for half in range(2):
            sB_, wv = fpiece(w_in_b, pc["w_in"], 512 + half * 512, 1024 + half * 512)
            for ci in range(2):
                c = half * 2 + ci
                pa, pg = psum.get(), psum.get()
                mm(pa[1][:, :Tn], [(wv[:, k, ci * 256:ci * 256 + 128], xb_t[:, k, :Tn]) for k in range(8)], [sB_, xbB_t], [pa[0]])
                mm(pg[1][:, :Tn], [(wv[:, k, ci * 256 + 128:ci * 256 + 256], xb_t[:, k, :Tn]) for k in range(8)], [sB_, xbB_t], [pg[0]])
                sg = f32t.get()
                act(sg[1][:, :Tn], pg[1][:, :Tn], AF.Sigmoid, [pg[0]], [sg[0]])
                tt("dve", vbuf[:, c, 30:30 + Tn], pa[1][:, :Tn], sg[1][:, :Tn], ALU.mult, [pa[0], sg[0]], [vbufB[c]])
                yield
        taps = []
        for k in range(31):
            for c in range(4):
                taps.append((c, k))
        tap_i = [0]

        def filler(n):
            for _ in range(n):
                if tap_i[0] >= len(taps):
                    return
                c, k = taps[tap_i[0]]
                tap_i[0] += 1
                o = cacc[:, c, 0:Tn]
                base = PK["conv_w"][0] + c * 31
                ceng = "pool" if c >= 4 - CONV_POOL_CH else "dve"
                if k == 0:
                    ts(ceng, o, vbuf[:, c, 0:Tn], pk[:, base:base + 1], pkc("conv_b", c), ALU.mult, ALU.add, [vbufB[c], pkB], [caccB[c]])
                elif ceng == "dve":
                    stt("dve", o, vbuf[:, c, k:k + Tn], pk[:, base + k:base + k + 1], o, ALU.mult, ALU.add, [vbufB[c], pkB, caccB[c]], [caccB[c]])
                else:
                    tmp_ = ctmp.get()
                    ts("pool", tmp_[1][:], vbuf[:, c, k:k + Tn], pk[:, base + k:base + k + 1], None, ALU.mult, None, [vbufB[c], pkB], [tmp_[0]])
                    tt("pool", o, o, tmp_[1][:], ALU.add, [caccB[c], tmp_[0]], [caccB[c]])

        glist = [(c_, sg_) for c_ in range(4) for sg_ in range(T // TS)]
        bu_next = [ssm_s0(glist[0][0], glist[0][1] * TS)] if GRP else [None]
        gidx = [0]

        def after_s2():
            flush_y()
            gidx[0] += 1
            if gidx[0] < len(glist):
                bu_next[0] = ssm_s0(glist[gidx[0]][0], glist[gidx[0]][1] * TS)

        for c in range(4):
            yp = ypsR.get()
            for sg_ in range(T // TS):
                if GRP:
                    for _ in ssm_group(c, sg_ * TS, yp, filler, bu_next[0], after_s2):
                        yield
                else:
                    for q in range(4):
                        pi = c * 4 + q
                        ssm_segment(pi, sg_ * TS, TS, Hre[:, pi:pi + 1], Him[:, pi:pi + 1], HB[pi], yp, q == 0, q == 3)
                        filler(8)
                        yield
            flush_y()
            zp = f32t.get()
            stt("dve", zp[1][:, :Tn], u32[:, c, :Tn], pkc("ssm_d", c), yp[1][:, :Tn], ALU.mult, ALU.add, [u32B[c], pkB, yp[0]], [zp[0]])
            act(z32[:, c, :Tn], zp[1][:, :Tn], AF.Gelu_apprx_tanh, [zp[0]], [z32B[c]])
            act(zb[:, c, :Tn], zp[1][:, :Tn], AF.Gelu_apprx_tanh, [zp[0]], [zbB[c]])
            yield
        while tap_i[0] < len(taps):
            filler(4)
            yield
        for co in range(4):
            ps = psum.get()
            mm(ps[1][:, :Tn], [(glu_sb[:, k, co * 128:(co + 1) * 128], zb[:, k, :Tn]) for k in range(4)], [gluB] + zbB, [ps[0]])
            sg = f32t.get()
            act(sg[1][:, :Tn], ps[1][:, :Tn], AF.Sigmoid, [ps[0], pkB], [sg[0]], bias=pkc("glu_b", co))
            tt("dve", mixin[:, co, :Tn], z32[:, co, :Tn], sg[1][:, :Tn], ALU.mult, [z32B[co], sg[0]], [mixB[co]])
            yield
        for c in range(4):
            if tail_fn is not None:
                tail_fn(c, Tn)
            cp("pool", vbuf[:, c, 0:30], vbuf[:, c, Tn:Tn + 30], [vbufB[c]], [vbufB[c]])

        def silu_out(c, t_ap, tB, g_ap, b_ap):
            act(mixin[:, 4 + c, :Tn], t_ap, AF.Silu, [tB, pkB], [mixB[4 + c]], scale=g_ap, bias=b_ap)
        pre_win[0] = fpiece(w_in_b, pc["w_in"], 0, 512)
        layer_norm(4, [cacc[:, c, :Tn] for c in range(4)], caccB, ones_c[:], Tn, "cln_g", "cln_b", silu_out)
        yield

    pre_win = [None]
    pre_wout = [None]

    def drain(g):
        for _ in g:
            pass

    def merge(ga, gb):
        da = db = False
        while not (da and db):
            for _ in range(MRA):
                if not da:
                    try:
                        next(ga)
                    except StopIteration:
                        da = True
            for _ in range(MRB):
                if not db:
                    try:
                        next(gb)
                    except StopIteration:
                        db = True

    def run_tile(Tn, xb_t, xbB_t, x32_src, segs, conv_segs, att_segs, y_dst, is_sample=False, part="all"):
        if part in ("all", "front"):
            s0B, w0 = pre_win[0] if pre_win[0] is not None else fpiece(w_in_b, pc["w_in"], 0, 512)
            pre_win[0] = None
            for c in range(4):
                ps = psum.get()
                mm(ps[1][:, :Tn], [(w0[:, k, c * 128:(c + 1) * 128], xb_t[:, k, :Tn]) for k in range(8)], [s0B, xbB_t], [ps[0]])
                cp("act", u32[:, c, :Tn], ps[1][:, :Tn], [ps[0]], [u32B[c]])
                cp("dve", ub[:, c, :Tn], u32[:, c, :Tn], [u32B[c]], [ubB[c]])
            for c in range(4):
                yp = ypsR.get()
                for (col0, L, Hr_fn, Hi_fn, HB_fn) in segs:
                    for q in range(4):
                        pi = c * 4 + q
                        ssm_segment(pi, col0, L, Hr_fn(pi), Hi_fn(pi), HB_fn(pi), yp, q == 0, q == 3)
                    yield
                zp = f32t.get()
                stt("dve", zp[1][:, :Tn], u32[:, c, :Tn], pkc("ssm_d", c), yp[1][:, :Tn], ALU.mult, ALU.add, [u32B[c], pkB, yp[0]], [zp[0]])
                act(z32[:, c, :Tn], zp[1][:, :Tn], AF.Gelu_apprx_tanh, [zp[0]], [z32B[c]])
                act(zb[:, c, :Tn], zp[1][:, :Tn], AF.Gelu_apprx_tanh, [zp[0]], [zbB[c]])
            for co in range(4):
                ps = psum.get()
                mm(ps[1][:, :Tn], [(glu_sb[:, k, co * 128:(co + 1) * 128], zb[:, k, :Tn]) for k in range(4)], [gluB] + zbB, [ps[0]])
                sg = f32t.get()
                act(sg[1][:, :Tn], ps[1][:, :Tn], AF.Sigmoid, [ps[0], pkB], [sg[0]], bias=pkc("glu_b", co))
                tt("dve", mixin[:, co, :Tn], z32[:, co, :Tn], sg[1][:, :Tn], ALU.mult, [z32B[co], sg[0]], [mixB[co]])
            for half in range(2):
                sB_, wv = fpiece(w_in_b, pc["w_in"], 512 + half * 512, 1024 + half * 512)
                for ci in range(2):
                    c = half * 2 + ci
                    pa, pg = psum.get(), psum.get()
                    mm(pa[1][:, :Tn], [(wv[:, k, ci * 256:ci * 256 + 128], xb_t[:, k, :Tn]) for k in range(8)], [sB_, xbB_t], [pa[0]])
                    mm(pg[1][:, :Tn], [(wv[:, k, ci * 256 + 128:ci * 256 + 256], xb_t[:, k, :Tn]) for k in range(8)], [sB_, xbB_t], [pg[0]])
                    sg = f32t.get()
                    act(sg[1][:, :Tn], pg[1][:, :Tn], AF.Sigmoid, [pg[0]], [sg[0]])
                    eng = "dve"
                    if is_sample:
                        vt = f32t.get()
                        tt("dve", vt[1][:, :Tn], pa[1][:, :Tn], sg[1][:, :Tn], ALU.mult, [pa[0], sg[0]], [vt[0]])
                        for (col0, L, halo_fn, tail_fn) in conv_segs:
                            halo_fn(c)
                            cp("pool", vbuf[:, c, 30:30 + L], vt[1][:, col0:col0 + L], [vt[0]], [vbufB[c]])
                            conv_chunk(c, col0, L, eng)
                            tail_fn(c, L)
                    else:
                        (col0, L, halo_fn, tail_fn) = conv_segs[0]
                        tt("dve", vbuf[:, c, 30:30 + Tn], pa[1][:, :Tn], sg[1][:, :Tn], ALU.mult, [pa[0], sg[0]], [vbufB[c]])
                        conv_chunk(c, 0, Tn, eng)
                        if tail_fn is not None:
                            tail_fn(c, Tn)
                        cp("pool", vbuf[:, c, 0:30], vbuf[:, c, Tn:Tn + 30], [vbufB[c]], [vbufB[c]])

            def silu_out(c, t_ap, tB, g_ap, b_ap):
                act(mixin[:, 4 + c, :Tn], t_ap, AF.Silu, [tB, pkB], [mixB[4 + c]], scale=g_ap, bias=b_ap)
            layer_norm(4, [cacc[:, c, :Tn] for c in range(4)], caccB, ones_c[:], Tn, "cln_g", "cln_b", silu_out)
        if part in ("all", "back"):
            dma("sp", R0[:, :, :Tn], x32_src, R0B[0], writes=R0B)
            for half in range(2):
                if half == 0 and pre_wout[0] is not None:
                    sB_, wv = pre_wout[0]
                    pre_wout[0] = None
                else:
                    sB_, wv = wpiece(w_out_b, pc["w_out"], half * 512, half * 512 + 512)
                for ci in range(4):
                    co = half * 4 + ci
                    yield
                    ps = psum.get()
                    mm(ps[1][:, :Tn], [(wv[:, k, ci * 128:(ci + 1) * 128], mixin[:, k, :Tn]) for k in range(8)], [sB_] + mixB, [ps[0]])
                    stt("dve", R0[:, co, :Tn], R0[:, co, :Tn], ALPHA, ps[1][:, :Tn], ALU.mult, ALU.add, [R0B[co], ps[0]], [R0B[co]])
            ln_to_resid(Tn, "ln1_g", "ln1_b")
            qps = []
            for half in range(2):
                sB_, wv = wpiece(w_q_b, pc["w_q"], half * 512, half * 512 + 512)
                for ci in range(4):
                    yield
                    ps = psum.get()
                    mm(ps[1][:, :Tn], [(wv[:, k, ci * 128:(ci + 1) * 128], xnb[:, k, :Tn]) for k in range(8)], [sB_] + xnbB, [ps[0]])
                    act(qb[:, half * 4 + ci, :Tn], ps[1][:, :Tn], AF.Identity, [ps[0]], [qbB[half * 4 + ci]], scale=1.0 / 16.0)
            for (col0, L, kv_loader) in att_segs:
                if kv_loader is not None:
                    kv_loader()
                for h in range(4):
                    pts = []
                    for mc in range(2):
                        yield
                        ps = psum.get()
                        mm(ps[1][:, :L], [(kT[:, h * 2 + dc, mc * 128:(mc + 1) * 128], qb[:, h * 2 + dc, col0:col0 + L]) for dc in range(2)],
                           [kTB, qbB[h * 2], qbB[h * 2 + 1]], [ps[0]])
                        pt = pTr.get()
                        act(pt[1][:, :L], ps[1][:, :L], AF.Exp, [ps[0]], [pt[0]])
                        pts.append(pt)
                    yield
                    ps = psum.get()
                    mm(ps[1][:, :L], [(ones_1[:], pts[mc][1][:, :L]) for mc in range(2)], [onesB, pts[0][0], pts[1][0]], [ps[0]])
                    rinv = f32t.get()
                    recip(rinv[1][:, :L], ps[1][:, :L], [ps[0]], [rinv[0]])
                    for dc in range(2):
                        po = psum.get()
                        mm(po[1][:, :L], [(vv[:, mc, h * 256 + dc * 128: h * 256 + dc * 128 + 128], pts[mc][1][:, :L]) for mc in range(2)],
                           [vvB, pts[0][0], pts[1][0]], [po[0]])
                        tt("dve", ob[:, h * 2 + dc, col0:col0 + L], po[1][:, :L], rinv[1][:, :L], ALU.mult, [po[0], rinv[0]], [obB[h * 2 + dc]])
            for half in range(2):
                sB_, wv = wpiece(w_o_b, pc["w_o"], half * 512, half * 512 + 512)
                for ci in range(4):
                    co = half * 4 + ci
                    yield
                    ps = psum.get()
                    mm(ps[1][:, :Tn], [(wv[:, k, ci * 128:(ci + 1) * 128], ob[:, k, :Tn]) for k in range(8)], [sB_] + obB, [ps[0]])
                    stt("dve", R0[:, co, :Tn], R1[:, co, :Tn], ALPHA, ps[1][:, :Tn], ALU.mult, ALU.add, [R1B[co], ps[0]], [R0B[co]])
            ln_to_resid(Tn, "ln2_g", "ln2_b")
            for piece in range(8):
                sB_, wv = wpiece(w1_b, pc["w1"], piece * 512, piece * 512 + 512)
                for hc in range(4):
                    hidx = piece * 4 + hc
                    yield
                    ps = psum.get()
                    mm(ps[1][:, :Tn], [(wv[:, k, hc * 128:(hc + 1) * 128], xnb[:, k, :Tn]) for k in range(8)], [sB_] + xnbB, [ps[0]])
                    rl = f32t.get()
                    act(rl[1][:, :Tn], ps[1][:, :Tn], AF.Relu, [ps[0], pkB], [rl[0]], bias=pkc("b1", hidx))
                    if SQ_ENG == "act":
                        act(hid[:, hidx, :Tn], rl[1][:, :Tn], AF.Square, [rl[0]], [hidB[hidx]])
                    else:
                        tt("dve", hid[:, hidx, :Tn], rl[1][:, :Tn], rl[1][:, :Tn], ALU.mult, [rl[0]], [hidB[hidx]])
            w2v = w2_b.rearrange("(k p) n -> p k n", p=128)
            for cp_ in range(4):
                yield
                pss = [psum.get(), psum.get()]
                for kh in range(2):
                    sB_, wv = ring_load(w2v[:, kh * 16:kh * 16 + 16, cp_ * 256:cp_ * 256 + 256], 16, 256, pc["w2"])
                    for oc in range(2):
                        for k in range(16):
                            mm1(pss[oc][1][:, :Tn], wv[:, k, oc * 128:(oc + 1) * 128], hid[:, kh * 16 + k, :Tn],
                                kh == 0 and k == 0, kh == 1 and k == 15, [sB_, hidB[kh * 16 + k]], [pss[oc][0]])
                for oc in range(2):
                    co = cp_ * 2 + oc
                    stt("dve", R0[:, co, :Tn], R1[:, co, :Tn], ALPHA, pss[oc][1][:, :Tn], ALU.mult, ALU.add, [R1B[co], pss[oc][0]], [R0B[co]])
                    act(R0[:, co, :Tn], R0[:, co, :Tn], AF.Identity, [R0B[co], pkB], [R0B[co]], bias=pkc("b2", co))
            if not is_sample:
                pre_wout[0] = wpiece(w_out_b, pc["w_out"], 0, 512)
            ln_to_resid(Tn, "ln3_g", "ln3_b")
            dma("sp", y_dst, R1[:, :, :Tn], R1B[0], reads=R1B, final=True)

    xT_v = xT.rearrange("(k p) t -> p k t", p=128)
    if RUN_KV:
        memTb = xb[1]
        dma("pool", memTb[:, :, 0:256], memT_d.rearrange("(k p) m -> p k m", p=128), xbB[1], writes=[xbB[1]])
        kv_i = [4]
        for (wb, pcb, dst_d, is_k) in ((w_k_b, pc["w_k"], kout_d, True), (w_v_b, pc["w_v"], vout_d, False)):
            for half in range(2):
                sB_, wv = wpiece(wb, pcb, half * 512, half * 512 + 512)
                if is_k and KVD[0] == "1":
                    for ci in range(4):
                        ps = psum.get()
                        mm(ps[1][:, :256], [(wv[:, k, ci * 128:(ci + 1) * 128], memTb[:, k, 0:256]) for k in range(8)], [sB_, xbB[1]], [ps[0]])
                        cp("act", kT[:, half * 4 + ci, :], ps[1][:, :256], [ps[0]], [kTB])
                for mc in range(2 if KVD[1] == "1" else 0):
                    ps = psum.get()
                    if KVD[3:4] == "h":
                        mm(ps[1][:, 0:256], [(memTb[:, k, mc * 128:(mc + 1) * 128], wv[:, k, 0:256]) for k in range(8)], [sB_, xbB[1]], [ps[0]])
                        mm(ps[1][:, 256:512], [(memTb[:, k, mc * 128:(mc + 1) * 128], wv[:, k, 256:512]) for k in range(8)], [sB_, xbB[1]], [ps[0]])
                    else:
                        mm(ps[1][:, :], [(memTb[:, k, mc * 128:(mc + 1) * 128], wv[:, k, :]) for k in range(8)], [sB_, xbB[1]], [ps[0]])
                    stgB, stg = big(kv_i[0])
                    kv_i[0] = 4 + (kv_i[0] - 4 + 1) % 4
                    if KVD[5:6] == "s":
                        for hh in range(2):
                            cp("act", stg[:, hh * 256:(hh + 1) * 256], ps[1][:, hh * 256:(hh + 1) * 256], [ps[0]], stgB)
                            if not is_k:
                                cp("dve", vv[:, mc, half * 512 + hh * 256:half * 512 + (hh + 1) * 256], ps[1][:, hh * 256:(hh + 1) * 256], [ps[0]], [vvB])
                    elif KVD[5:6] == "n":
                        pass
                    elif KVD[5:6] == "a":
                        cp("act", stg, ps[1][:], [ps[0]], stgB)
                    elif KVD[5:6] == "v":
                        cp("dve", stg, ps[1][:], [ps[0]], stgB)
                    else:
                        cp("act", stg, ps[1][:], [ps[0]], stgB)
                        if not is_k:
                            cp("dve", vv[:, mc, half * 512:(half + 1) * 512], stg, stgB, [vvB])
                    if KVD[2] == "1":
                        dma("sp", dst_d[mc * 128:(mc + 1) * 128, half * 512:(half + 1) * 512], stg, stgB[0], reads=stgB, final=True)

    memset("dve", Hre[:], 0.0, HB)
    memset("dve", Him[:], 0.0, HB)
    winuB, winu = fpiece(w_in_b, pc["w_in"], 0, 512)
    NPB = NPRE // T
    xi = [0]

    def load_xb(col):
        i = xi[0] % 2
        xi[0] += 1
        dma("pool", xb[i][:, :, :], xT_v[:, :, col:col + T], xbB[i], writes=[xbB[i]])
        return i

    nxt = load_xb((NPB - NPB_RUN) * T)
    qi = [0]
    pf_i = [0]
    def step_late_pc():
        while late_pc:
            try:
                next(late_pc[0])
                return
            except StopIteration:
                late_pc.pop(0)

    for pb in range(NPB - NPB_RUN, NPB):
        cur = nxt
        step_late_pc()
        nxt = load_xb((pb + 1) * T)
        for blk in range(T // 128):
            pu = psum.get()
            mm(pu[1][:, :], [(xb[cur][:, k, blk * 128:(blk + 1) * 128], winu[:, k, :]) for k in range(8)], [xbB[cur], winuB], [pu[0]])
            ubk = ghb.get()
            cp("act", ubk[1][:], pu[1][:], [pu[0]], [ubk[0]])
            if PF_BANKS:
                pfl = buR.items + lnR.items
                wr, wi = pfl[(pf_i[0] % 2) * 2], pfl[(pf_i[0] % 2) * 2 + 1]
                pf_i[0] += 1
            else:
                wr, wi = psum.get(), psum.get()

            def fn(e, ubk=ubk, wr=wr, wi=wi):
                ins = None
                for p_ in range(16):
                    e.matmul(wr[1][:, p_ * 32:(p_ + 1) * 32], Pre[:, p_, :], ubk[1][:, p_ * 32:(p_ + 1) * 32], start=True, stop=True)
                    ins = e.matmul(wi[1][:, p_ * 32:(p_ + 1) * 32], Pim[:, p_, :], ubk[1][:, p_ * 32:(p_ + 1) * 32], start=True, stop=True)
                return ins
            P.op("pe", fn, [lhsB, ubk[0]], [wr[0], wi[0]])
            base = (qi[0] % 2) * 2
            qi[0] += 1
            (q1B, q1), (q2B, q2) = big(base), big(base + 1)
            (q3B, q3), (q4B, q4) = big(4 + base), big(4 + base + 1)
            tt("dve", q1, wr[1][:], Bxr[:], ALU.mult, [wr[0], BxB], q1B)
            tt("dve", q2, wi[1][:], Bxi[:], ALU.mult, [wi[0], BxB], q2B)
            tt("dve", q3, wi[1][:], Bxr[:], ALU.mult, [wi[0], BxB], q3B)
            tt("dve", q4, wr[1][:], Bxi[:], ALU.mult, [wr[0], BxB], q4B)
            tt(SSM_POST, q1, q1, q2, ALU.subtract, q1B + q2B, q1B)
            tt(SSM_POST, q3, q3, q4, ALU.add, q3B + q4B, q3B)
            sr, si = tiny.get(), tiny.get()
            P.op("dve", (lambda sr, q1: lambda e: e.tensor_reduce(out=sr[1][:], in_=q1.rearrange("p (a b) -> p a b", a=16), axis=AX.X, op=ALU.add))(sr, q1), q1B, [sr[0]])
            P.op("dve", (lambda si, q3: lambda e: e.tensor_reduce(out=si[1][:], in_=q3.rearrange("p (a b) -> p a b", a=16), axis=AX.X, op=ALU.add))(si, q3), q3B, [si[0]])
            u1, u2, u3, u4 = tiny.get(), tiny.get(), tiny.get(), tiny.get()
            tt("dve", u1[1][:], a128r[:], Hre[:], ALU.mult, [a128B] + HB, [u1[0]])
            tt("dve", u2[1][:], a128i[:], Him[:], ALU.mult, [a128B] + HB, [u2[0]])
            tt("dve", u3[1][:], a128r[:], Him[:], ALU.mult, [a128B] + HB, [u3[0]])
            tt("dve", u4[1][:], a128i[:], Hre[:], ALU.mult, [a128B] + HB, [u4[0]])
            tt("dve", u1[1][:], u1[1][:], u2[1][:], ALU.subtract, [u1[0], u2[0]], [u1[0]])
            tt("dve", u3[1][:], u3[1][:], u4[1][:], ALU.add, [u3[0], u4[0]], [u3[0]])
            tt("dve", Hre[:], u1[1][:], sr[1][:], ALU.add, [u1[0], sr[0]], HB)
            tt("dve", Him[:], u3[1][:], si[1][:], ALU.add, [u3[0], si[0]] + HB, HB)

    pre_win[0] = (winuB, winu)
    for g_ in late_pc:
        for _ in g_:
            pass
    hxi = xi[0] % 2
    hx, hxB = xb[hxi], xbB[hxi]
    dma("pool", hx[:, :, 0:32], xT_v[:, :, NPRE - 32:NPRE], hxB, writes=[hxB])
    for half in range(2):
        sB_, wv = wpiece(w_in_b, pc["w_in"], 512 + half * 512, 1024 + half * 512)
        for ci in range(2):
            c = half * 2 + ci
            pa, pg = psum.get(), psum.get()
            mm(pa[1][:, :32], [(wv[:, k, ci * 256:ci * 256 + 128], hx[:, k, 0:32]) for k in range(8)], [sB_, hxB], [pa[0]])
            mm(pg[1][:, :32], [(wv[:, k, ci * 256 + 128:ci * 256 + 256], hx[:, k, 0:32]) for k in range(8)], [sB_, hxB], [pg[0]])
            sg = f32t.get()
            act(sg[1][:, :32], pg[1][:, :32], AF.Sigmoid, [pg[0]], [sg[0]])
            tt("dve", vbuf[:, c, 0:30], pa[1][:, 2:32], sg[1][:, 2:32], ALU.mult, [pa[0], sg[0]], [vbufB[c]])

    xs_v = xsT.rearrange("(k p) t -> p k t", p=128)
    hsinB = Buf("hs_in")

    def s_halo(s_):
        def f(c):
            dma("sp", vbuf[:, c, 0:30], convc_d[:, (c * 2 + s_) * 30:(c * 2 + s_) * 30 + 30], vbufB[c], writes=[vbufB[c]])
        return f

    def s_tail(s_):
        def f(c, L):
            dma("sp", convsout_d[:, (c * 2 + s_) * 30:(c * 2 + s_) * 30 + 30], vbuf[:, c, L:L + 30], vbufB[c], reads=[vbufB[c]], final=True)
        return f

    def s_kv(s_):
        def f():
            dma("pool", kT[:], kTs_d[s_].rearrange("(k p) m -> p k m", p=128), kTB, writes=[kTB])
            dma("pool", vv[:], vs_d[s_].rearrange("(k p) n -> p k n", p=128), vvB, writes=[vvB])
        return f

    def hs_fn(s_, reim):
        return lambda pi: Hs[:, pi * 4 + s_ * 2 + reim: pi * 4 + s_ * 2 + reim + 1]

    s_segs = [(s_ * 16, 16, hs_fn(s_, 0), hs_fn(s_, 1), lambda pi: HsB[pi]) for s_ in range(2)]

    def sample_gen(part, bi):
        if part in ("all", "front"):
            dma("sp", Hs[:], h0_d[:, :], hsinB, writes=HsB)
            dma("pool", xb[bi][:, :, 0:32], xs_v, xbB[bi], writes=[xbB[bi]])
        for _ in run_tile(32, xb[bi], xbB[bi], xs_v, s_segs,
                          [(s_ * 16, 16, s_halo(s_), s_tail(s_)) for s_ in range(2)],
                          [(s_ * 16, 16, s_kv(s_)) for s_ in range(2)],
                          ysT.rearrange("(k p) t -> p k t", p=128), is_sample=True, part=part):
            yield
        if part in ("all", "back"):
            hso = statsb.get()
            cp("dve", hso[1][:, 0:64], Hs[:], HsB, [hso[0]])
            dma("sp", hsout_d[:, :], hso[1][:, 0:64], hso[0], reads=[hso[0]], final=True)

    p_segs = [(s * TS, TS, lambda pi: Hre[:, pi:pi + 1], lambda pi: Him[:, pi:pi + 1], lambda pi: HB[pi]) for s in range(T // TS)]
    yT_v = yT.rearrange("(k p) t -> p k t", p=128)
    cur = nxt

    def p_tail(c, L):
        dma("sp", convout_d[:, c * 30:(c + 1) * 30], vbuf[:, c, L:L + 30], vbufB[c], reads=[vbufB[c]], final=True)

    def xload(it, buf_i):
        dma("pool", xb[buf_i][:, :, :], xT_v[:, :, NPRE + it * T: NPRE + (it + 1) * T], xbB[buf_i], writes=[xbB[buf_i]])

    if NT_RUN > 0:
        if NT_RUN > 1:
            xload(1, 1 - cur)
        drain(front_p(xb[cur], xbB[cur], p_tail if NT_RUN == 1 and NT == 1 else None))
    for it in range(NT_RUN):
        back = run_tile(T, xb[cur], xbB[cur], xT_v[:, :, NPRE + it * T: NPRE + (it + 1) * T], p_segs,
                        [(0, T, None, None)], [(0, T, None)], yT_v[:, :, it * T:(it + 1) * T], part="back")
        if it + 1 < NT_RUN:
            nb = 1 - cur
            fr = front_p(xb[nb], xbB[nb], p_tail if it + 1 == NT - 1 else None)
            if it + 2 < NT_RUN:
                xload(it + 2, cur)
            if PIPE:
                merge(back, fr)
            else:
                drain(back)
                drain(fr)
        else:
            if RUN_SAMPLE and PIPE:
                merge(back, sample_gen("front", 1 - cur))
            else:
                drain(back)
        cur = 1 - cur
    hst, hst2 = tiny.get(), tiny.get()
    cp("dve", hst[1][:], Hre[:], HB, [hst[0]])
    cp("dve", hst2[1][:], Him[:], HB, [hst2[0]])
    dma("sp", hout_d[:, 0:16], hst[1][:], hst[0], reads=[hst[0]], final=True)
    dma("sp", hout_d[:, 16:32], hst2[1][:], hst2[0], reads=[hst2[0]], final=True)

    if RUN_SAMPLE:
        if PIPE and NT_RUN > 0:
            drain(sample_gen("back", cur))
        else:
            drain(sample_gen("all", cur))

    P.emit(st)
    st.close()
    return nc


_NC_CACHE = {}


def _mode_major(a):
    sh = a.shape
    a = a.reshape((16, 2, 64) + sh[2:])
    return np.ascontiguousarray(np.moveaxis(a, 0, 2).reshape((128, 16) + sh[2:]))


def kernel(x_prompt, x_sample, state_ssm_re, state_ssm_im, cache_conv, cache_mem_k, cache_mem_v,
           mem_prompt, w_in, ssm_a_re, ssm_a_im, ssm_log_dt, ssm_b_re, ssm_b_im, ssm_c_re, ssm_c_im,
           ssm_d, glu_w, glu_b, conv_w, conv_b, conv_ln_g, conv_ln_b, w_out, ln1_g, ln1_b,
           mem_w_q, mem_w_k, mem_w_v, mem_w_o, ln2_g, ln2_b,
           mlp_w1, mlp_b1, mlp_w2, mlp_b2, ln3_g, ln3_b):
    f = lambda a: np.ascontiguousarray(np.asarray(a, dtype=np.float32))
    x_prompt, x_sample = f(x_prompt), f(x_sample)
    col = lambda v, n: f(v).reshape(n, 128).T
    pk = np.zeros((128, NPK), np.float32)

    def put(name, arr):
        pk[:, PK[name][0]:PK[name][1]] = arr
    put("ln1_g", col(ln1_g[0], 8)); put("ln1_b", col(ln1_b[0], 8))
    put("ln2_g", col(ln2_g[0], 8)); put("ln2_b", col(ln2_b[0], 8))
    put("ln3_g", col(ln3_g[0], 8)); put("ln3_b", col(ln3_b[0], 8))
    put("b1", col(mlp_b1[0], 32)); put("b2", col(mlp_b2[0], 8))
    put("glu_b", col(glu_b[0], 4)); put("ssm_d", col(ssm_d[0], 4))
    put("conv_b", col(conv_b[0], 4)); put("cln_g", col(conv_ln_g[0], 4)); put("cln_b", col(conv_ln_b[0], 4))
    cw = f(conv_w[0]).T.reshape(4, 128, 31).transpose(1, 0, 2).reshape(128, 124)
    put("conv_w", cw)
    put("a_re", _mode_major(f(ssm_a_re[0])[:, :, None])[:, :, 0])
    put("a_im", _mode_major(f(ssm_a_im[0])[:, :, None])[:, :, 0])
    ldt = np.repeat(f(ssm_log_dt[0])[:, None], 64, axis=1)
    put("logdt", _mode_major(ldt[:, :, None])[:, :, 0])
    put("tidx", np.tile(np.arange(128, dtype=np.float32)[None, :], (128, 1)))
    put("eidx", (127.0 - np.arange(128, dtype=np.float32))[:, None])
    rows = np.stack([f(ssm_a_re[0]).reshape(-1), f(ssm_a_im[0]).reshape(-1), ldt.reshape(-1)]).astype(np.float32)
    BT = np.zeros((2, 128, 2048), np.float32)
    CT = np.zeros((2, 128, 2048), np.float32)
    Bx = np.zeros((2, 128, 512), np.float32)
    for ri, (bsrc, csrc) in enumerate(((f(ssm_b_re[0]), f(ssm_c_re[0])), (f(ssm_b_im[0]), f(ssm_c_im[0])))):
        for g in range(32):
            pi, gp, gl = g // 2, g % 2, g % 8
            BT[ri, gl * 16:(gl + 1) * 16, pi * 128 + gp * 64: pi * 128 + gp * 64 + 64] = bsrc[g].T
            CT[ri, gp * 64:(gp + 1) * 64, pi * 128 + gl * 16: pi * 128 + gl * 16 + 16] = csrc[g].T
            Bx[ri, gp * 64:(gp + 1) * 64, pi * 32 + gp * 16: pi * 32 + gp * 16 + 16] = bsrc[g]
    shared = dict(w_in=f(w_in[0]), glu_w=f(glu_w[0]), w_out=f(w_out[0]), w_q=f(mem_w_q[0]), w_k=f(mem_w_k[0]),
                  w_v=f(mem_w_v[0]), w_o=f(mem_w_o[0]), w1=f(mlp_w1[0]), w2=f(mlp_w2[0]), pk=pk, rows=rows, BT=BT, CT=CT, Bx=Bx)
    xTs = [np.ascontiguousarray(x_prompt[b].T) for b in range(2)]
    in_maps = []
    for c in CORES:
        b, j = c // 4, c % 4
        xin = np.zeros((D, NPRE + SEG), np.float32)
        npre = j * SEG
        xin[:, NPRE - npre: NPRE + SEG] = xTs[b][:, 0:(j + 1) * SEG]
        ss = [2 * c, 2 * c + 1]
        xs = np.concatenate([x_sample[s].T for s in ss], axis=1)
        h0 = np.zeros((128, 16, 2, 2), np.float32)
        cc = np.zeros((128, 4, 2, 30), np.float32)
        for si, s in enumerate(ss):
            h0[:, :, si, 0] = _mode_major(f(state_ssm_re[0, s])[:, :, None])[:, :, 0]
            h0[:, :, si, 1] = _mode_major(f(state_ssm_im[0, s])[:, :, None])[:, :, 0]
            cc[:, :, si, :] = f(cache_conv[0, s]).T.reshape(4, 128, 30).transpose(1, 0, 2)
        kTs = np.stack([f(cache_mem_k[0, s]).reshape(256, D).T for s in ss])
        vs = np.stack([f(cache_mem_v[0, s]).reshape(256, D) for s in ss])
        m = dict(shared)
        m.update(xT=xin, xsT=np.ascontiguousarray(xs), h0=h0.reshape(128, 64), convc=cc.reshape(128, 240),
                 kTs=np.ascontiguousarray(kTs), vs=np.ascontiguousarray(vs), memT=np.ascontiguousarray(f(mem_prompt[b]).T))
        in_maps.append(m)
    if "nc" not in _NC_CACHE:
        _NC_CACHE["nc"] = build_program()
    res = run_bass_kernel_spmd(_NC_CACHE["nc"], in_maps, core_ids=list(range(len(CORES))))
    R = {c: res.results[i] for i, c in enumerate(CORES)}

    def unmode(a):
        return a.reshape(2, 64, 16).transpose(2, 0, 1).reshape(32, 64)
    y_prompt = np.zeros((2, 16384, D), np.float32)
    y_sample = np.zeros((16, 16, D), np.float32)
    p_re = np.zeros((1, 2, 32, 64), np.float32)
    p_im = np.zeros((1, 2, 32, 64), np.float32)
    p_conv = np.zeros((1, 2, 30, 512), np.float32)
    p_mk = np.zeros((1, 2, 256, 4, 256), np.float32)
    p_mv = np.zeros((1, 2, 256, 4, 256), np.float32)
    s_re = np.zeros((1, 16, 32, 64), np.float32)
    s_im = np.zeros((1, 16, 32, 64), np.float32)
    s_conv = np.zeros((1, 16, 30, 512), np.float32)
    for c in CORES:
        b, j = c // 4, c % 4
        r = R[c]
        y_prompt[b, j * SEG:(j + 1) * SEG, :] = r["yT"].T
        ys = r["ysT"].T
        hs = r["hsout"].reshape(128, 16, 2, 2)
        cs = r["convsout"].reshape(128, 4, 2, 30)
        for si in range(2):
            s = 2 * c + si
            y_sample[s] = ys[si * 16:(si + 1) * 16]
            s_re[0, s] = unmode(hs[:, :, si, 0])
            s_im[0, s] = unmode(hs[:, :, si, 1])
            s_conv[0, s] = cs[:, :, si, :].transpose(1, 0, 2).reshape(512, 30).T
        if j == 3:
            ho = r["hout"].reshape(128, 2, 16)
            p_re[0, b] = unmode(ho[:, 0, :])
            p_im[0, b] = unmode(ho[:, 1, :])
            p_conv[0, b] = r["convout"].reshape(128, 4, 30).transpose(1, 0, 2).reshape(512, 30).T
        if j == 0:
            p_mk[0, b] = r["kout"].reshape(256, 4, 256)
            p_mv[0, b] = r["vout"].reshape(256, 4, 256)
    return (y_prompt, y_sample, p_re, p_im, p_conv, p_mk, p_mv, s_re, s_im, s_conv)
```

```python
import os
from contextlib import ExitStack
import numpy as np
import concourse.bass as bass
import concourse.mybir as mybir
from concourse.bass_utils import run_bass_kernel_spmd

F32 = mybir.dt.float32
BF16 = mybir.dt.bfloat16
AF = mybir.ActivationFunctionType
ALU = mybir.AluOpType
AX = mybir.AxisListType

D = 1024
NCORE = 8
SEG = 4096
NPRE = 3 * SEG
T = 256
NT = SEG // T
TS = 128
LN_EPS = 1e-5
ALPHA = 2.0 ** 0.25
MAGIC = 12582912.0
TWO_PI = float(2 * np.pi)
PI = float(np.pi)

PK = {}
_o = 0
for _n, _w in [("ln1_g", 8), ("ln1_b", 8), ("ln2_g", 8), ("ln2_b", 8), ("ln3_g", 8), ("ln3_b", 8),
               ("b1", 32), ("b2", 8), ("glu_b", 4), ("ssm_d", 4), ("conv_b", 4), ("cln_g", 4), ("cln_b", 4),
               ("conv_w", 124), ("a_re", 16), ("a_im", 16), ("logdt", 16), ("tidx", 128), ("eidx", 1), ("eidx2", 1), ("pad", 2)]:
    PK[_n] = (_o, _o + _w)
    _o += _w
NPK = _o

SYNC_SAME = {e: (e in os.environ.get("K_SYNC", "act,dve,pool").split(",")) for e in ("act", "dve", "pool", "pe", "sp")}
NT_RUN = int(os.environ.get("K_NT", NT))
NGST = int(os.environ.get("K_NGST", "8"))
PIPE = bool(int(os.environ.get("K_PIPE", "1")))
GRP = bool(int(os.environ.get("K_GRP", "1")))
PFG2 = bool(int(os.environ.get("K_PFG2", "1")))
DELAY_Y = bool(int(os.environ.get("K_DY", "1")))
MRA = int(os.environ.get("K_MRA", "2"))
MRB = int(os.environ.get("K_MRB", "1"))
SSM_POST = os.environ.get("K_SSMPOST", "dve")
CONV_ENG = os.environ.get("K_CONV", "dve")
SQ_ENG = os.environ.get("K_SQ", "act")
SQ_MOD = int(os.environ.get("K_SQMOD", "2"))
CONV_POOL_CH = int(os.environ.get("K_CPC", "0"))
PF_BANKS = bool(int(os.environ.get("K_PFB", "1")))
RUN_SAMPLE = bool(int(os.environ.get("K_SAMPLE", "1")))
NPB_RUN = int(os.environ.get("K_NPB", NPRE // T))
RUN_PREP = bool(int(os.environ.get("K_PREP", "1")))
RUN_KV = bool(int(os.environ.get("K_KV", "1")))
RUN_PC = os.environ.get("K_PC", "all")
KVD = os.environ.get("K_KVD", "111")
PC_ROWS = int(os.environ.get("K_PCR", "128"))
PC_COLS = int(os.environ.get("K_PCC", "1024"))
CORES = [int(x) for x in os.environ.get("K_CORES", "0,1,2,3,4,5,6,7").split(",")]


class Ev:
    __slots__ = ("eng", "sem", "value", "op", "group")

    def __init__(self, eng):
        self.eng = eng
        self.sem = None
        self.value = None
        self.op = None
        self.group = None


class Buf:
    __slots__ = ("name", "w", "r", "dma_sem", "dma_count", "group")

    def __init__(self, name, group=None):
        self.name = name
        self.w = None
        self.r = []
        self.dma_sem = None
        self.dma_count = 0
        self.group = group


class DmaGroup:
    def __init__(self, name):
        self.name = name
        self.sem = None
        self.count = 0


class Op:
    __slots__ = ("eng", "fn", "waits", "ev", "is_dma", "signals")

    def __init__(self, eng, fn, waits, ev, is_dma):
        self.eng = eng
        self.fn = fn
        self.waits = waits
        self.ev = ev
        self.is_dma = is_dma
        self.signals = is_dma


class Prog:
    ENGINES = ("pe", "act", "dve", "pool", "sp")

    def __init__(self, nc):
        self.nc = nc
        self.ops = {e: [] for e in self.ENGINES}
        self.all_ops = []
        self.dma_bufs = []
        self.groups = []
        self.final_evs = []

    def group(self, name):
        g = DmaGroup(name)
        self.groups.append(g)
        return g

    def _deps(self, ev, reads, writes):
        waits = []
        for b in reads:
            if b.w is not None:
                waits.append(b.w)
        for b in writes:
            if b.w is not None:
                waits.append(b.w)
            waits.extend(b.r)
        for b in reads:
            b.r.append(ev)
        for b in writes:
            b.w = ev
            b.r = []
        out = []
        seen = set()
        for w in waits:
            if w is ev or id(w) in seen:
                continue
            seen.add(id(w))
            out.append(w)
        return out

    def op(self, eng, fn, reads=(), writes=()):
        ev = Ev(eng)
        waits = self._deps(ev, reads, writes)
        o = Op(eng, fn, waits, ev, False)
        ev.op = o
        self.ops[eng].append(o)
        self.all_ops.append(o)
        return o

    def dma(self, eng, fn, key, reads=(), writes=(), final=False):
        ev = Ev("dma")
        waits = self._deps(ev, reads, writes)
        if key.group is not None:
            ev.group = key.group
            key.group.count += 1
        else:
            if key.dma_count == 0:
                self.dma_bufs.append(key)
            key.dma_count += 1
            ev.sem = key
            ev.value = 16 * key.dma_count
        o = Op(eng, fn, waits, ev, True)
        ev.op = o
        self.ops[eng].append(o)
        self.all_ops.append(o)
        if final:
            self.final_evs.append(ev)
        return o

    def emit(self, stack):
        nc = self.nc
        for o in self.all_ops:
            for w in o.waits:
                if w.op is not None and not w.op.is_dma:
                    if w.eng == o.eng and not SYNC_SAME[o.eng]:
                        continue
                    w.op.signals = True
        esem = {}
        for e in ("pe", "act", "dve", "pool"):
            esem[e] = stack.enter_context(nc.semaphore("s_" + e))
            cnt = 0
            for o in self.ops[e]:
                if o.is_dma:
                    continue
                if o.signals:
                    cnt += 1
                    o.ev.sem = esem[e]
                    o.ev.value = cnt
        for b in self.dma_bufs:
            b.dma_sem = stack.enter_context(nc.semaphore("d_" + b.name))
        for g in self.groups:
            if g.count:
                g.sem = stack.enter_context(nc.semaphore("g_" + g.name))

        def resolve(ev):
            if ev.group is not None:
                return ev.group.sem, 16 * ev.group.count
            if isinstance(ev.sem, Buf):
                return ev.sem.dma_sem, ev.value
            return ev.sem, ev.value

        block = stack.enter_context(nc.Block())
        handles = {"pe": "tensor", "act": "scalar", "dve": "vector", "pool": "gpsimd", "sp": "sync"}
        final_evs = self.final_evs

        def make(e):
            ops = self.ops[e]

            def body(eng):
                waited = {}
                for o in ops:
                    for w in o.waits:
                        if w.eng == e and not w.op.is_dma and not SYNC_SAME[e]:
                            continue
                        sem, val = resolve(w)
                        assert sem is not None and val is not None, (e, w.eng)
                        k = id(sem)
                        if waited.get(k, 0) >= val:
                            continue
                        waited[k] = val
                        eng.wait_ge(sem, val)
                    ins = o.fn(eng)
                    if o.is_dma:
                        sem, _ = resolve(o.ev)
                        ins.then_inc(sem, 16)
                    elif o.signals:
                        ins.then_inc(o.ev.sem, 1)
                if e == "sp":
                    for ev in final_evs:
                        sem, val = resolve(ev)
                        if waited.get(id(sem), 0) >= val:
                            continue
                        waited[id(sem)] = val
                        eng.wait_ge(sem, val)
            return body

        for e in self.ENGINES:
            if self.ops[e] or e == "sp":
                getattr(block, handles[e])(make(e))


class Rot:
    def __init__(self, st, nc, name, shape, dtype, n, psum=False):
        self.items = []
        for i in range(n):
            alloc = nc.psum_tensor if psum else nc.sbuf_tensor
            t = st.enter_context(alloc(f"rt_{name}{i}", shape, dtype))
            self.items.append((Buf(f"{name}{i}"), t))
        self.i = 0

    def get(self):
        it = self.items[self.i % len(self.items)]
        self.i += 1
        return it


def build_program():
    nc = bass.Bass("TRN2", target_bir_lowering=False)
    st = ExitStack()
    P = Prog(nc)

    def din(name, shape):
        return nc.dram_tensor(name, shape, F32, kind="ExternalInput").ap()

    def dout(name, shape):
        return nc.dram_tensor(name, shape, F32, kind="ExternalOutput").ap()

    def dscr(name, shape):
        return nc.dram_tensor(name, shape, BF16, kind="Internal").ap()

    xT = din("xT", [D, NPRE + SEG])
    xsT = din("xsT", [D, 32])
    h0_d = din("h0", [128, 64])
    convc_d = din("convc", [128, 240])
    kTs_d = din("kTs", [2, D, 256])
    vs_d = din("vs", [2, 256, D])
    memT_d = din("memT", [D, 256])
    w_in_d = din("w_in", [D, 1536])
    glu_w_d = din("glu_w", [512, 512])
    w_out_d = din("w_out", [D, D])
    w_q_d = din("w_q", [D, D])
    w_k_d = din("w_k", [D, D])
    w_v_d = din("w_v", [D, D])
    w_o_d = din("w_o", [D, D])
    w1_d = din("w1", [D, 4096])
    w2_d = din("w2", [4096, D])
    pk_d = din("pk", [128, NPK])
    rows_d = din("rows", [3, 2048])
    BT_d = din("BT", [2, 128, 2048])
    CT_d = din("CT", [2, 128, 2048])
    Bx_d = din("Bx", [2, 128, 512])

    yT = dout("yT", [D, SEG])
    ysT = dout("ysT", [D, 32])
    hout_d = dout("hout", [128, 32])
    convout_d = dout("convout", [128, 120])
    kout_d = dout("kout", [256, D])
    vout_d = dout("vout", [256, D])
    hsout_d = dout("hsout", [128, 64])
    convsout_d = dout("convsout", [128, 240])

    w_in_b = dscr("w_in_b", [D, 1536])
    w_out_b = dscr("w_out_b", [D, D])
    w_q_b = dscr("w_q_b", [D, D])
    w_k_b = dscr("w_k_b", [D, D])
    w_v_b = dscr("w_v_b", [D, D])
    w_o_b = dscr("w_o_b", [D, D])
    w1_b = dscr("w1_b", [D, 4096])
    w2_b = dscr("w2_b", [4096, D])

    def sb(name, shape, dt=F32):
        return st.enter_context(nc.sbuf_tensor("sb_" + name, shape, dt))

    pk = sb("pk", [128, NPK])
    cgrp = P.group("consts")
    pkB = Buf("pk", group=cgrp)
    ones_m = sb("ones_m", [128, 128], BF16)
    ones_c = sb("ones_c", [128, 128], BF16)
    ones_1 = sb("ones_1", [128, 128], BF16)
    onesB = Buf("ones")
    R0 = sb("R0", [128, 8, T])
    R1 = sb("R1", [128, 8, T])
    xnb = sb("xnb", [128, 8, T], BF16)
    R0B = [Buf(f"R0_{c}") for c in range(8)]
    R1B = [Buf(f"R1_{c}") for c in range(8)]
    xnbB = [Buf(f"xnb_{c}") for c in range(8)]
    ob, obB = xnb, xnbB
    xb = [sb(f"xb{i}", [128, 8, T], BF16) for i in range(2)]
    xbB = [Buf(f"xb{i}") for i in range(2)]
    NRING = 3
    ring = [sb(f"ring{i}", [128, 4096], BF16) for i in range(NRING)]
    ringB = [Buf(f"ring{i}") for i in range(NRING)]
    ring_i = [0]
    fring = sb("fring", [128, 4096], BF16)
    fringB = Buf("fring")
    glu_sb = sb("glu_sb", [128, 4, 512], BF16)
    gluB = Buf("glu")
    u32 = sb("u32", [128, 4, T])
    ub = sb("ub", [128, 4, T], BF16)
    u32B = [Buf(f"u32_{c}") for c in range(4)]
    ubB = [Buf(f"ub_{c}") for c in range(4)]
    vbuf = sb("vbuf", [128, 4, 30 + T])
    vbufB = [Buf(f"vbuf_{c}") for c in range(4)]
    cacc = sb("cacc", [128, 4, T])
    caccB = [Buf(f"cacc_{c}") for c in range(4)]
    mixin = sb("mixin", [128, 8, T], BF16)
    mixB = [Buf(f"mix_{c}") for c in range(8)]
    z32 = sb("z32", [128, 4, T])
    zb = sb("zb", [128, 4, T], BF16)
    z32B = [Buf(f"z32_{c}") for c in range(4)]
    zbB = [Buf(f"zb_{c}") for c in range(4)]
    hid = sb("hid", [128, 32, T], BF16)
    hidB = [Buf(f"hid_{c}") for c in range(32)]
    qb, qbB = hid, hidB
    kT = sb("kT", [128, 8, 256], BF16)
    kTB = Buf("kT")
    vv = sb("vv", [128, 2, D], BF16)
    vvB = Buf("vv")
    costab = sb("costab", [128, 16, TS])
    sintab = sb("sintab", [128, 16, TS])
    rtab = sb("rtab", [128, 16, TS])
    tabB = Buf("tabs")
    BreT = sb("BreT", [128, 16, 128], BF16)
    BimT = sb("BimT", [128, 16, 128], BF16)
    CreT = sb("CreT", [128, 16, 128], BF16)
    CimT = sb("CimT", [128, 16, 128], BF16)
    Pre = sb("Pre", [128, 16, 128], BF16)
    Pim = sb("Pim", [128, 16, 128], BF16)
    lhsB = Buf("ssm_lhs")
    P2re = hid[:, 0:8, :].rearrange("p a b -> p (a b)").rearrange("p (c d) -> p c d", c=16)
    P2im = hid[:, 8:16, :].rearrange("p a b -> p (a b)").rearrange("p (c d) -> p c d", c=16)
    p2B = hidB[0:16]
    Bxr = sb("Bxr", [128, 512])
    Bxi = sb("Bxi", [128, 512])
    BxB = Buf("Bx")
    a128r = sb("a128r", [128, 16])
    a128i = sb("a128i", [128, 16])
    a128B = Buf("a128")
    Hre = sb("Hre", [128, 16])
    Him = sb("Him", [128, 16])
    HB = [Buf(f"H_{p}") for p in range(16)]
    Hs = sb("Hs", [128, 64])
    HsB = [Buf(f"Hs_{p}") for p in range(16)]

    s16 = Rot(st, nc, "s16_", [128, T], BF16, 4)
    f32t = Rot(st, nc, "f32t_", [128, T], F32, 5)
    sst = Rot(st, nc, "sst_", [128, 32 if GRP else TS], F32, 10)
    hbf = Rot(st, nc, "hbf_", [128, 32 if GRP else TS], BF16, 4)
    gst = Rot(st, nc, "gst_", [128, 512], F32, NGST)
    ctmp = Rot(st, nc, "ctmp_", [128, T], F32, 2)
    ghb = Rot(st, nc, "ghb_", [128, 512], BF16, 4)
    tiny = Rot(st, nc, "tiny_", [128, 16], F32, 8)
    pTr = Rot(st, nc, "pT_", [128, T], BF16, 4)
    statsb = Rot(st, nc, "stat_", [128, T], F32, 5)
    psum = Rot(st, nc, "ps", [128, 512], F32, 3, psum=True)
    ypsR = Rot(st, nc, "yps", [128, 512], F32, 1, psum=True)
    buR = Rot(st, nc, "bups", [128, 512], F32, 2, psum=True)
    lnR = Rot(st, nc, "lnps", [128, 512], F32, 2, psum=True)

    def big(i):
        src, bufs = (R0, R0B) if i < 4 else (R1, R1B)
        j = (i % 4) * 2
        return [bufs[j], bufs[j + 1]], src[:, j:j + 2, :].rearrange("p a b -> p (a b)")

    pkc = lambda name, i=0, n=1: pk[:, PK[name][0] + i: PK[name][0] + i + n]

    def tt(eng, out, a, b, op, reads, writes):
        P.op(eng, lambda e: e.tensor_tensor(out=out, in0=a, in1=b, op=op), reads, writes)

    def ts(eng, out, a, s1, s2, op0, op1, reads, writes):
        if op1 is None:
            P.op(eng, lambda e: e.tensor_scalar(out=out, in0=a, scalar1=s1, scalar2=None, op0=op0), reads, writes)
        else:
            P.op(eng, lambda e: e.tensor_scalar(out=out, in0=a, scalar1=s1, scalar2=s2, op0=op0, op1=op1), reads, writes)

    def stt(eng, out, a, s, b, op0, op1, reads, writes):
        P.op(eng, lambda e: e.scalar_tensor_tensor(out=out, in0=a, scalar=s, in1=b, op0=op0, op1=op1), reads, writes)

    def act(out, in_, func, reads, writes, scale=None, bias=None):
        kw = {}
        if scale is not None:
            kw["scale"] = scale
        if bias is not None:
            kw["bias"] = bias
        P.op("act", lambda e: e.activation(out=out, in_=in_, func=func, **kw), reads, writes)

    def cp(eng, out, in_, reads, writes):
        if eng == "act":
            act(out, in_, AF.Copy, reads, writes)
        else:
            P.op(eng, lambda e: e.tensor_copy(out=out, in_=in_), reads, writes)

    def recip(out, in_, reads, writes):
        P.op("dve", lambda e: e.reciprocal(out=out, in_=in_), reads, writes)

    def mm(out, pairs, reads, writes):
        n = len(pairs)

        def fn(e):
            ins = None
            for i, (l, r) in enumerate(pairs):
                ins = e.matmul(out, l, r, start=(i == 0), stop=(i == n - 1))
            return ins
        P.op("pe", fn, reads, writes)

    def mm1(out, l, r, start, stop, reads, writes):
        P.op("pe", lambda e: e.matmul(out, l, r, start=start, stop=stop), reads, writes)

    def dma(eng, out, in_, key, reads=(), writes=(), final=False):
        P.dma(eng, lambda e: e.dma_start(out=out, in_=in_), key, reads=reads, writes=writes, final=final)

    def memset(eng, ap, val, writes):
        P.op(eng, lambda e: e.memset(ap, val), (), writes)

    def range_reduce(out, in_, shift, reads, writes, tmp, tmpB):
        src = in_
        rd = list(reads)
        if shift != 0.0:
            ts("dve", out, in_, shift, None, ALU.add, None, reads, writes)
            src = out
            rd = list(writes)
        ts("dve", tmp, src, 1.0 / TWO_PI, MAGIC, ALU.mult, ALU.add, rd, tmpB)
        ts("dve", tmp, tmp, MAGIC, -TWO_PI, ALU.subtract, ALU.mult, tmpB, tmpB)
        tt("dve", out, src, tmp, ALU.add, rd + list(tmpB), writes)
        ts("dve", out, out, PI, -PI, ALU.min, ALU.max, writes, writes)

    dma("sp", pk[:], pk_d[:, :], pkB, writes=[pkB])
    dma("pool", glu_sb[:], glu_w_d.rearrange("(k p) n -> p k n", p=128), gluB, writes=[gluB])
    memset("dve", ones_m[:], 1.0 / 1024.0, [onesB])
    memset("dve", ones_c[:], 1.0 / 512.0, [onesB])
    memset("dve", ones_1[:], 1.0, [onesB])


    pcsB = [Buf(f"pcs{i}") for i in range(NRING)]
    pclB = [Buf(f"pcl{i}") for i in range(NRING)]

    def precast_gen(bl, dst, src, nrows, colmap=None):
        ncols = src.shape[1]
        nk = nrows // 128
        sv = src.rearrange("(k p) n -> p k n", p=128)
        dv = dst.rearrange("(k p) n -> p k n", p=128)
        if colmap is None:
            colmap = [(c0, c0, min(1024, ncols - c0)) for c0 in range(0, ncols, 1024)]
        for (d0, s0, n) in colmap:
            kstep = max(1, min(nk, 4096 // n))
            for k0 in range(0, nk, kstep):
                i = ring_i[0] % NRING
                ring_i[0] += 1
                view = ring[i][:, 0:kstep * n].rearrange("p (a b) -> p a b", a=kstep)
                dma("pool", view, sv[:, k0:k0 + kstep, s0:s0 + n], pclB[i], writes=[ringB[i]])
                dma("sp", dv[:, k0:k0 + kstep, d0:d0 + n], view, pcsB[i], reads=[ringB[i]], writes=[bl[i]])
                yield

    def precast(name, dst, src, nrows, colmap=None):
        bl = [Buf(f"pcb_{name}{i}") for i in range(NRING)]
        for _ in precast_gen(bl, dst, src, nrows, colmap):
            pass
        return bl

    def precast_later(name, dst, src, nrows):
        bl = [Buf(f"pcb_{name}{i}") for i in range(NRING)]
        late_pc.append(precast_gen(bl, dst, src, nrows))
        return bl

    late_pc = []

    win_map = [(0, 0, 512)]
    for i in range(4):
        win_map.append((512 + 256 * i, 512 + 128 * i, 128))
        win_map.append((512 + 256 * i + 128, 1024 + 128 * i, 128))
    pc = {}
    if RUN_PC == "none":
        def precast(name, dst, src, nrows, colmap=None):
            return [Buf("pcb_" + name)]
        precast_later = lambda name, dst, src, nrows: [Buf("pcb_" + name)]
    pc["w_in"] = precast("w_in", w_in_b, w_in_d, D, win_map)
    pc["w_k"] = precast("w_k", w_k_b, w_k_d, D)
    pc["w_v"] = precast("w_v", w_v_b, w_v_d, D)
    pc["w_out"] = precast_later("w_out", w_out_b, w_out_d, D)
    pc["w_q"] = precast_later("w_q", w_q_b, w_q_d, D)
    pc["w_o"] = precast_later("w_o", w_o_b, w_o_d, D)
    pc["w1"] = precast_later("w1", w1_b, w1_d, D)
    pc["w2"] = precast_later("w2", w2_b, w2_d, 4096)

    def ring_load(src_ap, a, b, pcb):
        i = ring_i[0] % NRING
        ring_i[0] += 1
        view = ring[i][:, 0:a * b].rearrange("p (a b) -> p a b", a=a)
        dma("sp", view, src_ap, ringB[i], reads=pcb, writes=[ringB[i]])
        return ringB[i], view

    def wpiece(wb, pcb, n0, n1):
        return ring_load(wb.rearrange("(k p) n -> p k n", p=128)[:, :, n0:n1], 8, n1 - n0, pcb)

    def fpiece(wb, pcb, n0, n1):
        view = fring[:, 0:8 * (n1 - n0)].rearrange("p (a b) -> p a b", a=8)
        dma("sp", view, wb.rearrange("(k p) n -> p k n", p=128)[:, :, n0:n1], fringB, reads=pcb, writes=[fringB])
        return fringB, view

    def disc(A, Bm, L, tmp, rdB, wB):
        T1, T2, T3, T4, T5, T6, T7 = tmp
        act(L, L, AF.Exp, rdB, wB)
        tt("dve", T1, A, L, ALU.mult, wB, wB)
        tt("dve", T5, Bm, L, ALU.mult, wB, wB)
        act(T6, T1, AF.Exp, wB, wB)
        range_reduce(T2, T5, 0.0, wB, wB, T7, wB)
        act(T2, T2, AF.Sin, wB, wB)
        range_reduce(T3, T5, PI / 2, wB, wB, T7, wB)
        act(T3, T3, AF.Sin, wB, wB)
        tt("dve", T3, T6, T3, ALU.mult, wB, wB)
        tt("dve", T2, T6, T2, ALU.mult, wB, wB)
        ts("dve", T3, T3, -1.0, None, ALU.add, None, wB, wB)
        tt("dve", T6, A, A, ALU.mult, wB, wB)
        tt("dve", T7, Bm, Bm, ALU.mult, wB, wB)
        tt("dve", T6, T6, T7, ALU.add, wB, wB)
        recip(T6, T6, wB, wB)
        tt("dve", L, T3, A, ALU.mult, wB, wB)
        tt("dve", T4, T2, Bm, ALU.mult, wB, wB)
        tt("dve", L, L, T4, ALU.add, wB, wB)
        tt("dve", L, L, T6, ALU.mult, wB, wB)
        tt("dve", T4, T2, A, ALU.mult, wB, wB)
        tt("dve", T3, T3, Bm, ALU.mult, wB, wB)
        tt("dve", T4, T4, T3, ALU.subtract, wB, wB)
        tt("dve", T4, T4, T6, ALU.mult, wB, wB)
        return dict(x1=T1, ang=T5, c_re=L, c_im=T4)

    if RUN_PREP:
        mB_ = [Buf("modeprep")]
        mA, mBm, mL = sb("mA", [128, 16]), sb("mBm", [128, 16]), sb("mL", [128, 16])
        mT = [sb(f"mT{i}", [128, 16]) for i in range(7)]
        cp("dve", mA[:], pkc("a_re", 0, 16), [pkB], mB_)
        cp("dve", mBm[:], pkc("a_im", 0, 16), [pkB], mB_)
        cp("dve", mL[:], pkc("logdt", 0, 16), [pkB], mB_)
        dm = disc(mA[:], mBm[:], mL[:], [t[:] for t in mT], mB_, mB_)
        m128a, m128m, mr = sb("m128a", [128, 16]), sb("m128m", [128, 16]), sb("mr", [128, 16])
        ts("dve", m128a[:], dm["ang"], 256.0 if PFG2 else 128.0, None, ALU.mult, None, mB_, mB_)
        act(m128m[:], dm["x1"], AF.Exp, mB_, mB_, scale=256.0 if PFG2 else 128.0)
        range_reduce(mT[1][:], m128a[:], 0.0, mB_, mB_, mT[6][:], mB_)
        act(mT[1][:], mT[1][:], AF.Sin, mB_, mB_)
        range_reduce(mT[2][:], m128a[:], PI / 2, mB_, mB_, mT[6][:], mB_)
        act(mT[2][:], mT[2][:], AF.Sin, mB_, mB_)
        tt("dve", a128r[:], m128m[:], mT[2][:], ALU.mult, mB_, [a128B])
        tt("dve", a128i[:], m128m[:], mT[1][:], ALU.mult, mB_ + [a128B], [a128B])
        act(mr[:], dm["x1"], AF.Exp, mB_, mB_)
        for p_ in range(16):
            tg, tg2 = gst.get(), gst.get()
            tg = (tg[0], tg[1][:, 0:TS])
            tg2 = (tg2[0], tg2[1][:, 0:TS])
            ts("dve", tg[1][:], pkc("tidx", 0, TS), dm["ang"][:, p_:p_ + 1], None, ALU.mult, None, [pkB] + mB_, [tg[0]])
            range_reduce(sintab[:, p_, :], tg[1][:], 0.0, [tg[0]], [tabB], tg2[1][:], [tg2[0]])
            act(sintab[:, p_, :], sintab[:, p_, :], AF.Sin, [tabB], [tabB])
            range_reduce(costab[:, p_, :], tg[1][:], PI / 2, [tg[0]], [tabB], tg2[1][:], [tg2[0]])
            act(costab[:, p_, :], costab[:, p_, :], AF.Sin, [tabB], [tabB])
            ts("dve", rtab[:, p_, :], pkc("tidx", 0, TS), 0.0, mr[:, p_:p_ + 1], ALU.mult, ALU.add, [pkB] + mB_, [tabB])
            memset("dve", rtab[:, p_, 0:1], 0.0, [tabB])
        bxrB, bxr_t = big(0)
        bxiB, bxi_t = big(1)
        dma("sp", bxr_t, Bx_d[0], bxrB[0], writes=bxrB)
        dma("sp", bxi_t, Bx_d[1], bxiB[0], writes=bxiB)
        for p_ in range(16):
            sl = slice(p_ * 32, p_ * 32 + 32)
            cr, ci = dm["c_re"][:, p_:p_ + 1], dm["c_im"][:, p_:p_ + 1]
            ta, tb_ = gst.get(), gst.get()
            ts("dve", ta[1][:, 0:32], bxi_t[:, sl], ci, None, ALU.mult, None, bxiB + mB_, [ta[0]])
            stt("dve", Bxr[:, sl], bxr_t[:, sl], cr, ta[1][:, 0:32], ALU.mult, ALU.subtract, bxrB + mB_ + [ta[0]], [BxB])
            ts("dve", tb_[1][:, 0:32], bxr_t[:, sl], ci, None, ALU.mult, None, bxrB + mB_, [tb_[0]])
            stt("dve", Bxi[:, sl], bxi_t[:, sl], cr, tb_[1][:, 0:32], ALU.mult, ALU.add, bxiB + mB_ + [tb_[0]], [BxB])

        rt = [R0[:, c, :] for c in range(8)] + [R1[:, c, :] for c in range(8)]
        rtB = R0B + R1B
        tmp_tiles = []
        tmpB = []
        for gi_ in range(4):
            gb_, gt_ = gst.items[gi_]
            tmpB.append(gb_)
            tmp_tiles += [gt_[:, 0:256], gt_[:, 256:512]]
        for blk in range(8):
            cs = slice(blk * 256, blk * 256 + 256)
            base_ = (blk % 2) * 7
            inT = rt[base_:base_ + 7]
            inB = rtB[base_:base_ + 7]
            rA, rBm, rL, btr, bti, ctr, cti = inT
            srcs_ = [rows_d[0:1, cs].partition_broadcast(128), rows_d[1:2, cs].partition_broadcast(128),
                     rows_d[2:3, cs].partition_broadcast(128), BT_d[0][:, cs], BT_d[1][:, cs], CT_d[0][:, cs], CT_d[1][:, cs]]
            for k_ in range(7):
                dma("sp", inT[k_], srcs_[k_], inB[k_], writes=[inB[k_]])
            rowB = inB + tmpB
            tmp = tmp_tiles[0:7]
            dr = disc(rA, rBm, rL, tmp, rowB, rowB)
            sA, sB_ = tmp[1], tmp[2]
            s3, s4 = tmp[5], tmp[6]
            osl = lambda t_: t_[:, blk * 2:blk * 2 + 2, :].rearrange("p a b -> p (a b)")
            tt("dve", sA, dr["c_re"], btr, ALU.mult, rowB, rowB)
            tt("dve", sB_, dr["c_im"], bti, ALU.mult, rowB, rowB)
            tt("dve", osl(BreT), sA, sB_, ALU.subtract, rowB, [lhsB])
            tt("dve", sA, dr["c_re"], bti, ALU.mult, rowB, rowB)
            tt("dve", sB_, dr["c_im"], btr, ALU.mult, rowB, rowB)
            tt("dve", osl(BimT), sA, sB_, ALU.add, rowB + [lhsB], [lhsB])
            e_ap = pkc("eidx")
            act(sA, dr["x1"], AF.Exp, rowB + [pkB], rowB, scale=e_ap)
            ts("dve", sB_, dr["ang"], e_ap, None, ALU.mult, None, rowB + [pkB], rowB)
            range_reduce(s3, sB_, 0.0, rowB, rowB, s4, rowB)
            act(s3, s3, AF.Sin, rowB, rowB)
            tt("dve", osl(Pim), sA, s3, ALU.mult, rowB + [lhsB], [lhsB])
            range_reduce(s3, sB_, PI / 2, rowB, rowB, s4, rowB)
            act(s3, s3, AF.Sin, rowB, rowB)
            tt("dve", osl(Pre), sA, s3, ALU.mult, rowB + [lhsB], [lhsB])
            if PFG2:
                e2_ap = pkc("eidx2")
                act(sA, dr["x1"], AF.Exp, rowB + [pkB], rowB, scale=e2_ap)
                ts("dve", sB_, dr["ang"], e2_ap, None, ALU.mult, None, rowB + [pkB], rowB)
                range_reduce(s3, sB_, 0.0, rowB, rowB, s4, rowB)
                act(s3, s3, AF.Sin, rowB, rowB)
                tt("dve", osl(P2im), sA, s3, ALU.mult, rowB + p2B, p2B)
                range_reduce(s3, sB_, PI / 2, rowB, rowB, s4, rowB)
                act(s3, s3, AF.Sin, rowB, rowB)
                tt("dve", osl(P2re), sA, s3, ALU.mult, rowB + p2B, p2B)
            cp("dve", osl(CreT), ctr, rowB + [lhsB], [lhsB])
            ts("dve", osl(CimT), cti, -1.0, None, ALU.mult, None, rowB + [lhsB], [lhsB])

    def layer_norm(nch, srcs, srcB, ones_ap, Tn, g_name, b_name, emit_out):
        pm, pe2 = lnR.get(), lnR.get()
        for c in range(nch):
            s1, s2 = s16.get(), s16.get()
            act(s1[1][:, :Tn], srcs[c], AF.Copy, [srcB[c]], [s1[0]])
            act(s2[1][:, :Tn], srcs[c], AF.Square, [srcB[c]], [s2[0]])
            mm1(pm[1][:, :Tn], ones_ap, s1[1][:, :Tn], c == 0, c == nch - 1, [s1[0], onesB], [pm[0]])
            mm1(pe2[1][:, :Tn], ones_ap, s2[1][:, :Tn], c == 0, c == nch - 1, [s2[0], onesB], [pe2[0]])
        mean, var, nmr = statsb.get(), statsb.get(), statsb.get()
        cp("act", mean[1][:, :Tn], pm[1][:, :Tn], [pm[0]], [mean[0]])
        tt("dve", var[1][:, :Tn], mean[1][:, :Tn], mean[1][:, :Tn], ALU.mult, [mean[0]], [var[0]])
        tt("dve", var[1][:, :Tn], pe2[1][:, :Tn], var[1][:, :Tn], ALU.subtract, [pe2[0], var[0]], [var[0]])
        ts("dve", var[1][:, :Tn], var[1][:, :Tn], LN_EPS, None, ALU.add, None, [var[0]], [var[0]])
        act(var[1][:, :Tn], var[1][:, :Tn], AF.Sqrt, [var[0]], [var[0]])
        recip(var[1][:, :Tn], var[1][:, :Tn], [var[0]], [var[0]])
        stt("dve", nmr[1][:, :Tn], mean[1][:, :Tn], -1.0, var[1][:, :Tn], ALU.mult, ALU.mult, [mean[0], var[0]], [nmr[0]])
        for c in range(nch):
            t_ = f32t.get()
            tt("dve", t_[1][:, :Tn], srcs[c], var[1][:, :Tn], ALU.mult, [srcB[c], var[0]], [t_[0]])
            tt("dve", t_[1][:, :Tn], t_[1][:, :Tn], nmr[1][:, :Tn], ALU.add, [t_[0], nmr[0]], [t_[0]])
            emit_out(c, t_[1][:, :Tn], t_[0], pkc(g_name, c), pkc(b_name, c))

    def ln_to_resid(Tn, g_name, b_name):
        def emit_out(c, t_ap, tB, g_ap, b_ap):
            act(R1[:, c, :Tn], t_ap, AF.Identity, [tB, pkB], [R1B[c]], scale=g_ap, bias=b_ap)
            act(xnb[:, c, :Tn], t_ap, AF.Identity, [tB, pkB], [xnbB[c]], scale=g_ap, bias=b_ap)
        layer_norm(8, [R0[:, c, :Tn] for c in range(8)], R0B, ones_m[:], Tn, g_name, b_name, emit_out)

    def ssm_segment(pi, col0, L, Hr_ap, Hi_ap, HBuf, ypsum, first, last):
        c = pi // 4
        bu = psum.get()
        bre, bim = bu[1][:, 0:L], bu[1][:, 256:256 + L]
        mm1(bre, BreT[:, pi, :], ub[:, c, col0:col0 + L], True, True, [lhsB, ubB[c]], [bu[0]])
        mm1(bim, BimT[:, pi, :], ub[:, c, col0:col0 + L], True, True, [lhsB, ubB[c]], [bu[0]])
        cosA, sinA, rA = costab[:, pi, 0:L], sintab[:, pi, 0:L], rtab[:, pi, 0:L]
        cos1, sin1 = costab[:, pi, 1:2], sintab[:, pi, 1:2]
        g0, tq = tiny.get(), tiny.get()
        tt("dve", tq[1][:, 0:1], Hi_ap, sin1, ALU.mult, [HBuf, tabB], [tq[0]])
        tt("dve", tq[1][:, 1:2], Hr_ap, cos1, ALU.mult, [HBuf, tabB, tq[0]], [tq[0]])
        tt("dve", tq[1][:, 2:3], Hr_ap, sin1, ALU.mult, [HBuf, tabB, tq[0]], [tq[0]])
        tt("dve", tq[1][:, 3:4], Hi_ap, cos1, ALU.mult, [HBuf, tabB, tq[0]], [tq[0]])
        tt("dve", g0[1][:, 0:1], tq[1][:, 1:2], tq[1][:, 0:1], ALU.subtract, [tq[0]], [g0[0]])
        tt("dve", g0[1][:, 1:2], tq[1][:, 3:4], tq[1][:, 2:3], ALU.add, [tq[0], g0[0]], [g0[0]])
        t1, t2, t3, t4 = sst.get(), sst.get(), sst.get(), sst.get()
        tt("dve", t1[1][:, :L], bre, cosA, ALU.mult, [bu[0], tabB], [t1[0]])
        tt("dve", t2[1][:, :L], bim, sinA, ALU.mult, [bu[0], tabB], [t2[0]])
        tt("dve", t3[1][:, :L], bim, cosA, ALU.mult, [bu[0], tabB], [t3[0]])
        tt("dve", t4[1][:, :L], bre, sinA, ALU.mult, [bu[0], tabB], [t4[0]])
        tt("dve", t1[1][:, :L], t1[1][:, :L], t2[1][:, :L], ALU.add, [t1[0], t2[0]], [t1[0]])
        tt("dve", t3[1][:, :L], t3[1][:, :L], t4[1][:, :L], ALU.subtract, [t3[0], t4[0]], [t3[0]])
        r1 = rtab[:, pi, 1:2]
        tt("dve", g0[1][:, 2:3], g0[1][:, 0:1], r1, ALU.mult, [g0[0], tabB], [g0[0]])
        tt("dve", g0[1][:, 3:4], g0[1][:, 1:2], r1, ALU.mult, [g0[0], tabB], [g0[0]])
        tt("dve", t1[1][:, 0:1], t1[1][:, 0:1], g0[1][:, 2:3], ALU.add, [t1[0], g0[0]], [t1[0]])
        tt("dve", t3[1][:, 0:1], t3[1][:, 0:1], g0[1][:, 3:4], ALU.add, [t3[0], g0[0]], [t3[0]])
        gre, gim = sst.get(), sst.get()
        P.op("dve", lambda e: e.tensor_tensor_scan(out=gre[1][:, :L], data0=rA, data1=t1[1][:, :L], initial=0.0, op0=ALU.mult, op1=ALU.add),
             [tabB, t1[0]], [gre[0]])
        P.op("dve", lambda e: e.tensor_tensor_scan(out=gim[1][:, :L], data0=rA, data1=t3[1][:, :L], initial=0.0, op0=ALU.mult, op1=ALU.add),
             [tabB, t3[0]], [gim[0]])
        p1, p2, p3, p4 = sst.get(), sst.get(), sst.get(), sst.get()
        tt("dve", p1[1][:, :L], gre[1][:, :L], cosA, ALU.mult, [gre[0], tabB], [p1[0]])
        tt("dve", p2[1][:, :L], gim[1][:, :L], sinA, ALU.mult, [gim[0], tabB], [p2[0]])
        tt("dve", p3[1][:, :L], gim[1][:, :L], cosA, ALU.mult, [gim[0], tabB], [p3[0]])
        tt("dve", p4[1][:, :L], gre[1][:, :L], sinA, ALU.mult, [gre[0], tabB], [p4[0]])
        hr, hi = hbf.get(), hbf.get()
        tt("dve", hr[1][:, :L], p1[1][:, :L], p2[1][:, :L], ALU.subtract, [p1[0], p2[0]], [hr[0]])
        tt("dve", hi[1][:, :L], p3[1][:, :L], p4[1][:, :L], ALU.add, [p3[0], p4[0]], [hi[0]])
        tt("dve", Hr_ap, p1[1][:, L - 1:L], p2[1][:, L - 1:L], ALU.subtract, [p1[0], p2[0]], [HBuf])
        tt("dve", Hi_ap, p3[1][:, L - 1:L], p4[1][:, L - 1:L], ALU.add, [p3[0], p4[0], HBuf], [HBuf])
        yo = ypsum[1][:, col0:col0 + L]
        mm1(yo, CreT[:, pi, :], hr[1][:, :L], first, False, [lhsB, hr[0]], [ypsum[0]])
        mm1(yo, CimT[:, pi, :], hi[1][:, :L], False, last, [lhsB, hi[0]], [ypsum[0]])

    def conv_chunk(c, col0, L, eng):
        o = cacc[:, c, col0:col0 + L]
        base = PK["conv_w"][0] + c * 31
        ts(eng, o, vbuf[:, c, 0:L], pk[:, base:base + 1], pkc("conv_b", c), ALU.mult, ALU.add, [vbufB[c], pkB], [caccB[c]])
        for k in range(1, 31):
            stt(eng, o, vbuf[:, c, k:k + L], pk[:, base + k:base + k + 1], o, ALU.mult, ALU.add, [vbufB[c], pkB, caccB[c]], [caccB[c]])

    def v3(t_):
        return t_.rearrange("p (a b) -> p a b", a=4)

    def ssm_s0(c, col0):
        bR, bI = buR.get(), buR.get()

        def fn(e):
            ins = None
            for q in range(4):
                e.matmul(bR[1][:, q * 128:(q + 1) * 128], BreT[:, 4 * c + q, :], ub[:, c, col0:col0 + 128], start=True, stop=True)
                ins = e.matmul(bI[1][:, q * 128:(q + 1) * 128], BimT[:, 4 * c + q, :], ub[:, c, col0:col0 + 128], start=True, stop=True)
            return ins
        P.op("pe", fn, [lhsB, ubB[c]], [bR[0], bI[0]])
        return bR, bI

    def ssm_group(c, col0, yp, filler, bu, after_s2):
        ps4 = slice(4 * c, 4 * c + 4)
        HBs = HB[4 * c:4 * c + 4]
        bR, bI = bu
        cosG, sinG = costab[:, ps4, :], sintab[:, ps4, :]
        rG = rtab[:, ps4, :].rearrange("p a b -> p (a b)")
        cos1, sin1, r1 = costab[:, ps4, 1], sintab[:, ps4, 1], rtab[:, ps4, 1]
        Hr, Hi = Hre[:, ps4], Him[:, ps4]
        tq, g0 = tiny.get(), tiny.get()
        tt("dve", tq[1][:, 0:4], Hi, sin1, ALU.mult, HBs + [tabB], [tq[0]])
        tt("dve", tq[1][:, 4:8], Hr, cos1, ALU.mult, HBs + [tabB, tq[0]], [tq[0]])
        tt("dve", tq[1][:, 8:12], Hr, sin1, ALU.mult, HBs + [tabB, tq[0]], [tq[0]])
        tt("dve", tq[1][:, 12:16], Hi, cos1, ALU.mult, HBs + [tabB, tq[0]], [tq[0]])
        tt("dve", g0[1][:, 0:4], tq[1][:, 4:8], tq[1][:, 0:4], ALU.subtract, [tq[0]], [g0[0]])
        tt("dve", g0[1][:, 4:8], tq[1][:, 12:16], tq[1][:, 8:12], ALU.add, [tq[0], g0[0]], [g0[0]])
        tt("dve", g0[1][:, 8:12], g0[1][:, 0:4], r1, ALU.mult, [g0[0], tabB], [g0[0]])
        tt("dve", g0[1][:, 12:16], g0[1][:, 4:8], r1, ALU.mult, [g0[0], tabB], [g0[0]])
        filler(3)
        yield
        A, B_, C, D_ = gst.get(), gst.get(), gst.get(), gst.get()
        tt("dve", v3(A[1][:]), v3(bR[1][:]), cosG, ALU.mult, [bR[0], tabB], [A[0]])
        tt("dve", v3(B_[1][:]), v3(bI[1][:]), sinG, ALU.mult, [bI[0], tabB], [B_[0]])
        filler(2)
        tt("dve", v3(C[1][:]), v3(bI[1][:]), cosG, ALU.mult, [bI[0], tabB], [C[0]])
        tt("dve", v3(D_[1][:]), v3(bR[1][:]), sinG, ALU.mult, [bR[0], tabB], [D_[0]])
        after_s2()
        filler(2)
        yield
        tt(SSM_POST, A[1][:], A[1][:], B_[1][:], ALU.add, [A[0], B_[0]], [A[0]])
        tt(SSM_POST, v3(A[1][:])[:, :, 0], v3(A[1][:])[:, :, 0], g0[1][:, 8:12], ALU.add, [A[0], g0[0]], [A[0]])
        tt(SSM_POST, C[1][:], C[1][:], D_[1][:], ALU.subtract, [C[0], D_[0]], [C[0]])
        tt(SSM_POST, v3(C[1][:])[:, :, 0], v3(C[1][:])[:, :, 0], g0[1][:, 12:16], ALU.add, [C[0], g0[0]], [C[0]])
        filler(4)
        yield
        GR, GI = gst.get(), gst.get()
        P.op("dve", lambda e: e.tensor_tensor_scan(out=GR[1][:], data0=rG, data1=A[1][:], initial=0.0, op0=ALU.mult, op1=ALU.add),
             [tabB, A[0]], [GR[0]])
        filler(2)
        P.op("dve", lambda e: e.tensor_tensor_scan(out=GI[1][:], data0=rG, data1=C[1][:], initial=0.0, op0=ALU.mult, op1=ALU.add),
             [tabB, C[0]], [GI[0]])
        filler(2)
        yield
        tt(SSM_POST, v3(B_[1][:]), v3(GR[1][:]), cosG, ALU.mult, [GR[0], tabB], [B_[0]])
        tt(SSM_POST, v3(D_[1][:]), v3(GI[1][:]), sinG, ALU.mult, [GI[0], tabB], [D_[0]])
        tt(SSM_POST, v3(A[1][:]), v3(GI[1][:]), cosG, ALU.mult, [GI[0], tabB], [A[0]])
        tt(SSM_POST, v3(C[1][:]), v3(GR[1][:]), sinG, ALU.mult, [GR[0], tabB], [C[0]])
        filler(4)
        yield
        hr, hi = ghb.get(), ghb.get()
        tt("dve", hr[1][:], B_[1][:], D_[1][:], ALU.subtract, [B_[0], D_[0]], [hr[0]])
        filler(1)
        tt("dve", hi[1][:], A[1][:], C[1][:], ALU.add, [A[0], C[0]], [hi[0]])
        filler(1)
        tt("dve", Hr, v3(B_[1][:])[:, :, 127], v3(D_[1][:])[:, :, 127], ALU.subtract, [B_[0], D_[0]], HBs)
        tt("dve", Hi, v3(A[1][:])[:, :, 127], v3(C[1][:])[:, :, 127], ALU.add, [A[0], C[0]] + HBs, HBs)
        yo = yp[1][:, col0:col0 + 128]

        def fn2(e):
            ins = None
            for q in range(4):
                e.matmul(yo, CreT[:, 4 * c + q, :], hr[1][:, q * 128:(q + 1) * 128], start=(q == 0), stop=False)
                ins = e.matmul(yo, CimT[:, 4 * c + q, :], hi[1][:, q * 128:(q + 1) * 128], start=False, stop=(q == 3))
            return ins
        if DELAY_Y:
            pending_y.append(lambda: P.op("pe", fn2, [lhsB, hr[0], hi[0]], [yp[0]]))
        else:
            P.op("pe", fn2, [lhsB, hr[0], hi[0]], [yp[0]])
        yield

    pending_y = []

    def flush_y():
        while pending_y:
            pending_y.pop(0)()

    def front_p(xb_t, xbB_t, tail_fn):
        Tn = T
        s0B, w0 = pre_win[0] if pre_win[0] is not None else fpiece(w_in_b, pc["w_in"], 0, 512)
        pre_win[0] = None
        for c in range(4):
            ps = psum.get()
            mm(ps[1][:, :Tn], [(w0[:, k, c * 128:(c + 1) * 128], xb_t[:, k, :Tn]) for k in range(8)], [s0B, xbB_t], [ps[0]])
            cp("act", u32[:, c, :Tn], ps[1][:, :Tn], [ps[0]], [u32B[c]])
            cp("act", ub[:, c, :Tn], u32[:, c, :Tn], [u32B[c]], [ubB[c]])
            yield
        for half in range(2):
            sB_, wv = fpiece(w_in_b, pc["w_in"], 512 + half * 512, 1024 + half * 512)
            for ci in range(2):
                c = half * 2 + ci
                pa, pg = psum.get(), psum.get()
                mm(pa[1][:, :Tn], [(wv[:, k, ci * 256:ci * 256 + 128], xb_t[:, k, :Tn]) for k in range(8)], [sB_, xbB_t], [pa[0]])
                mm(pg[1][:, :Tn], [(wv[:, k, ci * 256 + 128:ci * 256 + 256], xb_t[:, k, :Tn]) for k in range(8)], [sB_, xbB_t], [pg[0]])
                sg = f32t.get()
                act(sg[1][:, :Tn], pg[1][:, :Tn], AF.Sigmoid, [pg[0]], [sg[0]])
                tt("dve", vbuf[:, c, 30:30 + Tn], pa[1][:, :Tn], sg[1][:, :Tn], ALU.mult, [pa[0], sg[0]], [vbufB[c]])
                yield
        taps = []
        for k in range(31):
            for c in range(4):
                taps.append((c, k))
        tap_i = [0]

        def filler(n):
            for _ in range(n):
                if tap_i[0] >= len(taps):
                    return
                c, k = taps[tap_i[0]]
                tap_i[0] += 1
                o = cacc[:, c, 0:Tn]
                base = PK["conv_w"][0] + c * 31
                ceng = "pool" if c >= 4 - CONV_POOL_CH else "dve"
                if k == 0:
                    ts(ceng, o, vbuf[:, c, 0:Tn], pk[:, base:base + 1], pkc("conv_b", c), ALU.mult, ALU.add, [vbufB[c], pkB], [caccB[c]])
                elif ceng == "dve":
                    stt("dve", o, vbuf[:, c, k:k + Tn], pk[:, base + k:base + k + 1], o, ALU.mult, ALU.add, [vbufB[c], pkB, caccB[c]], [caccB[c]])
                else:
                    tmp_ = ctmp.get()
                    ts("pool", tmp_[1][:], vbuf[:, c, k:k + Tn], pk[:, base + k:base + k + 1], None, ALU.mult, None, [vbufB[c], pkB], [tmp_[0]])
                    tt("pool", o, o, tmp_[1][:], ALU.add, [caccB[c], tmp_[0]], [caccB[c]])

        glist = [(c_, sg_) for c_ in range(4) for sg_ in range(T // TS)]
        bu_next = [ssm_s0(glist[0][0], glist[0][1] * TS)] if GRP else [None]
        gidx = [0]

        def after_s2():
            flush_y()
            gidx[0] += 1
            if gidx[0] < len(glist):
                bu_next[0] = ssm_s0(glist[gidx[0]][0], glist[gidx[0]][1] * TS)

        for c in range(4):
            yp = ypsR.get()
            for sg_ in range(T // TS):
                if GRP:
                    for _ in ssm_group(c, sg_ * TS, yp, filler, bu_next[0], after_s2):
                        yield
                else:
                    for q in range(4):
                        pi = c * 4 + q
                        ssm_segment(pi, sg_ * TS, TS, Hre[:, pi:pi + 1], Him[:, pi:pi + 1], HB[pi], yp, q == 0, q == 3)
                        filler(8)
                        yield
            flush_y()
            zp = f32t.get()
            stt("dve", zp[1][:, :Tn], u32[:, c, :Tn], pkc("ssm_d", c), yp[1][:, :Tn], ALU.mult, ALU.add, [u32B[c], pkB, yp[0]], [zp[0]])
            act(z32[:, c, :Tn], zp[1][:, :Tn], AF.Gelu_apprx_tanh, [zp[0]], [z32B[c]])
            act(zb[:, c, :Tn], zp[1][:, :Tn], AF.Gelu_apprx_tanh, [zp[0]], [zbB[c]])
            yield
        while tap_i[0] < len(taps):
            filler(4)
            yield
        for co in range(4):
            ps = psum.get()
            mm(ps[1][:, :Tn], [(glu_sb[:, k, co * 128:(co + 1) * 128], zb[:, k, :Tn]) for k in range(4)], [gluB] + zbB, [ps[0]])
            sg = f32t.get()
            act(sg[1][:, :Tn], ps[1][:, :Tn], AF.Sigmoid, [ps[0], pkB], [sg[0]], bias=pkc("glu_b", co))
            tt("dve", mixin[:, co, :Tn], z32[:, co, :Tn], sg[1][:, :Tn], ALU.mult, [z32B[co], sg[0]], [mixB[co]])
            yield
        for c in range(4):
            if tail_fn is not None:
                tail_fn(c, Tn)
            cp("pool", vbuf[:, c, 0:30], vbuf[:, c, Tn:Tn + 30], [vbufB[c]], [vbufB[c]])

        def silu_out(c, t_ap, tB, g_ap, b_ap):
            act(mixin[:, 4 + c, :Tn], t_ap, AF.Silu, [tB, pkB], [mixB[4 + c]], scale=g_ap, bias=b_ap)
        pre_win[0] = fpiece(w_in_b, pc["w_in"], 0, 512)
        layer_norm(4, [cacc[:, c, :Tn] for c in range(4)], caccB, ones_c[:], Tn, "cln_g", "cln_b", silu_out)
        yield

    pre_win = [None]
    pre_wout = [None]

    def drain(g):
        for _ in g:
            pass

    def merge(ga, gb):
        da = db = False
        while not (da and db):
            for _ in range(MRA):
                if not da:
                    try:
                        next(ga)
                    except StopIteration:
                        da = True
            for _ in range(MRB):
                if not db:
                    try:
                        next(gb)
                    except StopIteration:
                        db = True

    def run_tile(Tn, xb_t, xbB_t, x32_src, segs, conv_segs, att_segs, y_dst, is_sample=False, part="all"):
        if part in ("all", "front"):
            s0B, w0 = pre_win[0] if pre_win[0] is not None else fpiece(w_in_b, pc["w_in"], 0, 512)
            pre_win[0] = None
            for c in range(4):
                ps = psum.get()
                mm(ps[1][:, :Tn], [(w0[:, k, c * 128:(c + 1) * 128], xb_t[:, k, :Tn]) for k in range(8)], [s0B, xbB_t], [ps[0]])
                cp("act", u32[:, c, :Tn], ps[1][:, :Tn], [ps[0]], [u32B[c]])
                cp("dve", ub[:, c, :Tn], u32[:, c, :Tn], [u32B[c]], [ubB[c]])
            for c in range(4):
                yp = ypsR.get()
                for (col0, L, Hr_fn, Hi_fn, HB_fn) in segs:
                    for q in range(4):
                        pi = c * 4 + q
                        ssm_segment(pi, col0, L, Hr_fn(pi), Hi_fn(pi), HB_fn(pi), yp, q == 0, q == 3)
                    yield
                zp = f32t.get()
                stt("dve", zp[1][:, :Tn], u32[:, c, :Tn], pkc("ssm_d", c), yp[1][:, :Tn], ALU.mult, ALU.add, [u32B[c], pkB, yp[0]], [zp[0]])
                act(z32[:, c, :Tn], zp[1][:, :Tn], AF.Gelu_apprx_tanh, [zp[0]], [z32B[c]])
                act(zb[:, c, :Tn], zp[1][:, :Tn], AF.Gelu_apprx_tanh, [zp[0]], [zbB[c]])
            for co in range(4):
                ps = psum.get()
                mm(ps[1][:, :Tn], [(glu_sb[:, k, co * 128:(co + 1) * 128], zb[:, k, :Tn]) for k in range(4)], [gluB] + zbB, [ps[0]])
                sg = f32t.get()
                act(sg[1][:, :Tn], ps[1][:, :Tn], AF.Sigmoid, [ps[0], pkB], [sg[0]], bias=pkc("glu_b", co))
                tt("dve", mixin[:, co, :Tn], z32[:, co, :Tn], sg[1][:, :Tn], ALU.mult, [z32B[co], sg[0]], [mixB[co]])
            for half in range(2):
                sB_, wv = fpiece(w_in_b, pc["w_in"], 512 + half * 512, 1024 + half * 512)
                for ci in range(2):
                    c = half * 2 + ci
                    pa, pg = psum.get(), psum.get()
                    mm(pa[1][:, :Tn], [(wv[:, k, ci * 256:ci * 256 + 128], xb_t[:, k, :Tn]) for k in range(8)], [sB_, xbB_t], [pa[0]])
                    mm(pg[1][:, :Tn], [(wv[:, k, ci * 256 + 128:ci * 256 + 256], xb_t[:, k, :Tn]) for k in range(8)], [sB_, xbB_t], [pg[0]])
                    sg = f32t.get()
                    act(sg[1][:, :Tn], pg[1][:, :Tn], AF.Sigmoid, [pg[0]], [sg[0]])
                    eng = "dve"
                    if is_sample:
                        vt = f32t.get()
                        tt("dve", vt[1][:, :Tn], pa[1][:, :Tn], sg[1][:, :Tn], ALU.mult, [pa[0], sg[0]], [vt[0]])
                        for (col0, L, halo_fn, tail_fn) in conv_segs:
                            halo_fn(c)
                            cp("pool", vbuf[:, c, 30:30 + L], vt[1][:, col0:col0 + L], [vt[0]], [vbufB[c]])
                            conv_chunk(c, col0, L, eng)
                            tail_fn(c, L)
                    else:
                        (col0, L, halo_fn, tail_fn) = conv_segs[0]
                        tt("dve", vbuf[:, c, 30:30 + Tn], pa[1][:, :Tn], sg[1][:, :Tn], ALU.mult, [pa[0], sg[0]], [vbufB[c]])
                        conv_chunk(c, 0, Tn, eng)
                        if tail_fn is not None:
                            tail_fn(c, Tn)
                        cp("pool", vbuf[:, c, 0:30], vbuf[:, c, Tn:Tn + 30], [vbufB[c]], [vbufB[c]])

            def silu_out(c, t_ap, tB, g_ap, b_ap):
                act(mixin[:, 4 + c, :Tn], t_ap, AF.Silu, [tB, pkB], [mixB[4 + c]], scale=g_ap, bias=b_ap)
            layer_norm(4, [cacc[:, c, :Tn] for c in range(4)], caccB, ones_c[:], Tn, "cln_g", "cln_b", silu_out)
        if part in ("all", "back"):
            dma("sp", R0[:, :, :Tn], x32_src, R0B[0], writes=R0B)
            for half in range(2):
                if half == 0 and pre_wout[0] is not None:
                    sB_, wv = pre_wout[0]
                    pre_wout[0] = None
                else:
                    sB_, wv = wpiece(w_out_b, pc["w_out"], half * 512, half * 512 + 512)
                for ci in range(4):
                    co = half * 4 + ci
                    yield
                    ps = psum.get()
                    mm(ps[1][:, :Tn], [(wv[:, k, ci * 128:(ci + 1) * 128], mixin[:, k, :Tn]) for k in range(8)], [sB_] + mixB, [ps[0]])
                    stt("dve", R0[:, co, :Tn], R0[:, co, :Tn], ALPHA, ps[1][:, :Tn], ALU.mult, ALU.add, [R0B[co], ps[0]], [R0B[co]])
            ln_to_resid(Tn, "ln1_g", "ln1_b")
            qps = []
            for half in range(2):
                sB_, wv = wpiece(w_q_b, pc["w_q"], half * 512, half * 512 + 512)
                for ci in range(4):
                    yield
                    ps = psum.get()
                    mm(ps[1][:, :Tn], [(wv[:, k, ci * 128:(ci + 1) * 128], xnb[:, k, :Tn]) for k in range(8)], [sB_] + xnbB, [ps[0]])
                    act(qb[:, half * 4 + ci, :Tn], ps[1][:, :Tn], AF.Identity, [ps[0]], [qbB[half * 4 + ci]], scale=1.0 / 16.0)
            for (col0, L, kv_loader) in att_segs:
                if kv_loader is not None:
                    kv_loader()
                for h in range(4):
                    pts = []
                    for mc in range(2):
                        yield
                        ps = psum.get()
                        mm(ps[1][:, :L], [(kT[:, h * 2 + dc, mc * 128:(mc + 1) * 128], qb[:, h * 2 + dc, col0:col0 + L]) for dc in range(2)],
                           [kTB, qbB[h * 2], qbB[h * 2 + 1]], [ps[0]])
                        pt = pTr.get()
                        act(pt[1][:, :L], ps[1][:, :L], AF.Exp, [ps[0]], [pt[0]])
                        pts.append(pt)
                    yield
                    ps = psum.get()
                    mm(ps[1][:, :L], [(ones_1[:], pts[mc][1][:, :L]) for mc in range(2)], [onesB, pts[0][0], pts[1][0]], [ps[0]])
                    rinv = f32t.get()
                    recip(rinv[1][:, :L], ps[1][:, :L], [ps[0]], [rinv[0]])
                    for dc in range(2):
                        po = psum.get()
                        mm(po[1][:, :L], [(vv[:, mc, h * 256 + dc * 128: h * 256 + dc * 128 + 128], pts[mc][1][:, :L]) for mc in range(2)],
                           [vvB, pts[0][0], pts[1][0]], [po[0]])
                        tt("dve", ob[:, h * 2 + dc, col0:col0 + L], po[1][:, :L], rinv[1][:, :L], ALU.mult, [po[0], rinv[0]], [obB[h * 2 + dc]])
            for half in range(2):
                sB_, wv = wpiece(w_o_b, pc["w_o"], half * 512, half * 512 + 512)
                for ci in range(4):
                    co = half * 4 + ci
                    yield
                    ps = psum.get()
                    mm(ps[1][:, :Tn], [(wv[:, k, ci * 128:(ci + 1) * 128], ob[:, k, :Tn]) for k in range(8)], [sB_] + obB, [ps[0]])
                    stt("dve", R0[:, co, :Tn], R1[:, co, :Tn], ALPHA, ps[1][:, :Tn], ALU.mult, ALU.add, [R1B[co], ps[0]], [R0B[co]])
            ln_to_resid(Tn, "ln2_g", "ln2_b")
            for piece in range(8):
                sB_, wv = wpiece(w1_b, pc["w1"], piece * 512, piece * 512 + 512)
                for hc in range(4):
                    hidx = piece * 4 + hc
                    yield
                    ps = psum.get()
                    mm(ps[1][:, :Tn], [(wv[:, k, hc * 128:(hc + 1) * 128], xnb[:, k, :Tn]) for k in range(8)], [sB_] + xnbB, [ps[0]])
                    rl = f32t.get()
                    act(rl[1][:, :Tn], ps[1][:, :Tn], AF.Relu, [ps[0], pkB], [rl[0]], bias=pkc("b1", hidx))
                    if SQ_ENG == "act":
                        act(hid[:, hidx, :Tn], rl[1][:, :Tn], AF.Square, [rl[0]], [hidB[hidx]])
                    else:
                        tt("dve", hid[:, hidx, :Tn], rl[1][:, :Tn], rl[1][:, :Tn], ALU.mult, [rl[0]], [hidB[hidx]])
            w2v = w2_b.rearrange("(k p) n -> p k n", p=128)
            for cp_ in range(4):
                yield
                pss = [psum.get(), psum.get()]
                for kh in range(2):
                    sB_, wv = ring_load(w2v[:, kh * 16:kh * 16 + 16, cp_ * 256:cp_ * 256 + 256], 16, 256, pc["w2"])
                    for oc in range(2):
                        for k in range(16):
                            mm1(pss[oc][1][:, :Tn], wv[:, k, oc * 128:(oc + 1) * 128], hid[:, kh * 16 + k, :Tn],
                                kh == 0 and k == 0, kh == 1 and k == 15, [sB_, hidB[kh * 16 + k]], [pss[oc][0]])
                for oc in range(2):
                    co = cp_ * 2 + oc
                    stt("dve", R0[:, co, :Tn], R1[:, co, :Tn], ALPHA, pss[oc][1][:, :Tn], ALU.mult, ALU.add, [R1B[co], pss[oc][0]], [R0B[co]])
                    act(R0[:, co, :Tn], R0[:, co, :Tn], AF.Identity, [R0B[co], pkB], [R0B[co]], bias=pkc("b2", co))
            if not is_sample:
                pre_wout[0] = wpiece(w_out_b, pc["w_out"], 0, 512)
            ln_to_resid(Tn, "ln3_g", "ln3_b")
            dma("sp", y_dst, R1[:, :, :Tn], R1B[0], reads=R1B, final=True)

    xT_v = xT.rearrange("(k p) t -> p k t", p=128)
    if RUN_KV:
        memTb = xb[1]
        dma("pool", memTb[:, :, 0:256], memT_d.rearrange("(k p) m -> p k m", p=128), xbB[1], writes=[xbB[1]])
        kv_i = [4]
        for (wb, pcb, dst_d, is_k) in ((w_k_b, pc["w_k"], kout_d, True), (w_v_b, pc["w_v"], vout_d, False)):
            for half in range(2):
                sB_, wv = wpiece(wb, pcb, half * 512, half * 512 + 512)
                if is_k and KVD[0] == "1":
                    for ci in range(4):
                        ps = psum.get()
                        mm(ps[1][:, :256], [(wv[:, k, ci * 128:(ci + 1) * 128], memTb[:, k, 0:256]) for k in range(8)], [sB_, xbB[1]], [ps[0]])
                        cp("act", kT[:, half * 4 + ci, :], ps[1][:, :256], [ps[0]], [kTB])
                for mc in range(2 if KVD[1] == "1" else 0):
                    ps = psum.get()
                    if KVD[3:4] == "h":
                        mm(ps[1][:, 0:256], [(memTb[:, k, mc * 128:(mc + 1) * 128], wv[:, k, 0:256]) for k in range(8)], [sB_, xbB[1]], [ps[0]])
                        mm(ps[1][:, 256:512], [(memTb[:, k, mc * 128:(mc + 1) * 128], wv[:, k, 256:512]) for k in range(8)], [sB_, xbB[1]], [ps[0]])
                    else:
                        mm(ps[1][:, :], [(memTb[:, k, mc * 128:(mc + 1) * 128], wv[:, k, :]) for k in range(8)], [sB_, xbB[1]], [ps[0]])
                    stgB, stg = big(kv_i[0])
                    kv_i[0] = 4 + (kv_i[0] - 4 + 1) % 4
                    if KVD[5:6] == "s":
                        for hh in range(2):
                            cp("act", stg[:, hh * 256:(hh + 1) * 256], ps[1][:, hh * 256:(hh + 1) * 256], [ps[0]], stgB)
                            if not is_k:
                                cp("dve", vv[:, mc, half * 512 + hh * 256:half * 512 + (hh + 1) * 256], ps[1][:, hh * 256:(hh + 1) * 256], [ps[0]], [vvB])
                    elif KVD[5:6] == "n":
                        pass
                    elif KVD[5:6] == "a":
                        cp("act", stg, ps[1][:], [ps[0]], stgB)
                    elif KVD[5:6] == "v":
                        cp("dve", stg, ps[1][:], [ps[0]], stgB)
                    else:
                        cp("act", stg, ps[1][:], [ps[0]], stgB)
                        if not is_k:
                            cp("dve", vv[:, mc, half * 512:(half + 1) * 512], stg, stgB, [vvB])
                    if KVD[2] == "1":
                        dma("sp", dst_d[mc * 128:(mc + 1) * 128, half * 512:(half + 1) * 512], stg, stgB[0], reads=stgB, final=True)

    memset("dve", Hre[:], 0.0, HB)
    memset("dve", Him[:], 0.0, HB)
    winuB, winu = fpiece(w_in_b, pc["w_in"], 0, 512)
    NPB = NPRE // T
    xi = [0]

    def load_xb(col):
        i = xi[0] % 2
        xi[0] += 1
        dma("pool", xb[i][:, :, :], xT_v[:, :, col:col + T], xbB[i], writes=[xbB[i]])
        return i

    nxt = load_xb((NPB - NPB_RUN) * T)
    qi = [0]
    pf_i = [0]
    def step_late_pc():
        while late_pc:
            try:
                next(late_pc[0])
                return
            except StopIteration:
                late_pc.pop(0)

    for pb in range(NPB - NPB_RUN, NPB):
        cur = nxt
        step_late_pc()
        nxt = load_xb((pb + 1) * T)
        if PFG2:
            ubks = []
            for blk in range(2):
                pu = psum.get()
                mm(pu[1][:, :], [(xb[cur][:, k, blk * 128:(blk + 1) * 128], winu[:, k, :]) for k in range(8)], [xbB[cur], winuB], [pu[0]])
                ubk = ghb.get()
                cp("act", ubk[1][:], pu[1][:], [pu[0]], [ubk[0]])
                ubks.append(ubk)
            pfl = buR.items + lnR.items
            wr, wi = pfl[(pf_i[0] % 2) * 2], pfl[(pf_i[0] % 2) * 2 + 1]
            pf_i[0] += 1

            def fn(e, ubks=ubks, wr=wr, wi=wi):
                ins = None
                for p_ in range(16):
                    sl_ = slice(p_ * 32, (p_ + 1) * 32)
                    e.matmul(wr[1][:, sl_], P2re[:, p_, :], ubks[0][1][:, sl_], start=True, stop=False)
                    e.matmul(wr[1][:, sl_], Pre[:, p_, :], ubks[1][1][:, sl_], start=False, stop=True)
                    e.matmul(wi[1][:, sl_], P2im[:, p_, :], ubks[0][1][:, sl_], start=True, stop=False)
                    ins = e.matmul(wi[1][:, sl_], Pim[:, p_, :], ubks[1][1][:, sl_], start=False, stop=True)
                return ins
            P.op("pe", fn, [lhsB, ubks[0][0], ubks[1][0]] + p2B, [wr[0], wi[0]])
            blocks_ = [(wr, wi)]
        else:
            blocks_ = None
        for blk in range(T // 128 if not PFG2 else 1):
            if not PFG2:
                pu = psum.get()
                mm(pu[1][:, :], [(xb[cur][:, k, blk * 128:(blk + 1) * 128], winu[:, k, :]) for k in range(8)], [xbB[cur], winuB], [pu[0]])
                ubk = ghb.get()
                cp("act", ubk[1][:], pu[1][:], [pu[0]], [ubk[0]])
                pfl = buR.items + lnR.items
                wr, wi = pfl[(pf_i[0] % 2) * 2], pfl[(pf_i[0] % 2) * 2 + 1]
                pf_i[0] += 1

                def fn(e, ubk=ubk, wr=wr, wi=wi):
                    ins = None
                    for p_ in range(16):
                        e.matmul(wr[1][:, p_ * 32:(p_ + 1) * 32], Pre[:, p_, :], ubk[1][:, p_ * 32:(p_ + 1) * 32], start=True, stop=True)
                        ins = e.matmul(wi[1][:, p_ * 32:(p_ + 1) * 32], Pim[:, p_, :], ubk[1][:, p_ * 32:(p_ + 1) * 32], start=True, stop=True)
                    return ins
                P.op("pe", fn, [lhsB, ubk[0]], [wr[0], wi[0]])
            else:
                wr, wi = blocks_[0]
            base = (qi[0] % 2) * 2
            qi[0] += 1
            (q1B, q1), (q2B, q2) = big(base), big(base + 1)
            (q3B, q3), (q4B, q4) = big(4 + base), big(4 + base + 1)
            tt("dve", q1, wr[1][:], Bxr[:], ALU.mult, [wr[0], BxB], q1B)
            tt("dve", q2, wi[1][:], Bxi[:], ALU.mult, [wi[0], BxB], q2B)
            tt("dve", q3, wi[1][:], Bxr[:], ALU.mult, [wi[0], BxB], q3B)
            tt("dve", q4, wr[1][:], Bxi[:], ALU.mult, [wr[0], BxB], q4B)
            tt(SSM_POST, q1, q1, q2, ALU.subtract, q1B + q2B, q1B)
            tt(SSM_POST, q3, q3, q4, ALU.add, q3B + q4B, q3B)
            sr, si = tiny.get(), tiny.get()
            P.op("dve", (lambda sr, q1: lambda e: e.tensor_reduce(out=sr[1][:], in_=q1.rearrange("p (a b) -> p a b", a=16), axis=AX.X, op=ALU.add))(sr, q1), q1B, [sr[0]])
            P.op("dve", (lambda si, q3: lambda e: e.tensor_reduce(out=si[1][:], in_=q3.rearrange("p (a b) -> p a b", a=16), axis=AX.X, op=ALU.add))(si, q3), q3B, [si[0]])
            u1, u2, u3, u4 = tiny.get(), tiny.get(), tiny.get(), tiny.get()
            tt("dve", u1[1][:], a128r[:], Hre[:], ALU.mult, [a128B] + HB, [u1[0]])
            tt("dve", u2[1][:], a128i[:], Him[:], ALU.mult, [a128B] + HB, [u2[0]])
            tt("dve", u3[1][:], a128r[:], Him[:], ALU.mult, [a128B] + HB, [u3[0]])
            tt("dve", u4[1][:], a128i[:], Hre[:], ALU.mult, [a128B] + HB, [u4[0]])
            tt("dve", u1[1][:], u1[1][:], u2[1][:], ALU.subtract, [u1[0], u2[0]], [u1[0]])
            tt("dve", u3[1][:], u3[1][:], u4[1][:], ALU.add, [u3[0], u4[0]], [u3[0]])
            tt("dve", Hre[:], u1[1][:], sr[1][:], ALU.add, [u1[0], sr[0]], HB)
            tt("dve", Him[:], u3[1][:], si[1][:], ALU.add, [u3[0], si[0]] + HB, HB)

    pre_win[0] = (winuB, winu)
    for g_ in late_pc:
        for _ in g_:
            pass
    hxi = xi[0] % 2
    hx, hxB = xb[hxi], xbB[hxi]
    dma("pool", hx[:, :, 0:32], xT_v[:, :, NPRE - 32:NPRE], hxB, writes=[hxB])
    for half in range(2):
        sB_, wv = wpiece(w_in_b, pc["w_in"], 512 + half * 512, 1024 + half * 512)
        for ci in range(2):
            c = half * 2 + ci
            pa, pg = psum.get(), psum.get()
            mm(pa[1][:, :32], [(wv[:, k, ci * 256:ci * 256 + 128], hx[:, k, 0:32]) for k in range(8)], [sB_, hxB], [pa[0]])
            mm(pg[1][:, :32], [(wv[:, k, ci * 256 + 128:ci * 256 + 256], hx[:, k, 0:32]) for k in range(8)], [sB_, hxB], [pg[0]])
            sg = f32t.get()
            act(sg[1][:, :32], pg[1][:, :32], AF.Sigmoid, [pg[0]], [sg[0]])
            tt("dve", vbuf[:, c, 0:30], pa[1][:, 2:32], sg[1][:, 2:32], ALU.mult, [pa[0], sg[0]], [vbufB[c]])

    xs_v = xsT.rearrange("(k p) t -> p k t", p=128)
    hsinB = Buf("hs_in")

    def s_halo(s_):
        def f(c):
            dma("sp", vbuf[:, c, 0:30], convc_d[:, (c * 2 + s_) * 30:(c * 2 + s_) * 30 + 30], vbufB[c], writes=[vbufB[c]])
        return f

    def s_tail(s_):
        def f(c, L):
            dma("sp", convsout_d[:, (c * 2 + s_) * 30:(c * 2 + s_) * 30 + 30], vbuf[:, c, L:L + 30], vbufB[c], reads=[vbufB[c]], final=True)
        return f

    def s_kv(s_):
        def f():
            dma("pool", kT[:], kTs_d[s_].rearrange("(k p) m -> p k m", p=128), kTB, writes=[kTB])
            dma("pool", vv[:], vs_d[s_].rearrange("(k p) n -> p k n", p=128), vvB, writes=[vvB])
        return f

    def hs_fn(s_, reim):
        return lambda pi: Hs[:, pi * 4 + s_ * 2 + reim: pi * 4 + s_ * 2 + reim + 1]

    s_segs = [(s_ * 16, 16, hs_fn(s_, 0), hs_fn(s_, 1), lambda pi: HsB[pi]) for s_ in range(2)]

    def sample_gen(part, bi):
        if part in ("all", "front"):
            dma("sp", Hs[:], h0_d[:, :], hsinB, writes=HsB)
            dma("pool", xb[bi][:, :, 0:32], xs_v, xbB[bi], writes=[xbB[bi]])
        for _ in run_tile(32, xb[bi], xbB[bi], xs_v, s_segs,
                          [(s_ * 16, 16, s_halo(s_), s_tail(s_)) for s_ in range(2)],
                          [(s_ * 16, 16, s_kv(s_)) for s_ in range(2)],
                          ysT.rearrange("(k p) t -> p k t", p=128), is_sample=True, part=part):
            yield
        if part in ("all", "back"):
            hso = statsb.get()
            cp("dve", hso[1][:, 0:64], Hs[:], HsB, [hso[0]])
            dma("sp", hsout_d[:, :], hso[1][:, 0:64], hso[0], reads=[hso[0]], final=True)

    p_segs = [(s * TS, TS, lambda pi: Hre[:, pi:pi + 1], lambda pi: Him[:, pi:pi + 1], lambda pi: HB[pi]) for s in range(T // TS)]
    yT_v = yT.rearrange("(k p) t -> p k t", p=128)
    cur = nxt

    def p_tail(c, L):
        dma("sp", convout_d[:, c * 30:(c + 1) * 30], vbuf[:, c, L:L + 30], vbufB[c], reads=[vbufB[c]], final=True)

    def xload(it, buf_i):
        dma("pool", xb[buf_i][:, :, :], xT_v[:, :, NPRE + it * T: NPRE + (it + 1) * T], xbB[buf_i], writes=[xbB[buf_i]])

    if NT_RUN > 0:
        if NT_RUN > 1:
            xload(1, 1 - cur)
        drain(front_p(xb[cur], xbB[cur], p_tail if NT_RUN == 1 and NT == 1 else None))
    for it in range(NT_RUN):
        back = run_tile(T, xb[cur], xbB[cur], xT_v[:, :, NPRE + it * T: NPRE + (it + 1) * T], p_segs,
                        [(0, T, None, None)], [(0, T, None)], yT_v[:, :, it * T:(it + 1) * T], part="back")
        if it + 1 < NT_RUN:
            nb = 1 - cur
            fr = front_p(xb[nb], xbB[nb], p_tail if it + 1 == NT - 1 else None)
            if it + 2 < NT_RUN:
                xload(it + 2, cur)
            if PIPE:
                merge(back, fr)
            else:
                drain(back)
                drain(fr)
        else:
            if RUN_SAMPLE and PIPE:
                merge(back, sample_gen("front", 1 - cur))
            else:
                drain(back)
        cur = 1 - cur
    hst, hst2 = tiny.get(), tiny.get()
    cp("dve", hst[1][:], Hre[:], HB, [hst[0]])
    cp("dve", hst2[1][:], Him[:], HB, [hst2[0]])
    dma("sp", hout_d[:, 0:16], hst[1][:], hst[0], reads=[hst[0]], final=True)
    dma("sp", hout_d[:, 16:32], hst2[1][:], hst2[0], reads=[hst2[0]], final=True)

    if RUN_SAMPLE:
        if PIPE and NT_RUN > 0:
            drain(sample_gen("back", cur))
        else:
            drain(sample_gen("all", cur))

    P.emit(st)
    st.close()
    return nc


_NC_CACHE = {}


def _mode_major(a):
    sh = a.shape
    a = a.reshape((16, 2, 64) + sh[2:])
    return np.ascontiguousarray(np.moveaxis(a, 0, 2).reshape((128, 16) + sh[2:]))


def kernel(x_prompt, x_sample, state_ssm_re, state_ssm_im, cache_conv, cache_mem_k, cache_mem_v,
           mem_prompt, w_in, ssm_a_re, ssm_a_im, ssm_log_dt, ssm_b_re, ssm_b_im, ssm_c_re, ssm_c_im,
           ssm_d, glu_w, glu_b, conv_w, conv_b, conv_ln_g, conv_ln_b, w_out, ln1_g, ln1_b,
           mem_w_q, mem_w_k, mem_w_v, mem_w_o, ln2_g, ln2_b,
           mlp_w1, mlp_b1, mlp_w2, mlp_b2, ln3_g, ln3_b):
    f = lambda a: np.ascontiguousarray(np.asarray(a, dtype=np.float32))
    x_prompt, x_sample = f(x_prompt), f(x_sample)
    col = lambda v, n: f(v).reshape(n, 128).T
    pk = np.zeros((128, NPK), np.float32)

    def put(name, arr):
        pk[:, PK[name][0]:PK[name][1]] = arr
    put("ln1_g", col(ln1_g[0], 8)); put("ln1_b", col(ln1_b[0], 8))
    put("ln2_g", col(ln2_g[0], 8)); put("ln2_b", col(ln2_b[0], 8))
    put("ln3_g", col(ln3_g[0], 8)); put("ln3_b", col(ln3_b[0], 8))
    put("b1", col(mlp_b1[0], 32)); put("b2", col(mlp_b2[0], 8))
    put("glu_b", col(glu_b[0], 4)); put("ssm_d", col(ssm_d[0], 4))
    put("conv_b", col(conv_b[0], 4)); put("cln_g", col(conv_ln_g[0], 4)); put("cln_b", col(conv_ln_b[0], 4))
    cw = f(conv_w[0]).T.reshape(4, 128, 31).transpose(1, 0, 2).reshape(128, 124)
    put("conv_w", cw)
    put("a_re", _mode_major(f(ssm_a_re[0])[:, :, None])[:, :, 0])
    put("a_im", _mode_major(f(ssm_a_im[0])[:, :, None])[:, :, 0])
    ldt = np.repeat(f(ssm_log_dt[0])[:, None], 64, axis=1)
    put("logdt", _mode_major(ldt[:, :, None])[:, :, 0])
    put("tidx", np.tile(np.arange(128, dtype=np.float32)[None, :], (128, 1)))
    put("eidx", (127.0 - np.arange(128, dtype=np.float32))[:, None])
    put("eidx2", (255.0 - np.arange(128, dtype=np.float32))[:, None])
    rows = np.stack([f(ssm_a_re[0]).reshape(-1), f(ssm_a_im[0]).reshape(-1), ldt.reshape(-1)]).astype(np.float32)
    BT = np.zeros((2, 128, 2048), np.float32)
    CT = np.zeros((2, 128, 2048), np.float32)
    Bx = np.zeros((2, 128, 512), np.float32)
    for ri, (bsrc, csrc) in enumerate(((f(ssm_b_re[0]), f(ssm_c_re[0])), (f(ssm_b_im[0]), f(ssm_c_im[0])))):
        for g in range(32):
            pi, gp, gl = g // 2, g % 2, g % 8
            BT[ri, gl * 16:(gl + 1) * 16, pi * 128 + gp * 64: pi * 128 + gp * 64 + 64] = bsrc[g].T
            CT[ri, gp * 64:(gp + 1) * 64, pi * 128 + gl * 16: pi * 128 + gl * 16 + 16] = csrc[g].T
            Bx[ri, gp * 64:(gp + 1) * 64, pi * 32 + gp * 16: pi * 32 + gp * 16 + 16] = bsrc[g]
    shared = dict(w_in=f(w_in[0]), glu_w=f(glu_w[0]), w_out=f(w_out[0]), w_q=f(mem_w_q[0]), w_k=f(mem_w_k[0]),
                  w_v=f(mem_w_v[0]), w_o=f(mem_w_o[0]), w1=f(mlp_w1[0]), w2=f(mlp_w2[0]), pk=pk, rows=rows, BT=BT, CT=CT, Bx=Bx)
    xTs = [np.ascontiguousarray(x_prompt[b].T) for b in range(2)]
    in_maps = []
    for c in CORES:
        b, j = c // 4, c % 4
        xin = np.zeros((D, NPRE + SEG), np.float32)
        npre = j * SEG
        xin[:, NPRE - npre: NPRE + SEG] = xTs[b][:, 0:(j + 1) * SEG]
        ss = [2 * c, 2 * c + 1]
        xs = np.concatenate([x_sample[s].T for s in ss], axis=1)
        h0 = np.zeros((128, 16, 2, 2), np.float32)
        cc = np.zeros((128, 4, 2, 30), np.float32)
        for si, s in enumerate(ss):
            h0[:, :, si, 0] = _mode_major(f(state_ssm_re[0, s])[:, :, None])[:, :, 0]
            h0[:, :, si, 1] = _mode_major(f(state_ssm_im[0, s])[:, :, None])[:, :, 0]
            cc[:, :, si, :] = f(cache_conv[0, s]).T.reshape(4, 128, 30).transpose(1, 0, 2)
        kTs = np.stack([f(cache_mem_k[0, s]).reshape(256, D).T for s in ss])
        vs = np.stack([f(cache_mem_v[0, s]).reshape(256, D) for s in ss])
        m = dict(shared)
        m.update(xT=xin, xsT=np.ascontiguousarray(xs), h0=h0.reshape(128, 64), convc=cc.reshape(128, 240),
                 kTs=np.ascontiguousarray(kTs), vs=np.ascontiguousarray(vs), memT=np.ascontiguousarray(f(mem_prompt[b]).T))
        in_maps.append(m)
    if "nc" not in _NC_CACHE:
        _NC_CACHE["nc"] = build_program()
    res = run_bass_kernel_spmd(_NC_CACHE["nc"], in_maps, core_ids=list(range(len(CORES))))
    R = {c: res.results[i] for i, c in enumerate(CORES)}

    def unmode(a):
        return a.reshape(2, 64, 16).transpose(2, 0, 1).reshape(32, 64)
    y_prompt = np.zeros((2, 16384, D), np.float32)
    y_sample = np.zeros((16, 16, D), np.float32)
    p_re = np.zeros((1, 2, 32, 64), np.float32)
    p_im = np.zeros((1, 2, 32, 64), np.float32)
    p_conv = np.zeros((1, 2, 30, 512), np.float32)
    p_mk = np.zeros((1, 2, 256, 4, 256), np.float32)
    p_mv = np.zeros((1, 2, 256, 4, 256), np.float32)
    s_re = np.zeros((1, 16, 32, 64), np.float32)
    s_im = np.zeros((1, 16, 32, 64), np.float32)
    s_conv = np.zeros((1, 16, 30, 512), np.float32)
    for c in CORES:
        b, j = c // 4, c % 4
        r = R[c]
        y_prompt[b, j * SEG:(j + 1) * SEG, :] = r["yT"].T
        ys = r["ysT"].T
        hs = r["hsout"].reshape(128, 16, 2, 2)
        cs = r["convsout"].reshape(128, 4, 2, 30)
        for si in range(2):
            s = 2 * c + si
            y_sample[s] = ys[si * 16:(si + 1) * 16]
            s_re[0, s] = unmode(hs[:, :, si, 0])
            s_im[0, s] = unmode(hs[:, :, si, 1])
            s_conv[0, s] = cs[:, :, si, :].transpose(1, 0, 2).reshape(512, 30).T
        if j == 3:
            ho = r["hout"].reshape(128, 2, 16)
            p_re[0, b] = unmode(ho[:, 0, :])
            p_im[0, b] = unmode(ho[:, 1, :])
            p_conv[0, b] = r["convout"].reshape(128, 4, 30).transpose(1, 0, 2).reshape(512, 30).T
        if j == 0:
            p_mk[0, b] = r["kout"].reshape(256, 4, 256)
            p_mv[0, b] = r["vout"].reshape(256, 4, 256)
    return (y_prompt, y_sample, p_re, p_im, p_conv, p_mk, p_mv, s_re, s_im, s_conv)
```

```python
import os
from contextlib import ExitStack
import numpy as np
import concourse.bass as bass
import concourse.mybir as mybir
from concourse.bass_utils import run_bass_kernel_spmd

F32 = mybir.dt.float32
BF16 = mybir.dt.bfloat16
AF = mybir.ActivationFunctionType
ALU = mybir.AluOpType
AX = mybir.AxisListType

D = 1024
NCORE = 8
SEG = 4096
NPRE = 3 * SEG
T = 256
NT = SEG // T
TS = 128
LN_EPS = 1e-5
ALPHA = 2.0 ** 0.25
MAGIC = 12582912.0
TWO_PI = float(2 * np.pi)
PI = float(np.pi)

PK = {}
_o = 0
for _n, _w in [("ln1_g", 8), ("ln1_b", 8), ("ln2_g", 8), ("ln2_b", 8), ("ln3_g", 8), ("ln3_b", 8),
               ("b1", 32), ("b2", 8), ("glu_b", 4), ("ssm_d", 4), ("conv_b", 4), ("cln_g", 4), ("cln_b", 4),
               ("conv_w", 124), ("a_re", 16), ("a_im", 16), ("logdt", 16), ("tidx", 128), ("eidx", 1), ("eidx2", 1), ("pad", 2)]:
    PK[_n] = (_o, _o + _w)
    _o += _w
NPK = _o

SYNC_SAME = {e: (e in os.environ.get("K_SYNC", "act,dve,pool").split(",")) for e in ("act", "dve", "pool", "pe", "sp")}
NT_RUN = int(os.environ.get("K_NT", NT))
NGST = int(os.environ.get("K_NGST", "8"))
PIPE = bool(int(os.environ.get("K_PIPE", "1")))
GRP = bool(int(os.environ.get("K_GRP", "1")))
PFG2 = bool(int(os.environ.get("K_PFG2", "1")))
DELAY_Y = bool(int(os.environ.get("K_DY", "1")))
MRA = int(os.environ.get("K_MRA", "3"))
MRB = int(os.environ.get("K_MRB", "2"))
SSM_POST = os.environ.get("K_SSMPOST", "dve")
CONV_ENG = os.environ.get("K_CONV", "dve")
SQ_ENG = os.environ.get("K_SQ", "act")
SQ_MOD = int(os.environ.get("K_SQMOD", "2"))
CONV_POOL_CH = int(os.environ.get("K_CPC", "0"))
PF_BANKS = bool(int(os.environ.get("K_PFB", "1")))
RUN_SAMPLE = bool(int(os.environ.get("K_SAMPLE", "1")))
NPB_RUN = int(os.environ.get("K_NPB", NPRE // T))
RUN_PREP = bool(int(os.environ.get("K_PREP", "1")))
RUN_KV = bool(int(os.environ.get("K_KV", "1")))
RUN_PC = os.environ.get("K_PC", "all")
KVD = os.environ.get("K_KVD", "111")
PC_ROWS = int(os.environ.get("K_PCR", "128"))
PC_COLS = int(os.environ.get("K_PCC", "1024"))
CORES = [int(x) for x in os.environ.get("K_CORES", "0,1,2,3,4,5,6,7").split(",")]


class Ev:
    __slots__ = ("eng", "sem", "value", "op", "group")

    def __init__(self, eng):
        self.eng = eng
        self.sem = None
        self.value = None
        self.op = None
        self.group = None


class Buf:
    __slots__ = ("name", "w", "r", "dma_sem", "dma_count", "group")

    def __init__(self, name, group=None):
        self.name = name
        self.w = None
        self.r = []
        self.dma_sem = None
        self.dma_count = 0
        self.group = group


class DmaGroup:
    def __init__(self, name):
        self.name = name
        self.sem = None
        self.count = 0


class Op:
    __slots__ = ("eng", "fn", "waits", "ev", "is_dma", "signals")

    def __init__(self, eng, fn, waits, ev, is_dma):
        self.eng = eng
        self.fn = fn
        self.waits = waits
        self.ev = ev
        self.is_dma = is_dma
        self.signals = is_dma


class Prog:
    ENGINES = ("pe", "act", "dve", "pool", "sp")

    def __init__(self, nc):
        self.nc = nc
        self.ops = {e: [] for e in self.ENGINES}
        self.all_ops = []
        self.dma_bufs = []
        self.groups = []
        self.final_evs = []

    def group(self, name):
        g = DmaGroup(name)
        self.groups.append(g)
        return g

    def _deps(self, ev, reads, writes):
        waits = []
        for b in reads:
            if b.w is not None:
                waits.append(b.w)
        for b in writes:
            if b.w is not None:
                waits.append(b.w)
            waits.extend(b.r)
        for b in reads:
            b.r.append(ev)
        for b in writes:
            b.w = ev
            b.r = []
        out = []
        seen = set()
        for w in waits:
            if w is ev or id(w) in seen:
                continue
            seen.add(id(w))
            out.append(w)
        return out

    def op(self, eng, fn, reads=(), writes=()):
        ev = Ev(eng)
        waits = self._deps(ev, reads, writes)
        o = Op(eng, fn, waits, ev, False)
        ev.op = o
        self.ops[eng].append(o)
        self.all_ops.append(o)
        return o

    def dma(self, eng, fn, key, reads=(), writes=(), final=False):
        ev = Ev("dma")
        waits = self._deps(ev, reads, writes)
        if key.group is not None:
            ev.group = key.group
            key.group.count += 1
        else:
            if key.dma_count == 0:
                self.dma_bufs.append(key)
            key.dma_count += 1
            ev.sem = key
            ev.value = 16 * key.dma_count
        o = Op(eng, fn, waits, ev, True)
        ev.op = o
        self.ops[eng].append(o)
        self.all_ops.append(o)
        if final:
            self.final_evs.append(ev)
        return o

    def emit(self, stack):
        nc = self.nc
        for o in self.all_ops:
            for w in o.waits:
                if w.op is not None and not w.op.is_dma:
                    if w.eng == o.eng and not SYNC_SAME[o.eng]:
                        continue
                    w.op.signals = True
        esem = {}
        for e in ("pe", "act", "dve", "pool"):
            esem[e] = stack.enter_context(nc.semaphore("s_" + e))
            cnt = 0
            for o in self.ops[e]:
                if o.is_dma:
                    continue
                if o.signals:
                    cnt += 1
                    o.ev.sem = esem[e]
                    o.ev.value = cnt
        for b in self.dma_bufs:
            b.dma_sem = stack.enter_context(nc.semaphore("d_" + b.name))
        for g in self.groups:
            if g.count:
                g.sem = stack.enter_context(nc.semaphore("g_" + g.name))

        def resolve(ev):
            if ev.group is not None:
                return ev.group.sem, 16 * ev.group.count
            if isinstance(ev.sem, Buf):
                return ev.sem.dma_sem, ev.value
            return ev.sem, ev.value

        block = stack.enter_context(nc.Block())
        handles = {"pe": "tensor", "act": "scalar", "dve": "vector", "pool": "gpsimd", "sp": "sync"}
        final_evs = self.final_evs

        def make(e):
            ops = self.ops[e]

            def body(eng):
                waited = {}
                for o in ops:
                    for w in o.waits:
                        if w.eng == e and not w.op.is_dma and not SYNC_SAME[e]:
                            continue
                        sem, val = resolve(w)
                        assert sem is not None and val is not None, (e, w.eng)
                        k = id(sem)
                        if waited.get(k, 0) >= val:
                            continue
                        waited[k] = val
                        eng.wait_ge(sem, val)
                    ins = o.fn(eng)
                    if o.is_dma:
                        sem, _ = resolve(o.ev)
                        ins.then_inc(sem, 16)
                    elif o.signals:
                        ins.then_inc(o.ev.sem, 1)
                if e == "sp":
                    for ev in final_evs:
                        sem, val = resolve(ev)
                        if waited.get(id(sem), 0) >= val:
                            continue
                        waited[id(sem)] = val
                        eng.wait_ge(sem, val)
            return body

        for e in self.ENGINES:
            if self.ops[e] or e == "sp":
                getattr(block, handles[e])(make(e))


class Rot:
    def __init__(self, st, nc, name, shape, dtype, n, psum=False):
        self.items = []
        for i in range(n):
            alloc = nc.psum_tensor if psum else nc.sbuf_tensor
            t = st.enter_context(alloc(f"rt_{name}{i}", shape, dtype))
            self.items.append((Buf(f"{name}{i}"), t))
        self.i = 0

    def get(self):
        it = self.items[self.i % len(self.items)]
        self.i += 1
        return it


def build_program():
    nc = bass.Bass("TRN2", target_bir_lowering=False)
    st = ExitStack()
    P = Prog(nc)

    def din(name, shape):
        return nc.dram_tensor(name, shape, F32, kind="ExternalInput").ap()

    def dout(name, shape):
        return nc.dram_tensor(name, shape, F32, kind="ExternalOutput").ap()

    def dscr(name, shape):
        return nc.dram_tensor(name, shape, BF16, kind="Internal").ap()

    xT = din("xT", [D, NPRE + SEG])
    xsT = din("xsT", [D, 32])
    h0_d = din("h0", [128, 64])
    convc_d = din("convc", [128, 240])
    kTs_d = din("kTs", [2, D, 256])
    vs_d = din("vs", [2, 256, D])
    memT_d = din("memT", [D, 256])
    w_in_d = din("w_in", [D, 1536])
    glu_w_d = din("glu_w", [512, 512])
    w_out_d = din("w_out", [D, D])
    w_q_d = din("w_q", [D, D])
    w_k_d = din("w_k", [D, D])
    w_v_d = din("w_v", [D, D])
    w_o_d = din("w_o", [D, D])
    w1_d = din("w1", [D, 4096])
    w2_d = din("w2", [4096, D])
    pk_d = din("pk", [128, NPK])
    rows_d = din("rows", [3, 2048])
    BT_d = din("BT", [2, 128, 2048])
    CT_d = din("CT", [2, 128, 2048])
    Bx_d = din("Bx", [2, 128, 512])

    yT = dout("yT", [D, SEG])
    ysT = dout("ysT", [D, 32])
    hout_d = dout("hout", [128, 32])
    convout_d = dout("convout", [128, 120])
    kout_d = dout("kout", [256, D])
    vout_d = dout("vout", [256, D])
    hsout_d = dout("hsout", [128, 64])
    convsout_d = dout("convsout", [128, 240])

    w_in_b = dscr("w_in_b", [D, 1536])
    w_out_b = dscr("w_out_b", [D, D])
    w_q_b = dscr("w_q_b", [D, D])
    w_k_b = dscr("w_k_b", [D, D])
    w_v_b = dscr("w_v_b", [D, D])
    w_o_b = dscr("w_o_b", [D, D])
    w1_b = dscr("w1_b", [D, 4096])
    w2_b = dscr("w2_b", [4096, D])

    def sb(name, shape, dt=F32):
        return st.enter_context(nc.sbuf_tensor("sb_" + name, shape, dt))

    pk = sb("pk", [128, NPK])
    cgrp = P.group("consts")
    pkB = Buf("pk", group=cgrp)
    ones_m = sb("ones_m", [128, 128], BF16)
    ones_c = sb("ones_c", [128, 128], BF16)
    ones_1 = sb("ones_1", [128, 128], BF16)
    onesB = Buf("ones")
    R0 = sb("R0", [128, 8, T])
    R1 = sb("R1", [128, 8, T])
    xnb = sb("xnb", [128, 8, T], BF16)
    R0B = [Buf(f"R0_{c}") for c in range(8)]
    R1B = [Buf(f"R1_{c}") for c in range(8)]
    xnbB = [Buf(f"xnb_{c}") for c in range(8)]
    ob, obB = xnb, xnbB
    xb = [sb(f"xb{i}", [128, 8, T], BF16) for i in range(2)]
    xbB = [Buf(f"xb{i}") for i in range(2)]
    NRING = 3
    ring = [sb(f"ring{i}", [128, 4096], BF16) for i in range(NRING)]
    ringB = [Buf(f"ring{i}") for i in range(NRING)]
    ring_i = [0]
    fring = sb("fring", [128, 4096], BF16)
    fringB = Buf("fring")
    glu_sb = sb("glu_sb", [128, 4, 512], BF16)
    gluB = Buf("glu")
    u32 = sb("u32", [128, 4, T])
    ub = sb("ub", [128, 4, T], BF16)
    u32B = [Buf(f"u32_{c}") for c in range(4)]
    ubB = [Buf(f"ub_{c}") for c in range(4)]
    vbuf = sb("vbuf", [128, 4, 30 + T])
    vbufB = [Buf(f"vbuf_{c}") for c in range(4)]
    cacc = sb("cacc", [128, 4, T])
    caccB = [Buf(f"cacc_{c}") for c in range(4)]
    mixin = sb("mixin", [128, 8, T], BF16)
    mixB = [Buf(f"mix_{c}") for c in range(8)]
    z32 = sb("z32", [128, 4, T])
    zb = sb("zb", [128, 4, T], BF16)
    z32B = [Buf(f"z32_{c}") for c in range(4)]
    zbB = [Buf(f"zb_{c}") for c in range(4)]
    hid = sb("hid", [128, 32, T], BF16)
    hidB = [Buf(f"hid_{c}") for c in range(32)]
    qb, qbB = hid, hidB
    kT = sb("kT", [128, 8, 256], BF16)
    kTB = Buf("kT")
    vv = sb("vv", [128, 2, D], BF16)
    vvB = Buf("vv")
    costab = sb("costab", [128, 16, TS])
    sintab = sb("sintab", [128, 16, TS])
    rtab = sb("rtab", [128, 16, TS])
    tabB = Buf("tabs")
    BreT = sb("BreT", [128, 16, 128], BF16)
    BimT = sb("BimT", [128, 16, 128], BF16)
    CreT = sb("CreT", [128, 16, 128], BF16)
    CimT = sb("CimT", [128, 16, 128], BF16)
    Pre = sb("Pre", [128, 16, 128], BF16)
    Pim = sb("Pim", [128, 16, 128], BF16)
    lhsB = Buf("ssm_lhs")
    P2re = hid[:, 0:8, :].rearrange("p a b -> p (a b)").rearrange("p (c d) -> p c d", c=16)
    P2im = hid[:, 8:16, :].rearrange("p a b -> p (a b)").rearrange("p (c d) -> p c d", c=16)
    p2B = hidB[0:16]
    Bxr = sb("Bxr", [128, 512])
    Bxi = sb("Bxi", [128, 512])
    BxB = Buf("Bx")
    a128r = sb("a128r", [128, 16])
    a128i = sb("a128i", [128, 16])
    a128B = Buf("a128")
    Hre = sb("Hre", [128, 16])
    Him = sb("Him", [128, 16])
    HB = [Buf(f"H_{p}") for p in range(16)]
    Hs = sb("Hs", [128, 64])
    HsB = [Buf(f"Hs_{p}") for p in range(16)]

    s16 = Rot(st, nc, "s16_", [128, T], BF16, 4)
    f32t = Rot(st, nc, "f32t_", [128, T], F32, 5)
    sst = Rot(st, nc, "sst_", [128, 32 if GRP else TS], F32, 10)
    hbf = Rot(st, nc, "hbf_", [128, 32 if GRP else TS], BF16, 4)
    gst = Rot(st, nc, "gst_", [128, 512], F32, NGST)
    ctmp = Rot(st, nc, "ctmp_", [128, T], F32, 2)
    ghb = Rot(st, nc, "ghb_", [128, 512], BF16, 4)
    tiny = Rot(st, nc, "tiny_", [128, 16], F32, 8)
    pTr = Rot(st, nc, "pT_", [128, T], BF16, 4)
    statsb = Rot(st, nc, "stat_", [128, T], F32, 5)
    psum = Rot(st, nc, "ps", [128, 512], F32, 3, psum=True)
    ypsR = Rot(st, nc, "yps", [128, 512], F32, 1, psum=True)
    buR = Rot(st, nc, "bups", [128, 512], F32, 2, psum=True)
    lnR = Rot(st, nc, "lnps", [128, 512], F32, 2, psum=True)

    def big(i):
        src, bufs = (R0, R0B) if i < 4 else (R1, R1B)
        j = (i % 4) * 2
        return [bufs[j], bufs[j + 1]], src[:, j:j + 2, :].rearrange("p a b -> p (a b)")

    pkc = lambda name, i=0, n=1: pk[:, PK[name][0] + i: PK[name][0] + i + n]

    def tt(eng, out, a, b, op, reads, writes):
        P.op(eng, lambda e: e.tensor_tensor(out=out, in0=a, in1=b, op=op), reads, writes)

    def ts(eng, out, a, s1, s2, op0, op1, reads, writes):
        if op1 is None:
            P.op(eng, lambda e: e.tensor_scalar(out=out, in0=a, scalar1=s1, scalar2=None, op0=op0), reads, writes)
        else:
            P.op(eng, lambda e: e.tensor_scalar(out=out, in0=a, scalar1=s1, scalar2=s2, op0=op0, op1=op1), reads, writes)

    def stt(eng, out, a, s, b, op0, op1, reads, writes):
        P.op(eng, lambda e: e.scalar_tensor_tensor(out=out, in0=a, scalar=s, in1=b, op0=op0, op1=op1), reads, writes)

    def act(out, in_, func, reads, writes, scale=None, bias=None):
        kw = {}
        if scale is not None:
            kw["scale"] = scale
        if bias is not None:
            kw["bias"] = bias
        P.op("act", lambda e: e.activation(out=out, in_=in_, func=func, **kw), reads, writes)

    def cp(eng, out, in_, reads, writes):
        if eng == "act":
            act(out, in_, AF.Copy, reads, writes)
        else:
            P.op(eng, lambda e: e.tensor_copy(out=out, in_=in_), reads, writes)

    def recip(out, in_, reads, writes):
        P.op("dve", lambda e: e.reciprocal(out=out, in_=in_), reads, writes)

    def mm(out, pairs, reads, writes):
        n = len(pairs)

        def fn(e):
            ins = None
            for i, (l, r) in enumerate(pairs):
                ins = e.matmul(out, l, r, start=(i == 0), stop=(i == n - 1))
            return ins
        P.op("pe", fn, reads, writes)

    def mm1(out, l, r, start, stop, reads, writes):
        P.op("pe", lambda e: e.matmul(out, l, r, start=start, stop=stop), reads, writes)

    def dma(eng, out, in_, key, reads=(), writes=(), final=False):
        P.dma(eng, lambda e: e.dma_start(out=out, in_=in_), key, reads=reads, writes=writes, final=final)

    def memset(eng, ap, val, writes):
        P.op(eng, lambda e: e.memset(ap, val), (), writes)

    def range_reduce(out, in_, shift, reads, writes, tmp, tmpB):
        src = in_
        rd = list(reads)
        if shift != 0.0:
            ts("dve", out, in_, shift, None, ALU.add, None, reads, writes)
            src = out
            rd = list(writes)
        ts("dve", tmp, src, 1.0 / TWO_PI, MAGIC, ALU.mult, ALU.add, rd, tmpB)
        ts("dve", tmp, tmp, MAGIC, -TWO_PI, ALU.subtract, ALU.mult, tmpB, tmpB)
        tt("dve", out, src, tmp, ALU.add, rd + list(tmpB), writes)
        ts("dve", out, out, PI, -PI, ALU.min, ALU.max, writes, writes)

    dma("sp", pk[:], pk_d[:, :], pkB, writes=[pkB])
    dma("pool", glu_sb[:], glu_w_d.rearrange("(k p) n -> p k n", p=128), gluB, writes=[gluB])
    memset("dve", ones_m[:], 1.0 / 1024.0, [onesB])
    memset("dve", ones_c[:], 1.0 / 512.0, [onesB])
    memset("dve", ones_1[:], 1.0, [onesB])


    pcsB = [Buf(f"pcs{i}") for i in range(NRING)]
    pclB = [Buf(f"pcl{i}") for i in range(NRING)]

    def precast_gen(bl, dst, src, nrows, colmap=None):
        ncols = src.shape[1]
        nk = nrows // 128
        sv = src.rearrange("(k p) n -> p k n", p=128)
        dv = dst.rearrange("(k p) n -> p k n", p=128)
        if colmap is None:
            colmap = [(c0, c0, min(1024, ncols - c0)) for c0 in range(0, ncols, 1024)]
        for (d0, s0, n) in colmap:
            kstep = max(1, min(nk, 4096 // n))
            for k0 in range(0, nk, kstep):
                i = ring_i[0] % NRING
                ring_i[0] += 1
                view = ring[i][:, 0:kstep * n].rearrange("p (a b) -> p a b", a=kstep)
                dma("pool", view, sv[:, k0:k0 + kstep, s0:s0 + n], pclB[i], writes=[ringB[i]])
                dma("sp", dv[:, k0:k0 + kstep, d0:d0 + n], view, pcsB[i], reads=[ringB[i]], writes=[bl[i]])
                yield

    def precast(name, dst, src, nrows, colmap=None):
        bl = [Buf(f"pcb_{name}{i}") for i in range(NRING)]
        for _ in precast_gen(bl, dst, src, nrows, colmap):
            pass
        return bl

    def precast_later(name, dst, src, nrows):
        bl = [Buf(f"pcb_{name}{i}") for i in range(NRING)]
        late_pc.append(precast_gen(bl, dst, src, nrows))
        return bl

    late_pc = []

    win_map = [(0, 0, 512)]
    for i in range(4):
        win_map.append((512 + 256 * i, 512 + 128 * i, 128))
        win_map.append((512 + 256 * i + 128, 1024 + 128 * i, 128))
    pc = {}
    if RUN_PC == "none":
        def precast(name, dst, src, nrows, colmap=None):
            return [Buf("pcb_" + name)]
        precast_later = lambda name, dst, src, nrows: [Buf("pcb_" + name)]
    pc["w_in"] = precast("w_in", w_in_b, w_in_d, D, win_map)
    pc["w_k"] = precast("w_k", w_k_b, w_k_d, D)
    pc["w_v"] = precast("w_v", w_v_b, w_v_d, D)
    pc["w_out"] = precast_later("w_out", w_out_b, w_out_d, D)
    pc["w_q"] = precast_later("w_q", w_q_b, w_q_d, D)
    pc["w_o"] = precast_later("w_o", w_o_b, w_o_d, D)
    pc["w1"] = precast_later("w1", w1_b, w1_d, D)
    pc["w2"] = precast_later("w2", w2_b, w2_d, 4096)

    def ring_load(src_ap, a, b, pcb):
        i = ring_i[0] % NRING
        ring_i[0] += 1
        view = ring[i][:, 0:a * b].rearrange("p (a b) -> p a b", a=a)
        dma("sp", view, src_ap, ringB[i], reads=pcb, writes=[ringB[i]])
        return ringB[i], view

    def wpiece(wb, pcb, n0, n1):
        return ring_load(wb.rearrange("(k p) n -> p k n", p=128)[:, :, n0:n1], 8, n1 - n0, pcb)

    def fpiece(wb, pcb, n0, n1):
        view = fring[:, 0:8 * (n1 - n0)].rearrange("p (a b) -> p a b", a=8)
        dma("sp", view, wb.rearrange("(k p) n -> p k n", p=128)[:, :, n0:n1], fringB, reads=pcb, writes=[fringB])
        return fringB, view

    def disc(A, Bm, L, tmp, rdB, wB):
        T1, T2, T3, T4, T5, T6, T7 = tmp
        act(L, L, AF.Exp, rdB, wB)
        tt("dve", T1, A, L, ALU.mult, wB, wB)
        tt("dve", T5, Bm, L, ALU.mult, wB, wB)
        act(T6, T1, AF.Exp, wB, wB)
        range_reduce(T2, T5, 0.0, wB, wB, T7, wB)
        act(T2, T2, AF.Sin, wB, wB)
        range_reduce(T3, T5, PI / 2, wB, wB, T7, wB)
        act(T3, T3, AF.Sin, wB, wB)
        tt("dve", T3, T6, T3, ALU.mult, wB, wB)
        tt("dve", T2, T6, T2, ALU.mult, wB, wB)
        ts("dve", T3, T3, -1.0, None, ALU.add, None, wB, wB)
        tt("dve", T6, A, A, ALU.mult, wB, wB)
        tt("dve", T7, Bm, Bm, ALU.mult, wB, wB)
        tt("dve", T6, T6, T7, ALU.add, wB, wB)
        recip(T6, T6, wB, wB)
        tt("dve", L, T3, A, ALU.mult, wB, wB)
        tt("dve", T4, T2, Bm, ALU.mult, wB, wB)
        tt("dve", L, L, T4, ALU.add, wB, wB)
        tt("dve", L, L, T6, ALU.mult, wB, wB)
        tt("dve", T4, T2, A, ALU.mult, wB, wB)
        tt("dve", T3, T3, Bm, ALU.mult, wB, wB)
        tt("dve", T4, T4, T3, ALU.subtract, wB, wB)
        tt("dve", T4, T4, T6, ALU.mult, wB, wB)
        return dict(x1=T1, ang=T5, c_re=L, c_im=T4)

    if RUN_PREP:
        mB_ = [Buf("modeprep")]
        mA, mBm, mL = sb("mA", [128, 16]), sb("mBm", [128, 16]), sb("mL", [128, 16])
        mT = [sb(f"mT{i}", [128, 16]) for i in range(7)]
        cp("dve", mA[:], pkc("a_re", 0, 16), [pkB], mB_)
        cp("dve", mBm[:], pkc("a_im", 0, 16), [pkB], mB_)
        cp("dve", mL[:], pkc("logdt", 0, 16), [pkB], mB_)
        dm = disc(mA[:], mBm[:], mL[:], [t[:] for t in mT], mB_, mB_)
        m128a, m128m, mr = sb("m128a", [128, 16]), sb("m128m", [128, 16]), sb("mr", [128, 16])
        ts("dve", m128a[:], dm["ang"], 256.0 if PFG2 else 128.0, None, ALU.mult, None, mB_, mB_)
        act(m128m[:], dm["x1"], AF.Exp, mB_, mB_, scale=256.0 if PFG2 else 128.0)
        range_reduce(mT[1][:], m128a[:], 0.0, mB_, mB_, mT[6][:], mB_)
        act(mT[1][:], mT[1][:], AF.Sin, mB_, mB_)
        range_reduce(mT[2][:], m128a[:], PI / 2, mB_, mB_, mT[6][:], mB_)
        act(mT[2][:], mT[2][:], AF.Sin, mB_, mB_)
        tt("dve", a128r[:], m128m[:], mT[2][:], ALU.mult, mB_, [a128B])
        tt("dve", a128i[:], m128m[:], mT[1][:], ALU.mult, mB_ + [a128B], [a128B])
        act(mr[:], dm["x1"], AF.Exp, mB_, mB_)
        for p_ in range(16):
            tg, tg2 = gst.get(), gst.get()
            tg = (tg[0], tg[1][:, 0:TS])
            tg2 = (tg2[0], tg2[1][:, 0:TS])
            ts("dve", tg[1][:], pkc("tidx", 0, TS), dm["ang"][:, p_:p_ + 1], None, ALU.mult, None, [pkB] + mB_, [tg[0]])
            range_reduce(sintab[:, p_, :], tg[1][:], 0.0, [tg[0]], [tabB], tg2[1][:], [tg2[0]])
            act(sintab[:, p_, :], sintab[:, p_, :], AF.Sin, [tabB], [tabB])
            range_reduce(costab[:, p_, :], tg[1][:], PI / 2, [tg[0]], [tabB], tg2[1][:], [tg2[0]])
            act(costab[:, p_, :], costab[:, p_, :], AF.Sin, [tabB], [tabB])
            ts("dve", rtab[:, p_, :], pkc("tidx", 0, TS), 0.0, mr[:, p_:p_ + 1], ALU.mult, ALU.add, [pkB] + mB_, [tabB])
            memset("dve", rtab[:, p_, 0:1], 0.0, [tabB])
        bxrB, bxr_t = big(0)
        bxiB, bxi_t = big(1)
        dma("sp", bxr_t, Bx_d[0], bxrB[0], writes=bxrB)
        dma("sp", bxi_t, Bx_d[1], bxiB[0], writes=bxiB)
        for p_ in range(16):
            sl = slice(p_ * 32, p_ * 32 + 32)
            cr, ci = dm["c_re"][:, p_:p_ + 1], dm["c_im"][:, p_:p_ + 1]
            ta, tb_ = gst.get(), gst.get()
            ts("dve", ta[1][:, 0:32], bxi_t[:, sl], ci, None, ALU.mult, None, bxiB + mB_, [ta[0]])
            stt("dve", Bxr[:, sl], bxr_t[:, sl], cr, ta[1][:, 0:32], ALU.mult, ALU.subtract, bxrB + mB_ + [ta[0]], [BxB])
            ts("dve", tb_[1][:, 0:32], bxr_t[:, sl], ci, None, ALU.mult, None, bxrB + mB_, [tb_[0]])
            stt("dve", Bxi[:, sl], bxi_t[:, sl], cr, tb_[1][:, 0:32], ALU.mult, ALU.add, bxiB + mB_ + [tb_[0]], [BxB])

        rt = [R0[:, c, :] for c in range(8)] + [R1[:, c, :] for c in range(8)]
        rtB = R0B + R1B
        tmp_tiles = []
        tmpB = []
        for gi_ in range(4):
            gb_, gt_ = gst.items[gi_]
            tmpB.append(gb_)
            tmp_tiles += [gt_[:, 0:256], gt_[:, 256:512]]
        for blk in range(8):
            cs = slice(blk * 256, blk * 256 + 256)
            base_ = (blk % 2) * 7
            inT = rt[base_:base_ + 7]
            inB = rtB[base_:base_ + 7]
            rA, rBm, rL, btr, bti, ctr, cti = inT
            srcs_ = [rows_d[0:1, cs].partition_broadcast(128), rows_d[1:2, cs].partition_broadcast(128),
                     rows_d[2:3, cs].partition_broadcast(128), BT_d[0][:, cs], BT_d[1][:, cs], CT_d[0][:, cs], CT_d[1][:, cs]]
            for k_ in range(7):
                dma("sp", inT[k_], srcs_[k_], inB[k_], writes=[inB[k_]])
            rowB = inB + tmpB
            tmp = tmp_tiles[0:7]
            dr = disc(rA, rBm, rL, tmp, rowB, rowB)
            sA, sB_ = tmp[1], tmp[2]
            s3, s4 = tmp[5], tmp[6]
            osl = lambda t_: t_[:, blk * 2:blk * 2 + 2, :].rearrange("p a b -> p (a b)")
            tt("dve", sA, dr["c_re"], btr, ALU.mult, rowB, rowB)
            tt("dve", sB_, dr["c_im"], bti, ALU.mult, rowB, rowB)
            tt("dve", osl(BreT), sA, sB_, ALU.subtract, rowB, [lhsB])
            tt("dve", sA, dr["c_re"], bti, ALU.mult, rowB, rowB)
            tt("dve", sB_, dr["c_im"], btr, ALU.mult, rowB, rowB)
            tt("dve", osl(BimT), sA, sB_, ALU.add, rowB + [lhsB], [lhsB])
            e_ap = pkc("eidx")
            act(sA, dr["x1"], AF.Exp, rowB + [pkB], rowB, scale=e_ap)
            ts("dve", sB_, dr["ang"], e_ap, None, ALU.mult, None, rowB + [pkB], rowB)
            range_reduce(s3, sB_, 0.0, rowB, rowB, s4, rowB)
            act(s3, s3, AF.Sin, rowB, rowB)
            tt("dve", osl(Pim), sA, s3, ALU.mult, rowB + [lhsB], [lhsB])
            range_reduce(s3, sB_, PI / 2, rowB, rowB, s4, rowB)
            act(s3, s3, AF.Sin, rowB, rowB)
            tt("dve", osl(Pre), sA, s3, ALU.mult, rowB + [lhsB], [lhsB])
            if PFG2:
                e2_ap = pkc("eidx2")
                act(sA, dr["x1"], AF.Exp, rowB + [pkB], rowB, scale=e2_ap)
                ts("dve", sB_, dr["ang"], e2_ap, None, ALU.mult, None, rowB + [pkB], rowB)
                range_reduce(s3, sB_, 0.0, rowB, rowB, s4, rowB)
                act(s3, s3, AF.Sin, rowB, rowB)
                tt("dve", osl(P2im), sA, s3, ALU.mult, rowB + p2B, p2B)
                range_reduce(s3, sB_, PI / 2, rowB, rowB, s4, rowB)
                act(s3, s3, AF.Sin, rowB, rowB)
                tt("dve", osl(P2re), sA, s3, ALU.mult, rowB + p2B, p2B)
            cp("dve", osl(CreT), ctr, rowB + [lhsB], [lhsB])
            ts("dve", osl(CimT), cti, -1.0, None, ALU.mult, None, rowB + [lhsB], [lhsB])

    def layer_norm(nch, srcs, srcB, ones_ap, Tn, g_name, b_name, emit_out):
        pm, pe2 = lnR.get(), lnR.get()
        for c in range(nch):
            s1, s2 = s16.get(), s16.get()
            act(s1[1][:, :Tn], srcs[c], AF.Copy, [srcB[c]], [s1[0]])
            act(s2[1][:, :Tn], srcs[c], AF.Square, [srcB[c]], [s2[0]])
            mm1(pm[1][:, :Tn], ones_ap, s1[1][:, :Tn], c == 0, c == nch - 1, [s1[0], onesB], [pm[0]])
            mm1(pe2[1][:, :Tn], ones_ap, s2[1][:, :Tn], c == 0, c == nch - 1, [s2[0], onesB], [pe2[0]])
        mean, var, nmr = statsb.get(), statsb.get(), statsb.get()
        cp("act", mean[1][:, :Tn], pm[1][:, :Tn], [pm[0]], [mean[0]])
        tt("dve", var[1][:, :Tn], mean[1][:, :Tn], mean[1][:, :Tn], ALU.mult, [mean[0]], [var[0]])
        tt("dve", var[1][:, :Tn], pe2[1][:, :Tn], var[1][:, :Tn], ALU.subtract, [pe2[0], var[0]], [var[0]])
        ts("dve", var[1][:, :Tn], var[1][:, :Tn], LN_EPS, None, ALU.add, None, [var[0]], [var[0]])
        act(var[1][:, :Tn], var[1][:, :Tn], AF.Sqrt, [var[0]], [var[0]])
        recip(var[1][:, :Tn], var[1][:, :Tn], [var[0]], [var[0]])
        stt("dve", nmr[1][:, :Tn], mean[1][:, :Tn], -1.0, var[1][:, :Tn], ALU.mult, ALU.mult, [mean[0], var[0]], [nmr[0]])
        for c in range(nch):
            t_ = f32t.get()
            tt("dve", t_[1][:, :Tn], srcs[c], var[1][:, :Tn], ALU.mult, [srcB[c], var[0]], [t_[0]])
            tt("dve", t_[1][:, :Tn], t_[1][:, :Tn], nmr[1][:, :Tn], ALU.add, [t_[0], nmr[0]], [t_[0]])
            emit_out(c, t_[1][:, :Tn], t_[0], pkc(g_name, c), pkc(b_name, c))

    def ln_to_resid(Tn, g_name, b_name):
        def emit_out(c, t_ap, tB, g_ap, b_ap):
            act(R1[:, c, :Tn], t_ap, AF.Identity, [tB, pkB], [R1B[c]], scale=g_ap, bias=b_ap)
            act(xnb[:, c, :Tn], t_ap, AF.Identity, [tB, pkB], [xnbB[c]], scale=g_ap, bias=b_ap)
        layer_norm(8, [R0[:, c, :Tn] for c in range(8)], R0B, ones_m[:], Tn, g_name, b_name, emit_out)

    def ssm_segment(pi, col0, L, Hr_ap, Hi_ap, HBuf, ypsum, first, last):
        c = pi // 4
        bu = psum.get()
        bre, bim = bu[1][:, 0:L], bu[1][:, 256:256 + L]
        mm1(bre, BreT[:, pi, :], ub[:, c, col0:col0 + L], True, True, [lhsB, ubB[c]], [bu[0]])
        mm1(bim, BimT[:, pi, :], ub[:, c, col0:col0 + L], True, True, [lhsB, ubB[c]], [bu[0]])
        cosA, sinA, rA = costab[:, pi, 0:L], sintab[:, pi, 0:L], rtab[:, pi, 0:L]
        cos1, sin1 = costab[:, pi, 1:2], sintab[:, pi, 1:2]
        g0, tq = tiny.get(), tiny.get()
        tt("dve", tq[1][:, 0:1], Hi_ap, sin1, ALU.mult, [HBuf, tabB], [tq[0]])
        tt("dve", tq[1][:, 1:2], Hr_ap, cos1, ALU.mult, [HBuf, tabB, tq[0]], [tq[0]])
        tt("dve", tq[1][:, 2:3], Hr_ap, sin1, ALU.mult, [HBuf, tabB, tq[0]], [tq[0]])
        tt("dve", tq[1][:, 3:4], Hi_ap, cos1, ALU.mult, [HBuf, tabB, tq[0]], [tq[0]])
        tt("dve", g0[1][:, 0:1], tq[1][:, 1:2], tq[1][:, 0:1], ALU.subtract, [tq[0]], [g0[0]])
        tt("dve", g0[1][:, 1:2], tq[1][:, 3:4], tq[1][:, 2:3], ALU.add, [tq[0], g0[0]], [g0[0]])
        t1, t2, t3, t4 = sst.get(), sst.get(), sst.get(), sst.get()
        tt("dve", t1[1][:, :L], bre, cosA, ALU.mult, [bu[0], tabB], [t1[0]])
        tt("dve", t2[1][:, :L], bim, sinA, ALU.mult, [bu[0], tabB], [t2[0]])
        tt("dve", t3[1][:, :L], bim, cosA, ALU.mult, [bu[0], tabB], [t3[0]])
        tt("dve", t4[1][:, :L], bre, sinA, ALU.mult, [bu[0], tabB], [t4[0]])
        tt("dve", t1[1][:, :L], t1[1][:, :L], t2[1][:, :L], ALU.add, [t1[0], t2[0]], [t1[0]])
        tt("dve", t3[1][:, :L], t3[1][:, :L], t4[1][:, :L], ALU.subtract, [t3[0], t4[0]], [t3[0]])
        r1 = rtab[:, pi, 1:2]
        tt("dve", g0[1][:, 2:3], g0[1][:, 0:1], r1, ALU.mult, [g0[0], tabB], [g0[0]])
        tt("dve", g0[1][:, 3:4], g0[1][:, 1:2], r1, ALU.mult, [g0[0], tabB], [g0[0]])
        tt("dve", t1[1][:, 0:1], t1[1][:, 0:1], g0[1][:, 2:3], ALU.add, [t1[0], g0[0]], [t1[0]])
        tt("dve", t3[1][:, 0:1], t3[1][:, 0:1], g0[1][:, 3:4], ALU.add, [t3[0], g0[0]], [t3[0]])
        gre, gim = sst.get(), sst.get()
        P.op("dve", lambda e: e.tensor_tensor_scan(out=gre[1][:, :L], data0=rA, data1=t1[1][:, :L], initial=0.0, op0=ALU.mult, op1=ALU.add),
             [tabB, t1[0]], [gre[0]])
        P.op("dve", lambda e: e.tensor_tensor_scan(out=gim[1][:, :L], data0=rA, data1=t3[1][:, :L], initial=0.0, op0=ALU.mult, op1=ALU.add),
             [tabB, t3[0]], [gim[0]])
        p1, p2, p3, p4 = sst.get(), sst.get(), sst.get(), sst.get()
        tt("dve", p1[1][:, :L], gre[1][:, :L], cosA, ALU.mult, [gre[0], tabB], [p1[0]])
        tt("dve", p2[1][:, :L], gim[1][:, :L], sinA, ALU.mult, [gim[0], tabB], [p2[0]])
        tt("dve", p3[1][:, :L], gim[1][:, :L], cosA, ALU.mult, [gim[0], tabB], [p3[0]])
        tt("dve", p4[1][:, :L], gre[1][:, :L], sinA, ALU.mult, [gre[0], tabB], [p4[0]])
        hr, hi = hbf.get(), hbf.get()
        tt("dve", hr[1][:, :L], p1[1][:, :L], p2[1][:, :L], ALU.subtract, [p1[0], p2[0]], [hr[0]])
        tt("dve", hi[1][:, :L], p3[1][:, :L], p4[1][:, :L], ALU.add, [p3[0], p4[0]], [hi[0]])
        tt("dve", Hr_ap, p1[1][:, L - 1:L], p2[1][:, L - 1:L], ALU.subtract, [p1[0], p2[0]], [HBuf])
        tt("dve", Hi_ap, p3[1][:, L - 1:L], p4[1][:, L - 1:L], ALU.add, [p3[0], p4[0], HBuf], [HBuf])
        yo = ypsum[1][:, col0:col0 + L]
        mm1(yo, CreT[:, pi, :], hr[1][:, :L], first, False, [lhsB, hr[0]], [ypsum[0]])
        mm1(yo, CimT[:, pi, :], hi[1][:, :L], False, last, [lhsB, hi[0]], [ypsum[0]])

    def conv_chunk(c, col0, L, eng):
        o = cacc[:, c, col0:col0 + L]
        base = PK["conv_w"][0] + c * 31
        ts(eng, o, vbuf[:, c, 0:L], pk[:, base:base + 1], pkc("conv_b", c), ALU.mult, ALU.add, [vbufB[c], pkB], [caccB[c]])
        for k in range(1, 31):
            stt(eng, o, vbuf[:, c, k:k + L], pk[:, base + k:base + k + 1], o, ALU.mult, ALU.add, [vbufB[c], pkB, caccB[c]], [caccB[c]])

    def v3(t_):
        return t_.rearrange("p (a b) -> p a b", a=4)

    def ssm_s0(c, col0):
        bR, bI = buR.get(), buR.get()

        def fn(e):
            ins = None
            for q in range(4):
                e.matmul(bR[1][:, q * 128:(q + 1) * 128], BreT[:, 4 * c + q, :], ub[:, c, col0:col0 + 128], start=True, stop=True)
                ins = e.matmul(bI[1][:, q * 128:(q + 1) * 128], BimT[:, 4 * c + q, :], ub[:, c, col0:col0 + 128], start=True, stop=True)
            return ins
        P.op("pe", fn, [lhsB, ubB[c]], [bR[0], bI[0]])
        return bR, bI

    def ssm_group(c, col0, yp, filler, bu, after_s2):
        ps4 = slice(4 * c, 4 * c + 4)
        HBs = HB[4 * c:4 * c + 4]
        bR, bI = bu
        cosG, sinG = costab[:, ps4, :], sintab[:, ps4, :]
        rG = rtab[:, ps4, :].rearrange("p a b -> p (a b)")
        cos1, sin1, r1 = costab[:, ps4, 1], sintab[:, ps4, 1], rtab[:, ps4, 1]
        Hr, Hi = Hre[:, ps4], Him[:, ps4]
        tq, g0 = tiny.get(), tiny.get()
        tt("dve", tq[1][:, 0:4], Hi, sin1, ALU.mult, HBs + [tabB], [tq[0]])
        tt("dve", tq[1][:, 4:8], Hr, cos1, ALU.mult, HBs + [tabB, tq[0]], [tq[0]])
        tt("dve", tq[1][:, 8:12], Hr, sin1, ALU.mult, HBs + [tabB, tq[0]], [tq[0]])
        tt("dve", tq[1][:, 12:16], Hi, cos1, ALU.mult, HBs + [tabB, tq[0]], [tq[0]])
        tt("dve", g0[1][:, 0:4], tq[1][:, 4:8], tq[1][:, 0:4], ALU.subtract, [tq[0]], [g0[0]])
        tt("dve", g0[1][:, 4:8], tq[1][:, 12:16], tq[1][:, 8:12], ALU.add, [tq[0], g0[0]], [g0[0]])
        tt("dve", g0[1][:, 8:12], g0[1][:, 0:4], r1, ALU.mult, [g0[0], tabB], [g0[0]])
        tt("dve", g0[1][:, 12:16], g0[1][:, 4:8], r1, ALU.mult, [g0[0], tabB], [g0[0]])
        filler(3)
        yield
        A, B_, C, D_ = gst.get(), gst.get(), gst.get(), gst.get()
        tt("dve", v3(A[1][:]), v3(bR[1][:]), cosG, ALU.mult, [bR[0], tabB], [A[0]])
        tt("dve", v3(B_[1][:]), v3(bI[1][:]), sinG, ALU.mult, [bI[0], tabB], [B_[0]])
        filler(2)
        tt("dve", v3(C[1][:]), v3(bI[1][:]), cosG, ALU.mult, [bI[0], tabB], [C[0]])
        tt("dve", v3(D_[1][:]), v3(bR[1][:]), sinG, ALU.mult, [bR[0], tabB], [D_[0]])
        after_s2()
        filler(2)
        yield
        tt(SSM_POST, A[1][:], A[1][:], B_[1][:], ALU.add, [A[0], B_[0]], [A[0]])
        tt(SSM_POST, v3(A[1][:])[:, :, 0], v3(A[1][:])[:, :, 0], g0[1][:, 8:12], ALU.add, [A[0], g0[0]], [A[0]])
        tt(SSM_POST, C[1][:], C[1][:], D_[1][:], ALU.subtract, [C[0], D_[0]], [C[0]])
        tt(SSM_POST, v3(C[1][:])[:, :, 0], v3(C[1][:])[:, :, 0], g0[1][:, 12:16], ALU.add, [C[0], g0[0]], [C[0]])
        filler(4)
        yield
        GR, GI = gst.get(), gst.get()
        P.op("dve", lambda e: e.tensor_tensor_scan(out=GR[1][:], data0=rG, data1=A[1][:], initial=0.0, op0=ALU.mult, op1=ALU.add),
             [tabB, A[0]], [GR[0]])
        filler(2)
        P.op("dve", lambda e: e.tensor_tensor_scan(out=GI[1][:], data0=rG, data1=C[1][:], initial=0.0, op0=ALU.mult, op1=ALU.add),
             [tabB, C[0]], [GI[0]])
        filler(2)
        yield
        tt(SSM_POST, v3(B_[1][:]), v3(GR[1][:]), cosG, ALU.mult, [GR[0], tabB], [B_[0]])
        tt(SSM_POST, v3(D_[1][:]), v3(GI[1][:]), sinG, ALU.mult, [GI[0], tabB], [D_[0]])
        tt(SSM_POST, v3(A[1][:]), v3(GI[1][:]), cosG, ALU.mult, [GI[0], tabB], [A[0]])
        tt(SSM_POST, v3(C[1][:]), v3(GR[1][:]), sinG, ALU.mult, [GR[0], tabB], [C[0]])
        filler(4)
        yield
        hr, hi = ghb.get(), ghb.get()
        tt("dve", hr[1][:], B_[1][:], D_[1][:], ALU.subtract, [B_[0], D_[0]], [hr[0]])
        filler(1)
        tt("dve", hi[1][:], A[1][:], C[1][:], ALU.add, [A[0], C[0]], [hi[0]])
        filler(1)
        tt("dve", Hr, v3(B_[1][:])[:, :, 127], v3(D_[1][:])[:, :, 127], ALU.subtract, [B_[0], D_[0]], HBs)
        tt("dve", Hi, v3(A[1][:])[:, :, 127], v3(C[1][:])[:, :, 127], ALU.add, [A[0], C[0]] + HBs, HBs)
        yo = yp[1][:, col0:col0 + 128]

        def fn2(e):
            ins = None
            for q in range(4):
                e.matmul(yo, CreT[:, 4 * c + q, :], hr[1][:, q * 128:(q + 1) * 128], start=(q == 0), stop=False)
                ins = e.matmul(yo, CimT[:, 4 * c + q, :], hi[1][:, q * 128:(q + 1) * 128], start=False, stop=(q == 3))
            return ins
        if DELAY_Y:
            pending_y.append(lambda: P.op("pe", fn2, [lhsB, hr[0], hi[0]], [yp[0]]))
        else:
            P.op("pe", fn2, [lhsB, hr[0], hi[0]], [yp[0]])
        yield

    pending_y = []

    def flush_y():
        while pending_y:
            pending_y.pop(0)()

    def front_p(xb_t, xbB_t, tail_fn):
        Tn = T
        s0B, w0 = pre_win[0] if pre_win[0] is not None else fpiece(w_in_b, pc["w_in"], 0, 512)
        pre_win[0] = None
        for c in range(4):
            ps = psum.get()
            mm(ps[1][:, :Tn], [(w0[:, k, c * 128:(c + 1) * 128], xb_t[:, k, :Tn]) for k in range(8)], [s0B, xbB_t], [ps[0]])
            cp("act", u32[:, c, :Tn], ps[1][:, :Tn], [ps[0]], [u32B[c]])
            cp("act", ub[:, c, :Tn], u32[:, c, :Tn], [u32B[c]], [ubB[c]])
            yield
        for half in range(2):
            sB_, wv = fpiece(w_in_b, pc["w_in"], 512 + half * 512, 1024 + half * 512)
            for ci in range(2):
                c = half * 2 + ci
                pa, pg = psum.get(), psum.get()
                mm(pa[1][:, :Tn], [(wv[:, k, ci * 256:ci * 256 + 128], xb_t[:, k, :Tn]) for k in range(8)], [sB_, xbB_t], [pa[0]])
                mm(pg[1][:, :Tn], [(wv[:, k, ci * 256 + 128:ci * 256 + 256], xb_t[:, k, :Tn]) for k in range(8)], [sB_, xbB_t], [pg[0]])
                sg = f32t.get()
                act(sg[1][:, :Tn], pg[1][:, :Tn], AF.Sigmoid, [pg[0]], [sg[0]])
                tt("dve", vbuf[:, c, 30:30 + Tn], pa[1][:, :Tn], sg[1][:, :Tn], ALU.mult, [pa[0], sg[0]], [vbufB[c]])
                yield
        taps = []
        for k in range(31):
            for c in range(4):
                taps.append((c, k))
        tap_i = [0]

        def filler(n):
            for _ in range(n):
                if tap_i[0] >= len(taps):
                    return
                c, k = taps[tap_i[0]]
                tap_i[0] += 1
                o = cacc[:, c, 0:Tn]
                base = PK["conv_w"][0] + c * 31
                ceng = "pool" if c >= 4 - CONV_POOL_CH else "dve"
                if k == 0:
                    ts(ceng, o, vbuf[:, c, 0:Tn], pk[:, base:base + 1], pkc("conv_b", c), ALU.mult, ALU.add, [vbufB[c], pkB], [caccB[c]])
                elif ceng == "dve":
                    stt("dve", o, vbuf[:, c, k:k + Tn], pk[:, base + k:base + k + 1], o, ALU.mult, ALU.add, [vbufB[c], pkB, caccB[c]], [caccB[c]])
                else:
                    tmp_ = ctmp.get()
                    ts("pool", tmp_[1][:], vbuf[:, c, k:k + Tn], pk[:, base + k:base + k + 1], None, ALU.mult, None, [vbufB[c], pkB], [tmp_[0]])
                    tt("pool", o, o, tmp_[1][:], ALU.add, [caccB[c], tmp_[0]], [caccB[c]])

        glist = [(c_, sg_) for c_ in range(4) for sg_ in range(T // TS)]
        bu_next = [ssm_s0(glist[0][0], glist[0][1] * TS)] if GRP else [None]
        gidx = [0]

        def after_s2():
            flush_y()
            gidx[0] += 1
            if gidx[0] < len(glist):
                bu_next[0] = ssm_s0(glist[gidx[0]][0], glist[gidx[0]][1] * TS)

        for c in range(4):
            yp = ypsR.get()
            for sg_ in range(T // TS):
                if GRP:
                    for _ in ssm_group(c, sg_ * TS, yp, filler, bu_next[0], after_s2):
                        yield
                else:
                    for q in range(4):
                        pi = c * 4 + q
                        ssm_segment(pi, sg_ * TS, TS, Hre[:, pi:pi + 1], Him[:, pi:pi + 1], HB[pi], yp, q == 0, q == 3)
                        filler(8)
                        yield
            flush_y()
            zp = f32t.get()
            stt("dve", zp[1][:, :Tn], u32[:, c, :Tn], pkc("ssm_d", c), yp[1][:, :Tn], ALU.mult, ALU.add, [u32B[c], pkB, yp[0]], [zp[0]])
            act(z32[:, c, :Tn], zp[1][:, :Tn], AF.Gelu_apprx_tanh, [zp[0]], [z32B[c]])
            act(zb[:, c, :Tn], zp[1][:, :Tn], AF.Gelu_apprx_tanh, [zp[0]], [zbB[c]])
            yield
        while tap_i[0] < len(taps):
            filler(4)
            yield
        for co in range(4):
            ps = psum.get()
            mm(ps[1][:, :Tn], [(glu_sb[:, k, co * 128:(co + 1) * 128], zb[:, k, :Tn]) for k in range(4)], [gluB] + zbB, [ps[0]])
            sg = f32t.get()
            act(sg[1][:, :Tn], ps[1][:, :Tn], AF.Sigmoid, [ps[0], pkB], [sg[0]], bias=pkc("glu_b", co))
            tt("dve", mixin[:, co, :Tn], z32[:, co, :Tn], sg[1][:, :Tn], ALU.mult, [z32B[co], sg[0]], [mixB[co]])
            yield
        for c in range(4):
            if tail_fn is not None:
                tail_fn(c, Tn)
            cp("pool", vbuf[:, c, 0:30], vbuf[:, c, Tn:Tn + 30], [vbufB[c]], [vbufB[c]])

        def silu_out(c, t_ap, tB, g_ap, b_ap):
            act(mixin[:, 4 + c, :Tn], t_ap, AF.Silu, [tB, pkB], [mixB[4 + c]], scale=g_ap, bias=b_ap)
        pre_win[0] = fpiece(w_in_b, pc["w_in"], 0, 512)
        layer_norm(4, [cacc[:, c, :Tn] for c in range(4)], caccB, ones_c[:], Tn, "cln_g", "cln_b", silu_out)
        yield

    pre_win = [None]
    pre_wout = [None]

    def drain(g):
        for _ in g:
            pass

    def merge(ga, gb):
        da = db = False
        while not (da and db):
            for _ in range(MRA):
                if not da:
                    try:
                        next(ga)
                    except StopIteration:
                        da = True
            for _ in range(MRB):
                if not db:
                    try:
                        next(gb)
                    except StopIteration:
                        db = True

    def run_tile(Tn, xb_t, xbB_t, x32_src, segs, conv_segs, att_segs, y_dst, is_sample=False, part="all"):
        if part in ("all", "front"):
            s0B, w0 = pre_win[0] if pre_win[0] is not None else fpiece(w_in_b, pc["w_in"], 0, 512)
            pre_win[0] = None
            for c in range(4):
                ps = psum.get()
                mm(ps[1][:, :Tn], [(w0[:, k, c * 128:(c + 1) * 128], xb_t[:, k, :Tn]) for k in range(8)], [s0B, xbB_t], [ps[0]])
                cp("act", u32[:, c, :Tn], ps[1][:, :Tn], [ps[0]], [u32B[c]])
                cp("dve", ub[:, c, :Tn], u32[:, c, :Tn], [u32B[c]], [ubB[c]])
            for c in range(4):
                yp = ypsR.get()
                for (col0, L, Hr_fn, Hi_fn, HB_fn) in segs:
                    for q in range(4):
                        pi = c * 4 + q
                        ssm_segment(pi, col0, L, Hr_fn(pi), Hi_fn(pi), HB_fn(pi), yp, q == 0, q == 3)
                    yield
                zp = f32t.get()
                stt("dve", zp[1][:, :Tn], u32[:, c, :Tn], pkc("ssm_d", c), yp[1][:, :Tn], ALU.mult, ALU.add, [u32B[c], pkB, yp[0]], [zp[0]])
                act(z32[:, c, :Tn], zp[1][:, :Tn], AF.Gelu_apprx_tanh, [zp[0]], [z32B[c]])
                act(zb[:, c, :Tn], zp[1][:, :Tn], AF.Gelu_apprx_tanh, [zp[0]], [zbB[c]])
            for co in range(4):
                ps = psum.get()
                mm(ps[1][:, :Tn], [(glu_sb[:, k, co * 128:(co + 1) * 128], zb[:, k, :Tn]) for k in range(4)], [gluB] + zbB, [ps[0]])
                sg = f32t.get()
                act(sg[1][:, :Tn], ps[1][:, :Tn], AF.Sigmoid, [ps[0], pkB], [sg[0]], bias=pkc("glu_b", co))
                tt("dve", mixin[:, co, :Tn], z32[:, co, :Tn], sg[1][:, :Tn], ALU.mult, [z32B[co], sg[0]], [mixB[co]])
            for half in range(2):
                sB_, wv = fpiece(w_in_b, pc["w_in"], 512 + half * 512, 1024 + half * 512)
                for ci in range(2):
                    c = half * 2 + ci
                    pa, pg = psum.get(), psum.get()
                    mm(pa[1][:, :Tn], [(wv[:, k, ci * 256:ci * 256 + 128], xb_t[:, k, :Tn]) for k in range(8)], [sB_, xbB_t], [pa[0]])
                    mm(pg[1][:, :Tn], [(wv[:, k, ci * 256 + 128:ci * 256 + 256], xb_t[:, k, :Tn]) for k in range(8)], [sB_, xbB_t], [pg[0]])
                    sg = f32t.get()
                    act(sg[1][:, :Tn], pg[1][:, :Tn], AF.Sigmoid, [pg[0]], [sg[0]])
                    eng = "dve"
                    if is_sample:
                        vt = f32t.get()
                        tt("dve", vt[1][:, :Tn], pa[1][:, :Tn], sg[1][:, :Tn], ALU.mult, [pa[0], sg[0]], [vt[0]])
                        for (col0, L, halo_fn, tail_fn) in conv_segs:
                            halo_fn(c)
                            cp("pool", vbuf[:, c, 30:30 + L], vt[1][:, col0:col0 + L], [vt[0]], [vbufB[c]])
                            conv_chunk(c, col0, L, eng)
                            tail_fn(c, L)
                    else:
                        (col0, L, halo_fn, tail_fn) = conv_segs[0]
                        tt("dve", vbuf[:, c, 30:30 + Tn], pa[1][:, :Tn], sg[1][:, :Tn], ALU.mult, [pa[0], sg[0]], [vbufB[c]])
                        conv_chunk(c, 0, Tn, eng)
                        if tail_fn is not None:
                            tail_fn(c, Tn)
                        cp("pool", vbuf[:, c, 0:30], vbuf[:, c, Tn:Tn + 30], [vbufB[c]], [vbufB[c]])

            def silu_out(c, t_ap, tB, g_ap, b_ap):
                act(mixin[:, 4 + c, :Tn], t_ap, AF.Silu, [tB, pkB], [mixB[4 + c]], scale=g_ap, bias=b_ap)
            layer_norm(4, [cacc[:, c, :Tn] for c in range(4)], caccB, ones_c[:], Tn, "cln_g", "cln_b", silu_out)
        if part in ("all", "back"):
            dma("sp", R0[:, :, :Tn], x32_src, R0B[0], writes=R0B)
            for half in range(2):
                if half == 0 and pre_wout[0] is not None:
                    sB_, wv = pre_wout[0]
                    pre_wout[0] = None
                else:
                    sB_, wv = wpiece(w_out_b, pc["w_out"], half * 512, half * 512 + 512)
                for ci in range(4):
                    co = half * 4 + ci
                    yield
                    ps = psum.get()
                    mm(ps[1][:, :Tn], [(wv[:, k, ci * 128:(ci + 1) * 128], mixin[:, k, :Tn]) for k in range(8)], [sB_] + mixB, [ps[0]])
                    stt("dve", R0[:, co, :Tn], R0[:, co, :Tn], ALPHA, ps[1][:, :Tn], ALU.mult, ALU.add, [R0B[co], ps[0]], [R0B[co]])
            ln_to_resid(Tn, "ln1_g", "ln1_b")
            qps = []
            for half in range(2):
                sB_, wv = wpiece(w_q_b, pc["w_q"], half * 512, half * 512 + 512)
                for ci in range(4):
                    yield
                    ps = psum.get()
                    mm(ps[1][:, :Tn], [(wv[:, k, ci * 128:(ci + 1) * 128], xnb[:, k, :Tn]) for k in range(8)], [sB_] + xnbB, [ps[0]])
                    act(qb[:, half * 4 + ci, :Tn], ps[1][:, :Tn], AF.Identity, [ps[0]], [qbB[half * 4 + ci]], scale=1.0 / 16.0)
            for (col0, L, kv_loader) in att_segs:
                if kv_loader is not None:
                    kv_loader()
                for h in range(4):
                    pts = []
                    for mc in range(2):
                        yield
                        ps = psum.get()
                        mm(ps[1][:, :L], [(kT[:, h * 2 + dc, mc * 128:(mc + 1) * 128], qb[:, h * 2 + dc, col0:col0 + L]) for dc in range(2)],
                           [kTB, qbB[h * 2], qbB[h * 2 + 1]], [ps[0]])
                        pt = pTr.get()
                        act(pt[1][:, :L], ps[1][:, :L], AF.Exp, [ps[0]], [pt[0]])
                        pts.append(pt)
                    yield
                    ps = psum.get()
                    mm(ps[1][:, :L], [(ones_1[:], pts[mc][1][:, :L]) for mc in range(2)], [onesB, pts[0][0], pts[1][0]], [ps[0]])
                    rinv = f32t.get()
                    recip(rinv[1][:, :L], ps[1][:, :L], [ps[0]], [rinv[0]])
                    for dc in range(2):
                        po = psum.get()
                        mm(po[1][:, :L], [(vv[:, mc, h * 256 + dc * 128: h * 256 + dc * 128 + 128], pts[mc][1][:, :L]) for mc in range(2)],
                           [vvB, pts[0][0], pts[1][0]], [po[0]])
                        tt("dve", ob[:, h * 2 + dc, col0:col0 + L], po[1][:, :L], rinv[1][:, :L], ALU.mult, [po[0], rinv[0]], [obB[h * 2 + dc]])
            for half in range(2):
                sB_, wv = wpiece(w_o_b, pc["w_o"], half * 512, half * 512 + 512)
                for ci in range(4):
                    co = half * 4 + ci
                    yield
                    ps = psum.get()
                    mm(ps[1][:, :Tn], [(wv[:, k, ci * 128:(ci + 1) * 128], ob[:, k, :Tn]) for k in range(8)], [sB_] + obB, [ps[0]])
                    stt("dve", R0[:, co, :Tn], R1[:, co, :Tn], ALPHA, ps[1][:, :Tn], ALU.mult, ALU.add, [R1B[co], ps[0]], [R0B[co]])
            ln_to_resid(Tn, "ln2_g", "ln2_b")
            for piece in range(8):
                sB_, wv = wpiece(w1_b, pc["w1"], piece * 512, piece * 512 + 512)
                for hc in range(4):
                    hidx = piece * 4 + hc
                    yield
                    ps = psum.get()
                    mm(ps[1][:, :Tn], [(wv[:, k, hc * 128:(hc + 1) * 128], xnb[:, k, :Tn]) for k in range(8)], [sB_] + xnbB, [ps[0]])
                    rl = f32t.get()
                    act(rl[1][:, :Tn], ps[1][:, :Tn], AF.Relu, [ps[0], pkB], [rl[0]], bias=pkc("b1", hidx))
                    if SQ_ENG == "act":
                        act(hid[:, hidx, :Tn], rl[1][:, :Tn], AF.Square, [rl[0]], [hidB[hidx]])
                    else:
                        tt("dve", hid[:, hidx, :Tn], rl[1][:, :Tn], rl[1][:, :Tn], ALU.mult, [rl[0]], [hidB[hidx]])
            w2v = w2_b.rearrange("(k p) n -> p k n", p=128)
            for cp_ in range(4):
                yield
                pss = [psum.get(), psum.get()]
                for kh in range(2):
                    sB_, wv = ring_load(w2v[:, kh * 16:kh * 16 + 16, cp_ * 256:cp_ * 256 + 256], 16, 256, pc["w2"])
                    for oc in range(2):
                        for k in range(16):
                            mm1(pss[oc][1][:, :Tn], wv[:, k, oc * 128:(oc + 1) * 128], hid[:, kh * 16 + k, :Tn],
                                kh == 0 and k == 0, kh == 1 and k == 15, [sB_, hidB[kh * 16 + k]], [pss[oc][0]])
                for oc in range(2):
                    co = cp_ * 2 + oc
                    stt("dve", R0[:, co, :Tn], R1[:, co, :Tn], ALPHA, pss[oc][1][:, :Tn], ALU.mult, ALU.add, [R1B[co], pss[oc][0]], [R0B[co]])
                    act(R0[:, co, :Tn], R0[:, co, :Tn], AF.Identity, [R0B[co], pkB], [R0B[co]], bias=pkc("b2", co))
            if not is_sample:
                pre_wout[0] = wpiece(w_out_b, pc["w_out"], 0, 512)
            ln_to_resid(Tn, "ln3_g", "ln3_b")
            dma("sp", y_dst, R1[:, :, :Tn], R1B[0], reads=R1B, final=True)

    xT_v = xT.rearrange("(k p) t -> p k t", p=128)
    if RUN_KV:
        memTb = xb[1]
        dma("pool", memTb[:, :, 0:256], memT_d.rearrange("(k p) m -> p k m", p=128), xbB[1], writes=[xbB[1]])
        kv_i = [4]
        for (wb, pcb, dst_d, is_k) in ((w_k_b, pc["w_k"], kout_d, True), (w_v_b, pc["w_v"], vout_d, False)):
            for half in range(2):
                sB_, wv = wpiece(wb, pcb, half * 512, half * 512 + 512)
                if is_k and KVD[0] == "1":
                    for ci in range(4):
                        ps = psum.get()
                        mm(ps[1][:, :256], [(wv[:, k, ci * 128:(ci + 1) * 128], memTb[:, k, 0:256]) for k in range(8)], [sB_, xbB[1]], [ps[0]])
                        cp("act", kT[:, half * 4 + ci, :], ps[1][:, :256], [ps[0]], [kTB])
                for mc in range(2 if KVD[1] == "1" else 0):
                    ps = psum.get()
                    if KVD[3:4] == "h":
                        mm(ps[1][:, 0:256], [(memTb[:, k, mc * 128:(mc + 1) * 128], wv[:, k, 0:256]) for k in range(8)], [sB_, xbB[1]], [ps[0]])
                        mm(ps[1][:, 256:512], [(memTb[:, k, mc * 128:(mc + 1) * 128], wv[:, k, 256:512]) for k in range(8)], [sB_, xbB[1]], [ps[0]])
                    else:
                        mm(ps[1][:, :], [(memTb[:, k, mc * 128:(mc + 1) * 128], wv[:, k, :]) for k in range(8)], [sB_, xbB[1]], [ps[0]])
                    stgB, stg = big(kv_i[0])
                    kv_i[0] = 4 + (kv_i[0] - 4 + 1) % 4
                    if KVD[5:6] == "s":
                        for hh in range(2):
                            cp("act", stg[:, hh * 256:(hh + 1) * 256], ps[1][:, hh * 256:(hh + 1) * 256], [ps[0]], stgB)
                            if not is_k:
                                cp("dve", vv[:, mc, half * 512 + hh * 256:half * 512 + (hh + 1) * 256], ps[1][:, hh * 256:(hh + 1) * 256], [ps[0]], [vvB])
                    elif KVD[5:6] == "n":
                        pass
                    elif KVD[5:6] == "a":
                        cp("act", stg, ps[1][:], [ps[0]], stgB)
                    elif KVD[5:6] == "v":
                        cp("dve", stg, ps[1][:], [ps[0]], stgB)
                    else:
                        cp("act", stg, ps[1][:], [ps[0]], stgB)
                        if not is_k:
                            cp("dve", vv[:, mc, half * 512:(half + 1) * 512], stg, stgB, [vvB])
                    if KVD[2] == "1":
                        dma("sp", dst_d[mc * 128:(mc + 1) * 128, half * 512:(half + 1) * 512], stg, stgB[0], reads=stgB, final=True)

    memset("dve", Hre[:], 0.0, HB)
    memset("dve", Him[:], 0.0, HB)
    winuB, winu = fpiece(w_in_b, pc["w_in"], 0, 512)
    NPB = NPRE // T
    xi = [0]

    def load_xb(col):
        i = xi[0] % 2
        xi[0] += 1
        dma("pool", xb[i][:, :, :], xT_v[:, :, col:col + T], xbB[i], writes=[xbB[i]])
        return i

    nxt = load_xb((NPB - NPB_RUN) * T)
    qi = [0]
    pf_i = [0]
    def step_late_pc():
        while late_pc:
            try:
                next(late_pc[0])
                return
            except StopIteration:
                late_pc.pop(0)

    for pb in range(NPB - NPB_RUN, NPB):
        cur = nxt
        step_late_pc()
        nxt = load_xb((pb + 1) * T)
        if PFG2:
            ubks = []
            for blk in range(2):
                pu = psum.get()
                mm(pu[1][:, :], [(xb[cur][:, k, blk * 128:(blk + 1) * 128], winu[:, k, :]) for k in range(8)], [xbB[cur], winuB], [pu[0]])
                ubk = ghb.get()
                cp("act", ubk[1][:], pu[1][:], [pu[0]], [ubk[0]])
                ubks.append(ubk)
            pfl = buR.items + lnR.items
            wr, wi = pfl[(pf_i[0] % 2) * 2], pfl[(pf_i[0] % 2) * 2 + 1]
            pf_i[0] += 1

            def fn(e, ubks=ubks, wr=wr, wi=wi):
                ins = None
                for p_ in range(16):
                    sl_ = slice(p_ * 32, (p_ + 1) * 32)
                    e.matmul(wr[1][:, sl_], P2re[:, p_, :], ubks[0][1][:, sl_], start=True, stop=False)
                    e.matmul(wr[1][:, sl_], Pre[:, p_, :], ubks[1][1][:, sl_], start=False, stop=True)
                    e.matmul(wi[1][:, sl_], P2im[:, p_, :], ubks[0][1][:, sl_], start=True, stop=False)
                    ins = e.matmul(wi[1][:, sl_], Pim[:, p_, :], ubks[1][1][:, sl_], start=False, stop=True)
                return ins
            P.op("pe", fn, [lhsB, ubks[0][0], ubks[1][0]] + p2B, [wr[0], wi[0]])
            blocks_ = [(wr, wi)]
        else:
            blocks_ = None
        for blk in range(T // 128 if not PFG2 else 1):
            if not PFG2:
                pu = psum.get()
                mm(pu[1][:, :], [(xb[cur][:, k, blk * 128:(blk + 1) * 128], winu[:, k, :]) for k in range(8)], [xbB[cur], winuB], [pu[0]])
                ubk = ghb.get()
                cp("act", ubk[1][:], pu[1][:], [pu[0]], [ubk[0]])
                pfl = buR.items + lnR.items
                wr, wi = pfl[(pf_i[0] % 2) * 2], pfl[(pf_i[0] % 2) * 2 + 1]
                pf_i[0] += 1

                def fn(e, ubk=ubk, wr=wr, wi=wi):
                    ins = None
                    for p_ in range(16):
                        e.matmul(wr[1][:, p_ * 32:(p_ + 1) * 32], Pre[:, p_, :], ubk[1][:, p_ * 32:(p_ + 1) * 32], start=True, stop=True)
                        ins = e.matmul(wi[1][:, p_ * 32:(p_ + 1) * 32], Pim[:, p_, :], ubk[1][:, p_ * 32:(p_ + 1) * 32], start=True, stop=True)
                    return ins
                P.op("pe", fn, [lhsB, ubk[0]], [wr[0], wi[0]])
            else:
                wr, wi = blocks_[0]
            base = (qi[0] % 2) * 2
            qi[0] += 1
            (q1B, q1), (q2B, q2) = big(base), big(base + 1)
            (q3B, q3), (q4B, q4) = big(4 + base), big(4 + base + 1)
            tt("dve", q1, wr[1][:], Bxr[:], ALU.mult, [wr[0], BxB], q1B)
            tt("dve", q2, wi[1][:], Bxi[:], ALU.mult, [wi[0], BxB], q2B)
            tt("dve", q3, wi[1][:], Bxr[:], ALU.mult, [wi[0], BxB], q3B)
            tt("dve", q4, wr[1][:], Bxi[:], ALU.mult, [wr[0], BxB], q4B)
            tt(SSM_POST, q1, q1, q2, ALU.subtract, q1B + q2B, q1B)
            tt(SSM_POST, q3, q3, q4, ALU.add, q3B + q4B, q3B)
            sr, si = tiny.get(), tiny.get()
            P.op("dve", (lambda sr, q1: lambda e: e.tensor_reduce(out=sr[1][:], in_=q1.rearrange("p (a b) -> p a b", a=16), axis=AX.X, op=ALU.add))(sr, q1), q1B, [sr[0]])
            P.op("dve", (lambda si, q3: lambda e: e.tensor_reduce(out=si[1][:], in_=q3.rearrange("p (a b) -> p a b", a=16), axis=AX.X, op=ALU.add))(si, q3), q3B, [si[0]])
            u1, u2, u3, u4 = tiny.get(), tiny.get(), tiny.get(), tiny.get()
            tt("dve", u1[1][:], a128r[:], Hre[:], ALU.mult, [a128B] + HB, [u1[0]])
            tt("dve", u2[1][:], a128i[:], Him[:], ALU.mult, [a128B] + HB, [u2[0]])
            tt("dve", u3[1][:], a128r[:], Him[:], ALU.mult, [a128B] + HB, [u3[0]])
            tt("dve", u4[1][:], a128i[:], Hre[:], ALU.mult, [a128B] + HB, [u4[0]])
            tt("dve", u1[1][:], u1[1][:], u2[1][:], ALU.subtract, [u1[0], u2[0]], [u1[0]])
            tt("dve", u3[1][:], u3[1][:], u4[1][:], ALU.add, [u3[0], u4[0]], [u3[0]])
            tt("dve", Hre[:], u1[1][:], sr[1][:], ALU.add, [u1[0], sr[0]], HB)
            tt("dve", Him[:], u3[1][:], si[1][:], ALU.add, [u3[0], si[0]] + HB, HB)

    pre_win[0] = (winuB, winu)
    for g_ in late_pc:
        for _ in g_:
            pass
    hxi = xi[0] % 2
    hx, hxB = xb[hxi], xbB[hxi]
    dma("pool", hx[:, :, 0:32], xT_v[:, :, NPRE - 32:NPRE], hxB, writes=[hxB])
    for half in range(2):
        sB_, wv = wpiece(w_in_b, pc["w_in"], 512 + half * 512, 1024 + half * 512)
        for ci in range(2):
            c = half * 2 + ci
            pa, pg = psum.get(), psum.get()
            mm(pa[1][:, :32], [(wv[:, k, ci * 256:ci * 256 + 128], hx[:, k, 0:32]) for k in range(8)], [sB_, hxB], [pa[0]])
            mm(pg[1][:, :32], [(wv[:, k, ci * 256 + 128:ci * 256 + 256], hx[:, k, 0:32]) for k in range(8)], [sB_, hxB], [pg[0]])
            sg = f32t.get()
            act(sg[1][:, :32], pg[1][:, :32], AF.Sigmoid, [pg[0]], [sg[0]])
            tt("dve", vbuf[:, c, 0:30], pa[1][:, 2:32], sg[1][:, 2:32], ALU.mult, [pa[0], sg[0]], [vbufB[c]])

    xs_v = xsT.rearrange("(k p) t -> p k t", p=128)
    hsinB = Buf("hs_in")

    def s_halo(s_):
        def f(c):
            dma("sp", vbuf[:, c, 0:30], convc_d[:, (c * 2 + s_) * 30:(c * 2 + s_) * 30 + 30], vbufB[c], writes=[vbufB[c]])
        return f

    def s_tail(s_):
        def f(c, L):
            dma("sp", convsout_d[:, (c * 2 + s_) * 30:(c * 2 + s_) * 30 + 30], vbuf[:, c, L:L + 30], vbufB[c], reads=[vbufB[c]], final=True)
        return f

    def s_kv(s_):
        def f():
            dma("pool", kT[:], kTs_d[s_].rearrange("(k p) m -> p k m", p=128), kTB, writes=[kTB])
            dma("pool", vv[:], vs_d[s_].rearrange("(k p) n -> p k n", p=128), vvB, writes=[vvB])
        return f

    def hs_fn(s_, reim):
        return lambda pi: Hs[:, pi * 4 + s_ * 2 + reim: pi * 4 + s_ * 2 + reim + 1]

    s_segs = [(s_ * 16, 16, hs_fn(s_, 0), hs_fn(s_, 1), lambda pi: HsB[pi]) for s_ in range(2)]

    def sample_gen(part, bi):
        if part in ("all", "front"):
            dma("sp", Hs[:], h0_d[:, :], hsinB, writes=HsB)
            dma("pool", xb[bi][:, :, 0:32], xs_v, xbB[bi], writes=[xbB[bi]])
        for _ in run_tile(32, xb[bi], xbB[bi], xs_v, s_segs,
                          [(s_ * 16, 16, s_halo(s_), s_tail(s_)) for s_ in range(2)],
                          [(s_ * 16, 16, s_kv(s_)) for s_ in range(2)],
                          ysT.rearrange("(k p) t -> p k t", p=128), is_sample=True, part=part):
            yield
        if part in ("all", "back"):
            hso = statsb.get()
            cp("dve", hso[1][:, 0:64], Hs[:], HsB, [hso[0]])
            dma("sp", hsout_d[:, :], hso[1][:, 0:64], hso[0], reads=[hso[0]], final=True)

    p_segs = [(s * TS, TS, lambda pi: Hre[:, pi:pi + 1], lambda pi: Him[:, pi:pi + 1], lambda pi: HB[pi]) for s in range(T // TS)]
    yT_v = yT.rearrange("(k p) t -> p k t", p=128)
    cur = nxt

    def p_tail(c, L):
        dma("sp", convout_d[:, c * 30:(c + 1) * 30], vbuf[:, c, L:L + 30], vbufB[c], reads=[vbufB[c]], final=True)

    def xload(it, buf_i):
        dma("pool", xb[buf_i][:, :, :], xT_v[:, :, NPRE + it * T: NPRE + (it + 1) * T], xbB[buf_i], writes=[xbB[buf_i]])

    if NT_RUN > 0:
        if NT_RUN > 1:
            xload(1, 1 - cur)
        drain(front_p(xb[cur], xbB[cur], p_tail if NT_RUN == 1 and NT == 1 else None))
    for it in range(NT_RUN):
        back = run_tile(T, xb[cur], xbB[cur], xT_v[:, :, NPRE + it * T: NPRE + (it + 1) * T], p_segs,
                        [(0, T, None, None)], [(0, T, None)], yT_v[:, :, it * T:(it + 1) * T], part="back")
        if it + 1 < NT_RUN:
            nb = 1 - cur
            fr = front_p(xb[nb], xbB[nb], p_tail if it + 1 == NT - 1 else None)
            if it + 2 < NT_RUN:
                xload(it + 2, cur)
            if PIPE:
                merge(back, fr)
            else:
                drain(back)
                drain(fr)
        else:
            if RUN_SAMPLE and PIPE:
                merge(back, sample_gen("front", 1 - cur))
            else:
                drain(back)
        cur = 1 - cur
    hst, hst2 = tiny.get(), tiny.get()
    cp("dve", hst[1][:], Hre[:], HB, [hst[0]])
    cp("dve", hst2[1][:], Him[:], HB, [hst2[0]])
    dma("sp", hout_d[:, 0:16], hst[1][:], hst[0], reads=[hst[0]], final=True)
    dma("sp", hout_d[:, 16:32], hst2[1][:], hst2[0], reads=[hst2[0]], final=True)

    if RUN_SAMPLE:
        if PIPE and NT_RUN > 0:
            drain(sample_gen("back", cur))
        else:
            drain(sample_gen("all", cur))

    P.emit(st)
    st.close()
    return nc


_NC_CACHE = {}


def _mode_major(a):
    sh = a.shape
    a = a.reshape((16, 2, 64) + sh[2:])
    return np.ascontiguousarray(np.moveaxis(a, 0, 2).reshape((128, 16) + sh[2:]))


def kernel(x_prompt, x_sample, state_ssm_re, state_ssm_im, cache_conv, cache_mem_k, cache_mem_v,
           mem_prompt, w_in, ssm_a_re, ssm_a_im, ssm_log_dt, ssm_b_re, ssm_b_im, ssm_c_re, ssm_c_im,
           ssm_d, glu_w, glu_b, conv_w, conv_b, conv_ln_g, conv_ln_b, w_out, ln1_g, ln1_b,
           mem_w_q, mem_w_k, mem_w_v, mem_w_o, ln2_g, ln2_b,
           mlp_w1, mlp_b1, mlp_w2, mlp_b2, ln3_g, ln3_b):
    f = lambda a: np.ascontiguousarray(np.asarray(a, dtype=np.float32))
    x_prompt, x_sample = f(x_prompt), f(x_sample)
    col = lambda v, n: f(v).reshape(n, 128).T
    pk = np.zeros((128, NPK), np.float32)

    def put(name, arr):
        pk[:, PK[name][0]:PK[name][1]] = arr
    put("ln1_g", col(ln1_g[0], 8)); put("ln1_b", col(ln1_b[0], 8))
    put("ln2_g", col(ln2_g[0], 8)); put("ln2_b", col(ln2_b[0], 8))
    put("ln3_g", col(ln3_g[0], 8)); put("ln3_b", col(ln3_b[0], 8))
    put("b1", col(mlp_b1[0], 32)); put("b2", col(mlp_b2[0], 8))
    put("glu_b", col(glu_b[0], 4)); put("ssm_d", col(ssm_d[0], 4))
    put("conv_b", col(conv_b[0], 4)); put("cln_g", col(conv_ln_g[0], 4)); put("cln_b", col(conv_ln_b[0], 4))
    cw = f(conv_w[0]).T.reshape(4, 128, 31).transpose(1, 0, 2).reshape(128, 124)
    put("conv_w", cw)
    put("a_re", _mode_major(f(ssm_a_re[0])[:, :, None])[:, :, 0])
    put("a_im", _mode_major(f(ssm_a_im[0])[:, :, None])[:, :, 0])
    ldt = np.repeat(f(ssm_log_dt[0])[:, None], 64, axis=1)
    put("logdt", _mode_major(ldt[:, :, None])[:, :, 0])
    put("tidx", np.tile(np.arange(128, dtype=np.float32)[None, :], (128, 1)))
    put("eidx", (127.0 - np.arange(128, dtype=np.float32))[:, None])
    put("eidx2", (255.0 - np.arange(128, dtype=np.float32))[:, None])
    rows = np.stack([f(ssm_a_re[0]).reshape(-1), f(ssm_a_im[0]).reshape(-1), ldt.reshape(-1)]).astype(np.float32)
    BT = np.zeros((2, 128, 2048), np.float32)
    CT = np.zeros((2, 128, 2048), np.float32)
    Bx = np.zeros((2, 128, 512), np.float32)
    for ri, (bsrc, csrc) in enumerate(((f(ssm_b_re[0]), f(ssm_c_re[0])), (f(ssm_b_im[0]), f(ssm_c_im[0])))):
        for g in range(32):
            pi, gp, gl = g // 2, g % 2, g % 8
            BT[ri, gl * 16:(gl + 1) * 16, pi * 128 + gp * 64: pi * 128 + gp * 64 + 64] = bsrc[g].T
            CT[ri, gp * 64:(gp + 1) * 64, pi * 128 + gl * 16: pi * 128 + gl * 16 + 16] = csrc[g].T
            Bx[ri, gp * 64:(gp + 1) * 64, pi * 32 + gp * 16: pi * 32 + gp * 16 + 16] = bsrc[g]
    shared = dict(w_in=f(w_in[0]), glu_w=f(glu_w[0]), w_out=f(w_out[0]), w_q=f(mem_w_q[0]), w_k=f(mem_w_k[0]),
                  w_v=f(mem_w_v[0]), w_o=f(mem_w_o[0]), w1=f(mlp_w1[0]), w2=f(mlp_w2[0]), pk=pk, rows=rows, BT=BT, CT=CT, Bx=Bx)
    xTs = [np.ascontiguousarray(x_prompt[b].T) for b in range(2)]
    in_maps = []
    for c in CORES:
        b, j = c // 4, c % 4
        xin = np.zeros((D, NPRE + SEG), np.float32)
        npre = j * SEG
        xin[:, NPRE - npre: NPRE + SEG] = xTs[b][:, 0:(j + 1) * SEG]
        ss = [2 * c, 2 * c + 1]
        xs = np.concatenate([x_sample[s].T for s in ss], axis=1)
        h0 = np.zeros((128, 16, 2, 2), np.float32)
        cc = np.zeros((128, 4, 2, 30), np.float32)
        for si, s in enumerate(ss):
            h0[:, :, si, 0] = _mode_major(f(state_ssm_re[0, s])[:, :, None])[:, :, 0]
            h0[:, :, si, 1] = _mode_major(f(state_ssm_im[0, s])[:, :, None])[:, :, 0]
            cc[:, :, si, :] = f(cache_conv[0, s]).T.reshape(4, 128, 30).transpose(1, 0, 2)
        kTs = np.stack([f(cache_mem_k[0, s]).reshape(256, D).T for s in ss])
        vs = np.stack([f(cache_mem_v[0, s]).reshape(256, D) for s in ss])
        m = dict(shared)
        m.update(xT=xin, xsT=np.ascontiguousarray(xs), h0=h0.reshape(128, 64), convc=cc.reshape(128, 240),
                 kTs=np.ascontiguousarray(kTs), vs=np.ascontiguousarray(vs), memT=np.ascontiguousarray(f(mem_prompt[b]).T))
        in_maps.append(m)
    if "nc" not in _NC_CACHE:
        _NC_CACHE["nc"] = build_program()
    res = run_bass_kernel_spmd(_NC_CACHE["nc"], in_maps, core_ids=list(range(len(CORES))))
    R = {c: res.results[i] for i, c in enumerate(CORES)}

    def unmode(a):
        return a.reshape(2, 64, 16).transpose(2, 0, 1).reshape(32, 64)
    y_prompt = np.zeros((2, 16384, D), np.float32)
    y_sample = np.zeros((16, 16, D), np.float32)
    p_re = np.zeros((1, 2, 32, 64), np.float32)
    p_im = np.zeros((1, 2, 32, 64), np.float32)
    p_conv = np.zeros((1, 2, 30, 512), np.float32)
    p_mk = np.zeros((1, 2, 256, 4, 256), np.float32)
    p_mv = np.zeros((1, 2, 256, 4, 256), np.float32)
    s_re = np.zeros((1, 16, 32, 64), np.float32)
    s_im = np.zeros((1, 16, 32, 64), np.float32)
    s_conv = np.zeros((1, 16, 30, 512), np.float32)
    for c in CORES:
        b, j = c // 4, c % 4
        r = R[c]
        y_prompt[b, j * SEG:(j + 1) * SEG, :] = r["yT"].T
        ys = r["ysT"].T
        hs = r["hsout"].reshape(128, 16, 2, 2)
        cs = r["convsout"].reshape(128, 4, 2, 30)
        for si in range(2):
            s = 2 * c + si
            y_sample[s] = ys[si * 16:(si + 1) * 16]
            s_re[0, s] = unmode(hs[:, :, si, 0])
            s_im[0, s] = unmode(hs[:, :, si, 1])
            s_conv[0, s] = cs[:, :, si, :].transpose(1, 0, 2).reshape(512, 30).T
        if j == 3:
            ho = r["hout"].reshape(128, 2, 16)
            p_re[0, b] = unmode(ho[:, 0, :])
            p_im[0, b] = unmode(ho[:, 1, :])
            p_conv[0, b] = r["convout"].reshape(128, 4, 30).transpose(1, 0, 2).reshape(512, 30).T
        if j == 0:
            p_mk[0, b] = r["kout"].reshape(256, 4, 256)
            p_mv[0, b] = r["vout"].reshape(256, 4, 256)
    return (y_prompt, y_sample, p_re, p_im, p_conv, p_mk, p_mv, s_re, s_im, s_conv)
```

```python
import os
from contextlib import ExitStack
import numpy as np
import concourse.bass as bass
import concourse.mybir as mybir
from concourse.bass_utils import run_bass_kernel_spmd

F32 = mybir.dt.float32
BF16 = mybir.dt.bfloat16
AF = mybir.ActivationFunctionType
ALU = mybir.AluOpType
AX = mybir.AxisListType

D = 1024
NCORE = 8
SEG = 4096
NPRE = 3 * SEG
T = 256
NT = SEG // T
TS = 128
LN_EPS = 1e-5
ALPHA = 2.0 ** 0.25
MAGIC = 12582912.0
TWO_PI = float(2 * np.pi)
PI = float(np.pi)

PK = {}
_o = 0
for _n, _w in [("ln1_g", 8), ("ln1_b", 8), ("ln2_g", 8), ("ln2_b", 8), ("ln3_g", 8), ("ln3_b", 8),
               ("b1", 32), ("b2", 8), ("glu_b", 4), ("ssm_d", 4), ("conv_b", 4), ("cln_g", 4), ("cln_b", 4),
               ("conv_w", 124), ("a_re", 16), ("a_im", 16), ("logdt", 16), ("tidx", 128), ("eidx", 1), ("eidx2", 1), ("pad", 2)]:
    PK[_n] = (_o, _o + _w)
    _o += _w
NPK = _o

SYNC_SAME = {e: (e in os.environ.get("K_SYNC", "act,dve,pool").split(",")) for e in ("act", "dve", "pool", "pe", "sp")}
NT_RUN = int(os.environ.get("K_NT", NT))
NGST = int(os.environ.get("K_NGST", "9"))
PIPE = bool(int(os.environ.get("K_PIPE", "1")))
GRP = bool(int(os.environ.get("K_GRP", "1")))
PFG2 = bool(int(os.environ.get("K_PFG2", "1")))
TAB_BATCH = bool(int(os.environ.get("K_TABB", "1")))
DELAY_Y = bool(int(os.environ.get("K_DY", "1")))
MRA = int(os.environ.get("K_MRA", "3"))
MRB = int(os.environ.get("K_MRB", "2"))
SSM_POST = os.environ.get("K_SSMPOST", "dve")
CONV_ENG = os.environ.get("K_CONV", "dve")
SQ_ENG = os.environ.get("K_SQ", "act")
SQ_MOD = int(os.environ.get("K_SQMOD", "2"))
CONV_POOL_CH = int(os.environ.get("K_CPC", "0"))
PF_BANKS = bool(int(os.environ.get("K_PFB", "1")))
RUN_SAMPLE = bool(int(os.environ.get("K_SAMPLE", "1")))
NPB_RUN = int(os.environ.get("K_NPB", NPRE // T))
RUN_PREP = bool(int(os.environ.get("K_PREP", "1")))
RUN_KV = bool(int(os.environ.get("K_KV", "1")))
RUN_PC = os.environ.get("K_PC", "all")
KVD = os.environ.get("K_KVD", "111")
PC_ROWS = int(os.environ.get("K_PCR", "128"))
PC_COLS = int(os.environ.get("K_PCC", "1024"))
CORES = [int(x) for x in os.environ.get("K_CORES", "0,1,2,3,4,5,6,7").split(",")]


class Ev:
    __slots__ = ("eng", "sem", "value", "op", "group")

    def __init__(self, eng):
        self.eng = eng
        self.sem = None
        self.value = None
        self.op = None
        self.group = None


class Buf:
    __slots__ = ("name", "w", "r", "dma_sem", "dma_count", "group")

    def __init__(self, name, group=None):
        self.name = name
        self.w = None
        self.r = []
        self.dma_sem = None
        self.dma_count = 0
        self.group = group


class DmaGroup:
    def __init__(self, name):
        self.name = name
        self.sem = None
        self.count = 0


class Op:
    __slots__ = ("eng", "fn", "waits", "ev", "is_dma", "signals")

    def __init__(self, eng, fn, waits, ev, is_dma):
        self.eng = eng
        self.fn = fn
        self.waits = waits
        self.ev = ev
        self.is_dma = is_dma
        self.signals = is_dma


class Prog:
    ENGINES = ("pe", "act", "dve", "pool", "sp")

    def __init__(self, nc):
        self.nc = nc
        self.ops = {e: [] for e in self.ENGINES}
        self.all_ops = []
        self.dma_bufs = []
        self.groups = []
        self.final_evs = []

    def group(self, name):
        g = DmaGroup(name)
        self.groups.append(g)
        return g

    def _deps(self, ev, reads, writes):
        waits = []
        for b in reads:
            if b.w is not None:
                waits.append(b.w)
        for b in writes:
            if b.w is not None:
                waits.append(b.w)
            waits.extend(b.r)
        for b in reads:
            b.r.append(ev)
        for b in writes:
            b.w = ev
            b.r = []
        out = []
        seen = set()
        for w in waits:
            if w is ev or id(w) in seen:
                continue
            seen.add(id(w))
            out.append(w)
        return out

    def op(self, eng, fn, reads=(), writes=()):
        ev = Ev(eng)
        waits = self._deps(ev, reads, writes)
        o = Op(eng, fn, waits, ev, False)
        ev.op = o
        self.ops[eng].append(o)
        self.all_ops.append(o)
        return o

    def dma(self, eng, fn, key, reads=(), writes=(), final=False):
        ev = Ev("dma")
        waits = self._deps(ev, reads, writes)
        if key.group is not None:
            ev.group = key.group
            key.group.count += 1
        else:
            if key.dma_count == 0:
                self.dma_bufs.append(key)
            key.dma_count += 1
            ev.sem = key
            ev.value = 16 * key.dma_count
        o = Op(eng, fn, waits, ev, True)
        ev.op = o
        self.ops[eng].append(o)
        self.all_ops.append(o)
        if final:
            self.final_evs.append(ev)
        return o

    def emit(self, stack):
        nc = self.nc
        for o in self.all_ops:
            for w in o.waits:
                if w.op is not None and not w.op.is_dma:
                    if w.eng == o.eng and not SYNC_SAME[o.eng]:
                        continue
                    w.op.signals = True
        esem = {}
        for e in ("pe", "act", "dve", "pool"):
            esem[e] = stack.enter_context(nc.semaphore("s_" + e))
            cnt = 0
            for o in self.ops[e]:
                if o.is_dma:
                    continue
                if o.signals:
                    cnt += 1
                    o.ev.sem = esem[e]
                    o.ev.value = cnt
        for b in self.dma_bufs:
            b.dma_sem = stack.enter_context(nc.semaphore("d_" + b.name))
        for g in self.groups:
            if g.count:
                g.sem = stack.enter_context(nc.semaphore("g_" + g.name))

        def resolve(ev):
            if ev.group is not None:
                return ev.group.sem, 16 * ev.group.count
            if isinstance(ev.sem, Buf):
                return ev.sem.dma_sem, ev.value
            return ev.sem, ev.value

        block = stack.enter_context(nc.Block())
        handles = {"pe": "tensor", "act": "scalar", "dve": "vector", "pool": "gpsimd", "sp": "sync"}
        final_evs = self.final_evs

        def make(e):
            ops = self.ops[e]

            def body(eng):
                waited = {}
                for o in ops:
                    for w in o.waits:
                        if w.eng == e and not w.op.is_dma and not SYNC_SAME[e]:
                            continue
                        sem, val = resolve(w)
                        assert sem is not None and val is not None, (e, w.eng)
                        k = id(sem)
                        if waited.get(k, 0) >= val:
                            continue
                        waited[k] = val
                        eng.wait_ge(sem, val)
                    ins = o.fn(eng)
                    if o.is_dma:
                        sem, _ = resolve(o.ev)
                        ins.then_inc(sem, 16)
                    elif o.signals:
                        ins.then_inc(o.ev.sem, 1)
                if e == "sp":
                    for ev in final_evs:
                        sem, val = resolve(ev)
                        if waited.get(id(sem), 0) >= val:
                            continue
                        waited[id(sem)] = val
                        eng.wait_ge(sem, val)
            return body

        for e in self.ENGINES:
            if self.ops[e] or e == "sp":
                getattr(block, handles[e])(make(e))


class Rot:
    def __init__(self, st, nc, name, shape, dtype, n, psum=False):
        self.items = []
        for i in range(n):
            alloc = nc.psum_tensor if psum else nc.sbuf_tensor
            t = st.enter_context(alloc(f"rt_{name}{i}", shape, dtype))
            self.items.append((Buf(f"{name}{i}"), t))
        self.i = 0

    def get(self):
        it = self.items[self.i % len(self.items)]
        self.i += 1
        return it


def build_program():
    nc = bass.Bass("TRN2", target_bir_lowering=False)
    st = ExitStack()
    P = Prog(nc)

    def din(name, shape):
        return nc.dram_tensor(name, shape, F32, kind="ExternalInput").ap()

    def dout(name, shape):
        return nc.dram_tensor(name, shape, F32, kind="ExternalOutput").ap()

    def dscr(name, shape):
        return nc.dram_tensor(name, shape, BF16, kind="Internal").ap()

    xT = din("xT", [D, NPRE + SEG])
    xsT = din("xsT", [D, 32])
    h0_d = din("h0", [128, 64])
    convc_d = din("convc", [128, 240])
    kTs_d = din("kTs", [2, D, 256])
    vs_d = din("vs", [2, 256, D])
    memT_d = din("memT", [D, 256])
    w_in_d = din("w_in", [D, 1536])
    glu_w_d = din("glu_w", [512, 512])
    w_out_d = din("w_out", [D, D])
    w_q_d = din("w_q", [D, D])
    w_k_d = din("w_k", [D, D])
    w_v_d = din("w_v", [D, D])
    w_o_d = din("w_o", [D, D])
    w1_d = din("w1", [D, 4096])
    w2_d = din("w2", [4096, D])
    pk_d = din("pk", [128, NPK])
    rows_d = din("rows", [3, 2048])
    BT_d = din("BT", [2, 128, 2048])
    CT_d = din("CT", [2, 128, 2048])
    Bx_d = din("Bx", [2, 128, 512])

    yT = dout("yT", [D, SEG])
    ysT = dout("ysT", [D, 32])
    hout_d = dout("hout", [128, 32])
    convout_d = dout("convout", [128, 120])
    kout_d = dout("kout", [256, D])
    vout_d = dout("vout", [256, D])
    hsout_d = dout("hsout", [128, 64])
    convsout_d = dout("convsout", [128, 240])

    w_in_b = dscr("w_in_b", [D, 1536])
    w_out_b = dscr("w_out_b", [D, D])
    w_q_b = dscr("w_q_b", [D, D])
    w_k_b = dscr("w_k_b", [D, D])
    w_v_b = dscr("w_v_b", [D, D])
    w_o_b = dscr("w_o_b", [D, D])
    w1_b = dscr("w1_b", [D, 4096])
    w2_b = dscr("w2_b", [4096, D])

    def sb(name, shape, dt=F32):
        return st.enter_context(nc.sbuf_tensor("sb_" + name, shape, dt))

    pk = sb("pk", [128, NPK])
    cgrp = P.group("consts")
    pkB = Buf("pk", group=cgrp)
    ones_m = sb("ones_m", [128, 128], BF16)
    ones_c = sb("ones_c", [128, 128], BF16)
    ones_1 = sb("ones_1", [128, 128], BF16)
    onesB = Buf("ones")
    R0 = sb("R0", [128, 8, T])
    R1 = sb("R1", [128, 8, T])
    xnb = sb("xnb", [128, 8, T], BF16)
    R0B = [Buf(f"R0_{c}") for c in range(8)]
    R1B = [Buf(f"R1_{c}") for c in range(8)]
    xnbB = [Buf(f"xnb_{c}") for c in range(8)]
    ob, obB = xnb, xnbB
    xb = [sb(f"xb{i}", [128, 8, T], BF16) for i in range(2)]
    xbB = [Buf(f"xb{i}") for i in range(2)]
    NRING = 3
    ring = [sb(f"ring{i}", [128, 4096], BF16) for i in range(NRING)]
    ringB = [Buf(f"ring{i}") for i in range(NRING)]
    ring_i = [0]
    fring = sb("fring", [128, 4096], BF16)
    fringB = Buf("fring")
    glu_sb = sb("glu_sb", [128, 4, 512], BF16)
    gluB = Buf("glu")
    u32 = sb("u32", [128, 4, T])
    ub = sb("ub", [128, 4, T], BF16)
    u32B = [Buf(f"u32_{c}") for c in range(4)]
    ubB = [Buf(f"ub_{c}") for c in range(4)]
    vbuf = sb("vbuf", [128, 4, 30 + T])
    vbufB = [Buf(f"vbuf_{c}") for c in range(4)]
    cacc = sb("cacc", [128, 4, T])
    caccB = [Buf(f"cacc_{c}") for c in range(4)]
    mixin = sb("mixin", [128, 8, T], BF16)
    mixB = [Buf(f"mix_{c}") for c in range(8)]
    z32 = sb("z32", [128, 4, T])
    zb = sb("zb", [128, 4, T], BF16)
    z32B = [Buf(f"z32_{c}") for c in range(4)]
    zbB = [Buf(f"zb_{c}") for c in range(4)]
    hid = sb("hid", [128, 32, T], BF16)
    hidB = [Buf(f"hid_{c}") for c in range(32)]
    qb, qbB = hid, hidB
    kT = sb("kT", [128, 8, 256], BF16)
    kTB = Buf("kT")
    vv = sb("vv", [128, 2, D], BF16)
    vvB = Buf("vv")
    costab = sb("costab", [128, 16, TS])
    sintab = sb("sintab", [128, 16, TS])
    rtab = sb("rtab", [128, 16, TS])
    tabB = Buf("tabs")
    BreT = sb("BreT", [128, 16, 128], BF16)
    BimT = sb("BimT", [128, 16, 128], BF16)
    CreT = sb("CreT", [128, 16, 128], BF16)
    CimT = sb("CimT", [128, 16, 128], BF16)
    Pre = sb("Pre", [128, 16, 128], BF16)
    Pim = sb("Pim", [128, 16, 128], BF16)
    lhsB = Buf("ssm_lhs")
    P2re = hid[:, 0:8, :].rearrange("p a b -> p (a b)").rearrange("p (c d) -> p c d", c=16)
    P2im = hid[:, 8:16, :].rearrange("p a b -> p (a b)").rearrange("p (c d) -> p c d", c=16)
    p2B = hidB[0:16]
    Bxr = sb("Bxr", [128, 512])
    Bxi = sb("Bxi", [128, 512])
    BxB = Buf("Bx")
    a128r = sb("a128r", [128, 16])
    a128i = sb("a128i", [128, 16])
    a128B = Buf("a128")
    Hre = sb("Hre", [128, 16])
    Him = sb("Him", [128, 16])
    HB = [Buf(f"H_{p}") for p in range(16)]
    Hs = sb("Hs", [128, 64])
    HsB = [Buf(f"Hs_{p}") for p in range(16)]

    s16 = Rot(st, nc, "s16_", [128, T], BF16, 4)
    f32t = Rot(st, nc, "f32t_", [128, T], F32, 5)
    sst = Rot(st, nc, "sst_", [128, 32 if GRP else TS], F32, 10)
    hbf = Rot(st, nc, "hbf_", [128, 32 if GRP else TS], BF16, 4)
    gst = Rot(st, nc, "gst_", [128, 512], F32, NGST)
    ctmp = Rot(st, nc, "ctmp_", [128, T], F32, 2) if CONV_POOL_CH > 0 else None
    ghb = Rot(st, nc, "ghb_", [128, 512], BF16, 4)
    tiny = Rot(st, nc, "tiny_", [128, 16], F32, 8)
    pTr = Rot(st, nc, "pT_", [128, T], BF16, 4)
    statsb = Rot(st, nc, "stat_", [128, T], F32, 5)
    psum = Rot(st, nc, "ps", [128, 512], F32, 3, psum=True)
    ypsR = Rot(st, nc, "yps", [128, 512], F32, 1, psum=True)
    buR = Rot(st, nc, "bups", [128, 512], F32, 2, psum=True)
    lnR = Rot(st, nc, "lnps", [128, 512], F32, 2, psum=True)

    def big(i):
        src, bufs = (R0, R0B) if i < 4 else (R1, R1B)
        j = (i % 4) * 2
        return [bufs[j], bufs[j + 1]], src[:, j:j + 2, :].rearrange("p a b -> p (a b)")

    pkc = lambda name, i=0, n=1: pk[:, PK[name][0] + i: PK[name][0] + i + n]

    def tt(eng, out, a, b, op, reads, writes):
        P.op(eng, lambda e: e.tensor_tensor(out=out, in0=a, in1=b, op=op), reads, writes)

    def ts(eng, out, a, s1, s2, op0, op1, reads, writes):
        if op1 is None:
            P.op(eng, lambda e: e.tensor_scalar(out=out, in0=a, scalar1=s1, scalar2=None, op0=op0), reads, writes)
        else:
            P.op(eng, lambda e: e.tensor_scalar(out=out, in0=a, scalar1=s1, scalar2=s2, op0=op0, op1=op1), reads, writes)

    def stt(eng, out, a, s, b, op0, op1, reads, writes):
        P.op(eng, lambda e: e.scalar_tensor_tensor(out=out, in0=a, scalar=s, in1=b, op0=op0, op1=op1), reads, writes)

    def act(out, in_, func, reads, writes, scale=None, bias=None):
        kw = {}
        if scale is not None:
            kw["scale"] = scale
        if bias is not None:
            kw["bias"] = bias
        P.op("act", lambda e: e.activation(out=out, in_=in_, func=func, **kw), reads, writes)

    def cp(eng, out, in_, reads, writes):
        if eng == "act":
            act(out, in_, AF.Copy, reads, writes)
        else:
            P.op(eng, lambda e: e.tensor_copy(out=out, in_=in_), reads, writes)

    def recip(out, in_, reads, writes):
        P.op("dve", lambda e: e.reciprocal(out=out, in_=in_), reads, writes)

    def mm(out, pairs, reads, writes):
        n = len(pairs)

        def fn(e):
            ins = None
            for i, (l, r) in enumerate(pairs):
                ins = e.matmul(out, l, r, start=(i == 0), stop=(i == n - 1))
            return ins
        P.op("pe", fn, reads, writes)

    def mm1(out, l, r, start, stop, reads, writes):
        P.op("pe", lambda e: e.matmul(out, l, r, start=start, stop=stop), reads, writes)

    def dma(eng, out, in_, key, reads=(), writes=(), final=False):
        P.dma(eng, lambda e: e.dma_start(out=out, in_=in_), key, reads=reads, writes=writes, final=final)

    def memset(eng, ap, val, writes):
        P.op(eng, lambda e: e.memset(ap, val), (), writes)

    def range_reduce(out, in_, shift, reads, writes, tmp, tmpB):
        src = in_
        rd = list(reads)
        if shift != 0.0:
            ts("dve", out, in_, shift, None, ALU.add, None, reads, writes)
            src = out
            rd = list(writes)
        ts("dve", tmp, src, 1.0 / TWO_PI, MAGIC, ALU.mult, ALU.add, rd, tmpB)
        ts("dve", tmp, tmp, MAGIC, -TWO_PI, ALU.subtract, ALU.mult, tmpB, tmpB)
        tt("dve", out, src, tmp, ALU.add, rd + list(tmpB), writes)
        ts("dve", out, out, PI, -PI, ALU.min, ALU.max, writes, writes)

    dma("sp", pk[:], pk_d[:, :], pkB, writes=[pkB])
    dma("pool", glu_sb[:], glu_w_d.rearrange("(k p) n -> p k n", p=128), gluB, writes=[gluB])
    memset("dve", ones_m[:], 1.0 / 1024.0, [onesB])
    memset("dve", ones_c[:], 1.0 / 512.0, [onesB])
    memset("dve", ones_1[:], 1.0, [onesB])


    pcsB = [Buf(f"pcs{i}") for i in range(NRING)]
    pclB = [Buf(f"pcl{i}") for i in range(NRING)]

    def precast_gen(bl, dst, src, nrows, colmap=None):
        ncols = src.shape[1]
        nk = nrows // 128
        sv = src.rearrange("(k p) n -> p k n", p=128)
        dv = dst.rearrange("(k p) n -> p k n", p=128)
        if colmap is None:
            colmap = [(c0, c0, min(1024, ncols - c0)) for c0 in range(0, ncols, 1024)]
        for (d0, s0, n) in colmap:
            kstep = max(1, min(nk, 4096 // n))
            for k0 in range(0, nk, kstep):
                i = ring_i[0] % NRING
                ring_i[0] += 1
                view = ring[i][:, 0:kstep * n].rearrange("p (a b) -> p a b", a=kstep)
                dma("pool", view, sv[:, k0:k0 + kstep, s0:s0 + n], pclB[i], writes=[ringB[i]])
                dma("sp", dv[:, k0:k0 + kstep, d0:d0 + n], view, pcsB[i], reads=[ringB[i]], writes=[bl[i]])
                yield

    def precast(name, dst, src, nrows, colmap=None):
        bl = [Buf(f"pcb_{name}{i}") for i in range(NRING)]
        for _ in precast_gen(bl, dst, src, nrows, colmap):
            pass
        return bl

    def precast_later(name, dst, src, nrows):
        bl = [Buf(f"pcb_{name}{i}") for i in range(NRING)]
        late_pc.append(precast_gen(bl, dst, src, nrows))
        return bl

    late_pc = []

    win_map = [(0, 0, 512)]
    for i in range(4):
        win_map.append((512 + 256 * i, 512 + 128 * i, 128))
        win_map.append((512 + 256 * i + 128, 1024 + 128 * i, 128))
    pc = {}
    if RUN_PC == "none":
        def precast(name, dst, src, nrows, colmap=None):
            return [Buf("pcb_" + name)]
        precast_later = lambda name, dst, src, nrows: [Buf("pcb_" + name)]
    pc["w_in"] = precast("w_in", w_in_b, w_in_d, D, win_map)
    pc["w_k"] = precast("w_k", w_k_b, w_k_d, D)
    pc["w_v"] = precast("w_v", w_v_b, w_v_d, D)
    pc["w_out"] = precast_later("w_out", w_out_b, w_out_d, D)
    pc["w_q"] = precast_later("w_q", w_q_b, w_q_d, D)
    pc["w_o"] = precast_later("w_o", w_o_b, w_o_d, D)
    pc["w1"] = precast_later("w1", w1_b, w1_d, D)
    pc["w2"] = precast_later("w2", w2_b, w2_d, 4096)

    def ring_load(src_ap, a, b, pcb):
        i = ring_i[0] % NRING
        ring_i[0] += 1
        view = ring[i][:, 0:a * b].rearrange("p (a b) -> p a b", a=a)
        dma("sp", view, src_ap, ringB[i], reads=pcb, writes=[ringB[i]])
        return ringB[i], view

    def wpiece(wb, pcb, n0, n1):
        return ring_load(wb.rearrange("(k p) n -> p k n", p=128)[:, :, n0:n1], 8, n1 - n0, pcb)

    def fpiece(wb, pcb, n0, n1):
        view = fring[:, 0:8 * (n1 - n0)].rearrange("p (a b) -> p a b", a=8)
        dma("sp", view, wb.rearrange("(k p) n -> p k n", p=128)[:, :, n0:n1], fringB, reads=pcb, writes=[fringB])
        return fringB, view

    def disc(A, Bm, L, tmp, rdB, wB):
        T1, T2, T3, T4, T5, T6, T7 = tmp
        act(L, L, AF.Exp, rdB, wB)
        tt("dve", T1, A, L, ALU.mult, wB, wB)
        tt("dve", T5, Bm, L, ALU.mult, wB, wB)
        act(T6, T1, AF.Exp, wB, wB)
        range_reduce(T2, T5, 0.0, wB, wB, T7, wB)
        act(T2, T2, AF.Sin, wB, wB)
        range_reduce(T3, T5, PI / 2, wB, wB, T7, wB)
        act(T3, T3, AF.Sin, wB, wB)
        tt("dve", T3, T6, T3, ALU.mult, wB, wB)
        tt("dve", T2, T6, T2, ALU.mult, wB, wB)
        ts("dve", T3, T3, -1.0, None, ALU.add, None, wB, wB)
        tt("dve", T6, A, A, ALU.mult, wB, wB)
        tt("dve", T7, Bm, Bm, ALU.mult, wB, wB)
        tt("dve", T6, T6, T7, ALU.add, wB, wB)
        recip(T6, T6, wB, wB)
        tt("dve", L, T3, A, ALU.mult, wB, wB)
        tt("dve", T4, T2, Bm, ALU.mult, wB, wB)
        tt("dve", L, L, T4, ALU.add, wB, wB)
        tt("dve", L, L, T6, ALU.mult, wB, wB)
        tt("dve", T4, T2, A, ALU.mult, wB, wB)
        tt("dve", T3, T3, Bm, ALU.mult, wB, wB)
        tt("dve", T4, T4, T3, ALU.subtract, wB, wB)
        tt("dve", T4, T4, T6, ALU.mult, wB, wB)
        return dict(x1=T1, ang=T5, c_re=L, c_im=T4)

    if RUN_PREP:
        mB_ = [Buf("modeprep")]
        mA, mBm, mL = sb("mA", [128, 16]), sb("mBm", [128, 16]), sb("mL", [128, 16])
        mT = [sb(f"mT{i}", [128, 16]) for i in range(7)]
        cp("dve", mA[:], pkc("a_re", 0, 16), [pkB], mB_)
        cp("dve", mBm[:], pkc("a_im", 0, 16), [pkB], mB_)
        cp("dve", mL[:], pkc("logdt", 0, 16), [pkB], mB_)
        dm = disc(mA[:], mBm[:], mL[:], [t[:] for t in mT], mB_, mB_)
        m128a, m128m, mr = sb("m128a", [128, 16]), sb("m128m", [128, 16]), sb("mr", [128, 16])
        ts("dve", m128a[:], dm["ang"], 256.0 if PFG2 else 128.0, None, ALU.mult, None, mB_, mB_)
        act(m128m[:], dm["x1"], AF.Exp, mB_, mB_, scale=256.0 if PFG2 else 128.0)
        range_reduce(mT[1][:], m128a[:], 0.0, mB_, mB_, mT[6][:], mB_)
        act(mT[1][:], mT[1][:], AF.Sin, mB_, mB_)
        range_reduce(mT[2][:], m128a[:], PI / 2, mB_, mB_, mT[6][:], mB_)
        act(mT[2][:], mT[2][:], AF.Sin, mB_, mB_)
        tt("dve", a128r[:], m128m[:], mT[2][:], ALU.mult, mB_, [a128B])
        tt("dve", a128i[:], m128m[:], mT[1][:], ALU.mult, mB_ + [a128B], [a128B])
        act(mr[:], dm["x1"], AF.Exp, mB_, mB_)
        if TAB_BATCH:
            argT = R0[:].rearrange("p a b -> p (a b)")
            tmpT = R1[:].rearrange("p a b -> p (a b)")
            fl = lambda t_: t_[:].rearrange("p a b -> p (a b)")
            v16 = lambda t_: t_.rearrange("p (a b) -> p a b", a=16)
            tt("dve", v16(argT), pkc("tidx", 0, TS).unsqueeze(1).to_broadcast([128, 16, TS]),
               dm["ang"].unsqueeze(2).to_broadcast([128, 16, TS]), ALU.mult, [pkB] + mB_, R0B)
            range_reduce(fl(sintab), argT, 0.0, R0B, [tabB], tmpT, R1B)
            act(fl(sintab), fl(sintab), AF.Sin, [tabB], [tabB])
            range_reduce(fl(costab), argT, PI / 2, R0B, [tabB], tmpT, R1B)
            act(fl(costab), fl(costab), AF.Sin, [tabB], [tabB])
            cp("dve", rtab[:], mr[:].unsqueeze(2).to_broadcast([128, 16, TS]), mB_, [tabB])
            memset("dve", rtab[:, :, 0:1], 0.0, [tabB])
        else:
            for p_ in range(16):
                tg, tg2 = gst.get(), gst.get()
                tg = (tg[0], tg[1][:, 0:TS])
                tg2 = (tg2[0], tg2[1][:, 0:TS])
                ts("dve", tg[1][:], pkc("tidx", 0, TS), dm["ang"][:, p_:p_ + 1], None, ALU.mult, None, [pkB] + mB_, [tg[0]])
                range_reduce(sintab[:, p_, :], tg[1][:], 0.0, [tg[0]], [tabB], tg2[1][:], [tg2[0]])
                act(sintab[:, p_, :], sintab[:, p_, :], AF.Sin, [tabB], [tabB])
                range_reduce(costab[:, p_, :], tg[1][:], PI / 2, [tg[0]], [tabB], tg2[1][:], [tg2[0]])
                act(costab[:, p_, :], costab[:, p_, :], AF.Sin, [tabB], [tabB])
                ts("dve", rtab[:, p_, :], pkc("tidx", 0, TS), 0.0, mr[:, p_:p_ + 1], ALU.mult, ALU.add, [pkB] + mB_, [tabB])
                memset("dve", rtab[:, p_, 0:1], 0.0, [tabB])
        bxrB, bxr_t = big(0)
        bxiB, bxi_t = big(1)
        dma("sp", bxr_t, Bx_d[0], bxrB[0], writes=bxrB)
        dma("sp", bxi_t, Bx_d[1], bxiB[0], writes=bxiB)
        if TAB_BATCH:
            v32 = lambda t_: t_.rearrange("p (a b) -> p a b", a=16)
            crb = dm["c_re"].unsqueeze(2).to_broadcast([128, 16, 32])
            cib = dm["c_im"].unsqueeze(2).to_broadcast([128, 16, 32])
            ta, tb_ = gst.get(), gst.get()
            tt("dve", v32(ta[1][:]), v32(bxi_t), cib, ALU.mult, bxiB + mB_, [ta[0]])
            tt("dve", v32(tb_[1][:]), v32(bxr_t), crb, ALU.mult, bxrB + mB_, [tb_[0]])
            tt("dve", Bxr[:], tb_[1][:], ta[1][:], ALU.subtract, [ta[0], tb_[0]], [BxB])
            tt("dve", v32(ta[1][:]), v32(bxr_t), cib, ALU.mult, bxrB + mB_ + [BxB], [ta[0]])
            tt("dve", v32(tb_[1][:]), v32(bxi_t), crb, ALU.mult, bxiB + mB_ + [BxB], [tb_[0]])
            tt("dve", Bxi[:], tb_[1][:], ta[1][:], ALU.add, [ta[0], tb_[0], BxB], [BxB])
        else:
            for p_ in range(16):
                sl = slice(p_ * 32, p_ * 32 + 32)
                cr, ci = dm["c_re"][:, p_:p_ + 1], dm["c_im"][:, p_:p_ + 1]
                ta, tb_ = gst.get(), gst.get()
                ts("dve", ta[1][:, 0:32], bxi_t[:, sl], ci, None, ALU.mult, None, bxiB + mB_, [ta[0]])
                stt("dve", Bxr[:, sl], bxr_t[:, sl], cr, ta[1][:, 0:32], ALU.mult, ALU.subtract, bxrB + mB_ + [ta[0]], [BxB])
                ts("dve", tb_[1][:, 0:32], bxr_t[:, sl], ci, None, ALU.mult, None, bxrB + mB_, [tb_[0]])
                stt("dve", Bxi[:, sl], bxi_t[:, sl], cr, tb_[1][:, 0:32], ALU.mult, ALU.add, bxiB + mB_ + [tb_[0]], [BxB])

        rt = [R0[:, c, :] for c in range(8)] + [R1[:, c, :] for c in range(8)]
        rtB = R0B + R1B
        tmp_tiles = []
        tmpB = []
        for gi_ in range(4):
            gb_, gt_ = gst.items[gi_]
            tmpB.append(gb_)
            tmp_tiles += [gt_[:, 0:256], gt_[:, 256:512]]
        for blk in range(8):
            cs = slice(blk * 256, blk * 256 + 256)
            base_ = (blk % 2) * 7
            inT = rt[base_:base_ + 7]
            inB = rtB[base_:base_ + 7]
            rA, rBm, rL, btr, bti, ctr, cti = inT
            srcs_ = [rows_d[0:1, cs].partition_broadcast(128), rows_d[1:2, cs].partition_broadcast(128),
                     rows_d[2:3, cs].partition_broadcast(128), BT_d[0][:, cs], BT_d[1][:, cs], CT_d[0][:, cs], CT_d[1][:, cs]]
            for k_ in range(7):
                dma("sp", inT[k_], srcs_[k_], inB[k_], writes=[inB[k_]])
            rowB = inB + tmpB
            tmp = tmp_tiles[0:7]
            dr = disc(rA, rBm, rL, tmp, rowB, rowB)
            sA, sB_ = tmp[1], tmp[2]
            s3, s4 = tmp[5], tmp[6]
            osl = lambda t_: t_[:, blk * 2:blk * 2 + 2, :].rearrange("p a b -> p (a b)")
            tt("dve", sA, dr["c_re"], btr, ALU.mult, rowB, rowB)
            tt("dve", sB_, dr["c_im"], bti, ALU.mult, rowB, rowB)
            tt("dve", osl(BreT), sA, sB_, ALU.subtract, rowB, [lhsB])
            tt("dve", sA, dr["c_re"], bti, ALU.mult, rowB, rowB)
            tt("dve", sB_, dr["c_im"], btr, ALU.mult, rowB, rowB)
            tt("dve", osl(BimT), sA, sB_, ALU.add, rowB + [lhsB], [lhsB])
            e_ap = pkc("eidx")
            act(sA, dr["x1"], AF.Exp, rowB + [pkB], rowB, scale=e_ap)
            ts("dve", sB_, dr["ang"], e_ap, None, ALU.mult, None, rowB + [pkB], rowB)
            range_reduce(s3, sB_, 0.0, rowB, rowB, s4, rowB)
            act(s3, s3, AF.Sin, rowB, rowB)
            tt("dve", osl(Pim), sA, s3, ALU.mult, rowB + [lhsB], [lhsB])
            range_reduce(s3, sB_, PI / 2, rowB, rowB, s4, rowB)
            act(s3, s3, AF.Sin, rowB, rowB)
            tt("dve", osl(Pre), sA, s3, ALU.mult, rowB + [lhsB], [lhsB])
            if PFG2:
                e2_ap = pkc("eidx2")
                act(sA, dr["x1"], AF.Exp, rowB + [pkB], rowB, scale=e2_ap)
                ts("dve", sB_, dr["ang"], e2_ap, None, ALU.mult, None, rowB + [pkB], rowB)
                range_reduce(s3, sB_, 0.0, rowB, rowB, s4, rowB)
                act(s3, s3, AF.Sin, rowB, rowB)
                tt("dve", osl(P2im), sA, s3, ALU.mult, rowB + p2B, p2B)
                range_reduce(s3, sB_, PI / 2, rowB, rowB, s4, rowB)
                act(s3, s3, AF.Sin, rowB, rowB)
                tt("dve", osl(P2re), sA, s3, ALU.mult, rowB + p2B, p2B)
            cp("dve", osl(CreT), ctr, rowB + [lhsB], [lhsB])
            ts("dve", osl(CimT), cti, -1.0, None, ALU.mult, None, rowB + [lhsB], [lhsB])

    def layer_norm(nch, srcs, srcB, ones_ap, Tn, g_name, b_name, emit_out):
        pm, pe2 = lnR.get(), lnR.get()
        for c in range(nch):
            s1, s2 = s16.get(), s16.get()
            act(s1[1][:, :Tn], srcs[c], AF.Copy, [srcB[c]], [s1[0]])
            act(s2[1][:, :Tn], srcs[c], AF.Square, [srcB[c]], [s2[0]])
            mm1(pm[1][:, :Tn], ones_ap, s1[1][:, :Tn], c == 0, c == nch - 1, [s1[0], onesB], [pm[0]])
            mm1(pe2[1][:, :Tn], ones_ap, s2[1][:, :Tn], c == 0, c == nch - 1, [s2[0], onesB], [pe2[0]])
        mean, var, nmr = statsb.get(), statsb.get(), statsb.get()
        cp("act", mean[1][:, :Tn], pm[1][:, :Tn], [pm[0]], [mean[0]])
        tt("dve", var[1][:, :Tn], mean[1][:, :Tn], mean[1][:, :Tn], ALU.mult, [mean[0]], [var[0]])
        tt("dve", var[1][:, :Tn], pe2[1][:, :Tn], var[1][:, :Tn], ALU.subtract, [pe2[0], var[0]], [var[0]])
        ts("dve", var[1][:, :Tn], var[1][:, :Tn], LN_EPS, None, ALU.add, None, [var[0]], [var[0]])
        act(var[1][:, :Tn], var[1][:, :Tn], AF.Sqrt, [var[0]], [var[0]])
        recip(var[1][:, :Tn], var[1][:, :Tn], [var[0]], [var[0]])
        stt("dve", nmr[1][:, :Tn], mean[1][:, :Tn], -1.0, var[1][:, :Tn], ALU.mult, ALU.mult, [mean[0], var[0]], [nmr[0]])
        for c in range(nch):
            t_ = f32t.get()
            tt("dve", t_[1][:, :Tn], srcs[c], var[1][:, :Tn], ALU.mult, [srcB[c], var[0]], [t_[0]])
            tt("dve", t_[1][:, :Tn], t_[1][:, :Tn], nmr[1][:, :Tn], ALU.add, [t_[0], nmr[0]], [t_[0]])
            emit_out(c, t_[1][:, :Tn], t_[0], pkc(g_name, c), pkc(b_name, c))

    def ln_to_resid(Tn, g_name, b_name):
        def emit_out(c, t_ap, tB, g_ap, b_ap):
            act(R1[:, c, :Tn], t_ap, AF.Identity, [tB, pkB], [R1B[c]], scale=g_ap, bias=b_ap)
            act(xnb[:, c, :Tn], t_ap, AF.Identity, [tB, pkB], [xnbB[c]], scale=g_ap, bias=b_ap)
        layer_norm(8, [R0[:, c, :Tn] for c in range(8)], R0B, ones_m[:], Tn, g_name, b_name, emit_out)

    def ssm_segment(pi, col0, L, Hr_ap, Hi_ap, HBuf, ypsum, first, last):
        c = pi // 4
        bu = psum.get()
        bre, bim = bu[1][:, 0:L], bu[1][:, 256:256 + L]
        mm1(bre, BreT[:, pi, :], ub[:, c, col0:col0 + L], True, True, [lhsB, ubB[c]], [bu[0]])
        mm1(bim, BimT[:, pi, :], ub[:, c, col0:col0 + L], True, True, [lhsB, ubB[c]], [bu[0]])
        cosA, sinA, rA = costab[:, pi, 0:L], sintab[:, pi, 0:L], rtab[:, pi, 0:L]
        cos1, sin1 = costab[:, pi, 1:2], sintab[:, pi, 1:2]
        g0, tq = tiny.get(), tiny.get()
        tt("dve", tq[1][:, 0:1], Hi_ap, sin1, ALU.mult, [HBuf, tabB], [tq[0]])
        tt("dve", tq[1][:, 1:2], Hr_ap, cos1, ALU.mult, [HBuf, tabB, tq[0]], [tq[0]])
        tt("dve", tq[1][:, 2:3], Hr_ap, sin1, ALU.mult, [HBuf, tabB, tq[0]], [tq[0]])
        tt("dve", tq[1][:, 3:4], Hi_ap, cos1, ALU.mult, [HBuf, tabB, tq[0]], [tq[0]])
        tt("dve", g0[1][:, 0:1], tq[1][:, 1:2], tq[1][:, 0:1], ALU.subtract, [tq[0]], [g0[0]])
        tt("dve", g0[1][:, 1:2], tq[1][:, 3:4], tq[1][:, 2:3], ALU.add, [tq[0], g0[0]], [g0[0]])
        t1, t2, t3, t4 = sst.get(), sst.get(), sst.get(), sst.get()
        tt("dve", t1[1][:, :L], bre, cosA, ALU.mult, [bu[0], tabB], [t1[0]])
        tt("dve", t2[1][:, :L], bim, sinA, ALU.mult, [bu[0], tabB], [t2[0]])
        tt("dve", t3[1][:, :L], bim, cosA, ALU.mult, [bu[0], tabB], [t3[0]])
        tt("dve", t4[1][:, :L], bre, sinA, ALU.mult, [bu[0], tabB], [t4[0]])
        tt("dve", t1[1][:, :L], t1[1][:, :L], t2[1][:, :L], ALU.add, [t1[0], t2[0]], [t1[0]])
        tt("dve", t3[1][:, :L], t3[1][:, :L], t4[1][:, :L], ALU.subtract, [t3[0], t4[0]], [t3[0]])
        r1 = rtab[:, pi, 1:2]
        tt("dve", g0[1][:, 2:3], g0[1][:, 0:1], r1, ALU.mult, [g0[0], tabB], [g0[0]])
        tt("dve", g0[1][:, 3:4], g0[1][:, 1:2], r1, ALU.mult, [g0[0], tabB], [g0[0]])
        tt("dve", t1[1][:, 0:1], t1[1][:, 0:1], g0[1][:, 2:3], ALU.add, [t1[0], g0[0]], [t1[0]])
        tt("dve", t3[1][:, 0:1], t3[1][:, 0:1], g0[1][:, 3:4], ALU.add, [t3[0], g0[0]], [t3[0]])
        gre, gim = sst.get(), sst.get()
        P.op("dve", lambda e: e.tensor_tensor_scan(out=gre[1][:, :L], data0=rA, data1=t1[1][:, :L], initial=0.0, op0=ALU.mult, op1=ALU.add),
             [tabB, t1[0]], [gre[0]])
        P.op("dve", lambda e: e.tensor_tensor_scan(out=gim[1][:, :L], data0=rA, data1=t3[1][:, :L], initial=0.0, op0=ALU.mult, op1=ALU.add),
             [tabB, t3[0]], [gim[0]])
        p1, p2, p3, p4 = sst.get(), sst.get(), sst.get(), sst.get()
        tt("dve", p1[1][:, :L], gre[1][:, :L], cosA, ALU.mult, [gre[0], tabB], [p1[0]])
        tt("dve", p2[1][:, :L], gim[1][:, :L], sinA, ALU.mult, [gim[0], tabB], [p2[0]])
        tt("dve", p3[1][:, :L], gim[1][:, :L], cosA, ALU.mult, [gim[0], tabB], [p3[0]])
        tt("dve", p4[1][:, :L], gre[1][:, :L], sinA, ALU.mult, [gre[0], tabB], [p4[0]])
        hr, hi = hbf.get(), hbf.get()
        tt("dve", hr[1][:, :L], p1[1][:, :L], p2[1][:, :L], ALU.subtract, [p1[0], p2[0]], [hr[0]])
        tt("dve", hi[1][:, :L], p3[1][:, :L], p4[1][:, :L], ALU.add, [p3[0], p4[0]], [hi[0]])
        tt("dve", Hr_ap, p1[1][:, L - 1:L], p2[1][:, L - 1:L], ALU.subtract, [p1[0], p2[0]], [HBuf])
        tt("dve", Hi_ap, p3[1][:, L - 1:L], p4[1][:, L - 1:L], ALU.add, [p3[0], p4[0], HBuf], [HBuf])
        yo = ypsum[1][:, col0:col0 + L]
        mm1(yo, CreT[:, pi, :], hr[1][:, :L], first, False, [lhsB, hr[0]], [ypsum[0]])
        mm1(yo, CimT[:, pi, :], hi[1][:, :L], False, last, [lhsB, hi[0]], [ypsum[0]])

    def conv_chunk(c, col0, L, eng):
        o = cacc[:, c, col0:col0 + L]
        base = PK["conv_w"][0] + c * 31
        ts(eng, o, vbuf[:, c, 0:L], pk[:, base:base + 1], pkc("conv_b", c), ALU.mult, ALU.add, [vbufB[c], pkB], [caccB[c]])
        for k in range(1, 31):
            stt(eng, o, vbuf[:, c, k:k + L], pk[:, base + k:base + k + 1], o, ALU.mult, ALU.add, [vbufB[c], pkB, caccB[c]], [caccB[c]])

    def v3(t_):
        return t_.rearrange("p (a b) -> p a b", a=4)

    def ssm_s0(c, col0):
        bR, bI = buR.get(), buR.get()

        def fn(e):
            ins = None
            for q in range(4):
                e.matmul(bR[1][:, q * 128:(q + 1) * 128], BreT[:, 4 * c + q, :], ub[:, c, col0:col0 + 128], start=True, stop=True)
                ins = e.matmul(bI[1][:, q * 128:(q + 1) * 128], BimT[:, 4 * c + q, :], ub[:, c, col0:col0 + 128], start=True, stop=True)
            return ins
        P.op("pe", fn, [lhsB, ubB[c]], [bR[0], bI[0]])
        return bR, bI

    def ssm_group(c, col0, yp, filler, bu, after_s2):
        ps4 = slice(4 * c, 4 * c + 4)
        HBs = HB[4 * c:4 * c + 4]
        bR, bI = bu
        cosG, sinG = costab[:, ps4, :], sintab[:, ps4, :]
        rG = rtab[:, ps4, :].rearrange("p a b -> p (a b)")
        cos1, sin1, r1 = costab[:, ps4, 1], sintab[:, ps4, 1], rtab[:, ps4, 1]
        Hr, Hi = Hre[:, ps4], Him[:, ps4]
        tq, g0 = tiny.get(), tiny.get()
        tt("dve", tq[1][:, 0:4], Hi, sin1, ALU.mult, HBs + [tabB], [tq[0]])
        tt("dve", tq[1][:, 4:8], Hr, cos1, ALU.mult, HBs + [tabB, tq[0]], [tq[0]])
        tt("dve", tq[1][:, 8:12], Hr, sin1, ALU.mult, HBs + [tabB, tq[0]], [tq[0]])
        tt("dve", tq[1][:, 12:16], Hi, cos1, ALU.mult, HBs + [tabB, tq[0]], [tq[0]])
        tt("dve", g0[1][:, 0:4], tq[1][:, 4:8], tq[1][:, 0:4], ALU.subtract, [tq[0]], [g0[0]])
        tt("dve", g0[1][:, 4:8], tq[1][:, 12:16], tq[1][:, 8:12], ALU.add, [tq[0], g0[0]], [g0[0]])
        tt("dve", g0[1][:, 8:12], g0[1][:, 0:4], r1, ALU.mult, [g0[0], tabB], [g0[0]])
        tt("dve", g0[1][:, 12:16], g0[1][:, 4:8], r1, ALU.mult, [g0[0], tabB], [g0[0]])
        filler(3)
        yield
        A, B_, C, D_ = gst.get(), gst.get(), gst.get(), gst.get()
        tt("dve", v3(A[1][:]), v3(bR[1][:]), cosG, ALU.mult, [bR[0], tabB], [A[0]])
        tt("dve", v3(B_[1][:]), v3(bI[1][:]), sinG, ALU.mult, [bI[0], tabB], [B_[0]])
        filler(2)
        tt("dve", v3(C[1][:]), v3(bI[1][:]), cosG, ALU.mult, [bI[0], tabB], [C[0]])
        tt("dve", v3(D_[1][:]), v3(bR[1][:]), sinG, ALU.mult, [bR[0], tabB], [D_[0]])
        after_s2()
        filler(2)
        yield
        tt(SSM_POST, A[1][:], A[1][:], B_[1][:], ALU.add, [A[0], B_[0]], [A[0]])
        tt(SSM_POST, v3(A[1][:])[:, :, 0], v3(A[1][:])[:, :, 0], g0[1][:, 8:12], ALU.add, [A[0], g0[0]], [A[0]])
        tt(SSM_POST, C[1][:], C[1][:], D_[1][:], ALU.subtract, [C[0], D_[0]], [C[0]])
        tt(SSM_POST, v3(C[1][:])[:, :, 0], v3(C[1][:])[:, :, 0], g0[1][:, 12:16], ALU.add, [C[0], g0[0]], [C[0]])
        filler(4)
        yield
        GR, GI = gst.get(), gst.get()
        P.op("dve", lambda e: e.tensor_tensor_scan(out=GR[1][:], data0=rG, data1=A[1][:], initial=0.0, op0=ALU.mult, op1=ALU.add),
             [tabB, A[0]], [GR[0]])
        filler(2)
        P.op("dve", lambda e: e.tensor_tensor_scan(out=GI[1][:], data0=rG, data1=C[1][:], initial=0.0, op0=ALU.mult, op1=ALU.add),
             [tabB, C[0]], [GI[0]])
        filler(2)
        yield
        tt(SSM_POST, v3(B_[1][:]), v3(GR[1][:]), cosG, ALU.mult, [GR[0], tabB], [B_[0]])
        tt(SSM_POST, v3(D_[1][:]), v3(GI[1][:]), sinG, ALU.mult, [GI[0], tabB], [D_[0]])
        tt(SSM_POST, v3(A[1][:]), v3(GI[1][:]), cosG, ALU.mult, [GI[0], tabB], [A[0]])
        tt(SSM_POST, v3(C[1][:]), v3(GR[1][:]), sinG, ALU.mult, [GR[0], tabB], [C[0]])
        filler(4)
        yield
        hr, hi = ghb.get(), ghb.get()
        tt("dve", hr[1][:], B_[1][:], D_[1][:], ALU.subtract, [B_[0], D_[0]], [hr[0]])
        filler(1)
        tt("dve", hi[1][:], A[1][:], C[1][:], ALU.add, [A[0], C[0]], [hi[0]])
        filler(1)
        tt("dve", Hr, v3(B_[1][:])[:, :, 127], v3(D_[1][:])[:, :, 127], ALU.subtract, [B_[0], D_[0]], HBs)
        tt("dve", Hi, v3(A[1][:])[:, :, 127], v3(C[1][:])[:, :, 127], ALU.add, [A[0], C[0]] + HBs, HBs)
        yo = yp[1][:, col0:col0 + 128]

        def fn2(e):
            ins = None
            for q in range(4):
                e.matmul(yo, CreT[:, 4 * c + q, :], hr[1][:, q * 128:(q + 1) * 128], start=(q == 0), stop=False)
                ins = e.matmul(yo, CimT[:, 4 * c + q, :], hi[1][:, q * 128:(q + 1) * 128], start=False, stop=(q == 3))
            return ins
        if DELAY_Y:
            pending_y.append(lambda: P.op("pe", fn2, [lhsB, hr[0], hi[0]], [yp[0]]))
        else:
            P.op("pe", fn2, [lhsB, hr[0], hi[0]], [yp[0]])
        yield

    pending_y = []

    def flush_y():
        while pending_y:
            pending_y.pop(0)()

    def front_p(xb_t, xbB_t, tail_fn):
        Tn = T
        s0B, w0 = pre_win[0] if pre_win[0] is not None else fpiece(w_in_b, pc["w_in"], 0, 512)
        pre_win[0] = None
        for c in range(4):
            ps = psum.get()
            mm(ps[1][:, :Tn], [(w0[:, k, c * 128:(c + 1) * 128], xb_t[:, k, :Tn]) for k in range(8)], [s0B, xbB_t], [ps[0]])
            cp("act", u32[:, c, :Tn], ps[1][:, :Tn], [ps[0]], [u32B[c]])
            cp("act", ub[:, c, :Tn], u32[:, c, :Tn], [u32B[c]], [ubB[c]])
            yield
        for half in range(2):
            sB_, wv = fpiece(w_in_b, pc["w_in"], 512 + half * 512, 1024 + half * 512)
            for ci in range(2):
                c = half * 2 + ci
                pa, pg = psum.get(), psum.get()
                mm(pa[1][:, :Tn], [(wv[:, k, ci * 256:ci * 256 + 128], xb_t[:, k, :Tn]) for k in range(8)], [sB_, xbB_t], [pa[0]])
                mm(pg[1][:, :Tn], [(wv[:, k, ci * 256 + 128:ci * 256 + 256], xb_t[:, k, :Tn]) for k in range(8)], [sB_, xbB_t], [pg[0]])
                sg = f32t.get()
                act(sg[1][:, :Tn], pg[1][:, :Tn], AF.Sigmoid, [pg[0]], [sg[0]])
                tt("dve", vbuf[:, c, 30:30 + Tn], pa[1][:, :Tn], sg[1][:, :Tn], ALU.mult, [pa[0], sg[0]], [vbufB[c]])
                yield
        taps = []
        for k in range(31):
            for c in range(4):
                taps.append((c, k))
        tap_i = [0]

        def filler(n):
            for _ in range(n):
                if tap_i[0] >= len(taps):
                    return
                c, k = taps[tap_i[0]]
                tap_i[0] += 1
                o = cacc[:, c, 0:Tn]
                base = PK["conv_w"][0] + c * 31
                ceng = "pool" if c >= 4 - CONV_POOL_CH else "dve"
                if k == 0:
                    ts(ceng, o, vbuf[:, c, 0:Tn], pk[:, base:base + 1], pkc("conv_b", c), ALU.mult, ALU.add, [vbufB[c], pkB], [caccB[c]])
                elif ceng == "dve":
                    stt("dve", o, vbuf[:, c, k:k + Tn], pk[:, base + k:base + k + 1], o, ALU.mult, ALU.add, [vbufB[c], pkB, caccB[c]], [caccB[c]])
                else:
                    tmp_ = ctmp.get()
                    ts("pool", tmp_[1][:], vbuf[:, c, k:k + Tn], pk[:, base + k:base + k + 1], None, ALU.mult, None, [vbufB[c], pkB], [tmp_[0]])
                    tt("pool", o, o, tmp_[1][:], ALU.add, [caccB[c], tmp_[0]], [caccB[c]])

        glist = [(c_, sg_) for c_ in range(4) for sg_ in range(T // TS)]
        bu_next = [ssm_s0(glist[0][0], glist[0][1] * TS)] if GRP else [None]
        gidx = [0]

        def after_s2():
            flush_y()
            gidx[0] += 1
            if gidx[0] < len(glist):
                bu_next[0] = ssm_s0(glist[gidx[0]][0], glist[gidx[0]][1] * TS)

        for c in range(4):
            yp = ypsR.get()
            for sg_ in range(T // TS):
                if GRP:
                    for _ in ssm_group(c, sg_ * TS, yp, filler, bu_next[0], after_s2):
                        yield
                else:
                    for q in range(4):
                        pi = c * 4 + q
                        ssm_segment(pi, sg_ * TS, TS, Hre[:, pi:pi + 1], Him[:, pi:pi + 1], HB[pi], yp, q == 0, q == 3)
                        filler(8)
                        yield
            flush_y()
            zp = f32t.get()
            stt("dve", zp[1][:, :Tn], u32[:, c, :Tn], pkc("ssm_d", c), yp[1][:, :Tn], ALU.mult, ALU.add, [u32B[c], pkB, yp[0]], [zp[0]])
            act(z32[:, c, :Tn], zp[1][:, :Tn], AF.Gelu_apprx_tanh, [zp[0]], [z32B[c]])
            act(zb[:, c, :Tn], zp[1][:, :Tn], AF.Gelu_apprx_tanh, [zp[0]], [zbB[c]])
            yield
        while tap_i[0] < len(taps):
            filler(4)
            yield
        for co in range(4):
            ps = psum.get()
            mm(ps[1][:, :Tn], [(glu_sb[:, k, co * 128:(co + 1) * 128], zb[:, k, :Tn]) for k in range(4)], [gluB] + zbB, [ps[0]])
            sg = f32t.get()
            act(sg[1][:, :Tn], ps[1][:, :Tn], AF.Sigmoid, [ps[0], pkB], [sg[0]], bias=pkc("glu_b", co))
            tt("dve", mixin[:, co, :Tn], z32[:, co, :Tn], sg[1][:, :Tn], ALU.mult, [z32B[co], sg[0]], [mixB[co]])
            yield
        for c in range(4):
            if tail_fn is not None:
                tail_fn(c, Tn)
            cp("pool", vbuf[:, c, 0:30], vbuf[:, c, Tn:Tn + 30], [vbufB[c]], [vbufB[c]])

        def silu_out(c, t_ap, tB, g_ap, b_ap):
            act(mixin[:, 4 + c, :Tn], t_ap, AF.Silu, [tB, pkB], [mixB[4 + c]], scale=g_ap, bias=b_ap)
        pre_win[0] = fpiece(w_in_b, pc["w_in"], 0, 512)
        layer_norm(4, [cacc[:, c, :Tn] for c in range(4)], caccB, ones_c[:], Tn, "cln_g", "cln_b", silu_out)
        yield

    pre_win = [None]
    pre_wout = [None]

    def drain(g):
        for _ in g:
            pass

    def merge(ga, gb):
        da = db = False
        while not (da and db):
            for _ in range(MRA):
                if not da:
                    try:
                        next(ga)
                    except StopIteration:
                        da = True
            for _ in range(MRB):
                if not db:
                    try:
                        next(gb)
                    except StopIteration:
                        db = True

    def run_tile(Tn, xb_t, xbB_t, x32_src, segs, conv_segs, att_segs, y_dst, is_sample=False, part="all"):
        if part in ("all", "front"):
            s0B, w0 = pre_win[0] if pre_win[0] is not None else fpiece(w_in_b, pc["w_in"], 0, 512)
            pre_win[0] = None
            for c in range(4):
                ps = psum.get()
                mm(ps[1][:, :Tn], [(w0[:, k, c * 128:(c + 1) * 128], xb_t[:, k, :Tn]) for k in range(8)], [s0B, xbB_t], [ps[0]])
                cp("act", u32[:, c, :Tn], ps[1][:, :Tn], [ps[0]], [u32B[c]])
                cp("dve", ub[:, c, :Tn], u32[:, c, :Tn], [u32B[c]], [ubB[c]])
            for c in range(4):
                yp = ypsR.get()
                for (col0, L, Hr_fn, Hi_fn, HB_fn) in segs:
                    for q in range(4):
                        pi = c * 4 + q
                        ssm_segment(pi, col0, L, Hr_fn(pi), Hi_fn(pi), HB_fn(pi), yp, q == 0, q == 3)
                    yield
                zp = f32t.get()
                stt("dve", zp[1][:, :Tn], u32[:, c, :Tn], pkc("ssm_d", c), yp[1][:, :Tn], ALU.mult, ALU.add, [u32B[c], pkB, yp[0]], [zp[0]])
                act(z32[:, c, :Tn], zp[1][:, :Tn], AF.Gelu_apprx_tanh, [zp[0]], [z32B[c]])
                act(zb[:, c, :Tn], zp[1][:, :Tn], AF.Gelu_apprx_tanh, [zp[0]], [zbB[c]])
            for co in range(4):
                ps = psum.get()
                mm(ps[1][:, :Tn], [(glu_sb[:, k, co * 128:(co + 1) * 128], zb[:, k, :Tn]) for k in range(4)], [gluB] + zbB, [ps[0]])
                sg = f32t.get()
                act(sg[1][:, :Tn], ps[1][:, :Tn], AF.Sigmoid, [ps[0], pkB], [sg[0]], bias=pkc("glu_b", co))
                tt("dve", mixin[:, co, :Tn], z32[:, co, :Tn], sg[1][:, :Tn], ALU.mult, [z32B[co], sg[0]], [mixB[co]])
            for half in range(2):
                sB_, wv = fpiece(w_in_b, pc["w_in"], 512 + half * 512, 1024 + half * 512)
                for ci in range(2):
                    c = half * 2 + ci
                    pa, pg = psum.get(), psum.get()
                    mm(pa[1][:, :Tn], [(wv[:, k, ci * 256:ci * 256 + 128], xb_t[:, k, :Tn]) for k in range(8)], [sB_, xbB_t], [pa[0]])
                    mm(pg[1][:, :Tn], [(wv[:, k, ci * 256 + 128:ci * 256 + 256], xb_t[:, k, :Tn]) for k in range(8)], [sB_, xbB_t], [pg[0]])
                    sg = f32t.get()
                    act(sg[1][:, :Tn], pg[1][:, :Tn], AF.Sigmoid, [pg[0]], [sg[0]])
                    eng = "dve"
                    if is_sample:
                        vt = f32t.get()
                        tt("dve", vt[1][:, :Tn], pa[1][:, :Tn], sg[1][:, :Tn], ALU.mult, [pa[0], sg[0]], [vt[0]])
                        for (col0, L, halo_fn, tail_fn) in conv_segs:
                            halo_fn(c)
                            cp("pool", vbuf[:, c, 30:30 + L], vt[1][:, col0:col0 + L], [vt[0]], [vbufB[c]])
                            conv_chunk(c, col0, L, eng)
                            tail_fn(c, L)
                    else:
                        (col0, L, halo_fn, tail_fn) = conv_segs[0]
                        tt("dve", vbuf[:, c, 30:30 + Tn], pa[1][:, :Tn], sg[1][:, :Tn], ALU.mult, [pa[0], sg[0]], [vbufB[c]])
                        conv_chunk(c, 0, Tn, eng)
                        if tail_fn is not None:
                            tail_fn(c, Tn)
                        cp("pool", vbuf[:, c, 0:30], vbuf[:, c, Tn:Tn + 30], [vbufB[c]], [vbufB[c]])

            def silu_out(c, t_ap, tB, g_ap, b_ap):
                act(mixin[:, 4 + c, :Tn], t_ap, AF.Silu, [tB, pkB], [mixB[4 + c]], scale=g_ap, bias=b_ap)
            layer_norm(4, [cacc[:, c, :Tn] for c in range(4)], caccB, ones_c[:], Tn, "cln_g", "cln_b", silu_out)
        if part in ("all", "back"):
            dma("sp", R0[:, :, :Tn], x32_src, R0B[0], writes=R0B)
            for half in range(2):
                if half == 0 and pre_wout[0] is not None:
                    sB_, wv = pre_wout[0]
                    pre_wout[0] = None
                else:
                    sB_, wv = wpiece(w_out_b, pc["w_out"], half * 512, half * 512 + 512)
                for ci in range(4):
                    co = half * 4 + ci
                    yield
                    ps = psum.get()
                    mm(ps[1][:, :Tn], [(wv[:, k, ci * 128:(ci + 1) * 128], mixin[:, k, :Tn]) for k in range(8)], [sB_] + mixB, [ps[0]])
                    stt("dve", R0[:, co, :Tn], R0[:, co, :Tn], ALPHA, ps[1][:, :Tn], ALU.mult, ALU.add, [R0B[co], ps[0]], [R0B[co]])
            ln_to_resid(Tn, "ln1_g", "ln1_b")
            qps = []
            for half in range(2):
                sB_, wv = wpiece(w_q_b, pc["w_q"], half * 512, half * 512 + 512)
                for ci in range(4):
                    yield
                    ps = psum.get()
                    mm(ps[1][:, :Tn], [(wv[:, k, ci * 128:(ci + 1) * 128], xnb[:, k, :Tn]) for k in range(8)], [sB_] + xnbB, [ps[0]])
                    act(qb[:, half * 4 + ci, :Tn], ps[1][:, :Tn], AF.Identity, [ps[0]], [qbB[half * 4 + ci]], scale=1.0 / 16.0)
            for (col0, L, kv_loader) in att_segs:
                if kv_loader is not None:
                    kv_loader()
                for h in range(4):
                    pts = []
                    for mc in range(2):
                        yield
                        ps = psum.get()
                        mm(ps[1][:, :L], [(kT[:, h * 2 + dc, mc * 128:(mc + 1) * 128], qb[:, h * 2 + dc, col0:col0 + L]) for dc in range(2)],
                           [kTB, qbB[h * 2], qbB[h * 2 + 1]], [ps[0]])
                        pt = pTr.get()
                        act(pt[1][:, :L], ps[1][:, :L], AF.Exp, [ps[0]], [pt[0]])
                        pts.append(pt)
                    yield
                    ps = psum.get()
                    mm(ps[1][:, :L], [(ones_1[:], pts[mc][1][:, :L]) for mc in range(2)], [onesB, pts[0][0], pts[1][0]], [ps[0]])
                    rinv = f32t.get()
                    recip(rinv[1][:, :L], ps[1][:, :L], [ps[0]], [rinv[0]])
                    for dc in range(2):
                        po = psum.get()
                        mm(po[1][:, :L], [(vv[:, mc, h * 256 + dc * 128: h * 256 + dc * 128 + 128], pts[mc][1][:, :L]) for mc in range(2)],
                           [vvB, pts[0][0], pts[1][0]], [po[0]])
                        tt("dve", ob[:, h * 2 + dc, col0:col0 + L], po[1][:, :L], rinv[1][:, :L], ALU.mult, [po[0], rinv[0]], [obB[h * 2 + dc]])
            for half in range(2):
                sB_, wv = wpiece(w_o_b, pc["w_o"], half * 512, half * 512 + 512)
                for ci in range(4):
                    co = half * 4 + ci
                    yield
                    ps = psum.get()
                    mm(ps[1][:, :Tn], [(wv[:, k, ci * 128:(ci + 1) * 128], ob[:, k, :Tn]) for k in range(8)], [sB_] + obB, [ps[0]])
                    stt("dve", R0[:, co, :Tn], R1[:, co, :Tn], ALPHA, ps[1][:, :Tn], ALU.mult, ALU.add, [R1B[co], ps[0]], [R0B[co]])
            ln_to_resid(Tn, "ln2_g", "ln2_b")
            for piece in range(8):
                sB_, wv = wpiece(w1_b, pc["w1"], piece * 512, piece * 512 + 512)
                for hc in range(4):
                    hidx = piece * 4 + hc
                    yield
                    ps = psum.get()
                    mm(ps[1][:, :Tn], [(wv[:, k, hc * 128:(hc + 1) * 128], xnb[:, k, :Tn]) for k in range(8)], [sB_] + xnbB, [ps[0]])
                    rl = f32t.get()
                    act(rl[1][:, :Tn], ps[1][:, :Tn], AF.Relu, [ps[0], pkB], [rl[0]], bias=pkc("b1", hidx))
                    if SQ_ENG == "act":
                        act(hid[:, hidx, :Tn], rl[1][:, :Tn], AF.Square, [rl[0]], [hidB[hidx]])
                    else:
                        tt("dve", hid[:, hidx, :Tn], rl[1][:, :Tn], rl[1][:, :Tn], ALU.mult, [rl[0]], [hidB[hidx]])
            w2v = w2_b.rearrange("(k p) n -> p k n", p=128)
            for cp_ in range(4):
                yield
                pss = [psum.get(), psum.get()]
                for kh in range(2):
                    sB_, wv = ring_load(w2v[:, kh * 16:kh * 16 + 16, cp_ * 256:cp_ * 256 + 256], 16, 256, pc["w2"])
                    for oc in range(2):
                        for k in range(16):
                            mm1(pss[oc][1][:, :Tn], wv[:, k, oc * 128:(oc + 1) * 128], hid[:, kh * 16 + k, :Tn],
                                kh == 0 and k == 0, kh == 1 and k == 15, [sB_, hidB[kh * 16 + k]], [pss[oc][0]])
                for oc in range(2):
                    co = cp_ * 2 + oc
                    stt("dve", R0[:, co, :Tn], R1[:, co, :Tn], ALPHA, pss[oc][1][:, :Tn], ALU.mult, ALU.add, [R1B[co], pss[oc][0]], [R0B[co]])
                    act(R0[:, co, :Tn], R0[:, co, :Tn], AF.Identity, [R0B[co], pkB], [R0B[co]], bias=pkc("b2", co))
            if not is_sample:
                pre_wout[0] = wpiece(w_out_b, pc["w_out"], 0, 512)
            ln_to_resid(Tn, "ln3_g", "ln3_b")
            dma("sp", y_dst, R1[:, :, :Tn], R1B[0], reads=R1B, final=True)

    xT_v = xT.rearrange("(k p) t -> p k t", p=128)
    if RUN_KV:
        memTb = xb[1]
        dma("pool", memTb[:, :, 0:256], memT_d.rearrange("(k p) m -> p k m", p=128), xbB[1], writes=[xbB[1]])
        kv_i = [4]
        for (wb, pcb, dst_d, is_k) in ((w_k_b, pc["w_k"], kout_d, True), (w_v_b, pc["w_v"], vout_d, False)):
            for half in range(2):
                sB_, wv = wpiece(wb, pcb, half * 512, half * 512 + 512)
                if is_k and KVD[0] == "1":
                    for ci in range(4):
                        ps = psum.get()
                        mm(ps[1][:, :256], [(wv[:, k, ci * 128:(ci + 1) * 128], memTb[:, k, 0:256]) for k in range(8)], [sB_, xbB[1]], [ps[0]])
                        cp("act", kT[:, half * 4 + ci, :], ps[1][:, :256], [ps[0]], [kTB])
                for mc in range(2 if KVD[1] == "1" else 0):
                    ps = psum.get()
                    if KVD[3:4] == "h":
                        mm(ps[1][:, 0:256], [(memTb[:, k, mc * 128:(mc + 1) * 128], wv[:, k, 0:256]) for k in range(8)], [sB_, xbB[1]], [ps[0]])
                        mm(ps[1][:, 256:512], [(memTb[:, k, mc * 128:(mc + 1) * 128], wv[:, k, 256:512]) for k in range(8)], [sB_, xbB[1]], [ps[0]])
                    else:
                        mm(ps[1][:, :], [(memTb[:, k, mc * 128:(mc + 1) * 128], wv[:, k, :]) for k in range(8)], [sB_, xbB[1]], [ps[0]])
                    stgB, stg = big(kv_i[0])
                    kv_i[0] = 4 + (kv_i[0] - 4 + 1) % 4
                    if KVD[5:6] == "s":
                        for hh in range(2):
                            cp("act", stg[:, hh * 256:(hh + 1) * 256], ps[1][:, hh * 256:(hh + 1) * 256], [ps[0]], stgB)
                            if not is_k:
                                cp("dve", vv[:, mc, half * 512 + hh * 256:half * 512 + (hh + 1) * 256], ps[1][:, hh * 256:(hh + 1) * 256], [ps[0]], [vvB])
                    elif KVD[5:6] == "n":
                        pass
                    elif KVD[5:6] == "a":
                        cp("act", stg, ps[1][:], [ps[0]], stgB)
                    elif KVD[5:6] == "v":
                        cp("dve", stg, ps[1][:], [ps[0]], stgB)
                    else:
                        cp("act", stg, ps[1][:], [ps[0]], stgB)
                        if not is_k:
                            cp("dve", vv[:, mc, half * 512:(half + 1) * 512], stg, stgB, [vvB])
                    if KVD[2] == "1":
                        dma("sp", dst_d[mc * 128:(mc + 1) * 128, half * 512:(half + 1) * 512], stg, stgB[0], reads=stgB, final=True)

    memset("dve", Hre[:], 0.0, HB)
    memset("dve", Him[:], 0.0, HB)
    winuB, winu = fpiece(w_in_b, pc["w_in"], 0, 512)
    NPB = NPRE // T
    xi = [0]

    def load_xb(col):
        i = xi[0] % 2
        xi[0] += 1
        dma("pool", xb[i][:, :, :], xT_v[:, :, col:col + T], xbB[i], writes=[xbB[i]])
        return i

    nxt = load_xb((NPB - NPB_RUN) * T)
    qi = [0]
    pf_i = [0]
    def step_late_pc():
        while late_pc:
            try:
                next(late_pc[0])
                return
            except StopIteration:
                late_pc.pop(0)

    for pb in range(NPB - NPB_RUN, NPB):
        cur = nxt
        step_late_pc()
        nxt = load_xb((pb + 1) * T)
        if PFG2:
            ubks = []
            for blk in range(2):
                pu = psum.get()
                mm(pu[1][:, :], [(xb[cur][:, k, blk * 128:(blk + 1) * 128], winu[:, k, :]) for k in range(8)], [xbB[cur], winuB], [pu[0]])
                ubk = ghb.get()
                cp("act", ubk[1][:], pu[1][:], [pu[0]], [ubk[0]])
                ubks.append(ubk)
            pfl = buR.items + lnR.items
            wr, wi = pfl[(pf_i[0] % 2) * 2], pfl[(pf_i[0] % 2) * 2 + 1]
            pf_i[0] += 1

            def fn(e, ubks=ubks, wr=wr, wi=wi):
                ins = None
                for p_ in range(16):
                    sl_ = slice(p_ * 32, (p_ + 1) * 32)
                    e.matmul(wr[1][:, sl_], P2re[:, p_, :], ubks[0][1][:, sl_], start=True, stop=False)
                    e.matmul(wr[1][:, sl_], Pre[:, p_, :], ubks[1][1][:, sl_], start=False, stop=True)
                    e.matmul(wi[1][:, sl_], P2im[:, p_, :], ubks[0][1][:, sl_], start=True, stop=False)
                    ins = e.matmul(wi[1][:, sl_], Pim[:, p_, :], ubks[1][1][:, sl_], start=False, stop=True)
                return ins
            P.op("pe", fn, [lhsB, ubks[0][0], ubks[1][0]] + p2B, [wr[0], wi[0]])
            blocks_ = [(wr, wi)]
        else:
            blocks_ = None
        for blk in range(T // 128 if not PFG2 else 1):
            if not PFG2:
                pu = psum.get()
                mm(pu[1][:, :], [(xb[cur][:, k, blk * 128:(blk + 1) * 128], winu[:, k, :]) for k in range(8)], [xbB[cur], winuB], [pu[0]])
                ubk = ghb.get()
                cp("act", ubk[1][:], pu[1][:], [pu[0]], [ubk[0]])
                pfl = buR.items + lnR.items
                wr, wi = pfl[(pf_i[0] % 2) * 2], pfl[(pf_i[0] % 2) * 2 + 1]
                pf_i[0] += 1

                def fn(e, ubk=ubk, wr=wr, wi=wi):
                    ins = None
                    for p_ in range(16):
                        e.matmul(wr[1][:, p_ * 32:(p_ + 1) * 32], Pre[:, p_, :], ubk[1][:, p_ * 32:(p_ + 1) * 32], start=True, stop=True)
                        ins = e.matmul(wi[1][:, p_ * 32:(p_ + 1) * 32], Pim[:, p_, :], ubk[1][:, p_ * 32:(p_ + 1) * 32], start=True, stop=True)
                    return ins
                P.op("pe", fn, [lhsB, ubk[0]], [wr[0], wi[0]])
            else:
                wr, wi = blocks_[0]
            base = (qi[0] % 2) * 2
            qi[0] += 1
            (q1B, q1), (q2B, q2) = big(base), big(base + 1)
            (q3B, q3), (q4B, q4) = big(4 + base), big(4 + base + 1)
            tt("dve", q1, wr[1][:], Bxr[:], ALU.mult, [wr[0], BxB], q1B)
            tt("dve", q2, wi[1][:], Bxi[:], ALU.mult, [wi[0], BxB], q2B)
            tt("dve", q3, wi[1][:], Bxr[:], ALU.mult, [wi[0], BxB], q3B)
            tt("dve", q4, wr[1][:], Bxi[:], ALU.mult, [wr[0], BxB], q4B)
            tt(SSM_POST, q1, q1, q2, ALU.subtract, q1B + q2B, q1B)
            tt(SSM_POST, q3, q3, q4, ALU.add, q3B + q4B, q3B)
            sr, si = tiny.get(), tiny.get()
            P.op("dve", (lambda sr, q1: lambda e: e.tensor_reduce(out=sr[1][:], in_=q1.rearrange("p (a b) -> p a b", a=16), axis=AX.X, op=ALU.add))(sr, q1), q1B, [sr[0]])
            P.op("dve", (lambda si, q3: lambda e: e.tensor_reduce(out=si[1][:], in_=q3.rearrange("p (a b) -> p a b", a=16), axis=AX.X, op=ALU.add))(si, q3), q3B, [si[0]])
            u1, u2, u3, u4 = tiny.get(), tiny.get(), tiny.get(), tiny.get()
            tt("dve", u1[1][:], a128r[:], Hre[:], ALU.mult, [a128B] + HB, [u1[0]])
            tt("dve", u2[1][:], a128i[:], Him[:], ALU.mult, [a128B] + HB, [u2[0]])
            tt("dve", u3[1][:], a128r[:], Him[:], ALU.mult, [a128B] + HB, [u3[0]])
            tt("dve", u4[1][:], a128i[:], Hre[:], ALU.mult, [a128B] + HB, [u4[0]])
            tt("dve", u1[1][:], u1[1][:], u2[1][:], ALU.subtract, [u1[0], u2[0]], [u1[0]])
            tt("dve", u3[1][:], u3[1][:], u4[1][:], ALU.add, [u3[0], u4[0]], [u3[0]])
            tt("dve", Hre[:], u1[1][:], sr[1][:], ALU.add, [u1[0], sr[0]], HB)
            tt("dve", Him[:], u3[1][:], si[1][:], ALU.add, [u3[0], si[0]] + HB, HB)

    pre_win[0] = (winuB, winu)
    for g_ in late_pc:
        for _ in g_:
            pass
    hxi = xi[0] % 2
    hx, hxB = xb[hxi], xbB[hxi]
    dma("pool", hx[:, :, 0:32], xT_v[:, :, NPRE - 32:NPRE], hxB, writes=[hxB])
    for half in range(2):
        sB_, wv = wpiece(w_in_b, pc["w_in"], 512 + half * 512, 1024 + half * 512)
        for ci in range(2):
            c = half * 2 + ci
            pa, pg = psum.get(), psum.get()
            mm(pa[1][:, :32], [(wv[:, k, ci * 256:ci * 256 + 128], hx[:, k, 0:32]) for k in range(8)], [sB_, hxB], [pa[0]])
            mm(pg[1][:, :32], [(wv[:, k, ci * 256 + 128:ci * 256 + 256], hx[:, k, 0:32]) for k in range(8)], [sB_, hxB], [pg[0]])
            sg = f32t.get()
            act(sg[1][:, :32], pg[1][:, :32], AF.Sigmoid, [pg[0]], [sg[0]])
            tt("dve", vbuf[:, c, 0:30], pa[1][:, 2:32], sg[1][:, 2:32], ALU.mult, [pa[0], sg[0]], [vbufB[c]])

    xs_v = xsT.rearrange("(k p) t -> p k t", p=128)
    hsinB = Buf("hs_in")

    def s_halo(s_):
        def f(c):
            dma("sp", vbuf[:, c, 0:30], convc_d[:, (c * 2 + s_) * 30:(c * 2 + s_) * 30 + 30], vbufB[c], writes=[vbufB[c]])
        return f

    def s_tail(s_):
        def f(c, L):
            dma("sp", convsout_d[:, (c * 2 + s_) * 30:(c * 2 + s_) * 30 + 30], vbuf[:, c, L:L + 30], vbufB[c], reads=[vbufB[c]], final=True)
        return f

    def s_kv(s_):
        def f():
            dma("pool", kT[:], kTs_d[s_].rearrange("(k p) m -> p k m", p=128), kTB, writes=[kTB])
            dma("pool", vv[:], vs_d[s_].rearrange("(k p) n -> p k n", p=128), vvB, writes=[vvB])
        return f

    def hs_fn(s_, reim):
        return lambda pi: Hs[:, pi * 4 + s_ * 2 + reim: pi * 4 + s_ * 2 + reim + 1]

    s_segs = [(s_ * 16, 16, hs_fn(s_, 0), hs_fn(s_, 1), lambda pi: HsB[pi]) for s_ in range(2)]

    def sample_gen(part, bi):
        if part in ("all", "front"):
            dma("sp", Hs[:], h0_d[:, :], hsinB, writes=HsB)
            dma("pool", xb[bi][:, :, 0:32], xs_v, xbB[bi], writes=[xbB[bi]])
        for _ in run_tile(32, xb[bi], xbB[bi], xs_v, s_segs,
                          [(s_ * 16, 16, s_halo(s_), s_tail(s_)) for s_ in range(2)],
                          [(s_ * 16, 16, s_kv(s_)) for s_ in range(2)],
                          ysT.rearrange("(k p) t -> p k t", p=128), is_sample=True, part=part):
            yield
        if part in ("all", "back"):
            hso = statsb.get()
            cp("dve", hso[1][:, 0:64], Hs[:], HsB, [hso[0]])
            dma("sp", hsout_d[:, :], hso[1][:, 0:64], hso[0], reads=[hso[0]], final=True)

    p_segs = [(s * TS, TS, lambda pi: Hre[:, pi:pi + 1], lambda pi: Him[:, pi:pi + 1], lambda pi: HB[pi]) for s in range(T // TS)]
    yT_v = yT.rearrange("(k p) t -> p k t", p=128)
    cur = nxt

    def p_tail(c, L):
        dma("sp", convout_d[:, c * 30:(c + 1) * 30], vbuf[:, c, L:L + 30], vbufB[c], reads=[vbufB[c]], final=True)

    def xload(it, buf_i):
        dma("pool", xb[buf_i][:, :, :], xT_v[:, :, NPRE + it * T: NPRE + (it + 1) * T], xbB[buf_i], writes=[xbB[buf_i]])

    if NT_RUN > 0:
        if NT_RUN > 1:
            xload(1, 1 - cur)
        drain(front_p(xb[cur], xbB[cur], p_tail if NT_RUN == 1 and NT == 1 else None))
    for it in range(NT_RUN):
        back = run_tile(T, xb[cur], xbB[cur], xT_v[:, :, NPRE + it * T: NPRE + (it + 1) * T], p_segs,
                        [(0, T, None, None)], [(0, T, None)], yT_v[:, :, it * T:(it + 1) * T], part="back")
        if it + 1 < NT_RUN:
            nb = 1 - cur
            fr = front_p(xb[nb], xbB[nb], p_tail if it + 1 == NT - 1 else None)
            if it + 2 < NT_RUN:
                xload(it + 2, cur)
            if PIPE:
                merge(back, fr)
            else:
                drain(back)
                drain(fr)
        else:
            if RUN_SAMPLE and PIPE:
                merge(back, sample_gen("front", 1 - cur))
            else:
                drain(back)
        cur = 1 - cur
    hst, hst2 = tiny.get(), tiny.get()
    cp("dve", hst[1][:], Hre[:], HB, [hst[0]])
    cp("dve", hst2[1][:], Him[:], HB, [hst2[0]])
    dma("sp", hout_d[:, 0:16], hst[1][:], hst[0], reads=[hst[0]], final=True)
    dma("sp", hout_d[:, 16:32], hst2[1][:], hst2[0], reads=[hst2[0]], final=True)

    if RUN_SAMPLE:
        if PIPE and NT_RUN > 0:
            drain(sample_gen("back", cur))
        else:
            drain(sample_gen("all", cur))

    P.emit(st)
    st.close()
    return nc


_NC_CACHE = {}


def _mode_major(a):
    sh = a.shape
    a = a.reshape((16, 2, 64) + sh[2:])
    return np.ascontiguousarray(np.moveaxis(a, 0, 2).reshape((128, 16) + sh[2:]))


def kernel(x_prompt, x_sample, state_ssm_re, state_ssm_im, cache_conv, cache_mem_k, cache_mem_v,
           mem_prompt, w_in, ssm_a_re, ssm_a_im, ssm_log_dt, ssm_b_re, ssm_b_im, ssm_c_re, ssm_c_im,
           ssm_d, glu_w, glu_b, conv_w, conv_b, conv_ln_g, conv_ln_b, w_out, ln1_g, ln1_b,
           mem_w_q, mem_w_k, mem_w_v, mem_w_o, ln2_g, ln2_b,
           mlp_w1, mlp_b1, mlp_w2, mlp_b2, ln3_g, ln3_b):
    f = lambda a: np.ascontiguousarray(np.asarray(a, dtype=np.float32))
    x_prompt, x_sample = f(x_prompt), f(x_sample)
    col = lambda v, n: f(v).reshape(n, 128).T
    pk = np.zeros((128, NPK), np.float32)

    def put(name, arr):
        pk[:, PK[name][0]:PK[name][1]] = arr
    put("ln1_g", col(ln1_g[0], 8)); put("ln1_b", col(ln1_b[0], 8))
    put("ln2_g", col(ln2_g[0], 8)); put("ln2_b", col(ln2_b[0], 8))
    put("ln3_g", col(ln3_g[0], 8)); put("ln3_b", col(ln3_b[0], 8))
    put("b1", col(mlp_b1[0], 32)); put("b2", col(mlp_b2[0], 8))
    put("glu_b", col(glu_b[0], 4)); put("ssm_d", col(ssm_d[0], 4))
    put("conv_b", col(conv_b[0], 4)); put("cln_g", col(conv_ln_g[0], 4)); put("cln_b", col(conv_ln_b[0], 4))
    cw = f(conv_w[0]).T.reshape(4, 128, 31).transpose(1, 0, 2).reshape(128, 124)
    put("conv_w", cw)
    put("a_re", _mode_major(f(ssm_a_re[0])[:, :, None])[:, :, 0])
    put("a_im", _mode_major(f(ssm_a_im[0])[:, :, None])[:, :, 0])
    ldt = np.repeat(f(ssm_log_dt[0])[:, None], 64, axis=1)
    put("logdt", _mode_major(ldt[:, :, None])[:, :, 0])
    put("tidx", np.tile(np.arange(128, dtype=np.float32)[None, :], (128, 1)))
    put("eidx", (127.0 - np.arange(128, dtype=np.float32))[:, None])
    put("eidx2", (255.0 - np.arange(128, dtype=np.float32))[:, None])
    rows = np.stack([f(ssm_a_re[0]).reshape(-1), f(ssm_a_im[0]).reshape(-1), ldt.reshape(-1)]).astype(np.float32)
    BT = np.zeros((2, 128, 2048), np.float32)
    CT = np.zeros((2, 128, 2048), np.float32)
    Bx = np.zeros((2, 128, 512), np.float32)
    for ri, (bsrc, csrc) in enumerate(((f(ssm_b_re[0]), f(ssm_c_re[0])), (f(ssm_b_im[0]), f(ssm_c_im[0])))):
        for g in range(32):
            pi, gp, gl = g // 2, g % 2, g % 8
            BT[ri, gl * 16:(gl + 1) * 16, pi * 128 + gp * 64: pi * 128 + gp * 64 + 64] = bsrc[g].T
            CT[ri, gp * 64:(gp + 1) * 64, pi * 128 + gl * 16: pi * 128 + gl * 16 + 16] = csrc[g].T
            Bx[ri, gp * 64:(gp + 1) * 64, pi * 32 + gp * 16: pi * 32 + gp * 16 + 16] = bsrc[g]
    shared = dict(w_in=f(w_in[0]), glu_w=f(glu_w[0]), w_out=f(w_out[0]), w_q=f(mem_w_q[0]), w_k=f(mem_w_k[0]),
                  w_v=f(mem_w_v[0]), w_o=f(mem_w_o[0]), w1=f(mlp_w1[0]), w2=f(mlp_w2[0]), pk=pk, rows=rows, BT=BT, CT=CT, Bx=Bx)
    xTs = [np.ascontiguousarray(x_prompt[b].T) for b in range(2)]
    in_maps = []
    for c in CORES:
        b, j = c // 4, c % 4
        xin = np.zeros((D, NPRE + SEG), np.float32)
        npre = j * SEG
        xin[:, NPRE - npre: NPRE + SEG] = xTs[b][:, 0:(j + 1) * SEG]
        ss = [2 * c, 2 * c + 1]
        xs = np.concatenate([x_sample[s].T for s in ss], axis=1)
        h0 = np.zeros((128, 16, 2, 2), np.float32)
        cc = np.zeros((128, 4, 2, 30), np.float32)
        for si, s in enumerate(ss):
            h0[:, :, si, 0] = _mode_major(f(state_ssm_re[0, s])[:, :, None])[:, :, 0]
            h0[:, :, si, 1] = _mode_major(f(state_ssm_im[0, s])[:, :, None])[:, :, 0]
            cc[:, :, si, :] = f(cache_conv[0, s]).T.reshape(4, 128, 30).transpose(1, 0, 2)
        kTs = np.stack([f(cache_mem_k[0, s]).reshape(256, D).T for s in ss])
        vs = np.stack([f(cache_mem_v[0, s]).reshape(256, D) for s in ss])
        m = dict(shared)
        m.update(xT=xin, xsT=np.ascontiguousarray(xs), h0=h0.reshape(128, 64), convc=cc.reshape(128, 240),
                 kTs=np.ascontiguousarray(kTs), vs=np.ascontiguousarray(vs), memT=np.ascontiguousarray(f(mem_prompt[b]).T))
        in_maps.append(m)
    if "nc" not in _NC_CACHE:
        _NC_CACHE["nc"] = build_program()
    res = run_bass_kernel_spmd(_NC_CACHE["nc"], in_maps, core_ids=list(range(len(CORES))))
    R = {c: res.results[i] for i, c in enumerate(CORES)}

    def unmode(a):
        return a.reshape(2, 64, 16).transpose(2, 0, 1).reshape(32, 64)
    y_prompt = np.zeros((2, 16384, D), np.float32)
    y_sample = np.zeros((16, 16, D), np.float32)
    p_re = np.zeros((1, 2, 32, 64), np.float32)
    p_im = np.zeros((1, 2, 32, 64), np.float32)
    p_conv = np.zeros((1, 2, 30, 512), np.float32)
    p_mk = np.zeros((1, 2, 256, 4, 256), np.float32)
    p_mv = np.zeros((1, 2, 256, 4, 256), np.float32)
    s_re = np.zeros((1, 16, 32, 64), np.float32)
    s_im = np.zeros((1, 16, 32, 64), np.float32)
    s_conv = np.zeros((1, 16, 30, 512), np.float32)
    for c in CORES:
        b, j = c // 4, c % 4
        r = R[c]
        y_prompt[b, j * SEG:(j + 1) * SEG, :] = r["yT"].T
        ys = r["ysT"].T
        hs = r["hsout"].reshape(128, 16, 2, 2)
        cs = r["convsout"].reshape(128, 4, 2, 30)
        for si in range(2):
            s = 2 * c + si
            y_sample[s] = ys[si * 16:(si + 1) * 16]
            s_re[0, s] = unmode(hs[:, :, si, 0])
            s_im[0, s] = unmode(hs[:, :, si, 1])
            s_conv[0, s] = cs[:, :, si, :].transpose(1, 0, 2).reshape(512, 30).T
        if j == 3:
            ho = r["hout"].reshape(128, 2, 16)
            p_re[0, b] = unmode(ho[:, 0, :])
            p_im[0, b] = unmode(ho[:, 1, :])
            p_conv[0, b] = r["convout"].reshape(128, 4, 30).transpose(1, 0, 2).reshape(512, 30).T
        if j == 0:
            p_mk[0, b] = r["kout"].reshape(256, 4, 256)
            p_mv[0, b] = r["vout"].reshape(256, 4, 256)
    return (y_prompt, y_sample, p_re, p_im, p_conv, p_mk, p_mv, s_re, s_im, s_conv)
```

```python
import os
from contextlib import ExitStack
import numpy as np
import concourse.bass as bass
import concourse.mybir as mybir
from concourse.bass_utils import run_bass_kernel_spmd

F32 = mybir.dt.float32
BF16 = mybir.dt.bfloat16
AF = mybir.ActivationFunctionType
ALU = mybir.AluOpType
AX = mybir.AxisListType

D = 1024
NCORE = 8
SEG = 4096
NPRE = 3 * SEG
T = 256
NT = SEG // T
TS = 128
LN_EPS = 1e-5
ALPHA = 2.0 ** 0.25
MAGIC = 12582912.0
TWO_PI = float(2 * np.pi)
PI = float(np.pi)

PK = {}
_o = 0
for _n, _w in [("ln1_g", 8), ("ln1_b", 8), ("ln2_g", 8), ("ln2_b", 8), ("ln3_g", 8), ("ln3_b", 8),
               ("b1", 32), ("b2", 8), ("glu_b", 4), ("ssm_d", 4), ("conv_b", 4), ("cln_g", 4), ("cln_b", 4),
               ("conv_w", 124), ("a_re", 16), ("a_im", 16), ("logdt", 16), ("tidx", 128), ("eidx", 1), ("eidx2", 1), ("pad", 2)]:
    PK[_n] = (_o, _o + _w)
    _o += _w
NPK = _o

SYNC_SAME = {e: (e in os.environ.get("K_SYNC", "act,dve,pool").split(",")) for e in ("act", "dve", "pool", "pe", "sp")}
NT_RUN = int(os.environ.get("K_NT", NT))
NGST = int(os.environ.get("K_NGST", "9"))
PIPE = bool(int(os.environ.get("K_PIPE", "1")))
GRP = bool(int(os.environ.get("K_GRP", "1")))
PFG2 = bool(int(os.environ.get("K_PFG2", "1")))
YQ = os.environ.get("K_YQ", "act")
TAB_BATCH = bool(int(os.environ.get("K_TABB", "1")))
DELAY_Y = bool(int(os.environ.get("K_DY", "1")))
MRA = int(os.environ.get("K_MRA", "3"))
MRB = int(os.environ.get("K_MRB", "2"))
SSM_POST = os.environ.get("K_SSMPOST", "dve")
CONV_ENG = os.environ.get("K_CONV", "dve")
SQ_ENG = os.environ.get("K_SQ", "act")
SQ_MOD = int(os.environ.get("K_SQMOD", "2"))
CONV_POOL_CH = int(os.environ.get("K_CPC", "0"))
PF_BANKS = bool(int(os.environ.get("K_PFB", "1")))
RUN_SAMPLE = bool(int(os.environ.get("K_SAMPLE", "1")))
NPB_RUN = int(os.environ.get("K_NPB", NPRE // T))
RUN_PREP = bool(int(os.environ.get("K_PREP", "1")))
RUN_KV = bool(int(os.environ.get("K_KV", "1")))
RUN_PC = os.environ.get("K_PC", "all")
KVD = os.environ.get("K_KVD", "111")
PC_ROWS = int(os.environ.get("K_PCR", "128"))
PC_COLS = int(os.environ.get("K_PCC", "1024"))
CORES = [int(x) for x in os.environ.get("K_CORES", "0,1,2,3,4,5,6,7").split(",")]


class Ev:
    __slots__ = ("eng", "sem", "value", "op", "group")

    def __init__(self, eng):
        self.eng = eng
        self.sem = None
        self.value = None
        self.op = None
        self.group = None


class Buf:
    __slots__ = ("name", "w", "r", "dma_sem", "dma_count", "group")

    def __init__(self, name, group=None):
        self.name = name
        self.w = None
        self.r = []
        self.dma_sem = None
        self.dma_count = 0
        self.group = group


class DmaGroup:
    def __init__(self, name):
        self.name = name
        self.sem = None
        self.count = 0


class Op:
    __slots__ = ("eng", "fn", "waits", "ev", "is_dma", "signals")

    def __init__(self, eng, fn, waits, ev, is_dma):
        self.eng = eng
        self.fn = fn
        self.waits = waits
        self.ev = ev
        self.is_dma = is_dma
        self.signals = is_dma


class Prog:
    ENGINES = ("pe", "act", "dve", "pool", "sp")

    def __init__(self, nc):
        self.nc = nc
        self.ops = {e: [] for e in self.ENGINES}
        self.all_ops = []
        self.dma_bufs = []
        self.groups = []
        self.final_evs = []

    def group(self, name):
        g = DmaGroup(name)
        self.groups.append(g)
        return g

    def _deps(self, ev, reads, writes):
        waits = []
        for b in reads:
            if b.w is not None:
                waits.append(b.w)
        for b in writes:
            if b.w is not None:
                waits.append(b.w)
            waits.extend(b.r)
        for b in reads:
            b.r.append(ev)
        for b in writes:
            b.w = ev
            b.r = []
        out = []
        seen = set()
        for w in waits:
            if w is ev or id(w) in seen:
                continue
            seen.add(id(w))
            out.append(w)
        return out

    def op(self, eng, fn, reads=(), writes=()):
        ev = Ev(eng)
        waits = self._deps(ev, reads, writes)
        o = Op(eng, fn, waits, ev, False)
        ev.op = o
        self.ops[eng].append(o)
        self.all_ops.append(o)
        return o

    def dma(self, eng, fn, key, reads=(), writes=(), final=False):
        ev = Ev("dma")
        waits = self._deps(ev, reads, writes)
        if key.group is not None:
            ev.group = key.group
            key.group.count += 1
        else:
            if key.dma_count == 0:
                self.dma_bufs.append(key)
            key.dma_count += 1
            ev.sem = key
            ev.value = 16 * key.dma_count
        o = Op(eng, fn, waits, ev, True)
        ev.op = o
        self.ops[eng].append(o)
        self.all_ops.append(o)
        if final:
            self.final_evs.append(ev)
        return o

    def emit(self, stack):
        nc = self.nc
        for o in self.all_ops:
            for w in o.waits:
                if w.op is not None and not w.op.is_dma:
                    if w.eng == o.eng and not SYNC_SAME[o.eng]:
                        continue
                    w.op.signals = True
        esem = {}
        for e in ("pe", "act", "dve", "pool"):
            esem[e] = stack.enter_context(nc.semaphore("s_" + e))
            cnt = 0
            for o in self.ops[e]:
                if o.is_dma:
                    continue
                if o.signals:
                    cnt += 1
                    o.ev.sem = esem[e]
                    o.ev.value = cnt
        for b in self.dma_bufs:
            b.dma_sem = stack.enter_context(nc.semaphore("d_" + b.name))
        for g in self.groups:
            if g.count:
                g.sem = stack.enter_context(nc.semaphore("g_" + g.name))

        def resolve(ev):
            if ev.group is not None:
                return ev.group.sem, 16 * ev.group.count
            if isinstance(ev.sem, Buf):
                return ev.sem.dma_sem, ev.value
            return ev.sem, ev.value

        block = stack.enter_context(nc.Block())
        handles = {"pe": "tensor", "act": "scalar", "dve": "vector", "pool": "gpsimd", "sp": "sync"}
        final_evs = self.final_evs

        def make(e):
            ops = self.ops[e]

            def body(eng):
                waited = {}
                for o in ops:
                    for w in o.waits:
                        if w.eng == e and not w.op.is_dma and not SYNC_SAME[e]:
                            continue
                        sem, val = resolve(w)
                        assert sem is not None and val is not None, (e, w.eng)
                        k = id(sem)
                        if waited.get(k, 0) >= val:
                            continue
                        waited[k] = val
                        eng.wait_ge(sem, val)
                    ins = o.fn(eng)
                    if o.is_dma:
                        sem, _ = resolve(o.ev)
                        ins.then_inc(sem, 16)
                    elif o.signals:
                        ins.then_inc(o.ev.sem, 1)
                if e == "sp":
                    for ev in final_evs:
                        sem, val = resolve(ev)
                        if waited.get(id(sem), 0) >= val:
                            continue
                        waited[id(sem)] = val
                        eng.wait_ge(sem, val)
            return body

        for e in self.ENGINES:
            if self.ops[e] or e == "sp":
                getattr(block, handles[e])(make(e))


class Rot:
    def __init__(self, st, nc, name, shape, dtype, n, psum=False):
        self.items = []
        for i in range(n):
            alloc = nc.psum_tensor if psum else nc.sbuf_tensor
            t = st.enter_context(alloc(f"rt_{name}{i}", shape, dtype))
            self.items.append((Buf(f"{name}{i}"), t))
        self.i = 0

    def get(self):
        it = self.items[self.i % len(self.items)]
        self.i += 1
        return it


def build_program():
    nc = bass.Bass("TRN2", target_bir_lowering=False)
    st = ExitStack()
    P = Prog(nc)

    def din(name, shape):
        return nc.dram_tensor(name, shape, F32, kind="ExternalInput").ap()

    def dout(name, shape):
        return nc.dram_tensor(name, shape, F32, kind="ExternalOutput").ap()

    def dscr(name, shape):
        return nc.dram_tensor(name, shape, BF16, kind="Internal").ap()

    xT = din("xT", [D, NPRE + SEG])
    xsT = din("xsT", [D, 32])
    h0_d = din("h0", [128, 64])
    convc_d = din("convc", [128, 240])
    kTs_d = din("kTs", [2, D, 256])
    vs_d = din("vs", [2, 256, D])
    memT_d = din("memT", [D, 256])
    w_in_d = din("w_in", [D, 1536])
    glu_w_d = din("glu_w", [512, 512])
    w_out_d = din("w_out", [D, D])
    w_q_d = din("w_q", [D, D])
    w_k_d = din("w_k", [D, D])
    w_v_d = din("w_v", [D, D])
    w_o_d = din("w_o", [D, D])
    w1_d = din("w1", [D, 4096])
    w2_d = din("w2", [4096, D])
    pk_d = din("pk", [128, NPK])
    rows_d = din("rows", [3, 2048])
    BT_d = din("BT", [2, 128, 2048])
    CT_d = din("CT", [2, 128, 2048])
    Bx_d = din("Bx", [2, 128, 512])

    yT = dout("yT", [D, SEG])
    ysT = dout("ysT", [D, 32])
    hout_d = dout("hout", [128, 32])
    convout_d = dout("convout", [128, 120])
    kout_d = dout("kout", [256, D])
    vout_d = dout("vout", [256, D])
    hsout_d = dout("hsout", [128, 64])
    convsout_d = dout("convsout", [128, 240])

    w_in_b = dscr("w_in_b", [D, 1536])
    w_out_b = dscr("w_out_b", [D, D])
    w_q_b = dscr("w_q_b", [D, D])
    w_k_b = dscr("w_k_b", [D, D])
    w_v_b = dscr("w_v_b", [D, D])
    w_o_b = dscr("w_o_b", [D, D])
    w1_b = dscr("w1_b", [D, 4096])
    w2_b = dscr("w2_b", [4096, D])

    def sb(name, shape, dt=F32):
        return st.enter_context(nc.sbuf_tensor("sb_" + name, shape, dt))

    pk = sb("pk", [128, NPK])
    cgrp = P.group("consts")
    pkB = Buf("pk", group=cgrp)
    ones_m = sb("ones_m", [128, 128], BF16)
    ones_c = sb("ones_c", [128, 128], BF16)
    ones_1 = sb("ones_1", [128, 128], BF16)
    onesB = Buf("ones")
    R0 = sb("R0", [128, 8, T])
    R1 = sb("R1", [128, 8, T])
    xnb = sb("xnb", [128, 8, T], BF16)
    R0B = [Buf(f"R0_{c}") for c in range(8)]
    R1B = [Buf(f"R1_{c}") for c in range(8)]
    xnbB = [Buf(f"xnb_{c}") for c in range(8)]
    ob, obB = xnb, xnbB
    xb = [sb(f"xb{i}", [128, 8, T], BF16) for i in range(2)]
    xbB = [Buf(f"xb{i}") for i in range(2)]
    NRING = 3
    ring = [sb(f"ring{i}", [128, 4096], BF16) for i in range(NRING)]
    ringB = [Buf(f"ring{i}") for i in range(NRING)]
    ring_i = [0]
    fring = sb("fring", [128, 4096], BF16)
    fringB = Buf("fring")
    glu_sb = sb("glu_sb", [128, 4, 512], BF16)
    gluB = Buf("glu")
    u32 = sb("u32", [128, 4, T])
    ub = sb("ub", [128, 4, T], BF16)
    u32B = [Buf(f"u32_{c}") for c in range(4)]
    ubB = [Buf(f"ub_{c}") for c in range(4)]
    vbuf = sb("vbuf", [128, 4, 30 + T])
    vbufB = [Buf(f"vbuf_{c}") for c in range(4)]
    cacc = sb("cacc", [128, 4, T])
    caccB = [Buf(f"cacc_{c}") for c in range(4)]
    mixin = sb("mixin", [128, 8, T], BF16)
    mixB = [Buf(f"mix_{c}") for c in range(8)]
    z32 = sb("z32", [128, 4, T])
    zb = sb("zb", [128, 4, T], BF16)
    z32B = [Buf(f"z32_{c}") for c in range(4)]
    zbB = [Buf(f"zb_{c}") for c in range(4)]
    hid = sb("hid", [128, 32, T], BF16)
    hidB = [Buf(f"hid_{c}") for c in range(32)]
    qb, qbB = hid, hidB
    kT = sb("kT", [128, 8, 256], BF16)
    kTB = Buf("kT")
    vv = sb("vv", [128, 2, D], BF16)
    vvB = Buf("vv")
    costab = sb("costab", [128, 16, TS])
    sintab = sb("sintab", [128, 16, TS])
    rtab = sb("rtab", [128, 16, TS])
    tabB = Buf("tabs")
    BreT = sb("BreT", [128, 16, 128], BF16)
    BimT = sb("BimT", [128, 16, 128], BF16)
    CreT = sb("CreT", [128, 16, 128], BF16)
    CimT = sb("CimT", [128, 16, 128], BF16)
    Pre = sb("Pre", [128, 16, 128], BF16)
    Pim = sb("Pim", [128, 16, 128], BF16)
    lhsB = Buf("ssm_lhs")
    P2re = hid[:, 0:8, :].rearrange("p a b -> p (a b)").rearrange("p (c d) -> p c d", c=16)
    P2im = hid[:, 8:16, :].rearrange("p a b -> p (a b)").rearrange("p (c d) -> p c d", c=16)
    p2B = hidB[0:16]
    Bxr = sb("Bxr", [128, 512])
    Bxi = sb("Bxi", [128, 512])
    BxB = Buf("Bx")
    a128r = sb("a128r", [128, 16])
    a128i = sb("a128i", [128, 16])
    a128B = Buf("a128")
    Hre = sb("Hre", [128, 16])
    Him = sb("Him", [128, 16])
    HB = [Buf(f"H_{p}") for p in range(16)]
    Hs = sb("Hs", [128, 64])
    HsB = [Buf(f"Hs_{p}") for p in range(16)]

    s16 = Rot(st, nc, "s16_", [128, T], BF16, 4)
    f32t = Rot(st, nc, "f32t_", [128, T], F32, 5)
    sst = Rot(st, nc, "sst_", [128, 32 if GRP else TS], F32, 10)
    hbf = Rot(st, nc, "hbf_", [128, 32 if GRP else TS], BF16, 4)
    gst = Rot(st, nc, "gst_", [128, 512], F32, NGST)
    ctmp = Rot(st, nc, "ctmp_", [128, T], F32, 2) if CONV_POOL_CH > 0 else None
    ghb = Rot(st, nc, "ghb_", [128, 512], BF16, 4)
    tiny = Rot(st, nc, "tiny_", [128, 16], F32, 8)
    pTr = Rot(st, nc, "pT_", [128, T], BF16, 4)
    statsb = Rot(st, nc, "stat_", [128, T], F32, 5)
    psum = Rot(st, nc, "ps", [128, 512], F32, 3, psum=True)
    ypsR = Rot(st, nc, "yps", [128, 512], F32, 1, psum=True)
    buR = Rot(st, nc, "bups", [128, 512], F32, 2, psum=True)
    lnR = Rot(st, nc, "lnps", [128, 512], F32, 2, psum=True)

    def big(i):
        src, bufs = (R0, R0B) if i < 4 else (R1, R1B)
        j = (i % 4) * 2
        return [bufs[j], bufs[j + 1]], src[:, j:j + 2, :].rearrange("p a b -> p (a b)")

    pkc = lambda name, i=0, n=1: pk[:, PK[name][0] + i: PK[name][0] + i + n]

    def tt(eng, out, a, b, op, reads, writes):
        P.op(eng, lambda e: e.tensor_tensor(out=out, in0=a, in1=b, op=op), reads, writes)

    def ts(eng, out, a, s1, s2, op0, op1, reads, writes):
        if op1 is None:
            P.op(eng, lambda e: e.tensor_scalar(out=out, in0=a, scalar1=s1, scalar2=None, op0=op0), reads, writes)
        else:
            P.op(eng, lambda e: e.tensor_scalar(out=out, in0=a, scalar1=s1, scalar2=s2, op0=op0, op1=op1), reads, writes)

    def stt(eng, out, a, s, b, op0, op1, reads, writes):
        P.op(eng, lambda e: e.scalar_tensor_tensor(out=out, in0=a, scalar=s, in1=b, op0=op0, op1=op1), reads, writes)

    def act(out, in_, func, reads, writes, scale=None, bias=None):
        kw = {}
        if scale is not None:
            kw["scale"] = scale
        if bias is not None:
            kw["bias"] = bias
        P.op("act", lambda e: e.activation(out=out, in_=in_, func=func, **kw), reads, writes)

    def cp(eng, out, in_, reads, writes):
        if eng == "act":
            act(out, in_, AF.Copy, reads, writes)
        else:
            P.op(eng, lambda e: e.tensor_copy(out=out, in_=in_), reads, writes)

    def recip(out, in_, reads, writes):
        P.op("dve", lambda e: e.reciprocal(out=out, in_=in_), reads, writes)

    def mm(out, pairs, reads, writes):
        n = len(pairs)

        def fn(e):
            ins = None
            for i, (l, r) in enumerate(pairs):
                ins = e.matmul(out, l, r, start=(i == 0), stop=(i == n - 1))
            return ins
        P.op("pe", fn, reads, writes)

    def mm1(out, l, r, start, stop, reads, writes):
        P.op("pe", lambda e: e.matmul(out, l, r, start=start, stop=stop), reads, writes)

    def dma(eng, out, in_, key, reads=(), writes=(), final=False):
        P.dma(eng, lambda e: e.dma_start(out=out, in_=in_), key, reads=reads, writes=writes, final=final)

    def memset(eng, ap, val, writes):
        P.op(eng, lambda e: e.memset(ap, val), (), writes)

    def range_reduce(out, in_, shift, reads, writes, tmp, tmpB):
        src = in_
        rd = list(reads)
        if shift != 0.0:
            ts("dve", out, in_, shift, None, ALU.add, None, reads, writes)
            src = out
            rd = list(writes)
        ts("dve", tmp, src, 1.0 / TWO_PI, MAGIC, ALU.mult, ALU.add, rd, tmpB)
        ts("dve", tmp, tmp, MAGIC, -TWO_PI, ALU.subtract, ALU.mult, tmpB, tmpB)
        tt("dve", out, src, tmp, ALU.add, rd + list(tmpB), writes)
        ts("dve", out, out, PI, -PI, ALU.min, ALU.max, writes, writes)

    dma("sp", pk[:], pk_d[:, :], pkB, writes=[pkB])
    dma("pool", glu_sb[:], glu_w_d.rearrange("(k p) n -> p k n", p=128), gluB, writes=[gluB])
    memset("dve", ones_m[:], 1.0 / 1024.0, [onesB])
    memset("dve", ones_c[:], 1.0 / 512.0, [onesB])
    memset("dve", ones_1[:], 1.0, [onesB])


    pcsB = [Buf(f"pcs{i}") for i in range(NRING)]
    pclB = [Buf(f"pcl{i}") for i in range(NRING)]

    def precast_gen(bl, dst, src, nrows, colmap=None):
        ncols = src.shape[1]
        nk = nrows // 128
        sv = src.rearrange("(k p) n -> p k n", p=128)
        dv = dst.rearrange("(k p) n -> p k n", p=128)
        if colmap is None:
            colmap = [(c0, c0, min(1024, ncols - c0)) for c0 in range(0, ncols, 1024)]
        for (d0, s0, n) in colmap:
            kstep = max(1, min(nk, 4096 // n))
            for k0 in range(0, nk, kstep):
                i = ring_i[0] % NRING
                ring_i[0] += 1
                view = ring[i][:, 0:kstep * n].rearrange("p (a b) -> p a b", a=kstep)
                dma("pool", view, sv[:, k0:k0 + kstep, s0:s0 + n], pclB[i], writes=[ringB[i]])
                dma("sp", dv[:, k0:k0 + kstep, d0:d0 + n], view, pcsB[i], reads=[ringB[i]], writes=[bl[i]])
                yield

    def precast(name, dst, src, nrows, colmap=None):
        bl = [Buf(f"pcb_{name}{i}") for i in range(NRING)]
        for _ in precast_gen(bl, dst, src, nrows, colmap):
            pass
        return bl

    def precast_later(name, dst, src, nrows):
        bl = [Buf(f"pcb_{name}{i}") for i in range(NRING)]
        late_pc.append(precast_gen(bl, dst, src, nrows))
        return bl

    late_pc = []

    win_map = [(0, 0, 512)]
    for i in range(4):
        win_map.append((512 + 256 * i, 512 + 128 * i, 128))
        win_map.append((512 + 256 * i + 128, 1024 + 128 * i, 128))
    pc = {}
    if RUN_PC == "none":
        def precast(name, dst, src, nrows, colmap=None):
            return [Buf("pcb_" + name)]
        precast_later = lambda name, dst, src, nrows: [Buf("pcb_" + name)]
    pc["w_in"] = precast("w_in", w_in_b, w_in_d, D, win_map)
    pc["w_k"] = precast("w_k", w_k_b, w_k_d, D)
    pc["w_v"] = precast("w_v", w_v_b, w_v_d, D)
    pc["w_out"] = precast_later("w_out", w_out_b, w_out_d, D)
    pc["w_q"] = precast_later("w_q", w_q_b, w_q_d, D)
    pc["w_o"] = precast_later("w_o", w_o_b, w_o_d, D)
    pc["w1"] = precast_later("w1", w1_b, w1_d, D)
    pc["w2"] = precast_later("w2", w2_b, w2_d, 4096)

    def ring_load(src_ap, a, b, pcb):
        i = ring_i[0] % NRING
        ring_i[0] += 1
        view = ring[i][:, 0:a * b].rearrange("p (a b) -> p a b", a=a)
        dma("sp", view, src_ap, ringB[i], reads=pcb, writes=[ringB[i]])
        return ringB[i], view

    def wpiece(wb, pcb, n0, n1):
        return ring_load(wb.rearrange("(k p) n -> p k n", p=128)[:, :, n0:n1], 8, n1 - n0, pcb)

    def fpiece(wb, pcb, n0, n1):
        view = fring[:, 0:8 * (n1 - n0)].rearrange("p (a b) -> p a b", a=8)
        dma("sp", view, wb.rearrange("(k p) n -> p k n", p=128)[:, :, n0:n1], fringB, reads=pcb, writes=[fringB])
        return fringB, view

    def disc(A, Bm, L, tmp, rdB, wB):
        T1, T2, T3, T4, T5, T6, T7 = tmp
        act(L, L, AF.Exp, rdB, wB)
        tt("dve", T1, A, L, ALU.mult, wB, wB)
        tt("dve", T5, Bm, L, ALU.mult, wB, wB)
        act(T6, T1, AF.Exp, wB, wB)
        range_reduce(T2, T5, 0.0, wB, wB, T7, wB)
        act(T2, T2, AF.Sin, wB, wB)
        range_reduce(T3, T5, PI / 2, wB, wB, T7, wB)
        act(T3, T3, AF.Sin, wB, wB)
        tt("dve", T3, T6, T3, ALU.mult, wB, wB)
        tt("dve", T2, T6, T2, ALU.mult, wB, wB)
        ts("dve", T3, T3, -1.0, None, ALU.add, None, wB, wB)
        tt("dve", T6, A, A, ALU.mult, wB, wB)
        tt("dve", T7, Bm, Bm, ALU.mult, wB, wB)
        tt("dve", T6, T6, T7, ALU.add, wB, wB)
        recip(T6, T6, wB, wB)
        tt("dve", L, T3, A, ALU.mult, wB, wB)
        tt("dve", T4, T2, Bm, ALU.mult, wB, wB)
        tt("dve", L, L, T4, ALU.add, wB, wB)
        tt("dve", L, L, T6, ALU.mult, wB, wB)
        tt("dve", T4, T2, A, ALU.mult, wB, wB)
        tt("dve", T3, T3, Bm, ALU.mult, wB, wB)
        tt("dve", T4, T4, T3, ALU.subtract, wB, wB)
        tt("dve", T4, T4, T6, ALU.mult, wB, wB)
        return dict(x1=T1, ang=T5, c_re=L, c_im=T4)

    if RUN_PREP:
        mB_ = [Buf("modeprep")]
        mA, mBm, mL = sb("mA", [128, 16]), sb("mBm", [128, 16]), sb("mL", [128, 16])
        mT = [sb(f"mT{i}", [128, 16]) for i in range(7)]
        cp("dve", mA[:], pkc("a_re", 0, 16), [pkB], mB_)
        cp("dve", mBm[:], pkc("a_im", 0, 16), [pkB], mB_)
        cp("dve", mL[:], pkc("logdt", 0, 16), [pkB], mB_)
        dm = disc(mA[:], mBm[:], mL[:], [t[:] for t in mT], mB_, mB_)
        m128a, m128m, mr = sb("m128a", [128, 16]), sb("m128m", [128, 16]), sb("mr", [128, 16])
        ts("dve", m128a[:], dm["ang"], 256.0 if PFG2 else 128.0, None, ALU.mult, None, mB_, mB_)
        act(m128m[:], dm["x1"], AF.Exp, mB_, mB_, scale=256.0 if PFG2 else 128.0)
        range_reduce(mT[1][:], m128a[:], 0.0, mB_, mB_, mT[6][:], mB_)
        act(mT[1][:], mT[1][:], AF.Sin, mB_, mB_)
        range_reduce(mT[2][:], m128a[:], PI / 2, mB_, mB_, mT[6][:], mB_)
        act(mT[2][:], mT[2][:], AF.Sin, mB_, mB_)
        tt("dve", a128r[:], m128m[:], mT[2][:], ALU.mult, mB_, [a128B])
        tt("dve", a128i[:], m128m[:], mT[1][:], ALU.mult, mB_ + [a128B], [a128B])
        act(mr[:], dm["x1"], AF.Exp, mB_, mB_)
        if TAB_BATCH:
            argT = R0[:].rearrange("p a b -> p (a b)")
            tmpT = R1[:].rearrange("p a b -> p (a b)")
            fl = lambda t_: t_[:].rearrange("p a b -> p (a b)")
            v16 = lambda t_: t_.rearrange("p (a b) -> p a b", a=16)
            tt("dve", v16(argT), pkc("tidx", 0, TS).unsqueeze(1).to_broadcast([128, 16, TS]),
               dm["ang"].unsqueeze(2).to_broadcast([128, 16, TS]), ALU.mult, [pkB] + mB_, R0B)
            range_reduce(fl(sintab), argT, 0.0, R0B, [tabB], tmpT, R1B)
            act(fl(sintab), fl(sintab), AF.Sin, [tabB], [tabB])
            range_reduce(fl(costab), argT, PI / 2, R0B, [tabB], tmpT, R1B)
            act(fl(costab), fl(costab), AF.Sin, [tabB], [tabB])
            cp("dve", rtab[:], mr[:].unsqueeze(2).to_broadcast([128, 16, TS]), mB_, [tabB])
            memset("dve", rtab[:, :, 0:1], 0.0, [tabB])
        else:
            for p_ in range(16):
                tg, tg2 = gst.get(), gst.get()
                tg = (tg[0], tg[1][:, 0:TS])
                tg2 = (tg2[0], tg2[1][:, 0:TS])
                ts("dve", tg[1][:], pkc("tidx", 0, TS), dm["ang"][:, p_:p_ + 1], None, ALU.mult, None, [pkB] + mB_, [tg[0]])
                range_reduce(sintab[:, p_, :], tg[1][:], 0.0, [tg[0]], [tabB], tg2[1][:], [tg2[0]])
                act(sintab[:, p_, :], sintab[:, p_, :], AF.Sin, [tabB], [tabB])
                range_reduce(costab[:, p_, :], tg[1][:], PI / 2, [tg[0]], [tabB], tg2[1][:], [tg2[0]])
                act(costab[:, p_, :], costab[:, p_, :], AF.Sin, [tabB], [tabB])
                ts("dve", rtab[:, p_, :], pkc("tidx", 0, TS), 0.0, mr[:, p_:p_ + 1], ALU.mult, ALU.add, [pkB] + mB_, [tabB])
                memset("dve", rtab[:, p_, 0:1], 0.0, [tabB])
        bxrB, bxr_t = big(0)
        bxiB, bxi_t = big(1)
        dma("sp", bxr_t, Bx_d[0], bxrB[0], writes=bxrB)
        dma("sp", bxi_t, Bx_d[1], bxiB[0], writes=bxiB)
        if TAB_BATCH:
            v32 = lambda t_: t_.rearrange("p (a b) -> p a b", a=16)
            crb = dm["c_re"].unsqueeze(2).to_broadcast([128, 16, 32])
            cib = dm["c_im"].unsqueeze(2).to_broadcast([128, 16, 32])
            ta, tb_ = gst.get(), gst.get()
            tt("dve", v32(ta[1][:]), v32(bxi_t), cib, ALU.mult, bxiB + mB_, [ta[0]])
            tt("dve", v32(tb_[1][:]), v32(bxr_t), crb, ALU.mult, bxrB + mB_, [tb_[0]])
            tt("dve", Bxr[:], tb_[1][:], ta[1][:], ALU.subtract, [ta[0], tb_[0]], [BxB])
            tt("dve", v32(ta[1][:]), v32(bxr_t), cib, ALU.mult, bxrB + mB_ + [BxB], [ta[0]])
            tt("dve", v32(tb_[1][:]), v32(bxi_t), crb, ALU.mult, bxiB + mB_ + [BxB], [tb_[0]])
            tt("dve", Bxi[:], tb_[1][:], ta[1][:], ALU.add, [ta[0], tb_[0], BxB], [BxB])
        else:
            for p_ in range(16):
                sl = slice(p_ * 32, p_ * 32 + 32)
                cr, ci = dm["c_re"][:, p_:p_ + 1], dm["c_im"][:, p_:p_ + 1]
                ta, tb_ = gst.get(), gst.get()
                ts("dve", ta[1][:, 0:32], bxi_t[:, sl], ci, None, ALU.mult, None, bxiB + mB_, [ta[0]])
                stt("dve", Bxr[:, sl], bxr_t[:, sl], cr, ta[1][:, 0:32], ALU.mult, ALU.subtract, bxrB + mB_ + [ta[0]], [BxB])
                ts("dve", tb_[1][:, 0:32], bxr_t[:, sl], ci, None, ALU.mult, None, bxrB + mB_, [tb_[0]])
                stt("dve", Bxi[:, sl], bxi_t[:, sl], cr, tb_[1][:, 0:32], ALU.mult, ALU.add, bxiB + mB_ + [tb_[0]], [BxB])

        rt = [R0[:, c, :] for c in range(8)] + [R1[:, c, :] for c in range(8)]
        rtB = R0B + R1B
        tmp_tiles = []
        tmpB = []
        for gi_ in range(4):
            gb_, gt_ = gst.items[gi_]
            tmpB.append(gb_)
            tmp_tiles += [gt_[:, 0:256], gt_[:, 256:512]]
        for blk in range(8):
            cs = slice(blk * 256, blk * 256 + 256)
            base_ = (blk % 2) * 7
            inT = rt[base_:base_ + 7]
            inB = rtB[base_:base_ + 7]
            rA, rBm, rL, btr, bti, ctr, cti = inT
            srcs_ = [rows_d[0:1, cs].partition_broadcast(128), rows_d[1:2, cs].partition_broadcast(128),
                     rows_d[2:3, cs].partition_broadcast(128), BT_d[0][:, cs], BT_d[1][:, cs], CT_d[0][:, cs], CT_d[1][:, cs]]
            for k_ in range(7):
                dma("sp", inT[k_], srcs_[k_], inB[k_], writes=[inB[k_]])
            rowB = inB + tmpB
            tmp = tmp_tiles[0:7]
            dr = disc(rA, rBm, rL, tmp, rowB, rowB)
            sA, sB_ = tmp[1], tmp[2]
            s3, s4 = tmp[5], tmp[6]
            osl = lambda t_: t_[:, blk * 2:blk * 2 + 2, :].rearrange("p a b -> p (a b)")
            tt("dve", sA, dr["c_re"], btr, ALU.mult, rowB, rowB)
            tt("dve", sB_, dr["c_im"], bti, ALU.mult, rowB, rowB)
            tt("dve", osl(BreT), sA, sB_, ALU.subtract, rowB, [lhsB])
            tt("dve", sA, dr["c_re"], bti, ALU.mult, rowB, rowB)
            tt("dve", sB_, dr["c_im"], btr, ALU.mult, rowB, rowB)
            tt("dve", osl(BimT), sA, sB_, ALU.add, rowB + [lhsB], [lhsB])
            e_ap = pkc("eidx")
            act(sA, dr["x1"], AF.Exp, rowB + [pkB], rowB, scale=e_ap)
            ts("dve", sB_, dr["ang"], e_ap, None, ALU.mult, None, rowB + [pkB], rowB)
            range_reduce(s3, sB_, 0.0, rowB, rowB, s4, rowB)
            act(s3, s3, AF.Sin, rowB, rowB)
            tt("dve", osl(Pim), sA, s3, ALU.mult, rowB + [lhsB], [lhsB])
            range_reduce(s3, sB_, PI / 2, rowB, rowB, s4, rowB)
            act(s3, s3, AF.Sin, rowB, rowB)
            tt("dve", osl(Pre), sA, s3, ALU.mult, rowB + [lhsB], [lhsB])
            if PFG2:
                e2_ap = pkc("eidx2")
                act(sA, dr["x1"], AF.Exp, rowB + [pkB], rowB, scale=e2_ap)
                ts("dve", sB_, dr["ang"], e2_ap, None, ALU.mult, None, rowB + [pkB], rowB)
                range_reduce(s3, sB_, 0.0, rowB, rowB, s4, rowB)
                act(s3, s3, AF.Sin, rowB, rowB)
                tt("dve", osl(P2im), sA, s3, ALU.mult, rowB + p2B, p2B)
                range_reduce(s3, sB_, PI / 2, rowB, rowB, s4, rowB)
                act(s3, s3, AF.Sin, rowB, rowB)
                tt("dve", osl(P2re), sA, s3, ALU.mult, rowB + p2B, p2B)
            cp("dve", osl(CreT), ctr, rowB + [lhsB], [lhsB])
            ts("dve", osl(CimT), cti, -1.0, None, ALU.mult, None, rowB + [lhsB], [lhsB])

    def layer_norm(nch, srcs, srcB, ones_ap, Tn, g_name, b_name, emit_out):
        pm, pe2 = lnR.get(), lnR.get()
        for c in range(nch):
            s1, s2 = s16.get(), s16.get()
            act(s1[1][:, :Tn], srcs[c], AF.Copy, [srcB[c]], [s1[0]])
            act(s2[1][:, :Tn], srcs[c], AF.Square, [srcB[c]], [s2[0]])
            mm1(pm[1][:, :Tn], ones_ap, s1[1][:, :Tn], c == 0, c == nch - 1, [s1[0], onesB], [pm[0]])
            mm1(pe2[1][:, :Tn], ones_ap, s2[1][:, :Tn], c == 0, c == nch - 1, [s2[0], onesB], [pe2[0]])
        mean, var, nmr = statsb.get(), statsb.get(), statsb.get()
        cp("act", mean[1][:, :Tn], pm[1][:, :Tn], [pm[0]], [mean[0]])
        tt("dve", var[1][:, :Tn], mean[1][:, :Tn], mean[1][:, :Tn], ALU.mult, [mean[0]], [var[0]])
        tt("dve", var[1][:, :Tn], pe2[1][:, :Tn], var[1][:, :Tn], ALU.subtract, [pe2[0], var[0]], [var[0]])
        ts("dve", var[1][:, :Tn], var[1][:, :Tn], LN_EPS, None, ALU.add, None, [var[0]], [var[0]])
        act(var[1][:, :Tn], var[1][:, :Tn], AF.Sqrt, [var[0]], [var[0]])
        recip(var[1][:, :Tn], var[1][:, :Tn], [var[0]], [var[0]])
        stt("dve", nmr[1][:, :Tn], mean[1][:, :Tn], -1.0, var[1][:, :Tn], ALU.mult, ALU.mult, [mean[0], var[0]], [nmr[0]])
        for c in range(nch):
            t_ = f32t.get()
            tt("dve", t_[1][:, :Tn], srcs[c], var[1][:, :Tn], ALU.mult, [srcB[c], var[0]], [t_[0]])
            tt("dve", t_[1][:, :Tn], t_[1][:, :Tn], nmr[1][:, :Tn], ALU.add, [t_[0], nmr[0]], [t_[0]])
            emit_out(c, t_[1][:, :Tn], t_[0], pkc(g_name, c), pkc(b_name, c))

    def ln_to_resid(Tn, g_name, b_name):
        def emit_out(c, t_ap, tB, g_ap, b_ap):
            act(R1[:, c, :Tn], t_ap, AF.Identity, [tB, pkB], [R1B[c]], scale=g_ap, bias=b_ap)
            act(xnb[:, c, :Tn], t_ap, AF.Identity, [tB, pkB], [xnbB[c]], scale=g_ap, bias=b_ap)
        layer_norm(8, [R0[:, c, :Tn] for c in range(8)], R0B, ones_m[:], Tn, g_name, b_name, emit_out)

    def ssm_segment(pi, col0, L, Hr_ap, Hi_ap, HBuf, ypsum, first, last):
        c = pi // 4
        bu = psum.get()
        bre, bim = bu[1][:, 0:L], bu[1][:, 256:256 + L]
        mm1(bre, BreT[:, pi, :], ub[:, c, col0:col0 + L], True, True, [lhsB, ubB[c]], [bu[0]])
        mm1(bim, BimT[:, pi, :], ub[:, c, col0:col0 + L], True, True, [lhsB, ubB[c]], [bu[0]])
        cosA, sinA, rA = costab[:, pi, 0:L], sintab[:, pi, 0:L], rtab[:, pi, 0:L]
        cos1, sin1 = costab[:, pi, 1:2], sintab[:, pi, 1:2]
        g0, tq = tiny.get(), tiny.get()
        tt("dve", tq[1][:, 0:1], Hi_ap, sin1, ALU.mult, [HBuf, tabB], [tq[0]])
        tt("dve", tq[1][:, 1:2], Hr_ap, cos1, ALU.mult, [HBuf, tabB, tq[0]], [tq[0]])
        tt("dve", tq[1][:, 2:3], Hr_ap, sin1, ALU.mult, [HBuf, tabB, tq[0]], [tq[0]])
        tt("dve", tq[1][:, 3:4], Hi_ap, cos1, ALU.mult, [HBuf, tabB, tq[0]], [tq[0]])
        tt("dve", g0[1][:, 0:1], tq[1][:, 1:2], tq[1][:, 0:1], ALU.subtract, [tq[0]], [g0[0]])
        tt("dve", g0[1][:, 1:2], tq[1][:, 3:4], tq[1][:, 2:3], ALU.add, [tq[0], g0[0]], [g0[0]])
        t1, t2, t3, t4 = sst.get(), sst.get(), sst.get(), sst.get()
        tt("dve", t1[1][:, :L], bre, cosA, ALU.mult, [bu[0], tabB], [t1[0]])
        tt("dve", t2[1][:, :L], bim, sinA, ALU.mult, [bu[0], tabB], [t2[0]])
        tt("dve", t3[1][:, :L], bim, cosA, ALU.mult, [bu[0], tabB], [t3[0]])
        tt("dve", t4[1][:, :L], bre, sinA, ALU.mult, [bu[0], tabB], [t4[0]])
        tt("dve", t1[1][:, :L], t1[1][:, :L], t2[1][:, :L], ALU.add, [t1[0], t2[0]], [t1[0]])
        tt("dve", t3[1][:, :L], t3[1][:, :L], t4[1][:, :L], ALU.subtract, [t3[0], t4[0]], [t3[0]])
        r1 = rtab[:, pi, 1:2]
        tt("dve", g0[1][:, 2:3], g0[1][:, 0:1], r1, ALU.mult, [g0[0], tabB], [g0[0]])
        tt("dve", g0[1][:, 3:4], g0[1][:, 1:2], r1, ALU.mult, [g0[0], tabB], [g0[0]])
        tt("dve", t1[1][:, 0:1], t1[1][:, 0:1], g0[1][:, 2:3], ALU.add, [t1[0], g0[0]], [t1[0]])
        tt("dve", t3[1][:, 0:1], t3[1][:, 0:1], g0[1][:, 3:4], ALU.add, [t3[0], g0[0]], [t3[0]])
        gre, gim = sst.get(), sst.get()
        P.op("dve", lambda e: e.tensor_tensor_scan(out=gre[1][:, :L], data0=rA, data1=t1[1][:, :L], initial=0.0, op0=ALU.mult, op1=ALU.add),
             [tabB, t1[0]], [gre[0]])
        P.op("dve", lambda e: e.tensor_tensor_scan(out=gim[1][:, :L], data0=rA, data1=t3[1][:, :L], initial=0.0, op0=ALU.mult, op1=ALU.add),
             [tabB, t3[0]], [gim[0]])
        p1, p2, p3, p4 = sst.get(), sst.get(), sst.get(), sst.get()
        tt("dve", p1[1][:, :L], gre[1][:, :L], cosA, ALU.mult, [gre[0], tabB], [p1[0]])
        tt("dve", p2[1][:, :L], gim[1][:, :L], sinA, ALU.mult, [gim[0], tabB], [p2[0]])
        tt("dve", p3[1][:, :L], gim[1][:, :L], cosA, ALU.mult, [gim[0], tabB], [p3[0]])
        tt("dve", p4[1][:, :L], gre[1][:, :L], sinA, ALU.mult, [gre[0], tabB], [p4[0]])
        hr, hi = hbf.get(), hbf.get()
        tt("dve", hr[1][:, :L], p1[1][:, :L], p2[1][:, :L], ALU.subtract, [p1[0], p2[0]], [hr[0]])
        tt("dve", hi[1][:, :L], p3[1][:, :L], p4[1][:, :L], ALU.add, [p3[0], p4[0]], [hi[0]])
        tt("dve", Hr_ap, p1[1][:, L - 1:L], p2[1][:, L - 1:L], ALU.subtract, [p1[0], p2[0]], [HBuf])
        tt("dve", Hi_ap, p3[1][:, L - 1:L], p4[1][:, L - 1:L], ALU.add, [p3[0], p4[0], HBuf], [HBuf])
        yo = ypsum[1][:, col0:col0 + L]
        mm1(yo, CreT[:, pi, :], hr[1][:, :L], first, False, [lhsB, hr[0]], [ypsum[0]])
        mm1(yo, CimT[:, pi, :], hi[1][:, :L], False, last, [lhsB, hi[0]], [ypsum[0]])

    def conv_chunk(c, col0, L, eng):
        o = cacc[:, c, col0:col0 + L]
        base = PK["conv_w"][0] + c * 31
        ts(eng, o, vbuf[:, c, 0:L], pk[:, base:base + 1], pkc("conv_b", c), ALU.mult, ALU.add, [vbufB[c], pkB], [caccB[c]])
        for k in range(1, 31):
            stt(eng, o, vbuf[:, c, k:k + L], pk[:, base + k:base + k + 1], o, ALU.mult, ALU.add, [vbufB[c], pkB, caccB[c]], [caccB[c]])

    def v3(t_):
        return t_.rearrange("p (a b) -> p a b", a=4)

    def ssm_s0(c, col0):
        bR, bI = buR.get(), buR.get()

        def fn(e):
            ins = None
            for q in range(4):
                e.matmul(bR[1][:, q * 128:(q + 1) * 128], BreT[:, 4 * c + q, :], ub[:, c, col0:col0 + 128], start=True, stop=True)
                ins = e.matmul(bI[1][:, q * 128:(q + 1) * 128], BimT[:, 4 * c + q, :], ub[:, c, col0:col0 + 128], start=True, stop=True)
            return ins
        P.op("pe", fn, [lhsB, ubB[c]], [bR[0], bI[0]])
        return bR, bI

    def ssm_group(c, col0, yp, filler, bu, after_s2):
        ps4 = slice(4 * c, 4 * c + 4)
        HBs = HB[4 * c:4 * c + 4]
        bR, bI = bu
        cosG, sinG = costab[:, ps4, :], sintab[:, ps4, :]
        rG = rtab[:, ps4, :].rearrange("p a b -> p (a b)")
        cos1, sin1, r1 = costab[:, ps4, 1], sintab[:, ps4, 1], rtab[:, ps4, 1]
        Hr, Hi = Hre[:, ps4], Him[:, ps4]
        tq, g0 = tiny.get(), tiny.get()
        tt("dve", tq[1][:, 0:4], Hi, sin1, ALU.mult, HBs + [tabB], [tq[0]])
        tt("dve", tq[1][:, 4:8], Hr, cos1, ALU.mult, HBs + [tabB, tq[0]], [tq[0]])
        tt("dve", tq[1][:, 8:12], Hr, sin1, ALU.mult, HBs + [tabB, tq[0]], [tq[0]])
        tt("dve", tq[1][:, 12:16], Hi, cos1, ALU.mult, HBs + [tabB, tq[0]], [tq[0]])
        tt("dve", g0[1][:, 0:4], tq[1][:, 4:8], tq[1][:, 0:4], ALU.subtract, [tq[0]], [g0[0]])
        tt("dve", g0[1][:, 4:8], tq[1][:, 12:16], tq[1][:, 8:12], ALU.add, [tq[0], g0[0]], [g0[0]])
        tt("dve", g0[1][:, 8:12], g0[1][:, 0:4], r1, ALU.mult, [g0[0], tabB], [g0[0]])
        tt("dve", g0[1][:, 12:16], g0[1][:, 4:8], r1, ALU.mult, [g0[0], tabB], [g0[0]])
        filler(3)
        yield
        A, B_, C, D_ = gst.get(), gst.get(), gst.get(), gst.get()
        tt("dve", v3(A[1][:]), v3(bR[1][:]), cosG, ALU.mult, [bR[0], tabB], [A[0]])
        tt("dve", v3(B_[1][:]), v3(bI[1][:]), sinG, ALU.mult, [bI[0], tabB], [B_[0]])
        filler(2)
        tt("dve", v3(C[1][:]), v3(bI[1][:]), cosG, ALU.mult, [bI[0], tabB], [C[0]])
        tt("dve", v3(D_[1][:]), v3(bR[1][:]), sinG, ALU.mult, [bR[0], tabB], [D_[0]])
        after_s2()
        filler(2)
        yield
        tt(SSM_POST, A[1][:], A[1][:], B_[1][:], ALU.add, [A[0], B_[0]], [A[0]])
        tt(SSM_POST, v3(A[1][:])[:, :, 0], v3(A[1][:])[:, :, 0], g0[1][:, 8:12], ALU.add, [A[0], g0[0]], [A[0]])
        tt(SSM_POST, C[1][:], C[1][:], D_[1][:], ALU.subtract, [C[0], D_[0]], [C[0]])
        tt(SSM_POST, v3(C[1][:])[:, :, 0], v3(C[1][:])[:, :, 0], g0[1][:, 12:16], ALU.add, [C[0], g0[0]], [C[0]])
        filler(4)
        yield
        GR, GI = gst.get(), gst.get()
        P.op("dve", lambda e: e.tensor_tensor_scan(out=GR[1][:], data0=rG, data1=A[1][:], initial=0.0, op0=ALU.mult, op1=ALU.add),
             [tabB, A[0]], [GR[0]])
        filler(2)
        P.op("dve", lambda e: e.tensor_tensor_scan(out=GI[1][:], data0=rG, data1=C[1][:], initial=0.0, op0=ALU.mult, op1=ALU.add),
             [tabB, C[0]], [GI[0]])
        filler(2)
        yield
        tt(SSM_POST, v3(B_[1][:]), v3(GR[1][:]), cosG, ALU.mult, [GR[0], tabB], [B_[0]])
        tt(SSM_POST, v3(D_[1][:]), v3(GI[1][:]), sinG, ALU.mult, [GI[0], tabB], [D_[0]])
        tt(SSM_POST, v3(A[1][:]), v3(GI[1][:]), cosG, ALU.mult, [GI[0], tabB], [A[0]])
        tt(SSM_POST, v3(C[1][:]), v3(GR[1][:]), sinG, ALU.mult, [GR[0], tabB], [C[0]])
        filler(4)
        yield
        hr, hi = ghb.get(), ghb.get()
        tt("dve", hr[1][:], B_[1][:], D_[1][:], ALU.subtract, [B_[0], D_[0]], [hr[0]])
        filler(1)
        tt("dve", hi[1][:], A[1][:], C[1][:], ALU.add, [A[0], C[0]], [hi[0]])
        filler(1)
        tt("dve", Hr, v3(B_[1][:])[:, :, 127], v3(D_[1][:])[:, :, 127], ALU.subtract, [B_[0], D_[0]], HBs)
        tt("dve", Hi, v3(A[1][:])[:, :, 127], v3(C[1][:])[:, :, 127], ALU.add, [A[0], C[0]] + HBs, HBs)
        yo = yp[1][:, col0:col0 + 128]

        def fn2(e):
            ins = None
            for q in range(4):
                e.matmul(yo, CreT[:, 4 * c + q, :], hr[1][:, q * 128:(q + 1) * 128], start=(q == 0), stop=False)
                ins = e.matmul(yo, CimT[:, 4 * c + q, :], hi[1][:, q * 128:(q + 1) * 128], start=False, stop=(q == 3))
            return ins
        if DELAY_Y:
            pending_y.append(lambda: P.op("pe", fn2, [lhsB, hr[0], hi[0]], [yp[0]]))
        else:
            P.op("pe", fn2, [lhsB, hr[0], hi[0]], [yp[0]])
        yield

    pending_y = []

    def flush_y():
        while pending_y:
            pending_y.pop(0)()

    def front_p(xb_t, xbB_t, tail_fn):
        Tn = T
        s0B, w0 = pre_win[0] if pre_win[0] is not None else fpiece(w_in_b, pc["w_in"], 0, 512)
        pre_win[0] = None
        for c in range(4):
            ps = psum.get()
            mm(ps[1][:, :Tn], [(w0[:, k, c * 128:(c + 1) * 128], xb_t[:, k, :Tn]) for k in range(8)], [s0B, xbB_t], [ps[0]])
            cp("act", u32[:, c, :Tn], ps[1][:, :Tn], [ps[0]], [u32B[c]])
            cp("act", ub[:, c, :Tn], u32[:, c, :Tn], [u32B[c]], [ubB[c]])
            yield
        for half in range(2):
            sB_, wv = fpiece(w_in_b, pc["w_in"], 512 + half * 512, 1024 + half * 512)
            for ci in range(2):
                c = half * 2 + ci
                pa, pg = psum.get(), psum.get()
                mm(pa[1][:, :Tn], [(wv[:, k, ci * 256:ci * 256 + 128], xb_t[:, k, :Tn]) for k in range(8)], [sB_, xbB_t], [pa[0]])
                mm(pg[1][:, :Tn], [(wv[:, k, ci * 256 + 128:ci * 256 + 256], xb_t[:, k, :Tn]) for k in range(8)], [sB_, xbB_t], [pg[0]])
                sg = f32t.get()
                act(sg[1][:, :Tn], pg[1][:, :Tn], AF.Sigmoid, [pg[0]], [sg[0]])
                tt("dve", vbuf[:, c, 30:30 + Tn], pa[1][:, :Tn], sg[1][:, :Tn], ALU.mult, [pa[0], sg[0]], [vbufB[c]])
                yield
        taps = []
        for k in range(31):
            for c in range(4):
                taps.append((c, k))
        tap_i = [0]

        def filler(n):
            for _ in range(n):
                if tap_i[0] >= len(taps):
                    return
                c, k = taps[tap_i[0]]
                tap_i[0] += 1
                o = cacc[:, c, 0:Tn]
                base = PK["conv_w"][0] + c * 31
                ceng = "pool" if c >= 4 - CONV_POOL_CH else "dve"
                if k == 0:
                    ts(ceng, o, vbuf[:, c, 0:Tn], pk[:, base:base + 1], pkc("conv_b", c), ALU.mult, ALU.add, [vbufB[c], pkB], [caccB[c]])
                elif ceng == "dve":
                    stt("dve", o, vbuf[:, c, k:k + Tn], pk[:, base + k:base + k + 1], o, ALU.mult, ALU.add, [vbufB[c], pkB, caccB[c]], [caccB[c]])
                else:
                    tmp_ = ctmp.get()
                    ts("pool", tmp_[1][:], vbuf[:, c, k:k + Tn], pk[:, base + k:base + k + 1], None, ALU.mult, None, [vbufB[c], pkB], [tmp_[0]])
                    tt("pool", o, o, tmp_[1][:], ALU.add, [caccB[c], tmp_[0]], [caccB[c]])

        glist = [(c_, sg_) for c_ in range(4) for sg_ in range(T // TS)]
        bu_next = [ssm_s0(glist[0][0], glist[0][1] * TS)] if GRP else [None]
        gidx = [0]

        def after_s2():
            flush_y()
            gidx[0] += 1
            if gidx[0] < len(glist):
                bu_next[0] = ssm_s0(glist[gidx[0]][0], glist[gidx[0]][1] * TS)

        for c in range(4):
            yp = ypsR.get()
            for sg_ in range(T // TS):
                if GRP:
                    for _ in ssm_group(c, sg_ * TS, yp, filler, bu_next[0], after_s2):
                        yield
                else:
                    for q in range(4):
                        pi = c * 4 + q
                        ssm_segment(pi, sg_ * TS, TS, Hre[:, pi:pi + 1], Him[:, pi:pi + 1], HB[pi], yp, q == 0, q == 3)
                        filler(8)
                        yield
            flush_y()
            zp = f32t.get()
            stt("dve", zp[1][:, :Tn], u32[:, c, :Tn], pkc("ssm_d", c), yp[1][:, :Tn], ALU.mult, ALU.add, [u32B[c], pkB, yp[0]], [zp[0]])
            act(z32[:, c, :Tn], zp[1][:, :Tn], AF.Gelu_apprx_tanh, [zp[0]], [z32B[c]])
            act(zb[:, c, :Tn], zp[1][:, :Tn], AF.Gelu_apprx_tanh, [zp[0]], [zbB[c]])
            yield
        while tap_i[0] < len(taps):
            filler(4)
            yield
        for co in range(4):
            ps = psum.get()
            mm(ps[1][:, :Tn], [(glu_sb[:, k, co * 128:(co + 1) * 128], zb[:, k, :Tn]) for k in range(4)], [gluB] + zbB, [ps[0]])
            sg = f32t.get()
            act(sg[1][:, :Tn], ps[1][:, :Tn], AF.Sigmoid, [ps[0], pkB], [sg[0]], bias=pkc("glu_b", co))
            tt("dve", mixin[:, co, :Tn], z32[:, co, :Tn], sg[1][:, :Tn], ALU.mult, [z32B[co], sg[0]], [mixB[co]])
            yield
        for c in range(4):
            if tail_fn is not None:
                tail_fn(c, Tn)
            cp("pool", vbuf[:, c, 0:30], vbuf[:, c, Tn:Tn + 30], [vbufB[c]], [vbufB[c]])

        def silu_out(c, t_ap, tB, g_ap, b_ap):
            act(mixin[:, 4 + c, :Tn], t_ap, AF.Silu, [tB, pkB], [mixB[4 + c]], scale=g_ap, bias=b_ap)
        pre_win[0] = fpiece(w_in_b, pc["w_in"], 0, 512)
        layer_norm(4, [cacc[:, c, :Tn] for c in range(4)], caccB, ones_c[:], Tn, "cln_g", "cln_b", silu_out)
        yield

    pre_win = [None]
    pre_wout = [None]

    def drain(g):
        for _ in g:
            pass

    def merge(ga, gb):
        da = db = False
        while not (da and db):
            for _ in range(MRA):
                if not da:
                    try:
                        next(ga)
                    except StopIteration:
                        da = True
            for _ in range(MRB):
                if not db:
                    try:
                        next(gb)
                    except StopIteration:
                        db = True

    def run_tile(Tn, xb_t, xbB_t, x32_src, segs, conv_segs, att_segs, y_dst, is_sample=False, part="all"):
        if part in ("all", "front"):
            s0B, w0 = pre_win[0] if pre_win[0] is not None else fpiece(w_in_b, pc["w_in"], 0, 512)
            pre_win[0] = None
            for c in range(4):
                ps = psum.get()
                mm(ps[1][:, :Tn], [(w0[:, k, c * 128:(c + 1) * 128], xb_t[:, k, :Tn]) for k in range(8)], [s0B, xbB_t], [ps[0]])
                cp("act", u32[:, c, :Tn], ps[1][:, :Tn], [ps[0]], [u32B[c]])
                cp("dve", ub[:, c, :Tn], u32[:, c, :Tn], [u32B[c]], [ubB[c]])
            for c in range(4):
                yp = ypsR.get()
                for (col0, L, Hr_fn, Hi_fn, HB_fn) in segs:
                    for q in range(4):
                        pi = c * 4 + q
                        ssm_segment(pi, col0, L, Hr_fn(pi), Hi_fn(pi), HB_fn(pi), yp, q == 0, q == 3)
                    yield
                zp = f32t.get()
                stt("dve", zp[1][:, :Tn], u32[:, c, :Tn], pkc("ssm_d", c), yp[1][:, :Tn], ALU.mult, ALU.add, [u32B[c], pkB, yp[0]], [zp[0]])
                act(z32[:, c, :Tn], zp[1][:, :Tn], AF.Gelu_apprx_tanh, [zp[0]], [z32B[c]])
                act(zb[:, c, :Tn], zp[1][:, :Tn], AF.Gelu_apprx_tanh, [zp[0]], [zbB[c]])
            for co in range(4):
                ps = psum.get()
                mm(ps[1][:, :Tn], [(glu_sb[:, k, co * 128:(co + 1) * 128], zb[:, k, :Tn]) for k in range(4)], [gluB] + zbB, [ps[0]])
                sg = f32t.get()
                act(sg[1][:, :Tn], ps[1][:, :Tn], AF.Sigmoid, [ps[0], pkB], [sg[0]], bias=pkc("glu_b", co))
                tt("dve", mixin[:, co, :Tn], z32[:, co, :Tn], sg[1][:, :Tn], ALU.mult, [z32B[co], sg[0]], [mixB[co]])
            for half in range(2):
                sB_, wv = fpiece(w_in_b, pc["w_in"], 512 + half * 512, 1024 + half * 512)
                for ci in range(2):
                    c = half * 2 + ci
                    pa, pg = psum.get(), psum.get()
                    mm(pa[1][:, :Tn], [(wv[:, k, ci * 256:ci * 256 + 128], xb_t[:, k, :Tn]) for k in range(8)], [sB_, xbB_t], [pa[0]])
                    mm(pg[1][:, :Tn], [(wv[:, k, ci * 256 + 128:ci * 256 + 256], xb_t[:, k, :Tn]) for k in range(8)], [sB_, xbB_t], [pg[0]])
                    sg = f32t.get()
                    act(sg[1][:, :Tn], pg[1][:, :Tn], AF.Sigmoid, [pg[0]], [sg[0]])
                    eng = "dve"
                    if is_sample:
                        vt = f32t.get()
                        tt("dve", vt[1][:, :Tn], pa[1][:, :Tn], sg[1][:, :Tn], ALU.mult, [pa[0], sg[0]], [vt[0]])
                        for (col0, L, halo_fn, tail_fn) in conv_segs:
                            halo_fn(c)
                            cp("pool", vbuf[:, c, 30:30 + L], vt[1][:, col0:col0 + L], [vt[0]], [vbufB[c]])
                            conv_chunk(c, col0, L, eng)
                            tail_fn(c, L)
                    else:
                        (col0, L, halo_fn, tail_fn) = conv_segs[0]
                        tt("dve", vbuf[:, c, 30:30 + Tn], pa[1][:, :Tn], sg[1][:, :Tn], ALU.mult, [pa[0], sg[0]], [vbufB[c]])
                        conv_chunk(c, 0, Tn, eng)
                        if tail_fn is not None:
                            tail_fn(c, Tn)
                        cp("pool", vbuf[:, c, 0:30], vbuf[:, c, Tn:Tn + 30], [vbufB[c]], [vbufB[c]])

            def silu_out(c, t_ap, tB, g_ap, b_ap):
                act(mixin[:, 4 + c, :Tn], t_ap, AF.Silu, [tB, pkB], [mixB[4 + c]], scale=g_ap, bias=b_ap)
            layer_norm(4, [cacc[:, c, :Tn] for c in range(4)], caccB, ones_c[:], Tn, "cln_g", "cln_b", silu_out)
        if part in ("all", "back"):
            dma("sp", R0[:, :, :Tn], x32_src, R0B[0], writes=R0B)
            for half in range(2):
                if half == 0 and pre_wout[0] is not None:
                    sB_, wv = pre_wout[0]
                    pre_wout[0] = None
                else:
                    sB_, wv = wpiece(w_out_b, pc["w_out"], half * 512, half * 512 + 512)
                for ci in range(4):
                    co = half * 4 + ci
                    yield
                    ps = psum.get()
                    mm(ps[1][:, :Tn], [(wv[:, k, ci * 128:(ci + 1) * 128], mixin[:, k, :Tn]) for k in range(8)], [sB_] + mixB, [ps[0]])
                    stt("dve", R0[:, co, :Tn], R0[:, co, :Tn], ALPHA, ps[1][:, :Tn], ALU.mult, ALU.add, [R0B[co], ps[0]], [R0B[co]])
            ln_to_resid(Tn, "ln1_g", "ln1_b")
            qps = []
            for half in range(2):
                sB_, wv = wpiece(w_q_b, pc["w_q"], half * 512, half * 512 + 512)
                for ci in range(4):
                    yield
                    ps = psum.get()
                    mm(ps[1][:, :Tn], [(wv[:, k, ci * 128:(ci + 1) * 128], xnb[:, k, :Tn]) for k in range(8)], [sB_] + xnbB, [ps[0]])
                    act(qb[:, half * 4 + ci, :Tn], ps[1][:, :Tn], AF.Identity, [ps[0]], [qbB[half * 4 + ci]], scale=1.0 / 16.0)
            for (col0, L, kv_loader) in att_segs:
                if kv_loader is not None:
                    kv_loader()
                for h in range(4):
                    pts = []
                    for mc in range(2):
                        yield
                        ps = psum.get()
                        mm(ps[1][:, :L], [(kT[:, h * 2 + dc, mc * 128:(mc + 1) * 128], qb[:, h * 2 + dc, col0:col0 + L]) for dc in range(2)],
                           [kTB, qbB[h * 2], qbB[h * 2 + 1]], [ps[0]])
                        pt = pTr.get()
                        act(pt[1][:, :L], ps[1][:, :L], AF.Exp, [ps[0]], [pt[0]])
                        pts.append(pt)
                    yield
                    ps = psum.get()
                    mm(ps[1][:, :L], [(ones_1[:], pts[mc][1][:, :L]) for mc in range(2)], [onesB, pts[0][0], pts[1][0]], [ps[0]])
                    rinv = f32t.get()
                    recip(rinv[1][:, :L], ps[1][:, :L], [ps[0]], [rinv[0]])
                    for dc in range(2):
                        po = psum.get()
                        mm(po[1][:, :L], [(vv[:, mc, h * 256 + dc * 128: h * 256 + dc * 128 + 128], pts[mc][1][:, :L]) for mc in range(2)],
                           [vvB, pts[0][0], pts[1][0]], [po[0]])
                        tt("dve", ob[:, h * 2 + dc, col0:col0 + L], po[1][:, :L], rinv[1][:, :L], ALU.mult, [po[0], rinv[0]], [obB[h * 2 + dc]])
            for half in range(2):
                sB_, wv = wpiece(w_o_b, pc["w_o"], half * 512, half * 512 + 512)
                for ci in range(4):
                    co = half * 4 + ci
                    yield
                    ps = psum.get()
                    mm(ps[1][:, :Tn], [(wv[:, k, ci * 128:(ci + 1) * 128], ob[:, k, :Tn]) for k in range(8)], [sB_] + obB, [ps[0]])
                    stt("dve", R0[:, co, :Tn], R1[:, co, :Tn], ALPHA, ps[1][:, :Tn], ALU.mult, ALU.add, [R1B[co], ps[0]], [R0B[co]])
            ln_to_resid(Tn, "ln2_g", "ln2_b")
            for piece in range(8):
                sB_, wv = wpiece(w1_b, pc["w1"], piece * 512, piece * 512 + 512)
                for hc in range(4):
                    hidx = piece * 4 + hc
                    yield
                    ps = psum.get()
                    mm(ps[1][:, :Tn], [(wv[:, k, hc * 128:(hc + 1) * 128], xnb[:, k, :Tn]) for k in range(8)], [sB_] + xnbB, [ps[0]])
                    rl = f32t.get()
                    act(rl[1][:, :Tn], ps[1][:, :Tn], AF.Relu, [ps[0], pkB], [rl[0]], bias=pkc("b1", hidx))
                    if SQ_ENG == "act":
                        act(hid[:, hidx, :Tn], rl[1][:, :Tn], AF.Square, [rl[0]], [hidB[hidx]])
                    else:
                        tt("dve", hid[:, hidx, :Tn], rl[1][:, :Tn], rl[1][:, :Tn], ALU.mult, [rl[0]], [hidB[hidx]])
            w2v = w2_b.rearrange("(k p) n -> p k n", p=128)
            for cp_ in range(4):
                yield
                pss = [psum.get(), psum.get()]
                for kh in range(2):
                    sB_, wv = ring_load(w2v[:, kh * 16:kh * 16 + 16, cp_ * 256:cp_ * 256 + 256], 16, 256, pc["w2"])
                    for oc in range(2):
                        for k in range(16):
                            mm1(pss[oc][1][:, :Tn], wv[:, k, oc * 128:(oc + 1) * 128], hid[:, kh * 16 + k, :Tn],
                                kh == 0 and k == 0, kh == 1 and k == 15, [sB_, hidB[kh * 16 + k]], [pss[oc][0]])
                for oc in range(2):
                    co = cp_ * 2 + oc
                    stt("dve", R0[:, co, :Tn], R1[:, co, :Tn], ALPHA, pss[oc][1][:, :Tn], ALU.mult, ALU.add, [R1B[co], pss[oc][0]], [R0B[co]])
                    act(R0[:, co, :Tn], R0[:, co, :Tn], AF.Identity, [R0B[co], pkB], [R0B[co]], bias=pkc("b2", co))
            if not is_sample:
                pre_wout[0] = wpiece(w_out_b, pc["w_out"], 0, 512)
            ln_to_resid(Tn, "ln3_g", "ln3_b")
            dma(YQ, y_dst, R1[:, :, :Tn], R1B[0], reads=R1B, final=True)

    xT_v = xT.rearrange("(k p) t -> p k t", p=128)
    if RUN_KV:
        memTb = xb[1]
        dma("pool", memTb[:, :, 0:256], memT_d.rearrange("(k p) m -> p k m", p=128), xbB[1], writes=[xbB[1]])
        kv_i = [4]
        for (wb, pcb, dst_d, is_k) in ((w_k_b, pc["w_k"], kout_d, True), (w_v_b, pc["w_v"], vout_d, False)):
            for half in range(2):
                sB_, wv = wpiece(wb, pcb, half * 512, half * 512 + 512)
                if is_k and KVD[0] == "1":
                    for ci in range(4):
                        ps = psum.get()
                        mm(ps[1][:, :256], [(wv[:, k, ci * 128:(ci + 1) * 128], memTb[:, k, 0:256]) for k in range(8)], [sB_, xbB[1]], [ps[0]])
                        cp("act", kT[:, half * 4 + ci, :], ps[1][:, :256], [ps[0]], [kTB])
                for mc in range(2 if KVD[1] == "1" else 0):
                    ps = psum.get()
                    if KVD[3:4] == "h":
                        mm(ps[1][:, 0:256], [(memTb[:, k, mc * 128:(mc + 1) * 128], wv[:, k, 0:256]) for k in range(8)], [sB_, xbB[1]], [ps[0]])
                        mm(ps[1][:, 256:512], [(memTb[:, k, mc * 128:(mc + 1) * 128], wv[:, k, 256:512]) for k in range(8)], [sB_, xbB[1]], [ps[0]])
                    else:
                        mm(ps[1][:, :], [(memTb[:, k, mc * 128:(mc + 1) * 128], wv[:, k, :]) for k in range(8)], [sB_, xbB[1]], [ps[0]])
                    stgB, stg = big(kv_i[0])
                    kv_i[0] = 4 + (kv_i[0] - 4 + 1) % 4
                    if KVD[5:6] == "s":
                        for hh in range(2):
                            cp("act", stg[:, hh * 256:(hh + 1) * 256], ps[1][:, hh * 256:(hh + 1) * 256], [ps[0]], stgB)
                            if not is_k:
                                cp("dve", vv[:, mc, half * 512 + hh * 256:half * 512 + (hh + 1) * 256], ps[1][:, hh * 256:(hh + 1) * 256], [ps[0]], [vvB])
                    elif KVD[5:6] == "n":
                        pass
                    elif KVD[5:6] == "a":
                        cp("act", stg, ps[1][:], [ps[0]], stgB)
                    elif KVD[5:6] == "v":
                        cp("dve", stg, ps[1][:], [ps[0]], stgB)
                    else:
                        cp("act", stg, ps[1][:], [ps[0]], stgB)
                        if not is_k:
                            cp("dve", vv[:, mc, half * 512:(half + 1) * 512], stg, stgB, [vvB])
                    if KVD[2] == "1":
                        dma("sp", dst_d[mc * 128:(mc + 1) * 128, half * 512:(half + 1) * 512], stg, stgB[0], reads=stgB, final=True)

    memset("dve", Hre[:], 0.0, HB)
    memset("dve", Him[:], 0.0, HB)
    winuB, winu = fpiece(w_in_b, pc["w_in"], 0, 512)
    NPB = NPRE // T
    xi = [0]

    def load_xb(col):
        i = xi[0] % 2
        xi[0] += 1
        dma("pool", xb[i][:, :, :], xT_v[:, :, col:col + T], xbB[i], writes=[xbB[i]])
        return i

    nxt = load_xb((NPB - NPB_RUN) * T)
    qi = [0]
    pf_i = [0]
    def step_late_pc():
        while late_pc:
            try:
                next(late_pc[0])
                return
            except StopIteration:
                late_pc.pop(0)

    for pb in range(NPB - NPB_RUN, NPB):
        cur = nxt
        step_late_pc()
        nxt = load_xb((pb + 1) * T)
        if PFG2:
            ubks = []
            for blk in range(2):
                pu = psum.get()
                mm(pu[1][:, :], [(xb[cur][:, k, blk * 128:(blk + 1) * 128], winu[:, k, :]) for k in range(8)], [xbB[cur], winuB], [pu[0]])
                ubk = ghb.get()
                cp("act", ubk[1][:], pu[1][:], [pu[0]], [ubk[0]])
                ubks.append(ubk)
            pfl = buR.items + lnR.items
            wr, wi = pfl[(pf_i[0] % 2) * 2], pfl[(pf_i[0] % 2) * 2 + 1]
            pf_i[0] += 1

            def fn(e, ubks=ubks, wr=wr, wi=wi):
                ins = None
                for p_ in range(16):
                    sl_ = slice(p_ * 32, (p_ + 1) * 32)
                    e.matmul(wr[1][:, sl_], P2re[:, p_, :], ubks[0][1][:, sl_], start=True, stop=False)
                    e.matmul(wr[1][:, sl_], Pre[:, p_, :], ubks[1][1][:, sl_], start=False, stop=True)
                    e.matmul(wi[1][:, sl_], P2im[:, p_, :], ubks[0][1][:, sl_], start=True, stop=False)
                    ins = e.matmul(wi[1][:, sl_], Pim[:, p_, :], ubks[1][1][:, sl_], start=False, stop=True)
                return ins
            P.op("pe", fn, [lhsB, ubks[0][0], ubks[1][0]] + p2B, [wr[0], wi[0]])
            blocks_ = [(wr, wi)]
        else:
            blocks_ = None
        for blk in range(T // 128 if not PFG2 else 1):
            if not PFG2:
                pu = psum.get()
                mm(pu[1][:, :], [(xb[cur][:, k, blk * 128:(blk + 1) * 128], winu[:, k, :]) for k in range(8)], [xbB[cur], winuB], [pu[0]])
                ubk = ghb.get()
                cp("act", ubk[1][:], pu[1][:], [pu[0]], [ubk[0]])
                pfl = buR.items + lnR.items
                wr, wi = pfl[(pf_i[0] % 2) * 2], pfl[(pf_i[0] % 2) * 2 + 1]
                pf_i[0] += 1

                def fn(e, ubk=ubk, wr=wr, wi=wi):
                    ins = None
                    for p_ in range(16):
                        e.matmul(wr[1][:, p_ * 32:(p_ + 1) * 32], Pre[:, p_, :], ubk[1][:, p_ * 32:(p_ + 1) * 32], start=True, stop=True)
                        ins = e.matmul(wi[1][:, p_ * 32:(p_ + 1) * 32], Pim[:, p_, :], ubk[1][:, p_ * 32:(p_ + 1) * 32], start=True, stop=True)
                    return ins
                P.op("pe", fn, [lhsB, ubk[0]], [wr[0], wi[0]])
            else:
                wr, wi = blocks_[0]
            base = (qi[0] % 2) * 2
            qi[0] += 1
            (q1B, q1), (q2B, q2) = big(base), big(base + 1)
            (q3B, q3), (q4B, q4) = big(4 + base), big(4 + base + 1)
            tt("dve", q1, wr[1][:], Bxr[:], ALU.mult, [wr[0], BxB], q1B)
            tt("dve", q2, wi[1][:], Bxi[:], ALU.mult, [wi[0], BxB], q2B)
            tt("dve", q3, wi[1][:], Bxr[:], ALU.mult, [wi[0], BxB], q3B)
            tt("dve", q4, wr[1][:], Bxi[:], ALU.mult, [wr[0], BxB], q4B)
            tt(SSM_POST, q1, q1, q2, ALU.subtract, q1B + q2B, q1B)
            tt(SSM_POST, q3, q3, q4, ALU.add, q3B + q4B, q3B)
            sr, si = tiny.get(), tiny.get()
            P.op("dve", (lambda sr, q1: lambda e: e.tensor_reduce(out=sr[1][:], in_=q1.rearrange("p (a b) -> p a b", a=16), axis=AX.X, op=ALU.add))(sr, q1), q1B, [sr[0]])
            P.op("dve", (lambda si, q3: lambda e: e.tensor_reduce(out=si[1][:], in_=q3.rearrange("p (a b) -> p a b", a=16), axis=AX.X, op=ALU.add))(si, q3), q3B, [si[0]])
            u1, u2, u3, u4 = tiny.get(), tiny.get(), tiny.get(), tiny.get()
            tt("dve", u1[1][:], a128r[:], Hre[:], ALU.mult, [a128B] + HB, [u1[0]])
            tt("dve", u2[1][:], a128i[:], Him[:], ALU.mult, [a128B] + HB, [u2[0]])
            tt("dve", u3[1][:], a128r[:], Him[:], ALU.mult, [a128B] + HB, [u3[0]])
            tt("dve", u4[1][:], a128i[:], Hre[:], ALU.mult, [a128B] + HB, [u4[0]])
            tt("dve", u1[1][:], u1[1][:], u2[1][:], ALU.subtract, [u1[0], u2[0]], [u1[0]])
            tt("dve", u3[1][:], u3[1][:], u4[1][:], ALU.add, [u3[0], u4[0]], [u3[0]])
            tt("dve", Hre[:], u1[1][:], sr[1][:], ALU.add, [u1[0], sr[0]], HB)
            tt("dve", Him[:], u3[1][:], si[1][:], ALU.add, [u3[0], si[0]] + HB, HB)

    pre_win[0] = (winuB, winu)
    for g_ in late_pc:
        for _ in g_:
            pass
    hxi = xi[0] % 2
    hx, hxB = xb[hxi], xbB[hxi]
    dma("pool", hx[:, :, 0:32], xT_v[:, :, NPRE - 32:NPRE], hxB, writes=[hxB])
    for half in range(2):
        sB_, wv = wpiece(w_in_b, pc["w_in"], 512 + half * 512, 1024 + half * 512)
        for ci in range(2):
            c = half * 2 + ci
            pa, pg = psum.get(), psum.get()
            mm(pa[1][:, :32], [(wv[:, k, ci * 256:ci * 256 + 128], hx[:, k, 0:32]) for k in range(8)], [sB_, hxB], [pa[0]])
            mm(pg[1][:, :32], [(wv[:, k, ci * 256 + 128:ci * 256 + 256], hx[:, k, 0:32]) for k in range(8)], [sB_, hxB], [pg[0]])
            sg = f32t.get()
            act(sg[1][:, :32], pg[1][:, :32], AF.Sigmoid, [pg[0]], [sg[0]])
            tt("dve", vbuf[:, c, 0:30], pa[1][:, 2:32], sg[1][:, 2:32], ALU.mult, [pa[0], sg[0]], [vbufB[c]])

    xs_v = xsT.rearrange("(k p) t -> p k t", p=128)
    hsinB = Buf("hs_in")

    def s_halo(s_):
        def f(c):
            dma("sp", vbuf[:, c, 0:30], convc_d[:, (c * 2 + s_) * 30:(c * 2 + s_) * 30 + 30], vbufB[c], writes=[vbufB[c]])
        return f

    def s_tail(s_):
        def f(c, L):
            dma("sp", convsout_d[:, (c * 2 + s_) * 30:(c * 2 + s_) * 30 + 30], vbuf[:, c, L:L + 30], vbufB[c], reads=[vbufB[c]], final=True)
        return f

    def s_kv(s_):
        def f():
            dma("pool", kT[:], kTs_d[s_].rearrange("(k p) m -> p k m", p=128), kTB, writes=[kTB])
            dma("pool", vv[:], vs_d[s_].rearrange("(k p) n -> p k n", p=128), vvB, writes=[vvB])
        return f

    def hs_fn(s_, reim):
        return lambda pi: Hs[:, pi * 4 + s_ * 2 + reim: pi * 4 + s_ * 2 + reim + 1]

    s_segs = [(s_ * 16, 16, hs_fn(s_, 0), hs_fn(s_, 1), lambda pi: HsB[pi]) for s_ in range(2)]

    def sample_gen(part, bi):
        if part in ("all", "front"):
            dma("sp", Hs[:], h0_d[:, :], hsinB, writes=HsB)
            dma("pool", xb[bi][:, :, 0:32], xs_v, xbB[bi], writes=[xbB[bi]])
        for _ in run_tile(32, xb[bi], xbB[bi], xs_v, s_segs,
                          [(s_ * 16, 16, s_halo(s_), s_tail(s_)) for s_ in range(2)],
                          [(s_ * 16, 16, s_kv(s_)) for s_ in range(2)],
                          ysT.rearrange("(k p) t -> p k t", p=128), is_sample=True, part=part):
            yield
        if part in ("all", "back"):
            hso = statsb.get()
            cp("dve", hso[1][:, 0:64], Hs[:], HsB, [hso[0]])
            dma("sp", hsout_d[:, :], hso[1][:, 0:64], hso[0], reads=[hso[0]], final=True)

    p_segs = [(s * TS, TS, lambda pi: Hre[:, pi:pi + 1], lambda pi: Him[:, pi:pi + 1], lambda pi: HB[pi]) for s in range(T // TS)]
    yT_v = yT.rearrange("(k p) t -> p k t", p=128)
    cur = nxt

    def p_tail(c, L):
        dma("sp", convout_d[:, c * 30:(c + 1) * 30], vbuf[:, c, L:L + 30], vbufB[c], reads=[vbufB[c]], final=True)

    def xload(it, buf_i):
        dma("pool", xb[buf_i][:, :, :], xT_v[:, :, NPRE + it * T: NPRE + (it + 1) * T], xbB[buf_i], writes=[xbB[buf_i]])

    if NT_RUN > 0:
        if NT_RUN > 1:
            xload(1, 1 - cur)
        drain(front_p(xb[cur], xbB[cur], p_tail if NT_RUN == 1 and NT == 1 else None))
    for it in range(NT_RUN):
        back = run_tile(T, xb[cur], xbB[cur], xT_v[:, :, NPRE + it * T: NPRE + (it + 1) * T], p_segs,
                        [(0, T, None, None)], [(0, T, None)], yT_v[:, :, it * T:(it + 1) * T], part="back")
        if it + 1 < NT_RUN:
            nb = 1 - cur
            fr = front_p(xb[nb], xbB[nb], p_tail if it + 1 == NT - 1 else None)
            if it + 2 < NT_RUN:
                xload(it + 2, cur)
            if PIPE:
                merge(back, fr)
            else:
                drain(back)
                drain(fr)
        else:
            if RUN_SAMPLE and PIPE:
                merge(back, sample_gen("front", 1 - cur))
            else:
                drain(back)
        cur = 1 - cur
    hst, hst2 = tiny.get(), tiny.get()
    cp("dve", hst[1][:], Hre[:], HB, [hst[0]])
    cp("dve", hst2[1][:], Him[:], HB, [hst2[0]])
    dma("sp", hout_d[:, 0:16], hst[1][:], hst[0], reads=[hst[0]], final=True)
    dma("sp", hout_d[:, 16:32], hst2[1][:], hst2[0], reads=[hst2[0]], final=True)

    if RUN_SAMPLE:
        if PIPE and NT_RUN > 0:
            drain(sample_gen("back", cur))
        else:
            drain(sample_gen("all", cur))

    P.emit(st)
    st.close()
    return nc


_NC_CACHE = {}


def _mode_major(a):
    sh = a.shape
    a = a.reshape((16, 2, 64) + sh[2:])
    return np.ascontiguousarray(np.moveaxis(a, 0, 2).reshape((128, 16) + sh[2:]))


def kernel(x_prompt, x_sample, state_ssm_re, state_ssm_im, cache_conv, cache_mem_k, cache_mem_v,
           mem_prompt, w_in, ssm_a_re, ssm_a_im, ssm_log_dt, ssm_b_re, ssm_b_im, ssm_c_re, ssm_c_im,
           ssm_d, glu_w, glu_b, conv_w, conv_b, conv_ln_g, conv_ln_b, w_out, ln1_g, ln1_b,
           mem_w_q, mem_w_k, mem_w_v, mem_w_o, ln2_g, ln2_b,
           mlp_w1, mlp_b1, mlp_w2, mlp_b2, ln3_g, ln3_b):
    f = lambda a: np.ascontiguousarray(np.asarray(a, dtype=np.float32))
    x_prompt, x_sample = f(x_prompt), f(x_sample)
    col = lambda v, n: f(v).reshape(n, 128).T
    pk = np.zeros((128, NPK), np.float32)

    def put(name, arr):
        pk[:, PK[name][0]:PK[name][1]] = arr
    put("ln1_g", col(ln1_g[0], 8)); put("ln1_b", col(ln1_b[0], 8))
    put("ln2_g", col(ln2_g[0], 8)); put("ln2_b", col(ln2_b[0], 8))
    put("ln3_g", col(ln3_g[0], 8)); put("ln3_b", col(ln3_b[0], 8))
    put("b1", col(mlp_b1[0], 32)); put("b2", col(mlp_b2[0], 8))
    put("glu_b", col(glu_b[0], 4)); put("ssm_d", col(ssm_d[0], 4))
    put("conv_b", col(conv_b[0], 4)); put("cln_g", col(conv_ln_g[0], 4)); put("cln_b", col(conv_ln_b[0], 4))
    cw = f(conv_w[0]).T.reshape(4, 128, 31).transpose(1, 0, 2).reshape(128, 124)
    put("conv_w", cw)
    put("a_re", _mode_major(f(ssm_a_re[0])[:, :, None])[:, :, 0])
    put("a_im", _mode_major(f(ssm_a_im[0])[:, :, None])[:, :, 0])
    ldt = np.repeat(f(ssm_log_dt[0])[:, None], 64, axis=1)
    put("logdt", _mode_major(ldt[:, :, None])[:, :, 0])
    put("tidx", np.tile(np.arange(128, dtype=np.float32)[None, :], (128, 1)))
    put("eidx", (127.0 - np.arange(128, dtype=np.float32))[:, None])
    put("eidx2", (255.0 - np.arange(128, dtype=np.float32))[:, None])
    rows = np.stack([f(ssm_a_re[0]).reshape(-1), f(ssm_a_im[0]).reshape(-1), ldt.reshape(-1)]).astype(np.float32)
    BT = np.zeros((2, 128, 2048), np.float32)
    CT = np.zeros((2, 128, 2048), np.float32)
    Bx = np.zeros((2, 128, 512), np.float32)
    for ri, (bsrc, csrc) in enumerate(((f(ssm_b_re[0]), f(ssm_c_re[0])), (f(ssm_b_im[0]), f(ssm_c_im[0])))):
        for g in range(32):
            pi, gp, gl = g // 2, g % 2, g % 8
            BT[ri, gl * 16:(gl + 1) * 16, pi * 128 + gp * 64: pi * 128 + gp * 64 + 64] = bsrc[g].T
            CT[ri, gp * 64:(gp + 1) * 64, pi * 128 + gl * 16: pi * 128 + gl * 16 + 16] = csrc[g].T
            Bx[ri, gp * 64:(gp + 1) * 64, pi * 32 + gp * 16: pi * 32 + gp * 16 + 16] = bsrc[g]
    shared = dict(w_in=f(w_in[0]), glu_w=f(glu_w[0]), w_out=f(w_out[0]), w_q=f(mem_w_q[0]), w_k=f(mem_w_k[0]),
                  w_v=f(mem_w_v[0]), w_o=f(mem_w_o[0]), w1=f(mlp_w1[0]), w2=f(mlp_w2[0]), pk=pk, rows=rows, BT=BT, CT=CT, Bx=Bx)
    xTs = [np.ascontiguousarray(x_prompt[b].T) for b in range(2)]
    in_maps = []
    for c in CORES:
        b, j = c // 4, c % 4
        xin = np.zeros((D, NPRE + SEG), np.float32)
        npre = j * SEG
        xin[:, NPRE - npre: NPRE + SEG] = xTs[b][:, 0:(j + 1) * SEG]
        ss = [2 * c, 2 * c + 1]
        xs = np.concatenate([x_sample[s].T for s in ss], axis=1)
        h0 = np.zeros((128, 16, 2, 2), np.float32)
        cc = np.zeros((128, 4, 2, 30), np.float32)
        for si, s in enumerate(ss):
            h0[:, :, si, 0] = _mode_major(f(state_ssm_re[0, s])[:, :, None])[:, :, 0]
            h0[:, :, si, 1] = _mode_major(f(state_ssm_im[0, s])[:, :, None])[:, :, 0]
            cc[:, :, si, :] = f(cache_conv[0, s]).T.reshape(4, 128, 30).transpose(1, 0, 2)
        kTs = np.stack([f(cache_mem_k[0, s]).reshape(256, D).T for s in ss])
        vs = np.stack([f(cache_mem_v[0, s]).reshape(256, D) for s in ss])
        m = dict(shared)
        m.update(xT=xin, xsT=np.ascontiguousarray(xs), h0=h0.reshape(128, 64), convc=cc.reshape(128, 240),
                 kTs=np.ascontiguousarray(kTs), vs=np.ascontiguousarray(vs), memT=np.ascontiguousarray(f(mem_prompt[b]).T))
        in_maps.append(m)
    if "nc" not in _NC_CACHE:
        _NC_CACHE["nc"] = build_program()
    res = run_bass_kernel_spmd(_NC_CACHE["nc"], in_maps, core_ids=list(range(len(CORES))))
    R = {c: res.results[i] for i, c in enumerate(CORES)}

    def unmode(a):
        return a.reshape(2, 64, 16).transpose(2, 0, 1).reshape(32, 64)
    y_prompt = np.zeros((2, 16384, D), np.float32)
    y_sample = np.zeros((16, 16, D), np.float32)
    p_re = np.zeros((1, 2, 32, 64), np.float32)
    p_im = np.zeros((1, 2, 32, 64), np.float32)
    p_conv = np.zeros((1, 2, 30, 512), np.float32)
    p_mk = np.zeros((1, 2, 256, 4, 256), np.float32)
    p_mv = np.zeros((1, 2, 256, 4, 256), np.float32)
    s_re = np.zeros((1, 16, 32, 64), np.float32)
    s_im = np.zeros((1, 16, 32, 64), np.float32)
    s_conv = np.zeros((1, 16, 30, 512), np.float32)
    for c in CORES:
        b, j = c // 4, c % 4
        r = R[c]
        y_prompt[b, j * SEG:(j + 1) * SEG, :] = r["yT"].T
        ys = r["ysT"].T
        hs = r["hsout"].reshape(128, 16, 2, 2)
        cs = r["convsout"].reshape(128, 4, 2, 30)
        for si in range(2):
            s = 2 * c + si
            y_sample[s] = ys[si * 16:(si + 1) * 16]
            s_re[0, s] = unmode(hs[:, :, si, 0])
            s_im[0, s] = unmode(hs[:, :, si, 1])
            s_conv[0, s] = cs[:, :, si, :].transpose(1, 0, 2).reshape(512, 30).T
        if j == 3:
            ho = r["hout"].reshape(128, 2, 16)
            p_re[0, b] = unmode(ho[:, 0, :])
            p_im[0, b] = unmode(ho[:, 1, :])
            p_conv[0, b] = r["convout"].reshape(128, 4, 30).transpose(1, 0, 2).reshape(512, 30).T
        if j == 0:
            p_mk[0, b] = r["kout"].reshape(256, 4, 256)
            p_mv[0, b] = r["vout"].reshape(256, 4, 256)
    return (y_prompt, y_sample, p_re, p_im, p_conv, p_mk, p_mv, s_re, s_im, s_conv)
```
